# Optimizing a Trainium2 kernel written in Bass

```python
import jax, jax.numpy as jnp
from jax import lax
import numpy as np

D_MODEL = 2048
BATCH = 2
SEQ = 8192
DEPTH = 1

CHUNK = 64
SB_HEADS = 16
SB_HEAD_DIM = 128
SB_WIDTH = SB_HEADS * SB_HEAD_DIM
SB_BLOCK = 128
SSD_EXPAND = 2
SSD_INNER = SSD_EXPAND * D_MODEL
SSD_HEAD_DIM = 64
SSD_HEADS = SSD_INNER // SSD_HEAD_DIM
SSD_GROUPS = 8
SSD_STATE = 128
SSD_CONV = 4
SSD_XBC = SSD_INNER + 2 * SSD_GROUPS * SSD_STATE
SSD_CHUNK = CHUNK
D_FF = 5632
FFN_CONV = 3
N_GATES = 2
SPLITS = (SB_WIDTH, SB_WIDTH, SB_WIDTH, SSD_INNER, SSD_XBC, SSD_HEADS, D_MODEL, D_MODEL)
N_IN = sum(SPLITS)
EPS = 1e-6

kernel_name = "hybrid_sb_ssd_convffn_block"


def rms_norm(x, g):
    xf = x.astype(jnp.float32)
    y = xf * lax.rsqrt(jnp.mean(xf * xf, axis=-1, keepdims=True) + EPS)
    return (y * g.astype(jnp.float32)).astype(x.dtype)


def causal_dwconv(x, w, b):
    k_len = w.shape[0]
    s = x.shape[1]
    xp = jnp.pad(x, ((0, 0), (k_len - 1, 0), (0, 0)))
    y = b
    for k in range(k_len):
        y = y + xp[:, k:k + s] * w[k]
    return y


def stick_breaking_attention(q, k, v):
    b, h, s, dh = q.shape
    nb = s // SB_BLOCK
    scale = dh ** -0.5
    q_blocks = q.reshape(b, h, nb, SB_BLOCK, dh).transpose(2, 0, 1, 3, 4)
    k_pos = jnp.arange(s)

    def block(args):
        q_blk, i = args
        z = jnp.einsum('bhqd,bhkd->bhqk', q_blk, k).astype(jnp.float32) * scale
        q_pos = i * SB_BLOCK + jnp.arange(SB_BLOCK)
        mask = k_pos[None, :] < q_pos[:, None]
        log_keep = jnp.where(mask, jax.nn.log_sigmoid(-z), 0.0)
        after = lax.cumsum(log_keep, axis=3, reverse=True) - log_keep
        w = jnp.where(mask, jnp.exp(jax.nn.log_sigmoid(z) + after), 0.0)
        return jnp.einsum('bhqk,bhkd->bhqd', w.astype(v.dtype), v)

    out = lax.map(block, (q_blocks, jnp.arange(nb)))
    return out.transpose(1, 2, 0, 3, 4).reshape(b, h, s, dh)


def ssd_scan(xh, dt, a, bm, cm):
    b, s, h, p = xh.shape
    g, n = bm.shape[-2:]
    hg = h // g
    nc, l = s // SSD_CHUNK, SSD_CHUNK
    x = (xh * dt[..., None]).reshape(b, nc, l, g, hg, p)
    la = (dt * a).astype(jnp.float32).reshape(b, nc, l, g, hg).transpose(0, 3, 4, 1, 2)
    cs = jnp.cumsum(la, axis=-1)
    bm = bm.reshape(b, nc, l, g, n)
    cm = cm.reshape(b, nc, l, g, n)
    causal = jnp.tril(jnp.ones((l, l), dtype=bool))
    seg = cs[..., :, None] - cs[..., None, :]
    decay = jnp.exp(jnp.where(causal, seg, -jnp.inf))
    cb = jnp.einsum('bclgn,bcsgn->bcgls', cm, bm)
    y_diag = jnp.einsum('bcgls,bghcls,bcsghp->bclghp', cb, decay, x)
    decay_to_end = jnp.exp(cs[..., -1:] - cs)
    states = jnp.einsum('bclgn,bghcl,bclghp->bcghpn', bm, decay_to_end, x)
    chunk_decay = jnp.exp(cs[..., -1])

    def step(carry, inp):
        st, dec = inp
        return carry * dec[..., None, None] + st, carry

    init = jnp.zeros((b, g, hg, p, n), dtype=states.dtype)
    _, prev = lax.scan(step, init, (states.transpose(1, 0, 2, 3, 4, 5), chunk_decay.transpose(3, 0, 1, 2)))
    prev = prev.transpose(1, 0, 2, 3, 4, 5)
    y_off = jnp.einsum('bclgn,bcghpn,bghcl->bclghp', cm, prev, jnp.exp(cs))
    return (y_diag + y_off).reshape(b, s, h, p).astype(xh.dtype)


def setup_inputs(seed: int = 0) -> dict:
    key = jax.random.key(seed)
    ks = jax.random.split(key, 24)
    nrm = lambda k, shape, scale: jax.random.normal(k, shape, jnp.float32) * scale
    gain = lambda k, dim: 1.0 + 0.05 * jax.random.normal(k, (DEPTH, dim), jnp.float32)
    dt0 = jnp.exp(jax.random.uniform(ks[5], (DEPTH, SSD_HEADS), jnp.float32, np.log(1e-3), np.log(1e-1)))
    return {
        "x": nrm(ks[0], (BATCH, SEQ, D_MODEL), 1.0),
        "norm_mix_pre": gain(ks[1], D_MODEL),
        "w_in": nrm(ks[2], (DEPTH, D_MODEL, N_IN), D_MODEL ** -0.5),
        "ssd_conv_w": nrm(ks[3], (DEPTH, SSD_CONV, SSD_XBC), SSD_CONV ** -0.5),
        "ssd_conv_b": nrm(ks[4], (DEPTH, SSD_XBC), 0.01),
        "dt_bias": dt0 + jnp.log(-jnp.expm1(-dt0)),
        "a_log": jnp.log(jax.random.uniform(ks[6], (DEPTH, SSD_HEADS), jnp.float32, 1.0, 16.0)),
        "d_skip": gain(ks[7], SSD_HEADS),
        "ssd_norm": gain(ks[8], SSD_INNER),
        "w_sb_proj": nrm(ks[9], (DEPTH, SB_WIDTH, D_MODEL), SB_WIDTH ** -0.5),
        "w_ssd_proj": nrm(ks[10], (DEPTH, SSD_INNER, D_MODEL), SSD_INNER ** -0.5),
        "w_out": nrm(ks[11], (DEPTH, D_MODEL, D_MODEL), D_MODEL ** -0.5),
        "norm_mix_post": gain(ks[12], D_MODEL),
        "norm_ffn_pre": gain(ks[13], D_MODEL),
        "w_up": nrm(ks[14], (DEPTH, D_MODEL, 2 * D_FF), D_MODEL ** -0.5),
        "ffn_conv_w": nrm(ks[15], (DEPTH, FFN_CONV, 2 * D_FF), FFN_CONV ** -0.5),
        "ffn_conv_b": nrm(ks[16], (DEPTH, 2 * D_FF), 0.01),
        "w_down": nrm(ks[17], (DEPTH, D_FF, D_MODEL), D_FF ** -0.5),
        "norm_ffn_post": gain(ks[18], D_MODEL),
    }


def reference(x, norm_mix_pre, w_in, ssd_conv_w, ssd_conv_b, dt_bias, a_log, d_skip, ssd_norm,
              w_sb_proj, w_ssd_proj, w_out, norm_mix_post, norm_ffn_pre, w_up, ffn_conv_w,
              ffn_conv_b, w_down, norm_ffn_post):
    b, s, _ = x.shape
    cuts = list(np.cumsum(SPLITS)[:-1])
    for i in range(DEPTH):
        h = rms_norm(x, norm_mix_pre[i])
        proj = h @ w_in[i]
        q, k, v, z, xbc, dt, g_sb, g_ssd = jnp.split(proj, cuts, axis=-1)
        to_heads = lambda t: t.reshape(b, s, SB_HEADS, SB_HEAD_DIM).transpose(0, 2, 1, 3)
        o_sb = stick_breaking_attention(to_heads(q), to_heads(k), to_heads(v))
        o_sb = o_sb.transpose(0, 2, 1, 3).reshape(b, s, SB_WIDTH)
        xbc = jax.nn.silu(causal_dwconv(xbc, ssd_conv_w[i], ssd_conv_b[i]))
        xs, bm, cm = jnp.split(xbc, [SSD_INNER, SSD_INNER + SSD_GROUPS * SSD_STATE], axis=-1)
        dt = jax.nn.softplus(dt + dt_bias[i])
        a = -jnp.exp(a_log[i])
        xh = xs.reshape(b, s, SSD_HEADS, SSD_HEAD_DIM)
        y = ssd_scan(xh, dt, a,
                     bm.reshape(b, s, SSD_GROUPS, SSD_STATE),
                     cm.reshape(b, s, SSD_GROUPS, SSD_STATE))
        y = (y + xh * d_skip[i][:, None]).reshape(b, s, SSD_INNER) * jax.nn.silu(z)
        y = rms_norm(y.reshape(b, s, SSD_GROUPS, SSD_INNER // SSD_GROUPS),
                     ssd_norm[i].reshape(SSD_GROUPS, SSD_INNER // SSD_GROUPS)).reshape(b, s, SSD_INNER)
        merged = (jax.nn.sigmoid(g_sb) * (o_sb @ w_sb_proj[i])
                  + jax.nn.sigmoid(g_ssd) * (y @ w_ssd_proj[i]))
        x = x + rms_norm(merged @ w_out[i], norm_mix_post[i])
        h = rms_norm(x, norm_ffn_pre[i])
        u = causal_dwconv(h @ w_up[i], ffn_conv_w[i], ffn_conv_b[i])
        gate, val = jnp.split(u, [D_FF], axis=-1)
        f = (jax.nn.gelu(gate, approximate=True) * val) @ w_down[i]
        x = x + rms_norm(f, norm_ffn_post[i])
    return x
```

```python
import bisect
import os
from contextlib import ExitStack

import numpy as np
import concourse.bass as bass
import concourse.mybir as mybir
from concourse.bass_utils import run_bass_kernel_spmd

F32 = mybir.dt.float32
BF16 = mybir.dt.bfloat16
AF = mybir.ActivationFunctionType
ALU = mybir.AluOpType

D = 2048
WIN = 8192
NT = 16
Q0L = 6016
NQ = 2176
EPS = 1e-6
SCALE = 128 ** -0.5
D_FF = 5632
NFC = 44
QBS = [(0, 128)] + [(128 + 512 * i, 512) for i in range(4)]


class Prog:
    def __init__(self):
        self.ops = []
        self.bars = []

    def barrier(self):
        self.bars.append(len(self.ops))

    def add(self, eng, fn, R=(), W=(), dk=None):
        self.ops.append((eng, fn, tuple(R), tuple(W), dk))

    def emit(self, nc, es):
        ops = self.ops
        n = len(ops)
        lastw, readers = {}, {}
        deps = []
        for i, (eng, fn, R, W, dk) in enumerate(ops):
            d = set()
            for r in R:
                j = lastw.get(r)
                if j is not None:
                    d.add(j)
            for w in W:
                j = lastw.get(w)
                if j is not None:
                    d.add(j)
                rs = readers.get(w)
                if rs:
                    d.update(rs)
            for r in R:
                readers.setdefault(r, []).append(i)
            for w in W:
                lastw[w] = i
                readers[w] = []
            d.discard(i)
            deps.append(d)
        sig = [False] * n
        cdeps = []
        dma_idx = {}
        for i, op in enumerate(ops):
            if op[4] is not None:
                dma_idx.setdefault(op[4], []).append(i)
        for i, d in enumerate(deps):
            eng = ops[i][0]
            best = {}
            dks = set()
            for j in d:
                oj = ops[j]
                if oj[4] is not None:
                    dks.add(oj[4])
                    continue
                if oj[0] == 'pe' and eng == 'pe':
                    continue
                if best.get(oj[0], -1) < j:
                    best[oj[0]] = j
            for j in best.values():
                sig[j] = True
            cdeps.append((best, dks))
        eng_ops = {}
        for i, op in enumerate(ops):
            if op[4] is None:
                eng_ops.setdefault(op[0], []).append(i)
        for p in self.bars:
            for E in ('pe', 'act', 'dve', 'pool', 'sp'):
                first = next((i for i in range(p, n) if ops[i][0] == E), None)
                if first is None:
                    continue
                best, dks = cdeps[first]
                for E2, lst in eng_ops.items():
                    if E2 == 'pe' and E == 'pe':
                        continue
                    pos = bisect.bisect_left(lst, p)
                    if pos > 0:
                        j = lst[pos - 1]
                        if best.get(E2, -1) < j:
                            best[E2] = j
                            sig[j] = True
                for k, lst in dma_idx.items():
                    if lst and lst[0] < p:
                        dks.add(k)
        seq = [0] * n
        cnt = {}
        for i, op in enumerate(ops):
            if op[4] is None and sig[i]:
                cnt[op[0]] = cnt.get(op[0], 0) + 1
                seq[i] = cnt[op[0]]
        esem = {e: es.enter_context(nc.semaphore("se_" + e)) for e in ('pe', 'act', 'dve', 'pool')}
        dsem = {}
        for k in dma_idx:
            dsem[k] = es.enter_context(nc.semaphore("sd_%d" % len(dsem)))
        block = es.enter_context(nc.Block())

        def run(ename):
            def body(e):
                waited = {}
                for i, (eng, fn, R, W, dk) in enumerate(ops):
                    if eng != ename:
                        continue
                    best, dks = cdeps[i]
                    for se, j in best.items():
                        key = ('e', se)
                        if waited.get(key, 0) < seq[j]:
                            e.wait_ge(esem[se], seq[j])
                            waited[key] = seq[j]
                    for k in dks:
                        val = 16 * bisect.bisect_left(dma_idx[k], i)
                        key = ('d', k)
                        if waited.get(key, 0) < val:
                            e.wait_ge(dsem[k], val)
                            waited[key] = val
                    ins = fn(e)
                    if dk is not None:
                        ins.then_inc(dsem[dk], 16)
                    elif sig[i]:
                        ins.then_inc(esem[eng], 1)
                if ename == 'sp':
                    for k, lst in dma_idx.items():
                        e.wait_ge(dsem[k], 16 * len(lst))
            return body

        block.tensor(run('pe'))
        block.scalar(run('act'))
        block.vector(run('dve'))
        block.gpsimd(run('pool'))
        block.sync(run('sp'))


def build_nc():
    nc = bass.Bass("TRN2", target_bir_lowering=False)
    P = Prog()

    def din(name, shape, dt=F32):
        return nc.dram_tensor(name, list(shape), dt, kind="ExternalInput").ap()

    xw = din("xw", [D, WIN])
    WA = din("WA", [16, 128, 16 * 384])
    WB = din("WB", [8, 128, 16 * 1288])
    W2A = din("W2A", [16, 128, 48 * 128])
    W2B = din("W2B", [16, 128, 32 * 128])
    WO = din("WO", [16, 128, 16 * 128])
    WU = din("WU", [NFC, 128, 16 * 256])
    WD = din("WD", [16, 128, NFC * 128])
    consts_d = din("consts", [128, 6 * 128])
    gains_d = din("gains", [128, 64])
    sconvw_d = din("sconvw", [128, 48 * 4])
    sconvb_d = din("sconvb", [128, 48])
    fconvw_d = din("fconvw", [128, 88 * 3])
    fconvb_d = din("fconvb", [128, 88])
    hvec_d = din("hvec", [128, 3 * 64])
    ssdn_d = din("ssdn", [128, 4096])
    valid_d = din("valid", [128, 64])
    hval_d = din("hval", [128, 1])
    outT = nc.dram_tensor("outT", [D, 2048], F32, kind="ExternalOutput").ap()
    skind = "ExternalOutput" if os.environ.get("KDEBUG") else "Internal"
    hT_d = nc.dram_tensor("hT_d", [D, WIN], BF16, kind=skind).ap()
    osb_d = nc.dram_tensor("osb_d", [D, NQ], BF16, kind=skind).ap()
    y_d = nc.dram_tensor("y_d", [4096, NQ], BF16, kind=skind).ap()

    xw_v = xw.rearrange("(k p) t -> p k t", p=128)
    hT_v = hT_d.rearrange("(k p) t -> p k t", p=128)
    osb_v = osb_d.rearrange("(k p) t -> p k t", p=128)
    y_v = y_d.rearrange("(k p) t -> p k t", p=128)
    out_v = outT.rearrange("(k p) t -> p k t", p=128)

    es = ExitStack()

    def sb(name, shape, dt=F32):
        return es.enter_context(nc.sbuf_tensor("s_" + name, list(shape), dt))

    ps = [es.enter_context(nc.psum_tensor("ps%d" % i, [128, 512], F32)) for i in range(8)]

    def mm(out, lhsT, rhs, start, stop, R, W):
        P.add('pe', lambda e: e.matmul(out, lhsT, rhs, start=start, stop=stop), R, W)

    def tr(out, in_, ident, R, W):
        P.add('pe', lambda e: e.transpose(out, in_, ident), R, W)

    def act(out, in_, func, R, W, scale=1.0, bias=0.0, accum=None):
        if accum is None:
            P.add('act', lambda e: e.activation(out=out, in_=in_, func=func, bias=bias, scale=scale), R, W)
        else:
            P.add('act', lambda e: e.activation(out=out, in_=in_, func=func, bias=bias, scale=scale,
                                                accum_out=accum), R, W)

    def tt(eng, out, a, b, op, R, W):
        P.add(eng, lambda e: e.tensor_tensor(out=out, in0=a, in1=b, op=op), R, W)

    def ts(eng, out, a, s1, s2, op0, op1, R, W):
        P.add(eng, lambda e: e.tensor_scalar(out=out, in0=a, scalar1=s1, scalar2=s2, op0=op0, op1=op1), R, W)

    def ts1(eng, out, a, s1, op0, R, W):
        P.add(eng, lambda e: e.tensor_single_scalar(out=out, in_=a, scalar=s1, op=op0), R, W)

    def stt(eng, out, a, s, b, op0, op1, R, W):
        eng = 'dve'
        P.add(eng, lambda e: e.scalar_tensor_tensor(out=out, in0=a, scalar=s, in1=b, op0=op0, op1=op1), R, W)

    def cp(eng, out, a, R, W):
        if eng == 'act':
            act(out, a, AF.Copy, R, W)
        else:
            P.add(eng, lambda e: e.tensor_copy(out=out, in_=a), R, W)

    def recip(out, a, R, W):
        P.add('dve', lambda e: e.reciprocal(out=out, in_=a), R, W)

    def mset(eng, out, val, W):
        P.add(eng, lambda e: e.memset(out, val), (), W)

    def dma(q, out, in_, R, W, dk, cast=False):
        if cast:
            P.add(q, lambda e: e.dma_start(out=out, in_=in_, max_dma_last_dim=4096), R, W, dk)
        else:
            P.add(q, lambda e: e.dma_start(out=out, in_=in_), R, W, dk)

    cf = sb("cf", [128, 6, 128])
    cb = sb("cb", [128, 6, 128], BF16)
    gains = sb("gains", [128, 4, 16])
    sconvw = sb("sconvw", [128, 48, 4])
    sconvb = sb("sconvb", [128, 48])
    fconvw = sb("fconvw", [128, 88, 3])
    fconvb = sb("fconvb", [128, 88])
    hvec = sb("hvec", [128, 3, 64])
    aneg = sb("aneg", [128, 64])
    valid = sb("valid", [128, 64])
    hval = sb("hval", [128, 1])
    dma('sp', cf[:].rearrange("p a b -> p (a b)"), consts_d[:, :], (), ['cf'], 'c0')
    dma('sp', gains[:].rearrange("p a b -> p (a b)"), gains_d[:, :], (), ['gains'], 'c0')
    dma('sp', sconvw[:].rearrange("p a b -> p (a b)"), sconvw_d[:, :], (), ['sconv'], 'c0')
    dma('sp', sconvb[:], sconvb_d[:, :], (), ['sconv'], 'c0')
    dma('sp', fconvw[:].rearrange("p a b -> p (a b)"), fconvw_d[:, :], (), ['fconv'], 'c0')
    dma('sp', fconvb[:], fconvb_d[:, :], (), ['fconv'], 'c0')
    dma('sp', hvec[:].rearrange("p a b -> p (a b)"), hvec_d[:, :], (), ['hvec'], 'c0')
    dma('sp', valid[:], valid_d[:, :], (), ['valid'], 'c0')
    dma('sp', hval[:], hval_d[:, :], (), ['hval'], 'c0')
    cp('dve', cb[:], cf[:], ['cf'], ['cb'])
    act(aneg[:], hvec[:, 1, :], AF.Exp, ['hvec'], ['aneg'])
    ts1('dve', aneg[:], aneg[:], -1.0, ALU.mult, ['aneg'], ['aneg'])
    LT, LE, GE, GT, ONES, IDN = range(6)

    hb = [sb("hb%d" % i, [128, 16, 512], BF16) for i in range(2)]
    hcnt = [0]

    def load_h(t0, n=512):
        s = hcnt[0] % 2
        hcnt[0] += 1
        dma('sp', hb[s][:, :, 0:n], hT_v[:, :, t0:t0 + n], [('hTd', t0 // 512)], [('hb', s)], ('hbld', s))
        return s

    pcnt = [0]

    def pbank():
        pcnt[0] += 1
        return pcnt[0] % 2

    with ExitStack() as s0:
        xt = [s0.enter_context(nc.sbuf_tensor("xt%d" % i, [128, 16, 512], F32)) for i in range(2)]
        sq = s0.enter_context(nc.sbuf_tensor("sq", [128, 16, 512], BF16))
        rs = s0.enter_context(nc.sbuf_tensor("rs0", [128, 512], F32))
        for t in range(NT):
            xs = t % 2
            hs = t % 2
            dma('sp', xt[xs][:], xw_v[:, :, t * 512:(t + 1) * 512], (), [('xt', xs)], ('xt', xs))
            act(sq[:], xt[xs][:], AF.Square, [('xt', xs)], ['sq'])
            b = pbank()
            for k in range(16):
                mm(ps[b][:, :], cb[:, ONES, :], sq[:, k, :], k == 0, k == 15, ['sq', 'cb'], [('ps', b)])
            act(rs[:], ps[b][:, :], AF.Sqrt, [('ps', b)], ['rs'], scale=1.0 / D, bias=EPS)
            recip(rs[:], rs[:], ['rs'], ['rs'])
            for k in range(16):
                stt('dve', hb[hs][:, k, :], xt[xs][:, k, :], gains[:, 0, k:k + 1], rs[:], ALU.mult, ALU.mult,
                    [('xt', xs), 'rs', 'gains'], [('hb', hs)])
            dma('sp', hT_v[:, :, t * 512:(t + 1) * 512], hb[hs][:], [('hb', hs)], [('hTd', t)], ('hbst', hs))
        hcnt[0] = 0

    P.barrier()
    with ExitStack() as sa:
        def sba(name, shape, dt=F32):
            return sa.enter_context(nc.sbuf_tensor(name, list(shape), dt))
        wq = [sba("wq%d" % i, [128, 16, 384], BF16) for i in range(2)]
        kT = [sba("kT%d" % i, [128, WIN], BF16) for i in range(2)]
        vv = [sba("vv%d" % i, [128, 64, 128], BF16) for i in range(2)]
        qT = [sba("qT%d" % i, [128, NQ], BF16) for i in range(2)]
        eb = [sba("eb%d" % i, [128, 512], F32) for i in range(2)]
        spb = [sba("spb%d" % i, [128, 512], BF16) for i in range(2)]
        gb = [sba("gb%d" % i, [128, 512], F32) for i in range(2)]
        wb = [sba("wb%d" % i, [128, 512], BF16) for i in range(2)]
        ob = [sba("ob%d" % i, [128, 512], BF16) for i in range(2)]
        ocnt = 0
        for h in range(16):
            ws = h % 2
            dma('pool', wq[ws][:].rearrange("p k c -> p (k c)"), WA[h, :, :], (), [('wq', ws)], ('wq', ws), cast=True)
            for t in range(NT):
                hs = load_h(t * 512)
                b = pbank()
                for k in range(16):
                    mm(ps[b][:, :], wq[ws][:, k, 128:256], hb[hs][:, k, :], k == 0, k == 15,
                       [('wq', ws), ('hb', hs)], [('ps', b)])
                cp('dve', kT[ws][:, t * 512:(t + 1) * 512], ps[b][:, :], [('ps', b)], [('kT', ws, t)])
                b = pbank()
                for s in range(4):
                    for k in range(16):
                        mm(ps[b][:, s * 128:(s + 1) * 128], hb[hs][:, k, s * 128:(s + 1) * 128], wq[ws][:, k, 256:384],
                           k == 0, k == 15, [('wq', ws), ('hb', hs)], [('ps', b)])
                cp('dve', vv[ws][:, t * 4:(t + 1) * 4, :].rearrange("p a b -> p (a b)"), ps[b][:, :],
                   [('ps', b)], [('vv', ws, t)])
                if t >= 11:
                    c0 = 384 if t == 11 else 0
                    n = 512 - c0
                    qc0 = t * 512 + c0 - Q0L
                    b = pbank()
                    for k in range(16):
                        mm(ps[b][:, 0:n], wq[ws][:, k, 0:128], hb[hs][:, k, c0:512], k == 0, k == 15,
                           [('wq', ws), ('hb', hs)], [('ps', b)])
                    cp('dve', qT[ws][:, qc0:qc0 + n], ps[b][:, 0:n], [('ps', b)], [('qT', ws)])
            if os.environ.get("KDEBUG") and h == 0:
                dk_ = nc.dram_tensor("dbg_k", [128, WIN], BF16, kind="ExternalOutput").ap()
                dv_ = nc.dram_tensor("dbg_v", [128, WIN], BF16, kind="ExternalOutput").ap()
                dq_ = nc.dram_tensor("dbg_q", [128, NQ], BF16, kind="ExternalOutput").ap()
                dma('sp', dk_[:, :], kT[0][:, :], [('kT', 0, t) for t in range(16)], ['dbgk'], 'dbg')
                dma('sp', dv_[:, :], vv[0][:].rearrange("p a b -> p (a b)"), [('vv', 0, t) for t in range(16)], ['dbgv'], 'dbg')
                dma('sp', dq_[:, :], qT[0][:, :], [('qT', 0)], ['dbgq'], 'dbg')
                dw_ = nc.dram_tensor("dbg_w", [128, 6144], BF16, kind="ExternalOutput").ap()
                dma('sp', dw_[:, :], wq[0][:].rearrange("p k c -> p (k c)"), [('wq', 0)], ['dbgw'], 'dbg')
            for (q0, nq) in QBS:
                gq0 = Q0L + q0
                blocks = list(range((gq0 + nq) // 128 - 1, -1, -1))
                nb = len(blocks)

                def geom(kb):
                    m = kb - gq0 // 128
                    c0 = 128 * max(m, 0)
                    return m >= 0, c0

                def zmm(i):
                    kb = blocks[i]
                    _, c0 = geom(kb)
                    zb = 2 + i % 2
                    mm(ps[zb][:, c0:nq], kT[ws][:, kb * 128:(kb + 1) * 128], qT[ws][:, q0 + c0:q0 + nq], True, True,
                       [('kT', ws, kb // 4), ('qT', ws)], [('ps', zb)])
                zmm(0)
                for i, kb in enumerate(blocks):
                    if i + 1 < nb:
                        zmm(i + 1)
                    diag, c0 = geom(kb)
                    zb = 2 + i % 2
                    s_ = i % 2
                    act(eb[s_][:, c0:nq], ps[zb][:, c0:nq], AF.Exp, [('ps', zb)], [('eb', s_)], scale=SCALE)
                    if diag:
                        tt('dve', eb[s_][:, c0:c0 + 128], eb[s_][:, c0:c0 + 128], cf[:, LT, :], ALU.mult,
                           [('eb', s_), 'cf'], [('eb', s_)])
                    act(spb[s_][:, c0:nq], eb[s_][:, c0:nq], AF.Ln, [('eb', s_)], [('spb', s_)], bias=1.0)
                    mm(ps[4][:, c0:nq], cb[:, GE, :], spb[s_][:, c0:nq], i == 0, False, [('spb', s_), 'cb'], [('ps', 4)])
                    act(gb[s_][:, c0:nq], ps[4][:, c0:nq], AF.Exp, [('ps', 4)], [('gb', s_)], scale=-1.0)
                    tt('pool', wb[s_][:, c0:nq], eb[s_][:, c0:nq], gb[s_][:, c0:nq], ALU.mult,
                       [('eb', s_), ('gb', s_)], [('wb', s_)])
                    mm(ps[5][:, c0:nq], vv[ws][:, kb, :], wb[s_][:, c0:nq], i == 0, i == nb - 1,
                       [('wb', s_), ('vv', ws, kb // 4)], [('ps', 5)])
                    mm(ps[4][:, c0:nq], cb[:, LT, :], spb[s_][:, c0:nq], False, i == nb - 1,
                       [('spb', s_), 'cb'], [('ps', 4)])
                os_ = ocnt % 2
                ocnt += 1
                cp('dve', ob[os_][:, 0:nq], ps[5][:, 0:nq], [('ps', 5)], [('ob', os_)])
                dma('sp', osb_d[h * 128:(h + 1) * 128, q0:q0 + nq], ob[os_][:, 0:nq], [('ob', os_)], ['osb_d'],
                    ('obst', os_))

    P.barrier()
    with ExitStack() as sbx:
        def sbb(name, shape, dt=F32):
            return sbx.enter_context(nc.sbuf_tensor(name, list(shape), dt))
        wB = sbb("wB", [128, 16, 1288], BF16)
        pre = sbb("pre", [128, 6, 515])
        acc = sbb("acc", [128, 6, 512])
        xa = sbb("xa", [128, 4, 512])
        BT = sbb("BT", [128, 512], BF16)
        CT = sbb("CT", [128, 512], BF16)
        nrm = sbb("nrm", [128, 512])
        S = sbb("S", [128, 512])
        Sb = sbb("Sb", [128, 512], BF16)
        dts = [sbb("dts%d" % i, [128, 32]) for i in range(2)]
        Eb = [sbb("Eb%d" % i, [128, 24]) for i in range(2)]
        rseg = [sbb("rseg%d" % i, [128, 1024]) for i in range(2)]
        dec = [sbb("dec%d" % i, [128, 1024]) for i in range(2)]
        Gb = [sbb("Gb%d" % i, [128, 1024], BF16) for i in range(2)]
        CBm = [sbb("CBm%d" % i, [128, 128]) for i in range(2)]
        xp = [sbb("xp%d" % i, [128, 512], BF16) for i in range(2)]
        xpp = [sbb("xpp%d" % i, [128, 512], BF16) for i in range(2)]
        xd = [sbb("xd%d" % i, [128, 512]) for i in range(2)]
        Btok = [sbb("Btok%d" % i, [128, 128], BF16) for i in range(2)]
        zs = [sbb("zs%d" % i, [128, 512]) for i in range(2)]
        y1 = [sbb("y1%d" % i, [128, 512]) for i in range(2)]
        y2 = [sbb("y2%d" % i, [128, 512]) for i in range(2)]
        junk = sbb("junk", [128, 512])
        st1 = [sbb("st1%d" % i, [128, 2]) for i in range(2)]
        yn = [sbb("yn%d" % i, [128, 512], BF16) for i in range(2)]
        yTs = [sbb("yTs%d" % i, [128, 4, 128], BF16) for i in range(2)]

        def v3(ap, h):
            return ap.rearrange("p (h l) -> p h l", h=h)

        for g in range(8):
            dma('pool', wB[:].rearrange("p k c -> p (k c)"), WB[g, :, :], (), ['wB'], 'wB', cast=True)
            dma('sp', nrm[:], ssdn_d[:, g * 512:(g + 1) * 512], (), ['nrm'], 'nrm')
            mset('dve', S[:], 0.0, ['S'])
            mset('dve', Sb[:], 0.0, ['Sb'])
            for t in range(NT):
                hs = load_h(t * 512)
                if t == 0:
                    mset('dve', pre[:, :, 0:3], 0.0, ['pre'])
                else:
                    cp('dve', pre[:, :, 0:3], pre[:, :, 512:515], ['pre'], ['pre'])
                for c in range(6):
                    b = pbank()
                    w0 = c * 128 if c < 4 else 512 + (c - 4) * 128
                    for k in range(16):
                        mm(ps[b][:, :], wB[:, k, w0:w0 + 128], hb[hs][:, k, :], k == 0, k == 15,
                           ['wB', ('hb', hs)], [('ps', b)])
                    cp('act', pre[:, c, 3:515], ps[b][:, :], [('ps', b)], ['pre'])
                for c in range(6):
                    ch = g * 4 + c if c < 4 else (32 + g if c == 4 else 40 + g)
                    eng = 'dve'
                    ts(eng, acc[:, c, :], pre[:, c, 0:512], sconvw[:, ch, 0:1], sconvb[:, ch:ch + 1], ALU.mult, ALU.add,
                       ['pre', 'sconv'], [('acc', c)])
                    for kk in range(1, 4):
                        stt(eng, acc[:, c, :], pre[:, c, kk:kk + 512], sconvw[:, ch, kk:kk + 1], acc[:, c, :],
                            ALU.mult, ALU.add, ['pre', 'sconv', ('acc', c)], [('acc', c)])
                    if c < 4:
                        act(xa[:, c, :], acc[:, c, :], AF.Silu, [('acc', c)], ['xa'])
                    elif c == 4:
                        act(BT[:], acc[:, c, :], AF.Silu, [('acc', c)], ['BT'])
                    else:
                        act(CT[:], acc[:, c, :], AF.Silu, [('acc', c)], ['CT'])
                for s in range(4):
                    ci = t * 4 + s
                    outc = ci >= 47
                    q = ci % 2
                    cs = slice(s * 128, (s + 1) * 128)
                    dt_, la_, tmp_, dtE_ = dts[q][:, 0:8], dts[q][:, 8:16], dts[q][:, 16:24], dts[q][:, 24:32]
                    for k in range(16):
                        mm(ps[3][:, 0:8], hb[hs][:, k, cs], wB[:, k, 1280:1288], k == 0, k == 15,
                           ['wB', ('hb', hs)], [('ps', 3)])
                    tt('dve', tmp_, ps[3][:, 0:8], hvec[:, 0, g * 8:(g + 1) * 8], ALU.add, [('ps', 3), 'hvec'], [('dts', q)])
                    act(tmp_, tmp_, AF.Exp, [('dts', q)], [('dts', q)])
                    act(tmp_, tmp_, AF.Ln, [('dts', q)], [('dts', q)], bias=1.0)
                    ts1('dve', dt_, tmp_, valid[:, ci:ci + 1], ALU.mult, [('dts', q), 'valid'], [('dts', q)])
                    tt('dve', la_, dt_, aneg[:, g * 8:(g + 1) * 8], ALU.mult, [('dts', q), 'aneg'], [('dts', q)])
                    mm(ps[3][:, 8:16], cf[:, LE, :], la_, True, True, [('dts', q), 'cf'], [('ps', 3)])
                    mm(ps[3][:, 16:24], cf[:, GT, :], la_, True, True, [('dts', q), 'cf'], [('ps', 3)])
                    mm(ps[3][:, 24:32], cf[:, ONES, :], la_, True, True, [('dts', q), 'cf'], [('ps', 3)])
                    act(Eb[q][:], ps[3][:, 8:32], AF.Exp, [('ps', 3)], [('Eb', q)])
                    Ecs, Edte, Etot = Eb[q][:, 0:8], Eb[q][:, 8:16], Eb[q][:, 16:24]
                    tt('dve', dtE_, dt_, Edte, ALU.mult, [('dts', q), ('Eb', q)], [('dts', q)])
                    if outc:
                        tt('dve', v3(rseg[q][:], 8), cf[:, LE, :].unsqueeze(1).to_broadcast([128, 8, 128]),
                           la_.unsqueeze(2).to_broadcast([128, 8, 128]), ALU.mult, [('dts', q), 'cf'], [('rseg', q)])
                        for hh in range(2):
                            mm(ps[4 + hh][:, :], cf[:, GT, :], rseg[q][:, hh * 512:(hh + 1) * 512], True, True,
                               [('rseg', q), 'cf'], [('ps', 4 + hh)])
                            act(dec[q][:, hh * 512:(hh + 1) * 512], ps[4 + hh][:, :], AF.Exp, [('ps', 4 + hh)], [('dec', q)])
                    for c in range(4):
                        tr(ps[6][:, c * 128:(c + 1) * 128], xa[:, c, cs], cf[:, IDN, :], ['xa', 'cf'], [('ps', 6)])
                    x3 = v3(ps[6][:, :], 8)
                    tt('dve', v3(xp[q][:], 8), x3, dt_.unsqueeze(2).to_broadcast([128, 8, 64]), ALU.mult,
                       [('ps', 6), ('dts', q)], [('xp', q)])
                    tt('dve', v3(xpp[q][:], 8), x3, dtE_.unsqueeze(2).to_broadcast([128, 8, 64]), ALU.mult,
                       [('ps', 6), ('dts', q)], [('xpp', q)])
                    if outc:
                        tt('dve', v3(xd[q][:], 8), x3,
                           hvec[:, 2, g * 8:(g + 1) * 8].unsqueeze(2).to_broadcast([128, 8, 64]), ALU.mult,
                           [('ps', 6), 'hvec'], [('xd', q)])
                    pbb = ps[7][:, 0:64].bitcast(BF16)
                    tr(pbb, BT[:, cs], cb[:, IDN, :], ['BT', 'cb'], [('ps', 7)])
                    cp('act', Btok[q][:], pbb, [('ps', 7)], [('Btok', q)])
                    if outc:
                        col0 = ci * 128 - Q0L
                        mm(ps[7][:, 128:256], BT[:, cs], CT[:, cs], True, True, ['BT', 'CT'], [('ps', 7)])
                        tt('dve', CBm[q][:], ps[7][:, 128:256], cf[:, LE, :], ALU.mult, [('ps', 7), 'cf'], [('CBm', q)])
                        tt('pool', v3(Gb[q][:], 8), v3(dec[q][:], 8), CBm[q][:].unsqueeze(1).to_broadcast([128, 8, 128]),
                           ALU.mult, [('dec', q), ('CBm', q)], [('Gb', q)])
                        for hh in range(8):
                            mm(ps[0][:, hh * 64:(hh + 1) * 64], Gb[q][:, hh * 128:(hh + 1) * 128], xp[q][:, hh * 64:(hh + 1) * 64],
                               True, True, [('Gb', q), ('xp', q)], [('ps', 0)])
                        mm(ps[1][:, :], CT[:, cs], Sb[:], True, True, ['CT', 'Sb'], [('ps', 1)])
                        tt('dve', v3(y1[q][:], 8), v3(ps[1][:, :], 8), Ecs.unsqueeze(2).to_broadcast([128, 8, 64]), ALU.mult,
                           [('ps', 1), ('Eb', q)], [('y1', q)])
                        tt('dve', y1[q][:], y1[q][:], ps[0][:, :], ALU.add, [('y1', q), ('ps', 0)], [('y1', q)])
                        tt('pool', y1[q][:], y1[q][:], xd[q][:], ALU.add, [('y1', q), ('xd', q)], [('y1', q)])
                        for k in range(16):
                            mm(ps[2][:, :], hb[hs][:, k, cs], wB[:, k, 768:1280], k == 0, k == 15,
                               ['wB', ('hb', hs)], [('ps', 2)])
                        act(zs[q][:], ps[2][:, :], AF.Silu, [('ps', 2)], [('zs', q)])
                        tt('pool', y2[q][:], y1[q][:], zs[q][:], ALU.mult, [('y1', q), ('zs', q)], [('y2', q)])
                        act(junk[:], y2[q][:], AF.Square, [('y2', q)], ['junk', ('st1', q)], accum=st1[q][:, 0:1])
                        act(st1[q][:, 1:2], st1[q][:, 0:1], AF.Sqrt, [('st1', q)], [('st1', q)], scale=1.0 / 512, bias=EPS)
                        recip(st1[q][:, 1:2], st1[q][:, 1:2], [('st1', q)], [('st1', q)])
                        stt('dve', yn[q][:], y2[q][:], st1[q][:, 1:2], nrm[:], ALU.mult, ALU.mult,
                            [('y2', q), ('st1', q), 'nrm'], [('yn', q)])
                        pyt = ps[7][:, 256:512].bitcast(BF16)
                        for c in range(4):
                            tr(pyt[:, c * 128:(c + 1) * 128], yn[q][:, c * 128:(c + 1) * 128], cb[:, IDN, :],
                               [('yn', q), 'cb'], [('ps', 7)])
                        cp('act', yTs[q][:].rearrange("p a b -> p (a b)"), pyt, [('ps', 7)], [('yTs', q)])
                        dma('sp', y_v[:, g * 4:(g + 1) * 4, col0:col0 + 128], yTs[q][:], [('yTs', q)], ['y_d'], ('yst', q))
                    mm(ps[2][:, :], Btok[q][:], xpp[q][:], True, True, [('Btok', q), ('xpp', q)], [('ps', 2)])
                    tt('dve', v3(S[:], 8), v3(S[:], 8), Etot.unsqueeze(2).to_broadcast([128, 8, 64]), ALU.mult,
                       ['S', ('Eb', q)], ['S'])
                    tt('dve', S[:], S[:], ps[2][:, :], ALU.add, ['S', ('ps', 2)], ['S'])
                    cp('pool', Sb[:], S[:], ['S'], ['Sb'])

    P.barrier()
    with ExitStack() as sc:
        def sbc(name, shape, dt=F32):
            return sc.enter_context(nc.sbuf_tensor(name, list(shape), dt))
        R1 = sbc("R1", [128, 48, 512], BF16)
        R4 = sbc("R4", [128, 8192])
        R5 = sbc("R5", [128, 16, 512])
        wsl = [sbc("wsl%d" % i, [128, 48 * 128], BF16) for i in range(3)]
        sqb = [sbc("sqb%d" % i, [128, 512], BF16) for i in range(2)]
        rsb = sbc("rsb", [128, 512])
        tmpc = [sbc("tmpc%d" % i, [128, 512]) for i in range(2)]
        ug = [sbc("ug%d" % i, [128, 514]) for i in range(1)]
        uv = [sbc("uv%d" % i, [128, 514]) for i in range(1)]
        carry = sbc("carry", [128, 88, 2])
        merged = R4[:, 0:4096].bitcast(BF16).rearrange("p (k t) -> p k t", k=16)
        xt4 = R4[:].rearrange("p (k t) -> p k t", k=16)
        h2 = hb[0]
        wcnt = [0]

        def wslot():
            wcnt[0] += 1
            return wcnt[0] % 3

        mset('dve', carry[:], 0.0, ['carry'])
        groups = [(0, 128)] + [(128 + 512 * i, 512) for i in range(4)]
        for gi, (c0g, n) in enumerate(groups):
            l0 = Q0L + c0g
            dma('sp', R1[:, 0:16, 0:n], osb_v[:, :, c0g:c0g + n], ['osb_d'], [('R1', k) for k in range(16)], 'ldo')
            dma('sp', R1[:, 16:48, 0:n], y_v[:, :, c0g:c0g + n], ['y_d'], [('R1', k) for k in range(16, 48)], 'ldy')
            dma('sp', hb[1][:, :, 0:n], hT_v[:, :, l0:l0 + n], [('hTd', l0 // 512)], [('hb', 1)], 'ldh')
            for c in range(16):
                s1, s2 = wslot(), wslot()
                dma('pool', wsl[s1][:, :], W2A[c, :, :], (), [('wsl', s1)], ('wsl', s1), cast=True)
                dma('pool', wsl[s2][:, 0:4096], W2B[c, :, :], (), [('wsl', s2)], ('wsl', s2), cast=True)
                for k in range(16):
                    mm(ps[0][:, 0:n], wsl[s1][:, k * 128:(k + 1) * 128], R1[:, k, 0:n], k == 0, k == 15,
                       [('wsl', s1), ('R1', k)], [('ps', 0)])
                for k in range(32):
                    mm(ps[1][:, 0:n], wsl[s2][:, k * 128:(k + 1) * 128], R1[:, 16 + k, 0:n], k == 0, k == 31,
                       [('wsl', s2), ('R1', 16 + k)], [('ps', 1)])
                for k in range(16):
                    mm(ps[2][:, 0:n], wsl[s1][:, (16 + k) * 128:(17 + k) * 128], hb[1][:, k, 0:n], k == 0, k == 15,
                       [('wsl', s1), ('hb', 1)], [('ps', 2)])
                for k in range(16):
                    mm(ps[3][:, 0:n], wsl[s1][:, (32 + k) * 128:(33 + k) * 128], hb[1][:, k, 0:n], k == 0, k == 15,
                       [('wsl', s1), ('hb', 1)], [('ps', 3)])
                a_, b_ = R5[:, 14, :], R5[:, 15, :]
                ka, kb_ = ('R5', 14), ('R5', 15)
                act(a_[:, 0:n], ps[2][:, 0:n], AF.Sigmoid, [('ps', 2)], [ka])
                act(b_[:, 0:n], ps[3][:, 0:n], AF.Sigmoid, [('ps', 3)], [kb_])
                tt('dve', a_[:, 0:n], a_[:, 0:n], ps[0][:, 0:n], ALU.mult, [ka, ('ps', 0)], [ka])
                tt('dve', b_[:, 0:n], b_[:, 0:n], ps[1][:, 0:n], ALU.mult, [kb_, ('ps', 1)], [kb_])
                tt('pool', merged[:, c, 0:n], a_[:, 0:n], b_[:, 0:n], ALU.add, [ka, kb_], [('R4', c // 2)])
            for c in range(16):
                s1 = wslot()
                dma('pool', wsl[s1][:, 0:2048], WO[c, :, :], (), [('wsl', s1)], ('wsl', s1), cast=True)
                b = pbank()
                for k in range(16):
                    mm(ps[b][:, 0:n], wsl[s1][:, k * 128:(k + 1) * 128], merged[:, k, 0:n], k == 0, k == 15,
                       [('wsl', s1), ('R4', k // 2)], [('ps', b)])
                cp('act', R5[:, c, 0:n], ps[b][:, 0:n], [('ps', b)], [('R5', c)])
                act(sqb[c % 2][:, 0:n], ps[b][:, 0:n], AF.Square, [('ps', b)], [('sqb', c % 2)])
                mm(ps[4][:, 0:n], cb[:, ONES, :], sqb[c % 2][:, 0:n], c == 0, c == 15, [('sqb', c % 2), 'cb'], [('ps', 4)])
            act(rsb[:, 0:n], ps[4][:, 0:n], AF.Sqrt, [('ps', 4)], ['rsb'], scale=1.0 / D, bias=EPS)
            recip(rsb[:, 0:n], rsb[:, 0:n], ['rsb'], ['rsb'])
            dma('sp', xt4[:, :, 0:n], xw_v[:, :, l0:l0 + n], (), [('R4', k) for k in range(16)], 'ldx')
            for c in range(16):
                stt('dve', tmpc[c % 2][:, 0:n], R5[:, c, 0:n], gains[:, 1, c:c + 1], rsb[:, 0:n], ALU.mult, ALU.mult,
                    [('R5', c), 'rsb', 'gains'], [('tmpc', c % 2)])
                tt('pool', xt4[:, c, 0:n], xt4[:, c, 0:n], tmpc[c % 2][:, 0:n], ALU.add, [('R4', c), ('tmpc', c % 2)],
                   [('R4', c)])
            for c in range(16):
                act(sqb[c % 2][:, 0:n], xt4[:, c, 0:n], AF.Square, [('R4', c)], [('sqb', c % 2)])
                mm(ps[4][:, 0:n], cb[:, ONES, :], sqb[c % 2][:, 0:n], c == 0, c == 15, [('sqb', c % 2), 'cb'], [('ps', 4)])
            act(rsb[:, 0:n], ps[4][:, 0:n], AF.Sqrt, [('ps', 4)], ['rsb'], scale=1.0 / D, bias=EPS)
            recip(rsb[:, 0:n], rsb[:, 0:n], ['rsb'], ['rsb'])
            for c in range(16):
                stt('dve', h2[:, c, 0:n], xt4[:, c, 0:n], gains[:, 2, c:c + 1], rsb[:, 0:n], ALU.mult, ALU.mult,
                    [('R4', c), 'rsb', 'gains'], [('hb', 0)])
            for fc in range(NFC):
                s1 = wslot()
                dma('pool', wsl[s1][:, 0:4096], WU[fc, :, :], (), [('wsl', s1)], ('wsl', s1), cast=True)
                q = 0
                for half, (pb_, ub, kq) in enumerate(((0, ug[q], ('ug', q)), (1, uv[q], ('uv', q)))):
                    for k in range(16):
                        mm(ps[pb_][:, 0:n], wsl[s1][:, k * 256 + half * 128:k * 256 + half * 128 + 128], h2[:, k, 0:n],
                           k == 0, k == 15, [('wsl', s1), ('hb', 0)], [('ps', pb_)])
                    cch = fc + half * NFC
                    cp('dve', ub[:, 0:2], carry[:, cch, :], ['carry'], [kq])
                    cp('act', ub[:, 2:2 + n], ps[pb_][:, 0:n], [('ps', pb_)], [kq])
                    if gi == 0:
                        ts1('dve', carry[:, cch, :], ub[:, n:n + 2], hval[:, 0:1], ALU.mult, [kq, 'hval'], ['carry'])
                    else:
                        cp('dve', carry[:, cch, :], ub[:, n:n + 2], [kq], ['carry'])
                    if gi > 0:
                        dst, kd = (R5[:, 0, :], ('R5', 0)) if half == 0 else (R5[:, 1, :], ('R5', 1))
                        eng = 'dve'
                        ts(eng, dst[:, 0:n], ub[:, 0:n], fconvw[:, cch, 0:1], fconvb[:, cch:cch + 1], ALU.mult, ALU.add,
                           [kq, 'fconv'], [kd])
                        stt(eng, dst[:, 0:n], ub[:, 1:n + 1], fconvw[:, cch, 1:2], dst[:, 0:n], ALU.mult, ALU.add,
                            [kq, 'fconv', kd], [kd])
                        stt(eng, dst[:, 0:n], ub[:, 2:n + 2], fconvw[:, cch, 2:3], dst[:, 0:n], ALU.mult, ALU.add,
                            [kq, 'fconv', kd], [kd])
                if gi > 0:
                    G_, V_, T_ = R5[:, 0, :], R5[:, 1, :], R5[:, 2 + fc % 2, :]
                    kT_ = ('R5', 2 + fc % 2)
                    tt('pool', T_[:, 0:n], G_[:, 0:n], G_[:, 0:n], ALU.mult, [('R5', 0)], [kT_])
                    ts('dve', T_[:, 0:n], T_[:, 0:n], 0.044715, 1.0, ALU.mult, ALU.add, [kT_], [kT_])
                    tt('pool', T_[:, 0:n], T_[:, 0:n], G_[:, 0:n], ALU.mult, [kT_, ('R5', 0)], [kT_])
                    act(T_[:, 0:n], T_[:, 0:n], AF.Sigmoid, [kT_], [kT_], scale=1.5957691216057308)
                    tt('dve', T_[:, 0:n], T_[:, 0:n], G_[:, 0:n], ALU.mult, [kT_, ('R5', 0)], [kT_])
                    tt('pool', R1[:, fc, 0:n], T_[:, 0:n], V_[:, 0:n], ALU.mult, [kT_, ('R5', 1)], [('R1', fc)])
            if gi == 0:
                continue
            for c in range(16):
                s1 = wslot()
                dma('pool', wsl[s1][:, 0:NFC * 128], WD[c, :, :], (), [('wsl', s1)], ('wsl', s1), cast=True)
                b = pbank()
                for k in range(NFC):
                    mm(ps[b][:, 0:n], wsl[s1][:, k * 128:(k + 1) * 128], R1[:, k, 0:n], k == 0, k == NFC - 1,
                       [('wsl', s1), ('R1', k)], [('ps', b)])
                cp('act', R5[:, c, 0:n], ps[b][:, 0:n], [('ps', b)], [('R5', c)])
                act(sqb[c % 2][:, 0:n], ps[b][:, 0:n], AF.Square, [('ps', b)], [('sqb', c % 2)])
                mm(ps[4][:, 0:n], cb[:, ONES, :], sqb[c % 2][:, 0:n], c == 0, c == 15, [('sqb', c % 2), 'cb'], [('ps', 4)])
            act(rsb[:, 0:n], ps[4][:, 0:n], AF.Sqrt, [('ps', 4)], ['rsb'], scale=1.0 / D, bias=EPS)
            recip(rsb[:, 0:n], rsb[:, 0:n], ['rsb'], ['rsb'])
            for c in range(16):
                stt('dve', tmpc[c % 2][:, 0:n], R5[:, c, 0:n], gains[:, 3, c:c + 1], rsb[:, 0:n], ALU.mult, ALU.mult,
                    [('R5', c), 'rsb', 'gains'], [('tmpc', c % 2)])
                tt('pool', R5[:, c, 0:n], xt4[:, c, 0:n], tmpc[c % 2][:, 0:n], ALU.add, [('R4', c), ('tmpc', c % 2)],
                   [('R5', c)])
            t0 = c0g - 128
            dma('sp', out_v[:, :, t0:t0 + n], R5[:, :, 0:n], [('R5', c) for c in range(16)], ['out'], 'stout')

        P.emit(nc, es)
    es.close()
    return nc


_NC = None


def _prep_weights(w_in, w_sb_proj, w_ssd_proj, w_out, w_up, w_down):
    def blk(w, cols):
        K = w.shape[0]
        sub = w[:, cols].reshape(K // 128, 128, len(cols))
        return np.ascontiguousarray(sub.transpose(1, 0, 2)).reshape(128, -1)
    ar = np.arange
    WA = np.stack([blk(w_in, np.concatenate([ar(h * 128, h * 128 + 128), ar(2048 + h * 128, 2048 + h * 128 + 128),
                                             ar(4096 + h * 128, 4096 + h * 128 + 128)])) for h in range(16)])
    WB = np.stack([blk(w_in, np.concatenate([ar(10240 + g * 512, 10240 + g * 512 + 512),
                                             ar(14336 + g * 128, 14336 + g * 128 + 128),
                                             ar(15360 + g * 128, 15360 + g * 128 + 128),
                                             ar(6144 + g * 512, 6144 + g * 512 + 512),
                                             ar(16384 + g * 8, 16384 + g * 8 + 8)])) for g in range(8)])
    W2A = np.stack([np.concatenate([blk(w_sb_proj, ar(c * 128, c * 128 + 128)),
                                    blk(w_in, ar(16448 + c * 128, 16448 + c * 128 + 128)),
                                    blk(w_in, ar(18496 + c * 128, 18496 + c * 128 + 128))], axis=1) for c in range(16)])
    W2B = np.stack([blk(w_ssd_proj, ar(c * 128, c * 128 + 128)) for c in range(16)])
    WO = np.stack([blk(w_out, ar(c * 128, c * 128 + 128)) for c in range(16)])
    WU = np.stack([blk(w_up, np.concatenate([ar(fc * 128, fc * 128 + 128), ar(D_FF + fc * 128, D_FF + fc * 128 + 128)]))
                   for fc in range(NFC)])
    WD = np.stack([blk(w_down, ar(c * 128, c * 128 + 128)) for c in range(16)])
    return dict(WA=WA, WB=WB, W2A=W2A, W2B=W2B, WO=WO, WU=WU, WD=WD)


def kernel(x, norm_mix_pre, w_in, ssd_conv_w, ssd_conv_b, dt_bias, a_log, d_skip, ssd_norm,
           w_sb_proj, w_ssd_proj, w_out, norm_mix_post, norm_ffn_pre, w_up, ffn_conv_w,
           ffn_conv_b, w_down, norm_ffn_post):
    global _NC
    in_maps = _in_maps(x, norm_mix_pre, w_in, ssd_conv_w, ssd_conv_b, dt_bias, a_log, d_skip, ssd_norm,
                       w_sb_proj, w_ssd_proj, w_out, norm_mix_post, norm_ffn_pre, w_up, ffn_conv_w,
                       ffn_conv_b, w_down, norm_ffn_post)
    if _NC is None:
        _NC = build_nc()
    res = run_bass_kernel_spmd(_NC, in_maps, core_ids=list(range(8)))
    out = np.empty((2, 8192, D), np.float32)
    for core in range(8):
        b, c = core // 4, core % 4
        out[b, 2048 * c:2048 * (c + 1), :] = res.results[core]["outT"].T
    return out


def _in_maps(x, norm_mix_pre, w_in, ssd_conv_w, ssd_conv_b, dt_bias, a_log, d_skip, ssd_norm,
             w_sb_proj, w_ssd_proj, w_out, norm_mix_post, norm_ffn_pre, w_up, ffn_conv_w,
             ffn_conv_b, w_down, norm_ffn_post):
    f32 = np.float32
    x = np.asarray(x, f32)
    shared = _prep_weights(np.asarray(w_in, f32)[0], np.asarray(w_sb_proj, f32)[0], np.asarray(w_ssd_proj, f32)[0],
                           np.asarray(w_out, f32)[0], np.asarray(w_up, f32)[0], np.asarray(w_down, f32)[0])
    r = np.arange(128)
    cm = np.stack([(r[:, None] < r[None, :]), (r[:, None] <= r[None, :]), (r[:, None] >= r[None, :]),
                   (r[:, None] > r[None, :]), np.ones((128, 128), bool), np.eye(128, dtype=bool)], axis=1).astype(f32)
    shared["consts"] = np.ascontiguousarray(cm.reshape(128, 768))

    def pk(v):
        return np.asarray(v, f32).reshape(-1, 128).T
    shared["gains"] = np.ascontiguousarray(np.stack([pk(norm_mix_pre[0]), pk(norm_mix_post[0]), pk(norm_ffn_pre[0]),
                                                     pk(norm_ffn_post[0])], axis=1).reshape(128, 64))
    scw = np.asarray(ssd_conv_w, f32)[0]
    shared["sconvw"] = np.ascontiguousarray(scw.reshape(4, 48, 128).transpose(2, 1, 0).reshape(128, 192))
    shared["sconvb"] = np.ascontiguousarray(pk(ssd_conv_b[0]))
    fcw = np.asarray(ffn_conv_w, f32)[0]
    shared["fconvw"] = np.ascontiguousarray(fcw.reshape(3, 88, 128).transpose(2, 1, 0).reshape(128, 264))
    shared["fconvb"] = np.ascontiguousarray(pk(ffn_conv_b[0]))
    hv = np.stack([np.asarray(dt_bias, f32)[0], np.asarray(a_log, f32)[0], np.asarray(d_skip, f32)[0]])
    shared["hvec"] = np.ascontiguousarray(np.broadcast_to(hv.reshape(1, 192), (128, 192)))
    shared["ssdn"] = np.ascontiguousarray(np.broadcast_to(np.asarray(ssd_norm, f32)[0][None, :], (128, 4096)))

    in_maps = []
    for core in range(8):
        b, c = core // 4, core % 4
        end = 2048 * (c + 1)
        start = end - WIN
        xwin = np.zeros((D, WIN), f32)
        lo = max(start, 0)
        xwin[:, lo - start:] = x[b, lo:end, :].T
        tok = start + np.arange(WIN)
        val = (tok >= 0).astype(f32).reshape(64, 128).T
        m = dict(shared)
        m["xw"] = xwin
        m["valid"] = np.ascontiguousarray(val)
        m["hval"] = np.full((128, 1), 1.0 if c > 0 else 0.0, f32)
        in_maps.append(m)
    return in_maps
```

```python
import bisect
import os
from contextlib import ExitStack

import numpy as np
import concourse.bass as bass
import concourse.mybir as mybir
from concourse.bass_utils import run_bass_kernel_spmd

F32 = mybir.dt.float32
BF16 = mybir.dt.bfloat16
AF = mybir.ActivationFunctionType
ALU = mybir.AluOpType

D = 2048
WIN = 8192
NT = 16
Q0L = 6016
NQ = 2176
EPS = 1e-6
SCALE = 128 ** -0.5
D_FF = 5632
NFC = 44
QBS = [(0, 128)] + [(128 + 512 * i, 512) for i in range(4)]


class Prog:
    def __init__(self):
        self.ops = []
        self.bars = []

    def barrier(self):
        self.bars.append(len(self.ops))

    def add(self, eng, fn, R=(), W=(), dk=None):
        self.ops.append((eng, fn, tuple(R), tuple(W), dk))

    def emit(self, nc, es):
        ops = self.ops
        n = len(ops)
        lastw, readers = {}, {}
        deps = []
        for i, (eng, fn, R, W, dk) in enumerate(ops):
            d = set()
            for r in R:
                j = lastw.get(r)
                if j is not None:
                    d.add(j)
            for w in W:
                j = lastw.get(w)
                if j is not None:
                    d.add(j)
                rs = readers.get(w)
                if rs:
                    d.update(rs)
            for r in R:
                readers.setdefault(r, []).append(i)
            for w in W:
                lastw[w] = i
                readers[w] = []
            d.discard(i)
            deps.append(d)
        sig = [False] * n
        cdeps = []
        dma_idx = {}
        for i, op in enumerate(ops):
            if op[4] is not None:
                dma_idx.setdefault(op[4], []).append(i)
        for i, d in enumerate(deps):
            eng = ops[i][0]
            best = {}
            dks = set()
            for j in d:
                oj = ops[j]
                if oj[4] is not None:
                    dks.add(oj[4])
                    continue
                if oj[0] == 'pe' and eng == 'pe':
                    continue
                if best.get(oj[0], -1) < j:
                    best[oj[0]] = j
            for j in best.values():
                sig[j] = True
            cdeps.append((best, dks))
        eng_ops = {}
        for i, op in enumerate(ops):
            if op[4] is None:
                eng_ops.setdefault(op[0], []).append(i)
        for p in self.bars:
            for E in ('pe', 'act', 'dve', 'pool', 'sp'):
                first = next((i for i in range(p, n) if ops[i][0] == E), None)
                if first is None:
                    continue
                best, dks = cdeps[first]
                for E2, lst in eng_ops.items():
                    if E2 == 'pe' and E == 'pe':
                        continue
                    pos = bisect.bisect_left(lst, p)
                    if pos > 0:
                        j = lst[pos - 1]
                        if best.get(E2, -1) < j:
                            best[E2] = j
                            sig[j] = True
                for k, lst in dma_idx.items():
                    if lst and lst[0] < p:
                        dks.add(k)
        seq = [0] * n
        cnt = {}
        for i, op in enumerate(ops):
            if op[4] is None and sig[i]:
                cnt[op[0]] = cnt.get(op[0], 0) + 1
                seq[i] = cnt[op[0]]
        esem = {e: es.enter_context(nc.semaphore("se_" + e)) for e in ('pe', 'act', 'dve', 'pool')}
        dsem = {}
        for k in dma_idx:
            dsem[k] = es.enter_context(nc.semaphore("sd_%d" % len(dsem)))
        block = es.enter_context(nc.Block())

        def run(ename):
            def body(e):
                waited = {}
                for i, (eng, fn, R, W, dk) in enumerate(ops):
                    if eng != ename:
                        continue
                    best, dks = cdeps[i]
                    for se, j in best.items():
                        key = ('e', se)
                        if waited.get(key, 0) < seq[j]:
                            e.wait_ge(esem[se], seq[j])
                            waited[key] = seq[j]
                    for k in dks:
                        val = 16 * bisect.bisect_left(dma_idx[k], i)
                        key = ('d', k)
                        if waited.get(key, 0) < val:
                            e.wait_ge(dsem[k], val)
                            waited[key] = val
                    ins = fn(e)
                    if dk is not None:
                        ins.then_inc(dsem[dk], 16)
                    elif sig[i]:
                        ins.then_inc(esem[eng], 1)
                if ename == 'sp':
                    for k, lst in dma_idx.items():
                        e.wait_ge(dsem[k], 16 * len(lst))
            return body

        block.tensor(run('pe'))
        block.scalar(run('act'))
        block.vector(run('dve'))
        block.gpsimd(run('pool'))
        block.sync(run('sp'))


def build_nc():
    nc = bass.Bass("TRN2", target_bir_lowering=False)
    P = Prog()

    def din(name, shape, dt=F32):
        return nc.dram_tensor(name, list(shape), dt, kind="ExternalInput").ap()

    xw = din("xw", [D, WIN])
    WA = din("WA", [16, 128, 16 * 384])
    WB = din("WB", [8, 128, 16 * 1288])
    W2A = din("W2A", [16, 128, 48 * 128])
    W2B = din("W2B", [16, 128, 32 * 128])
    WO = din("WO", [16, 128, 16 * 128])
    WU = din("WU", [NFC, 128, 16 * 256])
    WD = din("WD", [16, 128, NFC * 128])
    consts_d = din("consts", [128, 6 * 128])
    gains_d = din("gains", [128, 64])
    sconvw_d = din("sconvw", [128, 48 * 4])
    sconvb_d = din("sconvb", [128, 48])
    fconvw_d = din("fconvw", [128, 88 * 3])
    fconvb_d = din("fconvb", [128, 88])
    hvec_d = din("hvec", [128, 3 * 64])
    ssdn_d = din("ssdn", [128, 4096])
    valid_d = din("valid", [128, 64])
    hval_d = din("hval", [128, 1])
    outT = nc.dram_tensor("outT", [D, 2048], F32, kind="ExternalOutput").ap()
    skind = "ExternalOutput" if os.environ.get("KDEBUG") else "Internal"
    hT_d = nc.dram_tensor("hT_d", [D, WIN], BF16, kind=skind).ap()
    osb_d = nc.dram_tensor("osb_d", [D, NQ], BF16, kind=skind).ap()
    y_d = nc.dram_tensor("y_d", [4096, NQ], BF16, kind=skind).ap()

    xw_v = xw.rearrange("(k p) t -> p k t", p=128)
    hT_v = hT_d.rearrange("(k p) t -> p k t", p=128)
    osb_v = osb_d.rearrange("(k p) t -> p k t", p=128)
    y_v = y_d.rearrange("(k p) t -> p k t", p=128)
    out_v = outT.rearrange("(k p) t -> p k t", p=128)

    es = ExitStack()

    def sb(name, shape, dt=F32):
        return es.enter_context(nc.sbuf_tensor("s_" + name, list(shape), dt))

    ps = [es.enter_context(nc.psum_tensor("ps%d" % i, [128, 512], F32)) for i in range(8)]

    def mm(out, lhsT, rhs, start, stop, R, W):
        P.add('pe', lambda e: e.matmul(out, lhsT, rhs, start=start, stop=stop), R, W)

    def tr(out, in_, ident, R, W):
        P.add('pe', lambda e: e.transpose(out, in_, ident), R, W)

    def act(out, in_, func, R, W, scale=1.0, bias=0.0, accum=None):
        if accum is None:
            P.add('act', lambda e: e.activation(out=out, in_=in_, func=func, bias=bias, scale=scale), R, W)
        else:
            P.add('act', lambda e: e.activation(out=out, in_=in_, func=func, bias=bias, scale=scale,
                                                accum_out=accum), R, W)

    def tt(eng, out, a, b, op, R, W):
        P.add(eng, lambda e: e.tensor_tensor(out=out, in0=a, in1=b, op=op), R, W)

    def ts(eng, out, a, s1, s2, op0, op1, R, W):
        P.add(eng, lambda e: e.tensor_scalar(out=out, in0=a, scalar1=s1, scalar2=s2, op0=op0, op1=op1), R, W)

    def ts1(eng, out, a, s1, op0, R, W):
        P.add(eng, lambda e: e.tensor_single_scalar(out=out, in_=a, scalar=s1, op=op0), R, W)

    def stt(eng, out, a, s, b, op0, op1, R, W):
        eng = 'dve'
        P.add(eng, lambda e: e.scalar_tensor_tensor(out=out, in0=a, scalar=s, in1=b, op0=op0, op1=op1), R, W)

    def cp(eng, out, a, R, W):
        if eng == 'act':
            act(out, a, AF.Copy, R, W)
        else:
            P.add(eng, lambda e: e.tensor_copy(out=out, in_=a), R, W)

    def recip(out, a, R, W):
        P.add('dve', lambda e: e.reciprocal(out=out, in_=a), R, W)

    def mset(eng, out, val, W):
        P.add(eng, lambda e: e.memset(out, val), (), W)

    def dma(q, out, in_, R, W, dk, cast=False):
        if cast:
            P.add(q, lambda e: e.dma_start(out=out, in_=in_, max_dma_last_dim=4096), R, W, dk)
        else:
            P.add(q, lambda e: e.dma_start(out=out, in_=in_), R, W, dk)

    cf = sb("cf", [128, 6, 128])
    cb = sb("cb", [128, 6, 128], BF16)
    gains = sb("gains", [128, 4, 16])
    sconvw = sb("sconvw", [128, 48, 4])
    sconvb = sb("sconvb", [128, 48])
    fconvw = sb("fconvw", [128, 88, 3])
    fconvb = sb("fconvb", [128, 88])
    hvec = sb("hvec", [128, 3, 64])
    aneg = sb("aneg", [128, 64])
    valid = sb("valid", [128, 64])
    hval = sb("hval", [128, 1])
    dma('sp', cf[:].rearrange("p a b -> p (a b)"), consts_d[:, :], (), ['cf'], 'c0')
    dma('sp', gains[:].rearrange("p a b -> p (a b)"), gains_d[:, :], (), ['gains'], 'c0')
    dma('sp', sconvw[:].rearrange("p a b -> p (a b)"), sconvw_d[:, :], (), ['sconv'], 'c0')
    dma('sp', sconvb[:], sconvb_d[:, :], (), ['sconv'], 'c0')
    dma('sp', fconvw[:].rearrange("p a b -> p (a b)"), fconvw_d[:, :], (), ['fconv'], 'c0')
    dma('sp', fconvb[:], fconvb_d[:, :], (), ['fconv'], 'c0')
    dma('sp', hvec[:].rearrange("p a b -> p (a b)"), hvec_d[:, :], (), ['hvec'], 'c0')
    dma('sp', valid[:], valid_d[:, :], (), ['valid'], 'c0')
    dma('sp', hval[:], hval_d[:, :], (), ['hval'], 'c0')
    cp('dve', cb[:], cf[:], ['cf'], ['cb'])
    act(aneg[:], hvec[:, 1, :], AF.Exp, ['hvec'], ['aneg'])
    ts1('dve', aneg[:], aneg[:], -1.0, ALU.mult, ['aneg'], ['aneg'])
    LT, LE, GE, GT, ONES, IDN = range(6)

    hb = [sb("hb%d" % i, [128, 16, 512], BF16) for i in range(2)]
    hcnt = [0]

    def load_h(t0, n=512):
        s = hcnt[0] % 2
        hcnt[0] += 1
        dma('sp', hb[s][:, :, 0:n], hT_v[:, :, t0:t0 + n], [('hTd', t0 // 512)], [('hb', s)], ('hbld', s))
        return s

    pcnt = [0]

    def pbank():
        pcnt[0] += 1
        return pcnt[0] % 2

    with ExitStack() as s0:
        xt = [s0.enter_context(nc.sbuf_tensor("xt%d" % i, [128, 16, 512], F32)) for i in range(2)]
        sq = s0.enter_context(nc.sbuf_tensor("sq", [128, 16, 512], BF16))
        rs = s0.enter_context(nc.sbuf_tensor("rs0", [128, 512], F32))
        for t in range(NT):
            xs = t % 2
            hs = t % 2
            dma('sp', xt[xs][:], xw_v[:, :, t * 512:(t + 1) * 512], (), [('xt', xs)], ('xt', xs))
            act(sq[:], xt[xs][:], AF.Square, [('xt', xs)], ['sq'])
            b = pbank()
            for k in range(16):
                mm(ps[b][:, :], cb[:, ONES, :], sq[:, k, :], k == 0, k == 15, ['sq', 'cb'], [('ps', b)])
            act(rs[:], ps[b][:, :], AF.Sqrt, [('ps', b)], ['rs'], scale=1.0 / D, bias=EPS)
            recip(rs[:], rs[:], ['rs'], ['rs'])
            for k in range(16):
                stt('dve', hb[hs][:, k, :], xt[xs][:, k, :], gains[:, 0, k:k + 1], rs[:], ALU.mult, ALU.mult,
                    [('xt', xs), 'rs', 'gains'], [('hb', hs)])
            dma('sp', hT_v[:, :, t * 512:(t + 1) * 512], hb[hs][:], [('hb', hs)], [('hTd', t)], ('hbst', hs))
        hcnt[0] = 0

    P.barrier()
    with ExitStack() as sa:
        def sba(name, shape, dt=F32):
            return sa.enter_context(nc.sbuf_tensor(name, list(shape), dt))
        wq = [sba("wq%d" % i, [128, 16, 384], BF16) for i in range(2)]
        kT = [sba("kT%d" % i, [128, WIN], BF16) for i in range(2)]
        vv = [sba("vv%d" % i, [128, 64, 128], BF16) for i in range(2)]
        qT = [sba("qT%d" % i, [128, NQ], BF16) for i in range(2)]
        eb = [[sba("eb%d_%d" % (st, i), [128, 512], F32) for i in range(2)] for st in range(2)]
        spb = [[sba("spb%d_%d" % (st, i), [128, 512], BF16) for i in range(2)] for st in range(2)]
        gb = [[sba("gb%d_%d" % (st, i), [128, 512], F32) for i in range(2)] for st in range(2)]
        wb = [[sba("wb%d_%d" % (st, i), [128, 512], BF16) for i in range(2)] for st in range(2)]
        ob = [sba("ob%d" % i, [128, 512], BF16) for i in range(2)]
        SBANKS = [(2, 4, 5), (3, 6, 7)]
        PAIRS = [[(1664, 256), (1920, 256)], [(1152, 512), (640, 512)], [(128, 512), (0, 128)]]
        ocnt = 0
        for h in range(16):
            ws = h % 2
            dma('pool', wq[ws][:].rearrange("p k c -> p (k c)"), WA[h, :, :], (), [('wq', ws)], ('wq', ws), cast=True)
            for t in range(NT):
                hs = load_h(t * 512)
                b = pbank()
                for k in range(16):
                    mm(ps[b][:, :], wq[ws][:, k, 128:256], hb[hs][:, k, :], k == 0, k == 15,
                       [('wq', ws), ('hb', hs)], [('ps', b)])
                cp('dve', kT[ws][:, t * 512:(t + 1) * 512], ps[b][:, :], [('ps', b)], [('kT', ws, t)])
                b = pbank()
                for s in range(4):
                    for k in range(16):
                        mm(ps[b][:, s * 128:(s + 1) * 128], hb[hs][:, k, s * 128:(s + 1) * 128], wq[ws][:, k, 256:384],
                           k == 0, k == 15, [('wq', ws), ('hb', hs)], [('ps', b)])
                cp('dve', vv[ws][:, t * 4:(t + 1) * 4, :].rearrange("p a b -> p (a b)"), ps[b][:, :],
                   [('ps', b)], [('vv', ws, t)])
                if t >= 11:
                    c0 = 384 if t == 11 else 0
                    n = 512 - c0
                    qc0 = t * 512 + c0 - Q0L
                    b = pbank()
                    for k in range(16):
                        mm(ps[b][:, 0:n], wq[ws][:, k, 0:128], hb[hs][:, k, c0:512], k == 0, k == 15,
                           [('wq', ws), ('hb', hs)], [('ps', b)])
                    cp('dve', qT[ws][:, qc0:qc0 + n], ps[b][:, 0:n], [('ps', b)], [('qT', ws)])
            if os.environ.get("KDEBUG") and h == 0:
                dk_ = nc.dram_tensor("dbg_k", [128, WIN], BF16, kind="ExternalOutput").ap()
                dv_ = nc.dram_tensor("dbg_v", [128, WIN], BF16, kind="ExternalOutput").ap()
                dq_ = nc.dram_tensor("dbg_q", [128, NQ], BF16, kind="ExternalOutput").ap()
                dma('sp', dk_[:, :], kT[0][:, :], [('kT', 0, t) for t in range(16)], ['dbgk'], 'dbg')
                dma('sp', dv_[:, :], vv[0][:].rearrange("p a b -> p (a b)"), [('vv', 0, t) for t in range(16)], ['dbgv'], 'dbg')
                dma('sp', dq_[:, :], qT[0][:, :], [('qT', 0)], ['dbgq'], 'dbg')
                dw_ = nc.dram_tensor("dbg_w", [128, 6144], BF16, kind="ExternalOutput").ap()
                dma('sp', dw_[:, :], wq[0][:].rearrange("p k c -> p (k c)"), [('wq', 0)], ['dbgw'], 'dbg')
            for pair in PAIRS:
                sts = []
                for si, (q0, nq) in enumerate(pair):
                    gq0 = Q0L + q0
                    sts.append(dict(si=si, q0=q0, nq=nq, gq0=gq0, banks=SBANKS[si],
                                    blocks=list(range((gq0 + nq) // 128 - 1, -1, -1))))

                def geom(st, kb):
                    m = kb - st['gq0'] // 128
                    return m >= 0, 128 * max(m, 0)

                def zmm(st, i):
                    kb = st['blocks'][i]
                    _, c0 = geom(st, kb)
                    zb = st['banks'][0]
                    q0, nq = st['q0'], st['nq']
                    mm(ps[zb][:, c0:nq], kT[ws][:, kb * 128:(kb + 1) * 128], qT[ws][:, q0 + c0:q0 + nq], True, True,
                       [('kT', ws, kb // 4), ('qT', ws)], [('ps', zb)])
                for st in sts:
                    zmm(st, 0)
                for i in range(max(len(st['blocks']) for st in sts)):
                    live = [st for st in sts if i < len(st['blocks'])]
                    s_ = i % 2
                    for st in live:
                        si, nq = st['si'], st['nq']
                        zb = st['banks'][0]
                        diag, c0 = geom(st, st['blocks'][i])
                        act(eb[si][s_][:, c0:nq], ps[zb][:, c0:nq], AF.Exp, [('ps', zb)], [('eb', si, s_)], scale=SCALE)
                        if diag:
                            tt('dve', eb[si][s_][:, c0:c0 + 128], eb[si][s_][:, c0:c0 + 128], cf[:, LT, :], ALU.mult,
                               [('eb', si, s_), 'cf'], [('eb', si, s_)])
                        act(spb[si][s_][:, c0:nq], eb[si][s_][:, c0:nq], AF.Ln, [('eb', si, s_)], [('spb', si, s_)], bias=1.0)
                    for st in live:
                        if i + 1 < len(st['blocks']):
                            zmm(st, i + 1)
                    for st in live:
                        si, nq = st['si'], st['nq']
                        xb = st['banks'][1]
                        _, c0 = geom(st, st['blocks'][i])
                        mm(ps[xb][:, c0:nq], cb[:, GE, :], spb[si][s_][:, c0:nq], i == 0, False,
                           [('spb', si, s_), 'cb'], [('ps', xb)])
                    for st in live:
                        si, nq = st['si'], st['nq']
                        xb = st['banks'][1]
                        _, c0 = geom(st, st['blocks'][i])
                        act(gb[si][s_][:, c0:nq], ps[xb][:, c0:nq], AF.Exp, [('ps', xb)], [('gb', si, s_)], scale=-1.0)
                    for st in live:
                        si, nq = st['si'], st['nq']
                        _, c0 = geom(st, st['blocks'][i])
                        tt('pool', wb[si][s_][:, c0:nq], eb[si][s_][:, c0:nq], gb[si][s_][:, c0:nq], ALU.mult,
                           [('eb', si, s_), ('gb', si, s_)], [('wb', si, s_)])
                    for st in live:
                        si, nq = st['si'], st['nq']
                        xb, obk = st['banks'][1], st['banks'][2]
                        kb = st['blocks'][i]
                        _, c0 = geom(st, kb)
                        last = i == len(st['blocks']) - 1
                        mm(ps[obk][:, c0:nq], vv[ws][:, kb, :], wb[si][s_][:, c0:nq], i == 0, last,
                           [('wb', si, s_), ('vv', ws, kb // 4)], [('ps', obk)])
                        mm(ps[xb][:, c0:nq], cb[:, LT, :], spb[si][s_][:, c0:nq], False, last,
                           [('spb', si, s_), 'cb'], [('ps', xb)])
                for st in sts:
                    q0, nq, obk = st['q0'], st['nq'], st['banks'][2]
                    os_ = ocnt % 2
                    ocnt += 1
                    cp('dve', ob[os_][:, 0:nq], ps[obk][:, 0:nq], [('ps', obk)], [('ob', os_)])
                    dma('sp', osb_d[h * 128:(h + 1) * 128, q0:q0 + nq], ob[os_][:, 0:nq], [('ob', os_)], ['osb_d'],
                        ('obst', os_))

    P.barrier()
    with ExitStack() as sbx:
        def sbb(name, shape, dt=F32):
            return sbx.enter_context(nc.sbuf_tensor(name, list(shape), dt))
        wB = sbb("wB", [128, 16, 1288], BF16)
        pre = sbb("pre", [128, 6, 515])
        acc = sbb("acc", [128, 6, 512])
        xa = sbb("xa", [128, 4, 512])
        BT = sbb("BT", [128, 512], BF16)
        CT = sbb("CT", [128, 512], BF16)
        nrm = sbb("nrm", [128, 512])
        S = sbb("S", [128, 512])
        Sb = sbb("Sb", [128, 512], BF16)
        dts = [sbb("dts%d" % i, [128, 32]) for i in range(2)]
        Eb = [sbb("Eb%d" % i, [128, 24]) for i in range(2)]
        rseg = [sbb("rseg%d" % i, [128, 1024]) for i in range(2)]
        dec = [sbb("dec%d" % i, [128, 1024]) for i in range(2)]
        Gb = [sbb("Gb%d" % i, [128, 1024], BF16) for i in range(2)]
        CBm = [sbb("CBm%d" % i, [128, 128]) for i in range(2)]
        xp = [sbb("xp%d" % i, [128, 512], BF16) for i in range(2)]
        xpp = [sbb("xpp%d" % i, [128, 512], BF16) for i in range(2)]
        xd = [sbb("xd%d" % i, [128, 512]) for i in range(2)]
        Btok = [sbb("Btok%d" % i, [128, 128], BF16) for i in range(2)]
        zs = [sbb("zs%d" % i, [128, 512]) for i in range(2)]
        y1 = [sbb("y1%d" % i, [128, 512]) for i in range(2)]
        y2 = [sbb("y2%d" % i, [128, 512]) for i in range(2)]
        junk = sbb("junk", [128, 512])
        st1 = [sbb("st1%d" % i, [128, 2]) for i in range(2)]
        yn = [sbb("yn%d" % i, [128, 512], BF16) for i in range(2)]
        yTs = [sbb("yTs%d" % i, [128, 4, 128], BF16) for i in range(2)]

        def v3(ap, h):
            return ap.rearrange("p (h l) -> p h l", h=h)

        for g in range(8):
            dma('pool', wB[:].rearrange("p k c -> p (k c)"), WB[g, :, :], (), ['wB'], 'wB', cast=True)
            dma('sp', nrm[:], ssdn_d[:, g * 512:(g + 1) * 512], (), ['nrm'], 'nrm')
            mset('dve', S[:], 0.0, ['S'])
            mset('dve', Sb[:], 0.0, ['Sb'])
            for t in range(NT):
                hs = load_h(t * 512)
                if t == 0:
                    mset('dve', pre[:, :, 0:3], 0.0, ['pre'])
                else:
                    cp('dve', pre[:, :, 0:3], pre[:, :, 512:515], ['pre'], ['pre'])
                for c in range(6):
                    b = pbank()
                    w0 = c * 128 if c < 4 else 512 + (c - 4) * 128
                    for k in range(16):
                        mm(ps[b][:, :], wB[:, k, w0:w0 + 128], hb[hs][:, k, :], k == 0, k == 15,
                           ['wB', ('hb', hs)], [('ps', b)])
                    cp('act', pre[:, c, 3:515], ps[b][:, :], [('ps', b)], ['pre'])
                for c in range(6):
                    ch = g * 4 + c if c < 4 else (32 + g if c == 4 else 40 + g)
                    eng = 'dve'
                    ts(eng, acc[:, c, :], pre[:, c, 0:512], sconvw[:, ch, 0:1], sconvb[:, ch:ch + 1], ALU.mult, ALU.add,
                       ['pre', 'sconv'], [('acc', c)])
                    for kk in range(1, 4):
                        stt(eng, acc[:, c, :], pre[:, c, kk:kk + 512], sconvw[:, ch, kk:kk + 1], acc[:, c, :],
                            ALU.mult, ALU.add, ['pre', 'sconv', ('acc', c)], [('acc', c)])
                    if c < 4:
                        act(xa[:, c, :], acc[:, c, :], AF.Silu, [('acc', c)], ['xa'])
                    elif c == 4:
                        act(BT[:], acc[:, c, :], AF.Silu, [('acc', c)], ['BT'])
                    else:
                        act(CT[:], acc[:, c, :], AF.Silu, [('acc', c)], ['CT'])
                for s in range(4):
                    ci = t * 4 + s
                    outc = ci >= 47
                    q = ci % 2
                    cs = slice(s * 128, (s + 1) * 128)
                    dt_, la_, tmp_, dtE_ = dts[q][:, 0:8], dts[q][:, 8:16], dts[q][:, 16:24], dts[q][:, 24:32]
                    for k in range(16):
                        mm(ps[3][:, 0:8], hb[hs][:, k, cs], wB[:, k, 1280:1288], k == 0, k == 15,
                           ['wB', ('hb', hs)], [('ps', 3)])
                    tt('dve', tmp_, ps[3][:, 0:8], hvec[:, 0, g * 8:(g + 1) * 8], ALU.add, [('ps', 3), 'hvec'], [('dts', q)])
                    act(tmp_, tmp_, AF.Exp, [('dts', q)], [('dts', q)])
                    act(tmp_, tmp_, AF.Ln, [('dts', q)], [('dts', q)], bias=1.0)
                    ts1('dve', dt_, tmp_, valid[:, ci:ci + 1], ALU.mult, [('dts', q), 'valid'], [('dts', q)])
                    tt('dve', la_, dt_, aneg[:, g * 8:(g + 1) * 8], ALU.mult, [('dts', q), 'aneg'], [('dts', q)])
                    mm(ps[3][:, 8:16], cf[:, LE, :], la_, True, True, [('dts', q), 'cf'], [('ps', 3)])
                    mm(ps[3][:, 16:24], cf[:, GT, :], la_, True, True, [('dts', q), 'cf'], [('ps', 3)])
                    mm(ps[3][:, 24:32], cf[:, ONES, :], la_, True, True, [('dts', q), 'cf'], [('ps', 3)])
                    act(Eb[q][:], ps[3][:, 8:32], AF.Exp, [('ps', 3)], [('Eb', q)])
                    Ecs, Edte, Etot = Eb[q][:, 0:8], Eb[q][:, 8:16], Eb[q][:, 16:24]
                    tt('dve', dtE_, dt_, Edte, ALU.mult, [('dts', q), ('Eb', q)], [('dts', q)])
                    if outc:
                        tt('dve', v3(rseg[q][:], 8), cf[:, LE, :].unsqueeze(1).to_broadcast([128, 8, 128]),
                           la_.unsqueeze(2).to_broadcast([128, 8, 128]), ALU.mult, [('dts', q), 'cf'], [('rseg', q)])
                        for hh in range(2):
                            mm(ps[4 + hh][:, :], cf[:, GT, :], rseg[q][:, hh * 512:(hh + 1) * 512], True, True,
                               [('rseg', q), 'cf'], [('ps', 4 + hh)])
                            act(dec[q][:, hh * 512:(hh + 1) * 512], ps[4 + hh][:, :], AF.Exp, [('ps', 4 + hh)], [('dec', q)])
                    for c in range(4):
                        tr(ps[6][:, c * 128:(c + 1) * 128], xa[:, c, cs], cf[:, IDN, :], ['xa', 'cf'], [('ps', 6)])
                    x3 = v3(ps[6][:, :], 8)
                    tt('dve', v3(xp[q][:], 8), x3, dt_.unsqueeze(2).to_broadcast([128, 8, 64]), ALU.mult,
                       [('ps', 6), ('dts', q)], [('xp', q)])
                    tt('dve', v3(xpp[q][:], 8), x3, dtE_.unsqueeze(2).to_broadcast([128, 8, 64]), ALU.mult,
                       [('ps', 6), ('dts', q)], [('xpp', q)])
                    if outc:
                        tt('dve', v3(xd[q][:], 8), x3,
                           hvec[:, 2, g * 8:(g + 1) * 8].unsqueeze(2).to_broadcast([128, 8, 64]), ALU.mult,
                           [('ps', 6), 'hvec'], [('xd', q)])
                    pbb = ps[7][:, 0:64].bitcast(BF16)
                    tr(pbb, BT[:, cs], cb[:, IDN, :], ['BT', 'cb'], [('ps', 7)])
                    cp('act', Btok[q][:], pbb, [('ps', 7)], [('Btok', q)])
                    if outc:
                        col0 = ci * 128 - Q0L
                        mm(ps[7][:, 128:256], BT[:, cs], CT[:, cs], True, True, ['BT', 'CT'], [('ps', 7)])
                        tt('dve', CBm[q][:], ps[7][:, 128:256], cf[:, LE, :], ALU.mult, [('ps', 7), 'cf'], [('CBm', q)])
                        tt('pool', v3(Gb[q][:], 8), v3(dec[q][:], 8), CBm[q][:].unsqueeze(1).to_broadcast([128, 8, 128]),
                           ALU.mult, [('dec', q), ('CBm', q)], [('Gb', q)])
                        for hh in range(8):
                            mm(ps[0][:, hh * 64:(hh + 1) * 64], Gb[q][:, hh * 128:(hh + 1) * 128], xp[q][:, hh * 64:(hh + 1) * 64],
                               True, True, [('Gb', q), ('xp', q)], [('ps', 0)])
                        mm(ps[1][:, :], CT[:, cs], Sb[:], True, True, ['CT', 'Sb'], [('ps', 1)])
                        tt('dve', v3(y1[q][:], 8), v3(ps[1][:, :], 8), Ecs.unsqueeze(2).to_broadcast([128, 8, 64]), ALU.mult,
                           [('ps', 1), ('Eb', q)], [('y1', q)])
                        tt('dve', y1[q][:], y1[q][:], ps[0][:, :], ALU.add, [('y1', q), ('ps', 0)], [('y1', q)])
                        tt('pool', y1[q][:], y1[q][:], xd[q][:], ALU.add, [('y1', q), ('xd', q)], [('y1', q)])
                        for k in range(16):
                            mm(ps[2][:, :], hb[hs][:, k, cs], wB[:, k, 768:1280], k == 0, k == 15,
                               ['wB', ('hb', hs)], [('ps', 2)])
                        act(zs[q][:], ps[2][:, :], AF.Silu, [('ps', 2)], [('zs', q)])
                        tt('pool', y2[q][:], y1[q][:], zs[q][:], ALU.mult, [('y1', q), ('zs', q)], [('y2', q)])
                        act(junk[:], y2[q][:], AF.Square, [('y2', q)], ['junk', ('st1', q)], accum=st1[q][:, 0:1])
                        act(st1[q][:, 1:2], st1[q][:, 0:1], AF.Sqrt, [('st1', q)], [('st1', q)], scale=1.0 / 512, bias=EPS)
                        recip(st1[q][:, 1:2], st1[q][:, 1:2], [('st1', q)], [('st1', q)])
                        stt('dve', yn[q][:], y2[q][:], st1[q][:, 1:2], nrm[:], ALU.mult, ALU.mult,
                            [('y2', q), ('st1', q), 'nrm'], [('yn', q)])
                        pyt = ps[7][:, 256:512].bitcast(BF16)
                        for c in range(4):
                            tr(pyt[:, c * 128:(c + 1) * 128], yn[q][:, c * 128:(c + 1) * 128], cb[:, IDN, :],
                               [('yn', q), 'cb'], [('ps', 7)])
                        cp('act', yTs[q][:].rearrange("p a b -> p (a b)"), pyt, [('ps', 7)], [('yTs', q)])
                        dma('sp', y_v[:, g * 4:(g + 1) * 4, col0:col0 + 128], yTs[q][:], [('yTs', q)], ['y_d'], ('yst', q))
                    mm(ps[2][:, :], Btok[q][:], xpp[q][:], True, True, [('Btok', q), ('xpp', q)], [('ps', 2)])
                    tt('dve', v3(S[:], 8), v3(S[:], 8), Etot.unsqueeze(2).to_broadcast([128, 8, 64]), ALU.mult,
                       ['S', ('Eb', q)], ['S'])
                    tt('dve', S[:], S[:], ps[2][:, :], ALU.add, ['S', ('ps', 2)], ['S'])
                    cp('pool', Sb[:], S[:], ['S'], ['Sb'])

    P.barrier()
    with ExitStack() as sc:
        def sbc(name, shape, dt=F32):
            return sc.enter_context(nc.sbuf_tensor(name, list(shape), dt))
        R1 = sbc("R1", [128, 48, 512], BF16)
        R4 = sbc("R4", [128, 8192])
        R5 = sbc("R5", [128, 16, 512])
        wsl = [sbc("wsl%d" % i, [128, 48 * 128], BF16) for i in range(3)]
        sqb = [sbc("sqb%d" % i, [128, 512], BF16) for i in range(2)]
        rsb = sbc("rsb", [128, 512])
        tmpc = [sbc("tmpc%d" % i, [128, 512]) for i in range(2)]
        ug = [sbc("ug%d" % i, [128, 514]) for i in range(1)]
        uv = [sbc("uv%d" % i, [128, 514]) for i in range(1)]
        carry = sbc("carry", [128, 88, 2])
        merged = R4[:, 0:4096].bitcast(BF16).rearrange("p (k t) -> p k t", k=16)
        xt4 = R4[:].rearrange("p (k t) -> p k t", k=16)
        h2 = hb[0]
        wcnt = [0]

        def wslot():
            wcnt[0] += 1
            return wcnt[0] % 3

        mset('dve', carry[:], 0.0, ['carry'])
        groups = [(0, 128)] + [(128 + 512 * i, 512) for i in range(4)]
        for gi, (c0g, n) in enumerate(groups):
            l0 = Q0L + c0g
            dma('sp', R1[:, 0:16, 0:n], osb_v[:, :, c0g:c0g + n], ['osb_d'], [('R1', k) for k in range(16)], 'ldo')
            dma('sp', R1[:, 16:48, 0:n], y_v[:, :, c0g:c0g + n], ['y_d'], [('R1', k) for k in range(16, 48)], 'ldy')
            dma('sp', hb[1][:, :, 0:n], hT_v[:, :, l0:l0 + n], [('hTd', l0 // 512)], [('hb', 1)], 'ldh')
            for c in range(16):
                s1, s2 = wslot(), wslot()
                dma('pool', wsl[s1][:, :], W2A[c, :, :], (), [('wsl', s1)], ('wsl', s1), cast=True)
                dma('pool', wsl[s2][:, 0:4096], W2B[c, :, :], (), [('wsl', s2)], ('wsl', s2), cast=True)
                for k in range(16):
                    mm(ps[0][:, 0:n], wsl[s1][:, k * 128:(k + 1) * 128], R1[:, k, 0:n], k == 0, k == 15,
                       [('wsl', s1), ('R1', k)], [('ps', 0)])
                for k in range(32):
                    mm(ps[1][:, 0:n], wsl[s2][:, k * 128:(k + 1) * 128], R1[:, 16 + k, 0:n], k == 0, k == 31,
                       [('wsl', s2), ('R1', 16 + k)], [('ps', 1)])
                for k in range(16):
                    mm(ps[2][:, 0:n], wsl[s1][:, (16 + k) * 128:(17 + k) * 128], hb[1][:, k, 0:n], k == 0, k == 15,
                       [('wsl', s1), ('hb', 1)], [('ps', 2)])
                for k in range(16):
                    mm(ps[3][:, 0:n], wsl[s1][:, (32 + k) * 128:(33 + k) * 128], hb[1][:, k, 0:n], k == 0, k == 15,
                       [('wsl', s1), ('hb', 1)], [('ps', 3)])
                a_, b_ = R5[:, 14, :], R5[:, 15, :]
                ka, kb_ = ('R5', 14), ('R5', 15)
                act(a_[:, 0:n], ps[2][:, 0:n], AF.Sigmoid, [('ps', 2)], [ka])
                act(b_[:, 0:n], ps[3][:, 0:n], AF.Sigmoid, [('ps', 3)], [kb_])
                tt('dve', a_[:, 0:n], a_[:, 0:n], ps[0][:, 0:n], ALU.mult, [ka, ('ps', 0)], [ka])
                tt('dve', b_[:, 0:n], b_[:, 0:n], ps[1][:, 0:n], ALU.mult, [kb_, ('ps', 1)], [kb_])
                tt('pool', merged[:, c, 0:n], a_[:, 0:n], b_[:, 0:n], ALU.add, [ka, kb_], [('R4', c // 2)])
            for c in range(16):
                s1 = wslot()
                dma('pool', wsl[s1][:, 0:2048], WO[c, :, :], (), [('wsl', s1)], ('wsl', s1), cast=True)
                b = pbank()
                for k in range(16):
                    mm(ps[b][:, 0:n], wsl[s1][:, k * 128:(k + 1) * 128], merged[:, k, 0:n], k == 0, k == 15,
                       [('wsl', s1), ('R4', k // 2)], [('ps', b)])
                cp('act', R5[:, c, 0:n], ps[b][:, 0:n], [('ps', b)], [('R5', c)])
                act(sqb[c % 2][:, 0:n], ps[b][:, 0:n], AF.Square, [('ps', b)], [('sqb', c % 2)])
                mm(ps[4][:, 0:n], cb[:, ONES, :], sqb[c % 2][:, 0:n], c == 0, c == 15, [('sqb', c % 2), 'cb'], [('ps', 4)])
            act(rsb[:, 0:n], ps[4][:, 0:n], AF.Sqrt, [('ps', 4)], ['rsb'], scale=1.0 / D, bias=EPS)
            recip(rsb[:, 0:n], rsb[:, 0:n], ['rsb'], ['rsb'])
            dma('sp', xt4[:, :, 0:n], xw_v[:, :, l0:l0 + n], (), [('R4', k) for k in range(16)], 'ldx')
            for c in range(16):
                stt('dve', tmpc[c % 2][:, 0:n], R5[:, c, 0:n], gains[:, 1, c:c + 1], rsb[:, 0:n], ALU.mult, ALU.mult,
                    [('R5', c), 'rsb', 'gains'], [('tmpc', c % 2)])
                tt('pool', xt4[:, c, 0:n], xt4[:, c, 0:n], tmpc[c % 2][:, 0:n], ALU.add, [('R4', c), ('tmpc', c % 2)],
                   [('R4', c)])
            for c in range(16):
                act(sqb[c % 2][:, 0:n], xt4[:, c, 0:n], AF.Square, [('R4', c)], [('sqb', c % 2)])
                mm(ps[4][:, 0:n], cb[:, ONES, :], sqb[c % 2][:, 0:n], c == 0, c == 15, [('sqb', c % 2), 'cb'], [('ps', 4)])
            act(rsb[:, 0:n], ps[4][:, 0:n], AF.Sqrt, [('ps', 4)], ['rsb'], scale=1.0 / D, bias=EPS)
            recip(rsb[:, 0:n], rsb[:, 0:n], ['rsb'], ['rsb'])
            for c in range(16):
                stt('dve', h2[:, c, 0:n], xt4[:, c, 0:n], gains[:, 2, c:c + 1], rsb[:, 0:n], ALU.mult, ALU.mult,
                    [('R4', c), 'rsb', 'gains'], [('hb', 0)])
            for fc in range(NFC):
                s1 = wslot()
                dma('pool', wsl[s1][:, 0:4096], WU[fc, :, :], (), [('wsl', s1)], ('wsl', s1), cast=True)
                q = 0
                for half, (pb_, ub, kq) in enumerate(((0, ug[q], ('ug', q)), (1, uv[q], ('uv', q)))):
                    for k in range(16):
                        mm(ps[pb_][:, 0:n], wsl[s1][:, k * 256 + half * 128:k * 256 + half * 128 + 128], h2[:, k, 0:n],
                           k == 0, k == 15, [('wsl', s1), ('hb', 0)], [('ps', pb_)])
                    cch = fc + half * NFC
                    cp('dve', ub[:, 0:2], carry[:, cch, :], ['carry'], [kq])
                    cp('act', ub[:, 2:2 + n], ps[pb_][:, 0:n], [('ps', pb_)], [kq])
                    if gi == 0:
                        ts1('dve', carry[:, cch, :], ub[:, n:n + 2], hval[:, 0:1], ALU.mult, [kq, 'hval'], ['carry'])
                    else:
                        cp('dve', carry[:, cch, :], ub[:, n:n + 2], [kq], ['carry'])
                    if gi > 0:
                        dst, kd = (R5[:, 0, :], ('R5', 0)) if half == 0 else (R5[:, 1, :], ('R5', 1))
                        eng = 'dve'
                        ts(eng, dst[:, 0:n], ub[:, 0:n], fconvw[:, cch, 0:1], fconvb[:, cch:cch + 1], ALU.mult, ALU.add,
                           [kq, 'fconv'], [kd])
                        stt(eng, dst[:, 0:n], ub[:, 1:n + 1], fconvw[:, cch, 1:2], dst[:, 0:n], ALU.mult, ALU.add,
                            [kq, 'fconv', kd], [kd])
                        stt(eng, dst[:, 0:n], ub[:, 2:n + 2], fconvw[:, cch, 2:3], dst[:, 0:n], ALU.mult, ALU.add,
                            [kq, 'fconv', kd], [kd])
                if gi > 0:
                    G_, V_, T_ = R5[:, 0, :], R5[:, 1, :], R5[:, 2 + fc % 2, :]
                    kT_ = ('R5', 2 + fc % 2)
                    tt('pool', T_[:, 0:n], G_[:, 0:n], G_[:, 0:n], ALU.mult, [('R5', 0)], [kT_])
                    ts('dve', T_[:, 0:n], T_[:, 0:n], 0.044715, 1.0, ALU.mult, ALU.add, [kT_], [kT_])
                    tt('pool', T_[:, 0:n], T_[:, 0:n], G_[:, 0:n], ALU.mult, [kT_, ('R5', 0)], [kT_])
                    act(T_[:, 0:n], T_[:, 0:n], AF.Sigmoid, [kT_], [kT_], scale=1.5957691216057308)
                    tt('dve', T_[:, 0:n], T_[:, 0:n], G_[:, 0:n], ALU.mult, [kT_, ('R5', 0)], [kT_])
                    tt('pool', R1[:, fc, 0:n], T_[:, 0:n], V_[:, 0:n], ALU.mult, [kT_, ('R5', 1)], [('R1', fc)])
            if gi == 0:
                continue
            for c in range(16):
                s1 = wslot()
                dma('pool', wsl[s1][:, 0:NFC * 128], WD[c, :, :], (), [('wsl', s1)], ('wsl', s1), cast=True)
                b = pbank()
                for k in range(NFC):
                    mm(ps[b][:, 0:n], wsl[s1][:, k * 128:(k + 1) * 128], R1[:, k, 0:n], k == 0, k == NFC - 1,
                       [('wsl', s1), ('R1', k)], [('ps', b)])
                cp('act', R5[:, c, 0:n], ps[b][:, 0:n], [('ps', b)], [('R5', c)])
                act(sqb[c % 2][:, 0:n], ps[b][:, 0:n], AF.Square, [('ps', b)], [('sqb', c % 2)])
                mm(ps[4][:, 0:n], cb[:, ONES, :], sqb[c % 2][:, 0:n], c == 0, c == 15, [('sqb', c % 2), 'cb'], [('ps', 4)])
            act(rsb[:, 0:n], ps[4][:, 0:n], AF.Sqrt, [('ps', 4)], ['rsb'], scale=1.0 / D, bias=EPS)
            recip(rsb[:, 0:n], rsb[:, 0:n], ['rsb'], ['rsb'])
            for c in range(16):
                stt('dve', tmpc[c % 2][:, 0:n], R5[:, c, 0:n], gains[:, 3, c:c + 1], rsb[:, 0:n], ALU.mult, ALU.mult,
                    [('R5', c), 'rsb', 'gains'], [('tmpc', c % 2)])
                tt('pool', R5[:, c, 0:n], xt4[:, c, 0:n], tmpc[c % 2][:, 0:n], ALU.add, [('R4', c), ('tmpc', c % 2)],
                   [('R5', c)])
            t0 = c0g - 128
            dma('sp', out_v[:, :, t0:t0 + n], R5[:, :, 0:n], [('R5', c) for c in range(16)], ['out'], 'stout')

        P.emit(nc, es)
    es.close()
    return nc


_NC = None


def _prep_weights(w_in, w_sb_proj, w_ssd_proj, w_out, w_up, w_down):
    def blk(w, cols):
        K = w.shape[0]
        sub = w[:, cols].reshape(K // 128, 128, len(cols))
        return np.ascontiguousarray(sub.transpose(1, 0, 2)).reshape(128, -1)
    ar = np.arange
    WA = np.stack([blk(w_in, np.concatenate([ar(h * 128, h * 128 + 128), ar(2048 + h * 128, 2048 + h * 128 + 128),
                                             ar(4096 + h * 128, 4096 + h * 128 + 128)])) for h in range(16)])
    WB = np.stack([blk(w_in, np.concatenate([ar(10240 + g * 512, 10240 + g * 512 + 512),
                                             ar(14336 + g * 128, 14336 + g * 128 + 128),
                                             ar(15360 + g * 128, 15360 + g * 128 + 128),
                                             ar(6144 + g * 512, 6144 + g * 512 + 512),
                                             ar(16384 + g * 8, 16384 + g * 8 + 8)])) for g in range(8)])
    W2A = np.stack([np.concatenate([blk(w_sb_proj, ar(c * 128, c * 128 + 128)),
                                    blk(w_in, ar(16448 + c * 128, 16448 + c * 128 + 128)),
                                    blk(w_in, ar(18496 + c * 128, 18496 + c * 128 + 128))], axis=1) for c in range(16)])
    W2B = np.stack([blk(w_ssd_proj, ar(c * 128, c * 128 + 128)) for c in range(16)])
    WO = np.stack([blk(w_out, ar(c * 128, c * 128 + 128)) for c in range(16)])
    WU = np.stack([blk(w_up, np.concatenate([ar(fc * 128, fc * 128 + 128), ar(D_FF + fc * 128, D_FF + fc * 128 + 128)]))
                   for fc in range(NFC)])
    WD = np.stack([blk(w_down, ar(c * 128, c * 128 + 128)) for c in range(16)])
    return dict(WA=WA, WB=WB, W2A=W2A, W2B=W2B, WO=WO, WU=WU, WD=WD)


def kernel(x, norm_mix_pre, w_in, ssd_conv_w, ssd_conv_b, dt_bias, a_log, d_skip, ssd_norm,
           w_sb_proj, w_ssd_proj, w_out, norm_mix_post, norm_ffn_pre, w_up, ffn_conv_w,
           ffn_conv_b, w_down, norm_ffn_post):
    global _NC
    in_maps = _in_maps(x, norm_mix_pre, w_in, ssd_conv_w, ssd_conv_b, dt_bias, a_log, d_skip, ssd_norm,
                       w_sb_proj, w_ssd_proj, w_out, norm_mix_post, norm_ffn_pre, w_up, ffn_conv_w,
                       ffn_conv_b, w_down, norm_ffn_post)
    if _NC is None:
        _NC = build_nc()
    res = run_bass_kernel_spmd(_NC, in_maps, core_ids=list(range(8)))
    out = np.empty((2, 8192, D), np.float32)
    for core in range(8):
        b, c = core // 4, core % 4
        out[b, 2048 * c:2048 * (c + 1), :] = res.results[core]["outT"].T
    return out


def _in_maps(x, norm_mix_pre, w_in, ssd_conv_w, ssd_conv_b, dt_bias, a_log, d_skip, ssd_norm,
             w_sb_proj, w_ssd_proj, w_out, norm_mix_post, norm_ffn_pre, w_up, ffn_conv_w,
             ffn_conv_b, w_down, norm_ffn_post):
    f32 = np.float32
    x = np.asarray(x, f32)
    shared = _prep_weights(np.asarray(w_in, f32)[0], np.asarray(w_sb_proj, f32)[0], np.asarray(w_ssd_proj, f32)[0],
                           np.asarray(w_out, f32)[0], np.asarray(w_up, f32)[0], np.asarray(w_down, f32)[0])
    r = np.arange(128)
    cm = np.stack([(r[:, None] < r[None, :]), (r[:, None] <= r[None, :]), (r[:, None] >= r[None, :]),
                   (r[:, None] > r[None, :]), np.ones((128, 128), bool), np.eye(128, dtype=bool)], axis=1).astype(f32)
    shared["consts"] = np.ascontiguousarray(cm.reshape(128, 768))

    def pk(v):
        return np.asarray(v, f32).reshape(-1, 128).T
    shared["gains"] = np.ascontiguousarray(np.stack([pk(norm_mix_pre[0]), pk(norm_mix_post[0]), pk(norm_ffn_pre[0]),
                                                     pk(norm_ffn_post[0])], axis=1).reshape(128, 64))
    scw = np.asarray(ssd_conv_w, f32)[0]
    shared["sconvw"] = np.ascontiguousarray(scw.reshape(4, 48, 128).transpose(2, 1, 0).reshape(128, 192))
    shared["sconvb"] = np.ascontiguousarray(pk(ssd_conv_b[0]))
    fcw = np.asarray(ffn_conv_w, f32)[0]
    shared["fconvw"] = np.ascontiguousarray(fcw.reshape(3, 88, 128).transpose(2, 1, 0).reshape(128, 264))
    shared["fconvb"] = np.ascontiguousarray(pk(ffn_conv_b[0]))
    hv = np.stack([np.asarray(dt_bias, f32)[0], np.asarray(a_log, f32)[0], np.asarray(d_skip, f32)[0]])
    shared["hvec"] = np.ascontiguousarray(np.broadcast_to(hv.reshape(1, 192), (128, 192)))
    shared["ssdn"] = np.ascontiguousarray(np.broadcast_to(np.asarray(ssd_norm, f32)[0][None, :], (128, 4096)))

    in_maps = []
    for core in range(8):
        b, c = core // 4, core % 4
        end = 2048 * (c + 1)
        start = end - WIN
        xwin = np.zeros((D, WIN), f32)
        lo = max(start, 0)
        xwin[:, lo - start:] = x[b, lo:end, :].T
        tok = start + np.arange(WIN)
        val = (tok >= 0).astype(f32).reshape(64, 128).T
        m = dict(shared)
        m["xw"] = xwin
        m["valid"] = np.ascontiguousarray(val)
        m["hval"] = np.full((128, 1), 1.0 if c > 0 else 0.0, f32)
        in_maps.append(m)
    return in_maps
```

```python
import bisect
import os
from contextlib import ExitStack

import numpy as np
import concourse.bass as bass
import concourse.mybir as mybir
from concourse.bass_utils import run_bass_kernel_spmd

F32 = mybir.dt.float32
BF16 = mybir.dt.bfloat16
AF = mybir.ActivationFunctionType
ALU = mybir.AluOpType

D = 2048
WIN = 8192
NT = 16
Q0L = 6016
NQ = 2176
EPS = 1e-6
SCALE = 128 ** -0.5
D_FF = 5632
NFC = 44
QBS = [(0, 128)] + [(128 + 512 * i, 512) for i in range(4)]


class Prog:
    def __init__(self):
        self.ops = []
        self.bars = []

    def barrier(self):
        self.bars.append(len(self.ops))

    def add(self, eng, fn, R=(), W=(), dk=None):
        self.ops.append((eng, fn, tuple(R), tuple(W), dk))

    def emit(self, nc, es):
        ops = self.ops
        n = len(ops)
        lastw, readers = {}, {}
        deps = []
        for i, (eng, fn, R, W, dk) in enumerate(ops):
            d = set()
            for r in R:
                j = lastw.get(r)
                if j is not None:
                    d.add(j)
            for w in W:
                j = lastw.get(w)
                if j is not None:
                    d.add(j)
                rs = readers.get(w)
                if rs:
                    d.update(rs)
            for r in R:
                readers.setdefault(r, []).append(i)
            for w in W:
                lastw[w] = i
                readers[w] = []
            d.discard(i)
            deps.append(d)
        sig = [False] * n
        cdeps = []
        dma_idx = {}
        for i, op in enumerate(ops):
            if op[4] is not None:
                dma_idx.setdefault(op[4], []).append(i)
        for i, d in enumerate(deps):
            eng = ops[i][0]
            best = {}
            dks = set()
            for j in d:
                oj = ops[j]
                if oj[4] is not None:
                    dks.add(oj[4])
                    continue
                if oj[0] == 'pe' and eng == 'pe':
                    continue
                if best.get(oj[0], -1) < j:
                    best[oj[0]] = j
            for j in best.values():
                sig[j] = True
            cdeps.append((best, dks))
        eng_ops = {}
        for i, op in enumerate(ops):
            if op[4] is None:
                eng_ops.setdefault(op[0], []).append(i)
        for p in self.bars:
            for E in ('pe', 'act', 'dve', 'pool', 'sp'):
                first = next((i for i in range(p, n) if ops[i][0] == E), None)
                if first is None:
                    continue
                best, dks = cdeps[first]
                for E2, lst in eng_ops.items():
                    if E2 == 'pe' and E == 'pe':
                        continue
                    pos = bisect.bisect_left(lst, p)
                    if pos > 0:
                        j = lst[pos - 1]
                        if best.get(E2, -1) < j:
                            best[E2] = j
                            sig[j] = True
                for k, lst in dma_idx.items():
                    if lst and lst[0] < p:
                        dks.add(k)
        seq = [0] * n
        cnt = {}
        for i, op in enumerate(ops):
            if op[4] is None and sig[i]:
                cnt[op[0]] = cnt.get(op[0], 0) + 1
                seq[i] = cnt[op[0]]
        esem = {e: es.enter_context(nc.semaphore("se_" + e)) for e in ('pe', 'act', 'dve', 'pool')}
        dsem = {}
        for k in dma_idx:
            dsem[k] = es.enter_context(nc.semaphore("sd_%d" % len(dsem)))
        block = es.enter_context(nc.Block())

        def run(ename):
            def body(e):
                waited = {}
                for i, (eng, fn, R, W, dk) in enumerate(ops):
                    if eng != ename:
                        continue
                    best, dks = cdeps[i]
                    for se, j in best.items():
                        key = ('e', se)
                        if waited.get(key, 0) < seq[j]:
                            e.wait_ge(esem[se], seq[j])
                            waited[key] = seq[j]
                    for k in dks:
                        val = 16 * bisect.bisect_left(dma_idx[k], i)
                        key = ('d', k)
                        if waited.get(key, 0) < val:
                            e.wait_ge(dsem[k], val)
                            waited[key] = val
                    ins = fn(e)
                    if dk is not None:
                        ins.then_inc(dsem[dk], 16)
                    elif sig[i]:
                        ins.then_inc(esem[eng], 1)
                if ename == 'sp':
                    for k, lst in dma_idx.items():
                        e.wait_ge(dsem[k], 16 * len(lst))
            return body

        block.tensor(run('pe'))
        block.scalar(run('act'))
        block.vector(run('dve'))
        block.gpsimd(run('pool'))
        block.sync(run('sp'))


def build_nc():
    nc = bass.Bass("TRN2", target_bir_lowering=False)
    P = Prog()

    def din(name, shape, dt=F32):
        return nc.dram_tensor(name, list(shape), dt, kind="ExternalInput").ap()

    xw = din("xw", [D, WIN])
    WA = din("WA", [16, 128, 16 * 384])
    WB = din("WB", [8, 128, 16 * 1288])
    W2A = din("W2A", [16, 128, 48 * 128])
    W2B = din("W2B", [16, 128, 32 * 128])
    WO = din("WO", [16, 128, 16 * 128])
    WU = din("WU", [NFC, 128, 16 * 256])
    WD = din("WD", [16, 128, NFC * 128])
    consts_d = din("consts", [128, 6 * 128])
    gains_d = din("gains", [128, 64])
    sconvw_d = din("sconvw", [128, 48 * 4])
    sconvb_d = din("sconvb", [128, 48])
    fconvw_d = din("fconvw", [128, 88 * 3])
    fconvb_d = din("fconvb", [128, 88])
    hvec_d = din("hvec", [128, 3 * 64])
    ssdn_d = din("ssdn", [128, 4096])
    valid_d = din("valid", [128, 64])
    hval_d = din("hval", [128, 1])
    outT = nc.dram_tensor("outT", [D, 2048], F32, kind="ExternalOutput").ap()
    skind = "ExternalOutput" if os.environ.get("KDEBUG") else "Internal"
    hT_d = nc.dram_tensor("hT_d", [D, WIN], BF16, kind=skind).ap()
    osb_d = nc.dram_tensor("osb_d", [D, NQ], BF16, kind=skind).ap()
    y_d = nc.dram_tensor("y_d", [4096, NQ], BF16, kind=skind).ap()

    xw_v = xw.rearrange("(k p) t -> p k t", p=128)
    hT_v = hT_d.rearrange("(k p) t -> p k t", p=128)
    osb_v = osb_d.rearrange("(k p) t -> p k t", p=128)
    y_v = y_d.rearrange("(k p) t -> p k t", p=128)
    out_v = outT.rearrange("(k p) t -> p k t", p=128)

    es = ExitStack()

    def sb(name, shape, dt=F32):
        return es.enter_context(nc.sbuf_tensor("s_" + name, list(shape), dt))

    ps = [es.enter_context(nc.psum_tensor("ps%d" % i, [128, 512], F32)) for i in range(8)]

    def mm(out, lhsT, rhs, start, stop, R, W):
        P.add('pe', lambda e: e.matmul(out, lhsT, rhs, start=start, stop=stop), R, W)

    def tr(out, in_, ident, R, W):
        P.add('pe', lambda e: e.transpose(out, in_, ident), R, W)

    def act(out, in_, func, R, W, scale=1.0, bias=0.0, accum=None):
        if accum is None:
            P.add('act', lambda e: e.activation(out=out, in_=in_, func=func, bias=bias, scale=scale), R, W)
        else:
            P.add('act', lambda e: e.activation(out=out, in_=in_, func=func, bias=bias, scale=scale,
                                                accum_out=accum), R, W)

    def tt(eng, out, a, b, op, R, W):
        P.add(eng, lambda e: e.tensor_tensor(out=out, in0=a, in1=b, op=op), R, W)

    def ts(eng, out, a, s1, s2, op0, op1, R, W):
        P.add(eng, lambda e: e.tensor_scalar(out=out, in0=a, scalar1=s1, scalar2=s2, op0=op0, op1=op1), R, W)

    def ts1(eng, out, a, s1, op0, R, W):
        P.add(eng, lambda e: e.tensor_single_scalar(out=out, in_=a, scalar=s1, op=op0), R, W)

    def stt(eng, out, a, s, b, op0, op1, R, W):
        eng = 'dve'
        P.add(eng, lambda e: e.scalar_tensor_tensor(out=out, in0=a, scalar=s, in1=b, op0=op0, op1=op1), R, W)

    def cp(eng, out, a, R, W):
        if eng == 'act':
            act(out, a, AF.Copy, R, W)
        else:
            P.add(eng, lambda e: e.tensor_copy(out=out, in_=a), R, W)

    def recip(out, a, R, W):
        P.add('dve', lambda e: e.reciprocal(out=out, in_=a), R, W)

    def mset(eng, out, val, W):
        P.add(eng, lambda e: e.memset(out, val), (), W)

    def dma(q, out, in_, R, W, dk, cast=False):
        if cast:
            P.add(q, lambda e: e.dma_start(out=out, in_=in_, max_dma_last_dim=4096), R, W, dk)
        else:
            P.add(q, lambda e: e.dma_start(out=out, in_=in_), R, W, dk)

    cf = sb("cf", [128, 6, 128])
    cb = sb("cb", [128, 6, 128], BF16)
    gains = sb("gains", [128, 4, 16])
    sconvw = sb("sconvw", [128, 48, 4])
    sconvb = sb("sconvb", [128, 48])
    fconvw = sb("fconvw", [128, 88, 3])
    fconvb = sb("fconvb", [128, 88])
    hvec = sb("hvec", [128, 3, 64])
    aneg = sb("aneg", [128, 64])
    valid = sb("valid", [128, 64])
    hval = sb("hval", [128, 1])
    dma('sp', cf[:].rearrange("p a b -> p (a b)"), consts_d[:, :], (), ['cf'], 'c0')
    dma('sp', gains[:].rearrange("p a b -> p (a b)"), gains_d[:, :], (), ['gains'], 'c0')
    dma('sp', sconvw[:].rearrange("p a b -> p (a b)"), sconvw_d[:, :], (), ['sconv'], 'c0')
    dma('sp', sconvb[:], sconvb_d[:, :], (), ['sconv'], 'c0')
    dma('sp', fconvw[:].rearrange("p a b -> p (a b)"), fconvw_d[:, :], (), ['fconv'], 'c0')
    dma('sp', fconvb[:], fconvb_d[:, :], (), ['fconv'], 'c0')
    dma('sp', hvec[:].rearrange("p a b -> p (a b)"), hvec_d[:, :], (), ['hvec'], 'c0')
    dma('sp', valid[:], valid_d[:, :], (), ['valid'], 'c0')
    dma('sp', hval[:], hval_d[:, :], (), ['hval'], 'c0')
    cp('dve', cb[:], cf[:], ['cf'], ['cb'])
    act(aneg[:], hvec[:, 1, :], AF.Exp, ['hvec'], ['aneg'])
    ts1('dve', aneg[:], aneg[:], -1.0, ALU.mult, ['aneg'], ['aneg'])
    LT, LE, GE, GT, ONES, IDN = range(6)

    hb = [sb("hb%d" % i, [128, 16, 512], BF16) for i in range(2)]
    hcnt = [0]

    def load_h(t0, n=512):
        s = hcnt[0] % 2
        hcnt[0] += 1
        dma('sp', hb[s][:, :, 0:n], hT_v[:, :, t0:t0 + n], [('hTd', t0 // 512)], [('hb', s)], ('hbld', s))
        return s

    pcnt = [0]

    def pbank():
        pcnt[0] += 1
        return pcnt[0] % 2

    with ExitStack() as s0:
        xt = [s0.enter_context(nc.sbuf_tensor("xt%d" % i, [128, 16, 512], F32)) for i in range(2)]
        sq = s0.enter_context(nc.sbuf_tensor("sq", [128, 16, 512], BF16))
        rs = s0.enter_context(nc.sbuf_tensor("rs0", [128, 512], F32))
        for t in range(NT):
            xs = t % 2
            hs = t % 2
            dma('sp', xt[xs][:], xw_v[:, :, t * 512:(t + 1) * 512], (), [('xt', xs)], ('xt', xs))
            act(sq[:], xt[xs][:], AF.Square, [('xt', xs)], ['sq'])
            b = pbank()
            for k in range(16):
                mm(ps[b][:, :], cb[:, ONES, :], sq[:, k, :], k == 0, k == 15, ['sq', 'cb'], [('ps', b)])
            act(rs[:], ps[b][:, :], AF.Sqrt, [('ps', b)], ['rs'], scale=1.0 / D, bias=EPS)
            recip(rs[:], rs[:], ['rs'], ['rs'])
            for k in range(16):
                stt('dve', hb[hs][:, k, :], xt[xs][:, k, :], gains[:, 0, k:k + 1], rs[:], ALU.mult, ALU.mult,
                    [('xt', xs), 'rs', 'gains'], [('hb', hs)])
            dma('sp', hT_v[:, :, t * 512:(t + 1) * 512], hb[hs][:], [('hb', hs)], [('hTd', t)], ('hbst', hs))
        hcnt[0] = 0

    P.barrier()
    with ExitStack() as sa:
        def sba(name, shape, dt=F32):
            return sa.enter_context(nc.sbuf_tensor(name, list(shape), dt))
        wq = [sba("wq%d" % i, [128, 16, 384], BF16) for i in range(2)]
        kT = [sba("kT%d" % i, [128, WIN], BF16) for i in range(2)]
        vv = [sba("vv%d" % i, [128, 64, 128], BF16) for i in range(2)]
        qT = [sba("qT%d" % i, [128, NQ], BF16) for i in range(2)]
        eb = [[sba("eb%d_%d" % (st, i), [128, 512], F32) for i in range(2)] for st in range(2)]
        spb = [[sba("spb%d_%d" % (st, i), [128, 512], BF16) for i in range(2)] for st in range(2)]
        gb = [[sba("gb%d_%d" % (st, i), [128, 512], F32) for i in range(2)] for st in range(2)]
        wb = [[sba("wb%d_%d" % (st, i), [128, 512], BF16) for i in range(2)] for st in range(2)]
        ob = [sba("ob%d" % i, [128, 512], BF16) for i in range(2)]
        vtb = [sba("vtb%d" % i, [128, 512], BF16) for i in range(2)]
        SBANKS = [(2, 4, 5), (3, 6, 7)]
        PAIRS = [[(1664, 256), (1920, 256)], [(1152, 512), (640, 512)], [(128, 512), (0, 128)]]
        ocnt = 0
        for h in range(16):
            ws = h % 2
            dma('pool', wq[ws][:].rearrange("p k c -> p (k c)"), WA[h, :, :], (), [('wq', ws)], ('wq', ws), cast=True)
            for t in range(NT):
                hs = load_h(t * 512)
                b = pbank()
                for k in range(16):
                    mm(ps[b][:, :], wq[ws][:, k, 128:256], hb[hs][:, k, :], k == 0, k == 15,
                       [('wq', ws), ('hb', hs)], [('ps', b)])
                cp('dve', kT[ws][:, t * 512:(t + 1) * 512], ps[b][:, :], [('ps', b)], [('kT', ws, t)])
                b = pbank()
                for k in range(16):
                    mm(ps[b][:, :], wq[ws][:, k, 256:384], hb[hs][:, k, :], k == 0, k == 15,
                       [('wq', ws), ('hb', hs)], [('ps', b)])
                vs_ = t % 2
                cp('dve', vtb[vs_][:, :], ps[b][:, :], [('ps', b)], [('vtb', vs_)])
                b = pbank()
                pvt = ps[b][:, 0:256].bitcast(BF16)
                for s in range(4):
                    tr(pvt[:, s * 128:(s + 1) * 128], vtb[vs_][:, s * 128:(s + 1) * 128], cb[:, IDN, :],
                       [('vtb', vs_), 'cb'], [('ps', b)])
                cp('dve', vv[ws][:, t * 4:(t + 1) * 4, :].rearrange("p a b -> p (a b)"), pvt,
                   [('ps', b)], [('vv', ws, t)])
                if t >= 11:
                    c0 = 384 if t == 11 else 0
                    n = 512 - c0
                    qc0 = t * 512 + c0 - Q0L
                    b = pbank()
                    for k in range(16):
                        mm(ps[b][:, 0:n], wq[ws][:, k, 0:128], hb[hs][:, k, c0:512], k == 0, k == 15,
                           [('wq', ws), ('hb', hs)], [('ps', b)])
                    cp('dve', qT[ws][:, qc0:qc0 + n], ps[b][:, 0:n], [('ps', b)], [('qT', ws)])
            if os.environ.get("KDEBUG") and h == 0:
                dk_ = nc.dram_tensor("dbg_k", [128, WIN], BF16, kind="ExternalOutput").ap()
                dv_ = nc.dram_tensor("dbg_v", [128, WIN], BF16, kind="ExternalOutput").ap()
                dq_ = nc.dram_tensor("dbg_q", [128, NQ], BF16, kind="ExternalOutput").ap()
                dma('sp', dk_[:, :], kT[0][:, :], [('kT', 0, t) for t in range(16)], ['dbgk'], 'dbg')
                dma('sp', dv_[:, :], vv[0][:].rearrange("p a b -> p (a b)"), [('vv', 0, t) for t in range(16)], ['dbgv'], 'dbg')
                dma('sp', dq_[:, :], qT[0][:, :], [('qT', 0)], ['dbgq'], 'dbg')
                dw_ = nc.dram_tensor("dbg_w", [128, 6144], BF16, kind="ExternalOutput").ap()
                dma('sp', dw_[:, :], wq[0][:].rearrange("p k c -> p (k c)"), [('wq', 0)], ['dbgw'], 'dbg')
            for pair in PAIRS:
                sts = []
                for si, (q0, nq) in enumerate(pair):
                    gq0 = Q0L + q0
                    sts.append(dict(si=si, q0=q0, nq=nq, gq0=gq0, banks=SBANKS[si],
                                    blocks=list(range((gq0 + nq) // 128 - 1, -1, -1))))

                def geom(st, kb):
                    m = kb - st['gq0'] // 128
                    return m >= 0, 128 * max(m, 0)

                def zmm(st, i):
                    kb = st['blocks'][i]
                    _, c0 = geom(st, kb)
                    zb = st['banks'][0]
                    q0, nq = st['q0'], st['nq']
                    mm(ps[zb][:, c0:nq], kT[ws][:, kb * 128:(kb + 1) * 128], qT[ws][:, q0 + c0:q0 + nq], True, True,
                       [('kT', ws, kb // 4), ('qT', ws)], [('ps', zb)])
                for st in sts:
                    zmm(st, 0)
                for i in range(max(len(st['blocks']) for st in sts)):
                    live = [st for st in sts if i < len(st['blocks'])]
                    s_ = i % 2
                    for st in live:
                        si, nq = st['si'], st['nq']
                        zb = st['banks'][0]
                        diag, c0 = geom(st, st['blocks'][i])
                        act(eb[si][s_][:, c0:nq], ps[zb][:, c0:nq], AF.Exp, [('ps', zb)], [('eb', si, s_)], scale=SCALE)
                        if diag:
                            tt('dve', eb[si][s_][:, c0:c0 + 128], eb[si][s_][:, c0:c0 + 128], cf[:, LT, :], ALU.mult,
                               [('eb', si, s_), 'cf'], [('eb', si, s_)])
                        act(spb[si][s_][:, c0:nq], eb[si][s_][:, c0:nq], AF.Ln, [('eb', si, s_)], [('spb', si, s_)], bias=1.0)
                    for st in live:
                        if i + 1 < len(st['blocks']):
                            zmm(st, i + 1)
                    for st in live:
                        si, nq = st['si'], st['nq']
                        xb = st['banks'][1]
                        _, c0 = geom(st, st['blocks'][i])
                        mm(ps[xb][:, c0:nq], cb[:, GE, :], spb[si][s_][:, c0:nq], i == 0, False,
                           [('spb', si, s_), 'cb'], [('ps', xb)])
                    for st in live:
                        si, nq = st['si'], st['nq']
                        xb = st['banks'][1]
                        _, c0 = geom(st, st['blocks'][i])
                        act(gb[si][s_][:, c0:nq], ps[xb][:, c0:nq], AF.Exp, [('ps', xb)], [('gb', si, s_)], scale=-1.0)
                    for st in live:
                        si, nq = st['si'], st['nq']
                        _, c0 = geom(st, st['blocks'][i])
                        tt('dve', wb[si][s_][:, c0:nq], eb[si][s_][:, c0:nq], gb[si][s_][:, c0:nq], ALU.mult,
                           [('eb', si, s_), ('gb', si, s_)], [('wb', si, s_)])
                    for st in live:
                        si, nq = st['si'], st['nq']
                        xb, obk = st['banks'][1], st['banks'][2]
                        kb = st['blocks'][i]
                        _, c0 = geom(st, kb)
                        last = i == len(st['blocks']) - 1
                        mm(ps[obk][:, c0:nq], vv[ws][:, kb, :], wb[si][s_][:, c0:nq], i == 0, last,
                           [('wb', si, s_), ('vv', ws, kb // 4)], [('ps', obk)])
                        mm(ps[xb][:, c0:nq], cb[:, LT, :], spb[si][s_][:, c0:nq], False, last,
                           [('spb', si, s_), 'cb'], [('ps', xb)])
                for st in sts:
                    q0, nq, obk = st['q0'], st['nq'], st['banks'][2]
                    os_ = ocnt % 2
                    ocnt += 1
                    cp('dve', ob[os_][:, 0:nq], ps[obk][:, 0:nq], [('ps', obk)], [('ob', os_)])
                    dma('sp', osb_d[h * 128:(h + 1) * 128, q0:q0 + nq], ob[os_][:, 0:nq], [('ob', os_)], ['osb_d'],
                        ('obst', os_))

    P.barrier()
    with ExitStack() as sbx:
        def sbb(name, shape, dt=F32):
            return sbx.enter_context(nc.sbuf_tensor(name, list(shape), dt))
        wB = sbb("wB", [128, 16, 1288], BF16)
        pre = sbb("pre", [128, 6, 515])
        acc = sbb("acc", [128, 6, 512])
        xa = sbb("xa", [128, 4, 512])
        BT = sbb("BT", [128, 512], BF16)
        CT = sbb("CT", [128, 512], BF16)
        nrm = sbb("nrm", [128, 512])
        S = sbb("S", [128, 512])
        Sb = sbb("Sb", [128, 512], BF16)
        dts = [sbb("dts%d" % i, [128, 32]) for i in range(2)]
        Eb = [sbb("Eb%d" % i, [128, 24]) for i in range(2)]
        rseg = [sbb("rseg%d" % i, [128, 1024]) for i in range(2)]
        dec = [sbb("dec%d" % i, [128, 1024]) for i in range(2)]
        Gb = [sbb("Gb%d" % i, [128, 1024], BF16) for i in range(2)]
        CBm = [sbb("CBm%d" % i, [128, 128]) for i in range(2)]
        xp = [sbb("xp%d" % i, [128, 512], BF16) for i in range(2)]
        xpp = [sbb("xpp%d" % i, [128, 512], BF16) for i in range(2)]
        xd = [sbb("xd%d" % i, [128, 512]) for i in range(2)]
        Btok = [sbb("Btok%d" % i, [128, 128], BF16) for i in range(2)]
        zs = [sbb("zs%d" % i, [128, 512]) for i in range(2)]
        y1 = [sbb("y1%d" % i, [128, 512]) for i in range(2)]
        y2 = [sbb("y2%d" % i, [128, 512]) for i in range(2)]
        junk = sbb("junk", [128, 512])
        st1 = [sbb("st1%d" % i, [128, 2]) for i in range(2)]
        yn = [sbb("yn%d" % i, [128, 512], BF16) for i in range(2)]
        yTs = [sbb("yTs%d" % i, [128, 4, 128], BF16) for i in range(2)]

        def v3(ap, h):
            return ap.rearrange("p (h l) -> p h l", h=h)

        for g in range(8):
            dma('pool', wB[:].rearrange("p k c -> p (k c)"), WB[g, :, :], (), ['wB'], 'wB', cast=True)
            dma('sp', nrm[:], ssdn_d[:, g * 512:(g + 1) * 512], (), ['nrm'], 'nrm')
            mset('dve', S[:], 0.0, ['S'])
            mset('dve', Sb[:], 0.0, ['Sb'])
            for t in range(NT):
                hs = load_h(t * 512)
                if t == 0:
                    mset('dve', pre[:, :, 0:3], 0.0, ['pre'])
                else:
                    cp('dve', pre[:, :, 0:3], pre[:, :, 512:515], ['pre'], ['pre'])
                for c in range(6):
                    b = pbank()
                    w0 = c * 128 if c < 4 else 512 + (c - 4) * 128
                    for k in range(16):
                        mm(ps[b][:, :], wB[:, k, w0:w0 + 128], hb[hs][:, k, :], k == 0, k == 15,
                           ['wB', ('hb', hs)], [('ps', b)])
                    cp('act', pre[:, c, 3:515], ps[b][:, :], [('ps', b)], ['pre'])
                for c in range(6):
                    ch = g * 4 + c if c < 4 else (32 + g if c == 4 else 40 + g)
                    eng = 'dve'
                    ts(eng, acc[:, c, :], pre[:, c, 0:512], sconvw[:, ch, 0:1], sconvb[:, ch:ch + 1], ALU.mult, ALU.add,
                       ['pre', 'sconv'], [('acc', c)])
                    for kk in range(1, 4):
                        stt(eng, acc[:, c, :], pre[:, c, kk:kk + 512], sconvw[:, ch, kk:kk + 1], acc[:, c, :],
                            ALU.mult, ALU.add, ['pre', 'sconv', ('acc', c)], [('acc', c)])
                    if c < 4:
                        act(xa[:, c, :], acc[:, c, :], AF.Silu, [('acc', c)], ['xa'])
                    elif c == 4:
                        act(BT[:], acc[:, c, :], AF.Silu, [('acc', c)], ['BT'])
                    else:
                        act(CT[:], acc[:, c, :], AF.Silu, [('acc', c)], ['CT'])
                for s in range(4):
                    ci = t * 4 + s
                    outc = ci >= 47
                    q = ci % 2
                    cs = slice(s * 128, (s + 1) * 128)
                    dt_, la_, tmp_, dtE_ = dts[q][:, 0:8], dts[q][:, 8:16], dts[q][:, 16:24], dts[q][:, 24:32]
                    for k in range(16):
                        mm(ps[3][:, 0:8], hb[hs][:, k, cs], wB[:, k, 1280:1288], k == 0, k == 15,
                           ['wB', ('hb', hs)], [('ps', 3)])
                    tt('dve', tmp_, ps[3][:, 0:8], hvec[:, 0, g * 8:(g + 1) * 8], ALU.add, [('ps', 3), 'hvec'], [('dts', q)])
                    act(tmp_, tmp_, AF.Exp, [('dts', q)], [('dts', q)])
                    act(tmp_, tmp_, AF.Ln, [('dts', q)], [('dts', q)], bias=1.0)
                    ts1('dve', dt_, tmp_, valid[:, ci:ci + 1], ALU.mult, [('dts', q), 'valid'], [('dts', q)])
                    tt('dve', la_, dt_, aneg[:, g * 8:(g + 1) * 8], ALU.mult, [('dts', q), 'aneg'], [('dts', q)])
                    mm(ps[3][:, 8:16], cf[:, LE, :], la_, True, True, [('dts', q), 'cf'], [('ps', 3)])
                    mm(ps[3][:, 16:24], cf[:, GT, :], la_, True, True, [('dts', q), 'cf'], [('ps', 3)])
                    mm(ps[3][:, 24:32], cf[:, ONES, :], la_, True, True, [('dts', q), 'cf'], [('ps', 3)])
                    act(Eb[q][:], ps[3][:, 8:32], AF.Exp, [('ps', 3)], [('Eb', q)])
                    Ecs, Edte, Etot = Eb[q][:, 0:8], Eb[q][:, 8:16], Eb[q][:, 16:24]
                    tt('dve', dtE_, dt_, Edte, ALU.mult, [('dts', q), ('Eb', q)], [('dts', q)])
                    if outc:
                        tt('dve', v3(rseg[q][:], 8), cf[:, LE, :].unsqueeze(1).to_broadcast([128, 8, 128]),
                           la_.unsqueeze(2).to_broadcast([128, 8, 128]), ALU.mult, [('dts', q), 'cf'], [('rseg', q)])
                        for hh in range(2):
                            mm(ps[4 + hh][:, :], cf[:, GT, :], rseg[q][:, hh * 512:(hh + 1) * 512], True, True,
                               [('rseg', q), 'cf'], [('ps', 4 + hh)])
                            act(dec[q][:, hh * 512:(hh + 1) * 512], ps[4 + hh][:, :], AF.Exp, [('ps', 4 + hh)], [('dec', q)])
                    for c in range(4):
                        tr(ps[6][:, c * 128:(c + 1) * 128], xa[:, c, cs], cf[:, IDN, :], ['xa', 'cf'], [('ps', 6)])
                    x3 = v3(ps[6][:, :], 8)
                    tt('dve', v3(xp[q][:], 8), x3, dt_.unsqueeze(2).to_broadcast([128, 8, 64]), ALU.mult,
                       [('ps', 6), ('dts', q)], [('xp', q)])
                    tt('dve', v3(xpp[q][:], 8), x3, dtE_.unsqueeze(2).to_broadcast([128, 8, 64]), ALU.mult,
                       [('ps', 6), ('dts', q)], [('xpp', q)])
                    if outc:
                        tt('dve', v3(xd[q][:], 8), x3,
                           hvec[:, 2, g * 8:(g + 1) * 8].unsqueeze(2).to_broadcast([128, 8, 64]), ALU.mult,
                           [('ps', 6), 'hvec'], [('xd', q)])
                    pbb = ps[7][:, 0:64].bitcast(BF16)
                    tr(pbb, BT[:, cs], cb[:, IDN, :], ['BT', 'cb'], [('ps', 7)])
                    cp('act', Btok[q][:], pbb, [('ps', 7)], [('Btok', q)])
                    if outc:
                        col0 = ci * 128 - Q0L
                        mm(ps[7][:, 128:256], BT[:, cs], CT[:, cs], True, True, ['BT', 'CT'], [('ps', 7)])
                        tt('dve', CBm[q][:], ps[7][:, 128:256], cf[:, LE, :], ALU.mult, [('ps', 7), 'cf'], [('CBm', q)])
                        tt('pool', v3(Gb[q][:], 8), v3(dec[q][:], 8), CBm[q][:].unsqueeze(1).to_broadcast([128, 8, 128]),
                           ALU.mult, [('dec', q), ('CBm', q)], [('Gb', q)])
                        for hh in range(8):
                            mm(ps[0][:, hh * 64:(hh + 1) * 64], Gb[q][:, hh * 128:(hh + 1) * 128], xp[q][:, hh * 64:(hh + 1) * 64],
                               True, True, [('Gb', q), ('xp', q)], [('ps', 0)])
                        mm(ps[1][:, :], CT[:, cs], Sb[:], True, True, ['CT', 'Sb'], [('ps', 1)])
                        tt('dve', v3(y1[q][:], 8), v3(ps[1][:, :], 8), Ecs.unsqueeze(2).to_broadcast([128, 8, 64]), ALU.mult,
                           [('ps', 1), ('Eb', q)], [('y1', q)])
                        tt('dve', y1[q][:], y1[q][:], ps[0][:, :], ALU.add, [('y1', q), ('ps', 0)], [('y1', q)])
                        tt('pool', y1[q][:], y1[q][:], xd[q][:], ALU.add, [('y1', q), ('xd', q)], [('y1', q)])
                        for k in range(16):
                            mm(ps[2][:, :], hb[hs][:, k, cs], wB[:, k, 768:1280], k == 0, k == 15,
                               ['wB', ('hb', hs)], [('ps', 2)])
                        act(zs[q][:], ps[2][:, :], AF.Silu, [('ps', 2)], [('zs', q)])
                        tt('pool', y2[q][:], y1[q][:], zs[q][:], ALU.mult, [('y1', q), ('zs', q)], [('y2', q)])
                        act(junk[:], y2[q][:], AF.Square, [('y2', q)], ['junk', ('st1', q)], accum=st1[q][:, 0:1])
                        act(st1[q][:, 1:2], st1[q][:, 0:1], AF.Sqrt, [('st1', q)], [('st1', q)], scale=1.0 / 512, bias=EPS)
                        recip(st1[q][:, 1:2], st1[q][:, 1:2], [('st1', q)], [('st1', q)])
                        stt('dve', yn[q][:], y2[q][:], st1[q][:, 1:2], nrm[:], ALU.mult, ALU.mult,
                            [('y2', q), ('st1', q), 'nrm'], [('yn', q)])
                        pyt = ps[7][:, 256:512].bitcast(BF16)
                        for c in range(4):
                            tr(pyt[:, c * 128:(c + 1) * 128], yn[q][:, c * 128:(c + 1) * 128], cb[:, IDN, :],
                               [('yn', q), 'cb'], [('ps', 7)])
                        cp('act', yTs[q][:].rearrange("p a b -> p (a b)"), pyt, [('ps', 7)], [('yTs', q)])
                        dma('sp', y_v[:, g * 4:(g + 1) * 4, col0:col0 + 128], yTs[q][:], [('yTs', q)], ['y_d'], ('yst', q))
                    mm(ps[2][:, :], Btok[q][:], xpp[q][:], True, True, [('Btok', q), ('xpp', q)], [('ps', 2)])
                    tt('dve', v3(S[:], 8), v3(S[:], 8), Etot.unsqueeze(2).to_broadcast([128, 8, 64]), ALU.mult,
                       ['S', ('Eb', q)], ['S'])
                    tt('dve', S[:], S[:], ps[2][:, :], ALU.add, ['S', ('ps', 2)], ['S'])
                    cp('pool', Sb[:], S[:], ['S'], ['Sb'])

    P.barrier()
    with ExitStack() as sc:
        def sbc(name, shape, dt=F32):
            return sc.enter_context(nc.sbuf_tensor(name, list(shape), dt))
        R1 = sbc("R1", [128, 48, 512], BF16)
        R4 = sbc("R4", [128, 8192])
        R5 = sbc("R5", [128, 16, 512])
        wsl = [sbc("wsl%d" % i, [128, 48 * 128], BF16) for i in range(3)]
        sqb = [sbc("sqb%d" % i, [128, 512], BF16) for i in range(2)]
        rsb = sbc("rsb", [128, 512])
        tmpc = [sbc("tmpc%d" % i, [128, 512]) for i in range(2)]
        ug = [sbc("ug%d" % i, [128, 514]) for i in range(1)]
        uv = [sbc("uv%d" % i, [128, 514]) for i in range(1)]
        carry = sbc("carry", [128, 88, 2])
        merged = R4[:, 0:4096].bitcast(BF16).rearrange("p (k t) -> p k t", k=16)
        xt4 = R4[:].rearrange("p (k t) -> p k t", k=16)
        h2 = hb[0]
        wcnt = [0]

        def wslot():
            wcnt[0] += 1
            return wcnt[0] % 3

        mset('dve', carry[:], 0.0, ['carry'])
        groups = [(0, 128)] + [(128 + 512 * i, 512) for i in range(4)]
        for gi, (c0g, n) in enumerate(groups):
            l0 = Q0L + c0g
            dma('sp', R1[:, 0:16, 0:n], osb_v[:, :, c0g:c0g + n], ['osb_d'], [('R1', k) for k in range(16)], 'ldo')
            dma('sp', R1[:, 16:48, 0:n], y_v[:, :, c0g:c0g + n], ['y_d'], [('R1', k) for k in range(16, 48)], 'ldy')
            dma('sp', hb[1][:, :, 0:n], hT_v[:, :, l0:l0 + n], [('hTd', l0 // 512)], [('hb', 1)], 'ldh')
            for c in range(16):
                s1, s2 = wslot(), wslot()
                dma('pool', wsl[s1][:, :], W2A[c, :, :], (), [('wsl', s1)], ('wsl', s1), cast=True)
                dma('pool', wsl[s2][:, 0:4096], W2B[c, :, :], (), [('wsl', s2)], ('wsl', s2), cast=True)
                for k in range(16):
                    mm(ps[0][:, 0:n], wsl[s1][:, k * 128:(k + 1) * 128], R1[:, k, 0:n], k == 0, k == 15,
                       [('wsl', s1), ('R1', k)], [('ps', 0)])
                for k in range(32):
                    mm(ps[1][:, 0:n], wsl[s2][:, k * 128:(k + 1) * 128], R1[:, 16 + k, 0:n], k == 0, k == 31,
                       [('wsl', s2), ('R1', 16 + k)], [('ps', 1)])
                for k in range(16):
                    mm(ps[2][:, 0:n], wsl[s1][:, (16 + k) * 128:(17 + k) * 128], hb[1][:, k, 0:n], k == 0, k == 15,
                       [('wsl', s1), ('hb', 1)], [('ps', 2)])
                for k in range(16):
                    mm(ps[3][:, 0:n], wsl[s1][:, (32 + k) * 128:(33 + k) * 128], hb[1][:, k, 0:n], k == 0, k == 15,
                       [('wsl', s1), ('hb', 1)], [('ps', 3)])
                a_, b_ = R5[:, 14, :], R5[:, 15, :]
                ka, kb_ = ('R5', 14), ('R5', 15)
                act(a_[:, 0:n], ps[2][:, 0:n], AF.Sigmoid, [('ps', 2)], [ka])
                act(b_[:, 0:n], ps[3][:, 0:n], AF.Sigmoid, [('ps', 3)], [kb_])
                tt('dve', a_[:, 0:n], a_[:, 0:n], ps[0][:, 0:n], ALU.mult, [ka, ('ps', 0)], [ka])
                tt('dve', b_[:, 0:n], b_[:, 0:n], ps[1][:, 0:n], ALU.mult, [kb_, ('ps', 1)], [kb_])
                tt('pool', merged[:, c, 0:n], a_[:, 0:n], b_[:, 0:n], ALU.add, [ka, kb_], [('R4', c // 2)])
            for c in range(16):
                s1 = wslot()
                dma('pool', wsl[s1][:, 0:2048], WO[c, :, :], (), [('wsl', s1)], ('wsl', s1), cast=True)
                b = pbank()
                for k in range(16):
                    mm(ps[b][:, 0:n], wsl[s1][:, k * 128:(k + 1) * 128], merged[:, k, 0:n], k == 0, k == 15,
                       [('wsl', s1), ('R4', k // 2)], [('ps', b)])
                cp('act', R5[:, c, 0:n], ps[b][:, 0:n], [('ps', b)], [('R5', c)])
                act(sqb[c % 2][:, 0:n], ps[b][:, 0:n], AF.Square, [('ps', b)], [('sqb', c % 2)])
                mm(ps[4][:, 0:n], cb[:, ONES, :], sqb[c % 2][:, 0:n], c == 0, c == 15, [('sqb', c % 2), 'cb'], [('ps', 4)])
            act(rsb[:, 0:n], ps[4][:, 0:n], AF.Sqrt, [('ps', 4)], ['rsb'], scale=1.0 / D, bias=EPS)
            recip(rsb[:, 0:n], rsb[:, 0:n], ['rsb'], ['rsb'])
            dma('sp', xt4[:, :, 0:n], xw_v[:, :, l0:l0 + n], (), [('R4', k) for k in range(16)], 'ldx')
            for c in range(16):
                stt('dve', tmpc[c % 2][:, 0:n], R5[:, c, 0:n], gains[:, 1, c:c + 1], rsb[:, 0:n], ALU.mult, ALU.mult,
                    [('R5', c), 'rsb', 'gains'], [('tmpc', c % 2)])
                tt('pool', xt4[:, c, 0:n], xt4[:, c, 0:n], tmpc[c % 2][:, 0:n], ALU.add, [('R4', c), ('tmpc', c % 2)],
                   [('R4', c)])
            for c in range(16):
                act(sqb[c % 2][:, 0:n], xt4[:, c, 0:n], AF.Square, [('R4', c)], [('sqb', c % 2)])
                mm(ps[4][:, 0:n], cb[:, ONES, :], sqb[c % 2][:, 0:n], c == 0, c == 15, [('sqb', c % 2), 'cb'], [('ps', 4)])
            act(rsb[:, 0:n], ps[4][:, 0:n], AF.Sqrt, [('ps', 4)], ['rsb'], scale=1.0 / D, bias=EPS)
            recip(rsb[:, 0:n], rsb[:, 0:n], ['rsb'], ['rsb'])
            for c in range(16):
                stt('dve', h2[:, c, 0:n], xt4[:, c, 0:n], gains[:, 2, c:c + 1], rsb[:, 0:n], ALU.mult, ALU.mult,
                    [('R4', c), 'rsb', 'gains'], [('hb', 0)])
            for fc in range(NFC):
                s1 = wslot()
                dma('pool', wsl[s1][:, 0:4096], WU[fc, :, :], (), [('wsl', s1)], ('wsl', s1), cast=True)
                q = 0
                for half, (pb_, ub, kq) in enumerate(((0, ug[q], ('ug', q)), (1, uv[q], ('uv', q)))):
                    for k in range(16):
                        mm(ps[pb_][:, 0:n], wsl[s1][:, k * 256 + half * 128:k * 256 + half * 128 + 128], h2[:, k, 0:n],
                           k == 0, k == 15, [('wsl', s1), ('hb', 0)], [('ps', pb_)])
                    cch = fc + half * NFC
                    cp('dve', ub[:, 0:2], carry[:, cch, :], ['carry'], [kq])
                    cp('act', ub[:, 2:2 + n], ps[pb_][:, 0:n], [('ps', pb_)], [kq])
                    if gi == 0:
                        ts1('dve', carry[:, cch, :], ub[:, n:n + 2], hval[:, 0:1], ALU.mult, [kq, 'hval'], ['carry'])
                    else:
                        cp('dve', carry[:, cch, :], ub[:, n:n + 2], [kq], ['carry'])
                    if gi > 0:
                        dst, kd = (R5[:, 0, :], ('R5', 0)) if half == 0 else (R5[:, 1, :], ('R5', 1))
                        eng = 'dve'
                        ts(eng, dst[:, 0:n], ub[:, 0:n], fconvw[:, cch, 0:1], fconvb[:, cch:cch + 1], ALU.mult, ALU.add,
                           [kq, 'fconv'], [kd])
                        stt(eng, dst[:, 0:n], ub[:, 1:n + 1], fconvw[:, cch, 1:2], dst[:, 0:n], ALU.mult, ALU.add,
                            [kq, 'fconv', kd], [kd])
                        stt(eng, dst[:, 0:n], ub[:, 2:n + 2], fconvw[:, cch, 2:3], dst[:, 0:n], ALU.mult, ALU.add,
                            [kq, 'fconv', kd], [kd])
                if gi > 0:
                    G_, V_, T_ = R5[:, 0, :], R5[:, 1, :], R5[:, 2 + fc % 2, :]
                    kT_ = ('R5', 2 + fc % 2)
                    tt('pool', T_[:, 0:n], G_[:, 0:n], G_[:, 0:n], ALU.mult, [('R5', 0)], [kT_])
                    ts('dve', T_[:, 0:n], T_[:, 0:n], 0.044715, 1.0, ALU.mult, ALU.add, [kT_], [kT_])
                    tt('pool', T_[:, 0:n], T_[:, 0:n], G_[:, 0:n], ALU.mult, [kT_, ('R5', 0)], [kT_])
                    act(T_[:, 0:n], T_[:, 0:n], AF.Sigmoid, [kT_], [kT_], scale=1.5957691216057308)
                    tt('dve', T_[:, 0:n], T_[:, 0:n], G_[:, 0:n], ALU.mult, [kT_, ('R5', 0)], [kT_])
                    tt('pool', R1[:, fc, 0:n], T_[:, 0:n], V_[:, 0:n], ALU.mult, [kT_, ('R5', 1)], [('R1', fc)])
            if gi == 0:
                continue
            for c in range(16):
                s1 = wslot()
                dma('pool', wsl[s1][:, 0:NFC * 128], WD[c, :, :], (), [('wsl', s1)], ('wsl', s1), cast=True)
                b = pbank()
                for k in range(NFC):
                    mm(ps[b][:, 0:n], wsl[s1][:, k * 128:(k + 1) * 128], R1[:, k, 0:n], k == 0, k == NFC - 1,
                       [('wsl', s1), ('R1', k)], [('ps', b)])
                cp('act', R5[:, c, 0:n], ps[b][:, 0:n], [('ps', b)], [('R5', c)])
                act(sqb[c % 2][:, 0:n], ps[b][:, 0:n], AF.Square, [('ps', b)], [('sqb', c % 2)])
                mm(ps[4][:, 0:n], cb[:, ONES, :], sqb[c % 2][:, 0:n], c == 0, c == 15, [('sqb', c % 2), 'cb'], [('ps', 4)])
            act(rsb[:, 0:n], ps[4][:, 0:n], AF.Sqrt, [('ps', 4)], ['rsb'], scale=1.0 / D, bias=EPS)
            recip(rsb[:, 0:n], rsb[:, 0:n], ['rsb'], ['rsb'])
            for c in range(16):
                stt('dve', tmpc[c % 2][:, 0:n], R5[:, c, 0:n], gains[:, 3, c:c + 1], rsb[:, 0:n], ALU.mult, ALU.mult,
                    [('R5', c), 'rsb', 'gains'], [('tmpc', c % 2)])
                tt('pool', R5[:, c, 0:n], xt4[:, c, 0:n], tmpc[c % 2][:, 0:n], ALU.add, [('R4', c), ('tmpc', c % 2)],
                   [('R5', c)])
            t0 = c0g - 128
            dma('sp', out_v[:, :, t0:t0 + n], R5[:, :, 0:n], [('R5', c) for c in range(16)], ['out'], 'stout')

        P.emit(nc, es)
    es.close()
    return nc


_NC = None


def _prep_weights(w_in, w_sb_proj, w_ssd_proj, w_out, w_up, w_down):
    def blk(w, cols):
        K = w.shape[0]
        sub = w[:, cols].reshape(K // 128, 128, len(cols))
        return np.ascontiguousarray(sub.transpose(1, 0, 2)).reshape(128, -1)
    ar = np.arange
    WA = np.stack([blk(w_in, np.concatenate([ar(h * 128, h * 128 + 128), ar(2048 + h * 128, 2048 + h * 128 + 128),
                                             ar(4096 + h * 128, 4096 + h * 128 + 128)])) for h in range(16)])
    WB = np.stack([blk(w_in, np.concatenate([ar(10240 + g * 512, 10240 + g * 512 + 512),
                                             ar(14336 + g * 128, 14336 + g * 128 + 128),
                                             ar(15360 + g * 128, 15360 + g * 128 + 128),
                                             ar(6144 + g * 512, 6144 + g * 512 + 512),
                                             ar(16384 + g * 8, 16384 + g * 8 + 8)])) for g in range(8)])
    W2A = np.stack([np.concatenate([blk(w_sb_proj, ar(c * 128, c * 128 + 128)),
                                    blk(w_in, ar(16448 + c * 128, 16448 + c * 128 + 128)),
                                    blk(w_in, ar(18496 + c * 128, 18496 + c * 128 + 128))], axis=1) for c in range(16)])
    W2B = np.stack([blk(w_ssd_proj, ar(c * 128, c * 128 + 128)) for c in range(16)])
    WO = np.stack([blk(w_out, ar(c * 128, c * 128 + 128)) for c in range(16)])
    WU = np.stack([blk(w_up, np.concatenate([ar(fc * 128, fc * 128 + 128), ar(D_FF + fc * 128, D_FF + fc * 128 + 128)]))
                   for fc in range(NFC)])
    WD = np.stack([blk(w_down, ar(c * 128, c * 128 + 128)) for c in range(16)])
    return dict(WA=WA, WB=WB, W2A=W2A, W2B=W2B, WO=WO, WU=WU, WD=WD)


def kernel(x, norm_mix_pre, w_in, ssd_conv_w, ssd_conv_b, dt_bias, a_log, d_skip, ssd_norm,
           w_sb_proj, w_ssd_proj, w_out, norm_mix_post, norm_ffn_pre, w_up, ffn_conv_w,
           ffn_conv_b, w_down, norm_ffn_post):
    global _NC
    in_maps = _in_maps(x, norm_mix_pre, w_in, ssd_conv_w, ssd_conv_b, dt_bias, a_log, d_skip, ssd_norm,
                       w_sb_proj, w_ssd_proj, w_out, norm_mix_post, norm_ffn_pre, w_up, ffn_conv_w,
                       ffn_conv_b, w_down, norm_ffn_post)
    if _NC is None:
        _NC = build_nc()
    res = run_bass_kernel_spmd(_NC, in_maps, core_ids=list(range(8)))
    out = np.empty((2, 8192, D), np.float32)
    for core in range(8):
        b, c = core // 4, core % 4
        out[b, 2048 * c:2048 * (c + 1), :] = res.results[core]["outT"].T
    return out


def _in_maps(x, norm_mix_pre, w_in, ssd_conv_w, ssd_conv_b, dt_bias, a_log, d_skip, ssd_norm,
             w_sb_proj, w_ssd_proj, w_out, norm_mix_post, norm_ffn_pre, w_up, ffn_conv_w,
             ffn_conv_b, w_down, norm_ffn_post):
    f32 = np.float32
    x = np.asarray(x, f32)
    shared = _prep_weights(np.asarray(w_in, f32)[0], np.asarray(w_sb_proj, f32)[0], np.asarray(w_ssd_proj, f32)[0],
                           np.asarray(w_out, f32)[0], np.asarray(w_up, f32)[0], np.asarray(w_down, f32)[0])
    r = np.arange(128)
    cm = np.stack([(r[:, None] < r[None, :]), (r[:, None] <= r[None, :]), (r[:, None] >= r[None, :]),
                   (r[:, None] > r[None, :]), np.ones((128, 128), bool), np.eye(128, dtype=bool)], axis=1).astype(f32)
    shared["consts"] = np.ascontiguousarray(cm.reshape(128, 768))

    def pk(v):
        return np.asarray(v, f32).reshape(-1, 128).T
    shared["gains"] = np.ascontiguousarray(np.stack([pk(norm_mix_pre[0]), pk(norm_mix_post[0]), pk(norm_ffn_pre[0]),
                                                     pk(norm_ffn_post[0])], axis=1).reshape(128, 64))
    scw = np.asarray(ssd_conv_w, f32)[0]
    shared["sconvw"] = np.ascontiguousarray(scw.reshape(4, 48, 128).transpose(2, 1, 0).reshape(128, 192))
    shared["sconvb"] = np.ascontiguousarray(pk(ssd_conv_b[0]))
    fcw = np.asarray(ffn_conv_w, f32)[0]
    shared["fconvw"] = np.ascontiguousarray(fcw.reshape(3, 88, 128).transpose(2, 1, 0).reshape(128, 264))
    shared["fconvb"] = np.ascontiguousarray(pk(ffn_conv_b[0]))
    hv = np.stack([np.asarray(dt_bias, f32)[0], np.asarray(a_log, f32)[0], np.asarray(d_skip, f32)[0]])
    shared["hvec"] = np.ascontiguousarray(np.broadcast_to(hv.reshape(1, 192), (128, 192)))
    shared["ssdn"] = np.ascontiguousarray(np.broadcast_to(np.asarray(ssd_norm, f32)[0][None, :], (128, 4096)))

    in_maps = []
    for core in range(8):
        b, c = core // 4, core % 4
        end = 2048 * (c + 1)
        start = end - WIN
        xwin = np.zeros((D, WIN), f32)
        lo = max(start, 0)
        xwin[:, lo - start:] = x[b, lo:end, :].T
        tok = start + np.arange(WIN)
        val = (tok >= 0).astype(f32).reshape(64, 128).T
        m = dict(shared)
        m["xw"] = xwin
        m["valid"] = np.ascontiguousarray(val)
        m["hval"] = np.full((128, 1), 1.0 if c > 0 else 0.0, f32)
        in_maps.append(m)
    return in_maps
```

```python
import bisect
import os
from contextlib import ExitStack

import numpy as np
import concourse.bass as bass
import concourse.mybir as mybir
from concourse.bass_utils import run_bass_kernel_spmd

F32 = mybir.dt.float32
BF16 = mybir.dt.bfloat16
AF = mybir.ActivationFunctionType
ALU = mybir.AluOpType

D = 2048
WIN = 8192
NT = 16
Q0L = 6016
NQ = 2176
EPS = 1e-6
SCALE = 128 ** -0.5
D_FF = 5632
NFC = 44
QBS = [(0, 128)] + [(128 + 512 * i, 512) for i in range(4)]


class Prog:
    def __init__(self):
        self.ops = []
        self.bars = []

    def barrier(self):
        self.bars.append(len(self.ops))

    def add(self, eng, fn, R=(), W=(), dk=None):
        self.ops.append((eng, fn, tuple(R), tuple(W), dk))

    def emit(self, nc, es):
        ops = self.ops
        n = len(ops)
        lastw, readers = {}, {}
        deps = []
        for i, (eng, fn, R, W, dk) in enumerate(ops):
            d = set()
            for r in R:
                j = lastw.get(r)
                if j is not None:
                    d.add(j)
            for w in W:
                j = lastw.get(w)
                if j is not None:
                    d.add(j)
                rs = readers.get(w)
                if rs:
                    d.update(rs)
            for r in R:
                readers.setdefault(r, []).append(i)
            for w in W:
                lastw[w] = i
                readers[w] = []
            d.discard(i)
            deps.append(d)
        sig = [False] * n
        cdeps = []
        dma_idx = {}
        for i, op in enumerate(ops):
            if op[4] is not None:
                dma_idx.setdefault(op[4], []).append(i)
        for i, d in enumerate(deps):
            eng = ops[i][0]
            best = {}
            dks = set()
            for j in d:
                oj = ops[j]
                if oj[4] is not None:
                    dks.add(oj[4])
                    continue
                if oj[0] == 'pe' and eng == 'pe':
                    continue
                if best.get(oj[0], -1) < j:
                    best[oj[0]] = j
            for j in best.values():
                sig[j] = True
            cdeps.append((best, dks))
        eng_ops = {}
        for i, op in enumerate(ops):
            if op[4] is None:
                eng_ops.setdefault(op[0], []).append(i)
        for p in self.bars:
            for E in ('pe', 'act', 'dve', 'pool', 'sp'):
                first = next((i for i in range(p, n) if ops[i][0] == E), None)
                if first is None:
                    continue
                best, dks = cdeps[first]
                for E2, lst in eng_ops.items():
                    if E2 == 'pe' and E == 'pe':
                        continue
                    pos = bisect.bisect_left(lst, p)
                    if pos > 0:
                        j = lst[pos - 1]
                        if best.get(E2, -1) < j:
                            best[E2] = j
                            sig[j] = True
                for k, lst in dma_idx.items():
                    if lst and lst[0] < p:
                        dks.add(k)
        seq = [0] * n
        cnt = {}
        for i, op in enumerate(ops):
            if op[4] is None and sig[i]:
                cnt[op[0]] = cnt.get(op[0], 0) + 1
                seq[i] = cnt[op[0]]
        esem = {e: es.enter_context(nc.semaphore("se_" + e)) for e in ('pe', 'act', 'dve', 'pool')}
        dsem = {}
        for k in dma_idx:
            dsem[k] = es.enter_context(nc.semaphore("sd_%d" % len(dsem)))
        block = es.enter_context(nc.Block())

        def run(ename):
            def body(e):
                waited = {}
                for i, (eng, fn, R, W, dk) in enumerate(ops):
                    if eng != ename:
                        continue
                    best, dks = cdeps[i]
                    for se, j in best.items():
                        key = ('e', se)
                        if waited.get(key, 0) < seq[j]:
                            e.wait_ge(esem[se], seq[j])
                            waited[key] = seq[j]
                    for k in dks:
                        val = 16 * bisect.bisect_left(dma_idx[k], i)
                        key = ('d', k)
                        if waited.get(key, 0) < val:
                            e.wait_ge(dsem[k], val)
                            waited[key] = val
                    ins = fn(e)
                    if dk is not None:
                        ins.then_inc(dsem[dk], 16)
                    elif sig[i]:
                        ins.then_inc(esem[eng], 1)
                if ename == 'sp':
                    for k, lst in dma_idx.items():
                        e.wait_ge(dsem[k], 16 * len(lst))
            return body

        block.tensor(run('pe'))
        block.scalar(run('act'))
        block.vector(run('dve'))
        block.gpsimd(run('pool'))
        block.sync(run('sp'))


def build_nc():
    nc = bass.Bass("TRN2", target_bir_lowering=False)
    P = Prog()

    def din(name, shape, dt=F32):
        return nc.dram_tensor(name, list(shape), dt, kind="ExternalInput").ap()

    xw = din("xw", [D, WIN])
    WA = din("WA", [16, 128, 16 * 384])
    WB = din("WB", [8, 128, 16 * 1288])
    W2A = din("W2A", [16, 128, 48 * 128])
    W2B = din("W2B", [16, 128, 32 * 128])
    WO = din("WO", [16, 128, 16 * 128])
    WU = din("WU", [NFC, 128, 16 * 256])
    WD = din("WD", [16, 128, NFC * 128])
    consts_d = din("consts", [128, 6 * 128])
    gains_d = din("gains", [128, 64])
    sconvw_d = din("sconvw", [128, 48 * 4])
    sconvb_d = din("sconvb", [128, 48])
    fconvw_d = din("fconvw", [128, 88 * 3])
    fconvb_d = din("fconvb", [128, 88])
    hvec_d = din("hvec", [128, 3 * 64])
    ssdn_d = din("ssdn", [128, 4096])
    valid_d = din("valid", [128, 64])
    hval_d = din("hval", [128, 1])
    outT = nc.dram_tensor("outT", [D, 2048], F32, kind="ExternalOutput").ap()
    skind = "ExternalOutput" if os.environ.get("KDEBUG") else "Internal"
    hT_d = nc.dram_tensor("hT_d", [D, WIN], BF16, kind=skind).ap()
    osb_d = nc.dram_tensor("osb_d", [D, NQ], BF16, kind=skind).ap()
    y_d = nc.dram_tensor("y_d", [4096, NQ], BF16, kind=skind).ap()
    SCR = {"W2A": nc.dram_tensor("W2A_s", [16, 128, 48 * 128], BF16).ap(),
           "W2B": nc.dram_tensor("W2B_s", [16, 128, 32 * 128], BF16).ap(),
           "WO": nc.dram_tensor("WO_s", [16, 128, 16 * 128], BF16).ap(),
           "WU": nc.dram_tensor("WU_s", [NFC, 128, 16 * 256], BF16).ap(),
           "WD": nc.dram_tensor("WD_s", [16, 128, NFC * 128], BF16).ap()}
    SRC = {"W2A": W2A, "W2B": W2B, "WO": WO, "WU": WU, "WD": WD}

    xw_v = xw.rearrange("(k p) t -> p k t", p=128)
    hT_v = hT_d.rearrange("(k p) t -> p k t", p=128)
    osb_v = osb_d.rearrange("(k p) t -> p k t", p=128)
    y_v = y_d.rearrange("(k p) t -> p k t", p=128)
    out_v = outT.rearrange("(k p) t -> p k t", p=128)

    es = ExitStack()

    def sb(name, shape, dt=F32):
        return es.enter_context(nc.sbuf_tensor("s_" + name, list(shape), dt))

    ps = [es.enter_context(nc.psum_tensor("ps%d" % i, [128, 512], F32)) for i in range(8)]

    def mm(out, lhsT, rhs, start, stop, R, W):
        P.add('pe', lambda e: e.matmul(out, lhsT, rhs, start=start, stop=stop), R, W)

    def tr(out, in_, ident, R, W):
        P.add('pe', lambda e: e.transpose(out, in_, ident), R, W)

    def act(out, in_, func, R, W, scale=1.0, bias=0.0, accum=None):
        if accum is None:
            P.add('act', lambda e: e.activation(out=out, in_=in_, func=func, bias=bias, scale=scale), R, W)
        else:
            P.add('act', lambda e: e.activation(out=out, in_=in_, func=func, bias=bias, scale=scale,
                                                accum_out=accum), R, W)

    def tt(eng, out, a, b, op, R, W):
        P.add(eng, lambda e: e.tensor_tensor(out=out, in0=a, in1=b, op=op), R, W)

    def ts(eng, out, a, s1, s2, op0, op1, R, W):
        P.add(eng, lambda e: e.tensor_scalar(out=out, in0=a, scalar1=s1, scalar2=s2, op0=op0, op1=op1), R, W)

    def ts1(eng, out, a, s1, op0, R, W):
        P.add(eng, lambda e: e.tensor_single_scalar(out=out, in_=a, scalar=s1, op=op0), R, W)

    def stt(eng, out, a, s, b, op0, op1, R, W):
        eng = 'dve'
        P.add(eng, lambda e: e.scalar_tensor_tensor(out=out, in0=a, scalar=s, in1=b, op0=op0, op1=op1), R, W)

    def cp(eng, out, a, R, W):
        if eng == 'act':
            act(out, a, AF.Copy, R, W)
        else:
            P.add(eng, lambda e: e.tensor_copy(out=out, in_=a), R, W)

    def recip(out, a, R, W):
        P.add('dve', lambda e: e.reciprocal(out=out, in_=a), R, W)

    def mset(eng, out, val, W):
        P.add(eng, lambda e: e.memset(out, val), (), W)

    def dma(q, out, in_, R, W, dk, cast=False):
        if cast:
            P.add(q, lambda e: e.dma_start(out=out, in_=in_, max_dma_last_dim=4096), R, W, dk)
        else:
            P.add(q, lambda e: e.dma_start(out=out, in_=in_), R, W, dk)

    cf = sb("cf", [128, 6, 128])
    cb = sb("cb", [128, 6, 128], BF16)
    gains = sb("gains", [128, 4, 16])
    sconvw = sb("sconvw", [128, 48, 4])
    sconvb = sb("sconvb", [128, 48])
    fconvw = sb("fconvw", [128, 88, 3])
    fconvb = sb("fconvb", [128, 88])
    hvec = sb("hvec", [128, 3, 64])
    aneg = sb("aneg", [128, 64])
    valid = sb("valid", [128, 64])
    hval = sb("hval", [128, 1])
    dma('sp', cf[:].rearrange("p a b -> p (a b)"), consts_d[:, :], (), ['cf'], 'c0')
    dma('sp', gains[:].rearrange("p a b -> p (a b)"), gains_d[:, :], (), ['gains'], 'c0')
    dma('sp', sconvw[:].rearrange("p a b -> p (a b)"), sconvw_d[:, :], (), ['sconv'], 'c0')
    dma('sp', sconvb[:], sconvb_d[:, :], (), ['sconv'], 'c0')
    dma('sp', fconvw[:].rearrange("p a b -> p (a b)"), fconvw_d[:, :], (), ['fconv'], 'c0')
    dma('sp', fconvb[:], fconvb_d[:, :], (), ['fconv'], 'c0')
    dma('sp', hvec[:].rearrange("p a b -> p (a b)"), hvec_d[:, :], (), ['hvec'], 'c0')
    dma('sp', valid[:], valid_d[:, :], (), ['valid'], 'c0')
    dma('sp', hval[:], hval_d[:, :], (), ['hval'], 'c0')
    cp('dve', cb[:], cf[:], ['cf'], ['cb'])
    act(aneg[:], hvec[:, 1, :], AF.Exp, ['hvec'], ['aneg'])
    ts1('dve', aneg[:], aneg[:], -1.0, ALU.mult, ['aneg'], ['aneg'])
    LT, LE, GE, GT, ONES, IDN = range(6)

    hb = [sb("hb%d" % i, [128, 16, 512], BF16) for i in range(2)]
    hcnt = [0]

    def load_h(t0, n=512):
        s = hcnt[0] % 2
        hcnt[0] += 1
        dma('sp', hb[s][:, :, 0:n], hT_v[:, :, t0:t0 + n], [('hTd', t0 // 512)], [('hb', s)], ('hbld', s))
        return s

    pcnt = [0]

    def pbank():
        pcnt[0] += 1
        return pcnt[0] % 2

    with ExitStack() as s0:
        xt = [s0.enter_context(nc.sbuf_tensor("xt%d" % i, [128, 16, 512], F32)) for i in range(2)]
        sq = s0.enter_context(nc.sbuf_tensor("sq", [128, 16, 512], BF16))
        rs = s0.enter_context(nc.sbuf_tensor("rs0", [128, 512], F32))
        for t in range(NT):
            xs = t % 2
            hs = t % 2
            dma('sp', xt[xs][:], xw_v[:, :, t * 512:(t + 1) * 512], (), [('xt', xs)], ('xt', xs))
            act(sq[:], xt[xs][:], AF.Square, [('xt', xs)], ['sq'])
            b = pbank()
            for k in range(16):
                mm(ps[b][:, :], cb[:, ONES, :], sq[:, k, :], k == 0, k == 15, ['sq', 'cb'], [('ps', b)])
            act(rs[:], ps[b][:, :], AF.Sqrt, [('ps', b)], ['rs'], scale=1.0 / D, bias=EPS)
            recip(rs[:], rs[:], ['rs'], ['rs'])
            for k in range(16):
                stt('dve', hb[hs][:, k, :], xt[xs][:, k, :], gains[:, 0, k:k + 1], rs[:], ALU.mult, ALU.mult,
                    [('xt', xs), 'rs', 'gains'], [('hb', hs)])
            dma('sp', hT_v[:, :, t * 512:(t + 1) * 512], hb[hs][:], [('hb', hs)], [('hTd', t)], ('hbst', hs))
        hcnt[0] = 0

    P.barrier()
    with ExitStack() as sa:
        def sba(name, shape, dt=F32):
            return sa.enter_context(nc.sbuf_tensor(name, list(shape), dt))
        wq = [sba("wq%d" % i, [128, 16, 384], BF16) for i in range(2)]
        kT = [sba("kT%d" % i, [128, WIN], BF16) for i in range(2)]
        vv = [sba("vv%d" % i, [128, 64, 128], BF16) for i in range(2)]
        qT = [sba("qT%d" % i, [128, NQ], BF16) for i in range(2)]
        eb = [[sba("eb%d_%d" % (st, i), [128, 512], F32) for i in range(2)] for st in range(2)]
        spb = [[sba("spb%d_%d" % (st, i), [128, 512], BF16) for i in range(2)] for st in range(2)]
        gb = [[sba("gb%d_%d" % (st, i), [128, 512], F32) for i in range(2)] for st in range(2)]
        wb = [[sba("wb%d_%d" % (st, i), [128, 512], BF16) for i in range(2)] for st in range(2)]
        ob = [sba("ob%d" % i, [128, 512], BF16) for i in range(2)]
        vtb = [sba("vtb%d" % i, [128, 512], BF16) for i in range(2)]
        SBANKS = [(2, 4, 5), (3, 6, 7)]
        PAIRS = [[(1664, 256), (1920, 256)], [(1152, 512), (640, 512)], [(128, 512), (0, 128)]]
        ocnt = 0
        def capture(fn):
            saved = P.ops
            P.ops = []
            fn()
            out = P.ops
            P.ops = saved
            return out

        def proj_chunks(h):
            ws = h % 2
            chunks = []
            hs_box = [0]

            def part_k(t):
                if t == 0:
                    dma('pool', wq[ws][:].rearrange("p k c -> p (k c)"), WA[h, :, :], (), [('wq', ws)], ('wq', ws),
                        cast=True)
                hs_box[0] = load_h(t * 512)
                hs = hs_box[0]
                b = pbank()
                for k in range(16):
                    mm(ps[b][:, :], wq[ws][:, k, 128:256], hb[hs][:, k, :], k == 0, k == 15,
                       [('wq', ws), ('hb', hs)], [('ps', b)])
                cp('dve', kT[ws][:, t * 512:(t + 1) * 512], ps[b][:, :], [('ps', b)], [('kT', ws, t)])

            def part_v(t):
                hs = hs_box[0]
                b = pbank()
                for k in range(16):
                    mm(ps[b][:, :], wq[ws][:, k, 256:384], hb[hs][:, k, :], k == 0, k == 15,
                       [('wq', ws), ('hb', hs)], [('ps', b)])
                vs_ = t % 2
                cp('dve', vtb[vs_][:, :], ps[b][:, :], [('ps', b)], [('vtb', vs_)])
                b = pbank()
                pvt = ps[b][:, 0:256].bitcast(BF16)
                for s in range(4):
                    tr(pvt[:, s * 128:(s + 1) * 128], vtb[vs_][:, s * 128:(s + 1) * 128], cb[:, IDN, :],
                       [('vtb', vs_), 'cb'], [('ps', b)])
                cp('dve', vv[ws][:, t * 4:(t + 1) * 4, :].rearrange("p a b -> p (a b)"), pvt,
                   [('ps', b)], [('vv', ws, t)])

            def part_q(t):
                hs = hs_box[0]
                c0 = 384 if t == 11 else 0
                n = 512 - c0
                qc0 = t * 512 + c0 - Q0L
                b = pbank()
                for k in range(16):
                    mm(ps[b][:, 0:n], wq[ws][:, k, 0:128], hb[hs][:, k, c0:512], k == 0, k == 15,
                       [('wq', ws), ('hb', hs)], [('ps', b)])
                cp('dve', qT[ws][:, qc0:qc0 + n], ps[b][:, 0:n], [('ps', b)], [('qT', ws)])

            for t in range(NT):
                chunks.append(capture(lambda: part_k(t)))
                chunks.append(capture(lambda: part_v(t)))
                if t >= 11:
                    chunks.append(capture(lambda: part_q(t)))
            return chunks

        for ch in proj_chunks(0):
            P.ops.extend(ch)
        for h in range(16):
            ws = h % 2
            nxt = proj_chunks(h + 1) if h + 1 < 16 else []
            tot_rounds = sum(max((Q0L + q0 + nq) // 128 for (q0, nq) in pair) for pair in PAIRS)
            every = max(1, tot_rounds // (len(nxt) + 1)) if nxt else 0
            rounds_done = 0
            if os.environ.get("KDEBUG") and h == 0:
                dk_ = nc.dram_tensor("dbg_k", [128, WIN], BF16, kind="ExternalOutput").ap()
                dv_ = nc.dram_tensor("dbg_v", [128, WIN], BF16, kind="ExternalOutput").ap()
                dq_ = nc.dram_tensor("dbg_q", [128, NQ], BF16, kind="ExternalOutput").ap()
                dma('sp', dk_[:, :], kT[0][:, :], [('kT', 0, t) for t in range(16)], ['dbgk'], 'dbg')
                dma('sp', dv_[:, :], vv[0][:].rearrange("p a b -> p (a b)"), [('vv', 0, t) for t in range(16)], ['dbgv'], 'dbg')
                dma('sp', dq_[:, :], qT[0][:, :], [('qT', 0)], ['dbgq'], 'dbg')
                dw_ = nc.dram_tensor("dbg_w", [128, 6144], BF16, kind="ExternalOutput").ap()
                dma('sp', dw_[:, :], wq[0][:].rearrange("p k c -> p (k c)"), [('wq', 0)], ['dbgw'], 'dbg')
            for pair in PAIRS:
                sts = []
                for si, (q0, nq) in enumerate(pair):
                    gq0 = Q0L + q0
                    sts.append(dict(si=si, q0=q0, nq=nq, gq0=gq0, banks=SBANKS[si],
                                    blocks=list(range((gq0 + nq) // 128 - 1, -1, -1))))

                def geom(st, kb):
                    m = kb - st['gq0'] // 128
                    return m >= 0, 128 * max(m, 0)

                def zmm(st, i):
                    kb = st['blocks'][i]
                    _, c0 = geom(st, kb)
                    zb = st['banks'][0]
                    q0, nq = st['q0'], st['nq']
                    mm(ps[zb][:, c0:nq], kT[ws][:, kb * 128:(kb + 1) * 128], qT[ws][:, q0 + c0:q0 + nq], True, True,
                       [('kT', ws, kb // 4), ('qT', ws)], [('ps', zb)])
                for st in sts:
                    zmm(st, 0)
                for i in range(max(len(st['blocks']) for st in sts)):
                    live = [st for st in sts if i < len(st['blocks'])]
                    s_ = i % 2
                    for st in live:
                        si, nq = st['si'], st['nq']
                        zb = st['banks'][0]
                        diag, c0 = geom(st, st['blocks'][i])
                        act(eb[si][s_][:, c0:nq], ps[zb][:, c0:nq], AF.Exp, [('ps', zb)], [('eb', si, s_)], scale=SCALE)
                        if diag:
                            tt('dve', eb[si][s_][:, c0:c0 + 128], eb[si][s_][:, c0:c0 + 128], cf[:, LT, :], ALU.mult,
                               [('eb', si, s_), 'cf'], [('eb', si, s_)])
                        act(spb[si][s_][:, c0:nq], eb[si][s_][:, c0:nq], AF.Ln, [('eb', si, s_)], [('spb', si, s_)], bias=1.0)
                    for st in live:
                        if i + 1 < len(st['blocks']):
                            zmm(st, i + 1)
                    for st in live:
                        si, nq = st['si'], st['nq']
                        xb = st['banks'][1]
                        _, c0 = geom(st, st['blocks'][i])
                        mm(ps[xb][:, c0:nq], cb[:, GE, :], spb[si][s_][:, c0:nq], i == 0, False,
                           [('spb', si, s_), 'cb'], [('ps', xb)])
                    for st in live:
                        si, nq = st['si'], st['nq']
                        xb = st['banks'][1]
                        _, c0 = geom(st, st['blocks'][i])
                        act(gb[si][s_][:, c0:nq], ps[xb][:, c0:nq], AF.Exp, [('ps', xb)], [('gb', si, s_)], scale=-1.0)
                    for st in live:
                        si, nq = st['si'], st['nq']
                        _, c0 = geom(st, st['blocks'][i])
                        tt('dve', wb[si][s_][:, c0:nq], eb[si][s_][:, c0:nq], gb[si][s_][:, c0:nq], ALU.mult,
                           [('eb', si, s_), ('gb', si, s_)], [('wb', si, s_)])
                    for st in live:
                        si, nq = st['si'], st['nq']
                        xb, obk = st['banks'][1], st['banks'][2]
                        kb = st['blocks'][i]
                        _, c0 = geom(st, kb)
                        last = i == len(st['blocks']) - 1
                        mm(ps[obk][:, c0:nq], vv[ws][:, kb, :], wb[si][s_][:, c0:nq], i == 0, last,
                           [('wb', si, s_), ('vv', ws, kb // 4)], [('ps', obk)])
                        mm(ps[xb][:, c0:nq], cb[:, LT, :], spb[si][s_][:, c0:nq], False, last,
                           [('spb', si, s_), 'cb'], [('ps', xb)])
                    rounds_done += 1
                    if nxt and rounds_done % every == 0:
                        P.ops.extend(nxt.pop(0))
                for st in sts:
                    q0, nq, obk = st['q0'], st['nq'], st['banks'][2]
                    os_ = ocnt % 2
                    ocnt += 1
                    cp('dve', ob[os_][:, 0:nq], ps[obk][:, 0:nq], [('ps', obk)], [('ob', os_)])
                    dma('sp', osb_d[h * 128:(h + 1) * 128, q0:q0 + nq], ob[os_][:, 0:nq], [('ob', os_)], ['osb_d'],
                        ('obst', os_))
            while nxt:
                P.ops.extend(nxt.pop(0))

    P.barrier()
    with ExitStack() as sbx:
        def sbb(name, shape, dt=F32):
            return sbx.enter_context(nc.sbuf_tensor(name, list(shape), dt))
        wB = sbb("wB", [128, 16, 1288], BF16)
        pre = sbb("pre", [128, 6, 515])
        acc = sbb("acc", [128, 6, 512])
        xa = sbb("xa", [128, 4, 512])
        BT = sbb("BT", [128, 512], BF16)
        CT = sbb("CT", [128, 512], BF16)
        nrm = sbb("nrm", [128, 512])
        S = sbb("S", [128, 512])
        Sb = sbb("Sb", [128, 512], BF16)
        dts = [sbb("dts%d" % i, [128, 32]) for i in range(2)]
        Eb = [sbb("Eb%d" % i, [128, 24]) for i in range(2)]
        rseg = [sbb("rseg%d" % i, [128, 1024]) for i in range(2)]
        dec = [sbb("dec%d" % i, [128, 1024]) for i in range(2)]
        Gb = [sbb("Gb%d" % i, [128, 1024], BF16) for i in range(2)]
        CBm = [sbb("CBm%d" % i, [128, 128]) for i in range(2)]
        xp = [sbb("xp%d" % i, [128, 512], BF16) for i in range(2)]
        xpp = [sbb("xpp%d" % i, [128, 512], BF16) for i in range(2)]
        xd = [sbb("xd%d" % i, [128, 512]) for i in range(2)]
        Btok = [sbb("Btok%d" % i, [128, 128], BF16) for i in range(2)]
        zs = [sbb("zs%d" % i, [128, 512]) for i in range(2)]
        y1 = [sbb("y1%d" % i, [128, 512]) for i in range(2)]
        y2 = [sbb("y2%d" % i, [128, 512]) for i in range(2)]
        junk = sbb("junk", [128, 512])
        st1 = [sbb("st1%d" % i, [128, 2]) for i in range(2)]
        yn = [sbb("yn%d" % i, [128, 512], BF16) for i in range(2)]
        yTs = [sbb("yTs%d" % i, [128, 4, 128], BF16) for i in range(2)]

        def v3(ap, h):
            return ap.rearrange("p (h l) -> p h l", h=h)

        for g in range(8):
            dma('pool', wB[:].rearrange("p k c -> p (k c)"), WB[g, :, :], (), ['wB'], 'wB', cast=True)
            dma('sp', nrm[:], ssdn_d[:, g * 512:(g + 1) * 512], (), ['nrm'], 'nrm')
            mset('dve', S[:], 0.0, ['S'])
            mset('dve', Sb[:], 0.0, ['Sb'])
            for t in range(NT):
                hs = load_h(t * 512)
                if t == 0:
                    mset('dve', pre[:, :, 0:3], 0.0, ['pre'])
                else:
                    cp('dve', pre[:, :, 0:3], pre[:, :, 512:515], ['pre'], ['pre'])
                for c in range(6):
                    b = pbank()
                    w0 = c * 128 if c < 4 else 512 + (c - 4) * 128
                    for k in range(16):
                        mm(ps[b][:, :], wB[:, k, w0:w0 + 128], hb[hs][:, k, :], k == 0, k == 15,
                           ['wB', ('hb', hs)], [('ps', b)])
                    cp('act', pre[:, c, 3:515], ps[b][:, :], [('ps', b)], ['pre'])
                for c in range(6):
                    ch = g * 4 + c if c < 4 else (32 + g if c == 4 else 40 + g)
                    eng = 'dve'
                    ts(eng, acc[:, c, :], pre[:, c, 0:512], sconvw[:, ch, 0:1], sconvb[:, ch:ch + 1], ALU.mult, ALU.add,
                       ['pre', 'sconv'], [('acc', c)])
                    for kk in range(1, 4):
                        stt(eng, acc[:, c, :], pre[:, c, kk:kk + 512], sconvw[:, ch, kk:kk + 1], acc[:, c, :],
                            ALU.mult, ALU.add, ['pre', 'sconv', ('acc', c)], [('acc', c)])
                    if c < 4:
                        act(xa[:, c, :], acc[:, c, :], AF.Silu, [('acc', c)], ['xa'])
                    elif c == 4:
                        act(BT[:], acc[:, c, :], AF.Silu, [('acc', c)], ['BT'])
                    else:
                        act(CT[:], acc[:, c, :], AF.Silu, [('acc', c)], ['CT'])
                for s in range(4):
                    ci = t * 4 + s
                    outc = ci >= 47
                    q = ci % 2
                    cs = slice(s * 128, (s + 1) * 128)
                    dt_, la_, tmp_, dtE_ = dts[q][:, 0:8], dts[q][:, 8:16], dts[q][:, 16:24], dts[q][:, 24:32]
                    for k in range(16):
                        mm(ps[3][:, 0:8], hb[hs][:, k, cs], wB[:, k, 1280:1288], k == 0, k == 15,
                           ['wB', ('hb', hs)], [('ps', 3)])
                    tt('dve', tmp_, ps[3][:, 0:8], hvec[:, 0, g * 8:(g + 1) * 8], ALU.add, [('ps', 3), 'hvec'], [('dts', q)])
                    act(tmp_, tmp_, AF.Exp, [('dts', q)], [('dts', q)])
                    act(tmp_, tmp_, AF.Ln, [('dts', q)], [('dts', q)], bias=1.0)
                    ts1('dve', dt_, tmp_, valid[:, ci:ci + 1], ALU.mult, [('dts', q), 'valid'], [('dts', q)])
                    tt('dve', la_, dt_, aneg[:, g * 8:(g + 1) * 8], ALU.mult, [('dts', q), 'aneg'], [('dts', q)])
                    mm(ps[3][:, 8:16], cf[:, LE, :], la_, True, True, [('dts', q), 'cf'], [('ps', 3)])
                    mm(ps[3][:, 16:24], cf[:, GT, :], la_, True, True, [('dts', q), 'cf'], [('ps', 3)])
                    mm(ps[3][:, 24:32], cf[:, ONES, :], la_, True, True, [('dts', q), 'cf'], [('ps', 3)])
                    act(Eb[q][:], ps[3][:, 8:32], AF.Exp, [('ps', 3)], [('Eb', q)])
                    Ecs, Edte, Etot = Eb[q][:, 0:8], Eb[q][:, 8:16], Eb[q][:, 16:24]
                    tt('dve', dtE_, dt_, Edte, ALU.mult, [('dts', q), ('Eb', q)], [('dts', q)])
                    if outc:
                        tt('dve', v3(rseg[q][:], 8), cf[:, LE, :].unsqueeze(1).to_broadcast([128, 8, 128]),
                           la_.unsqueeze(2).to_broadcast([128, 8, 128]), ALU.mult, [('dts', q), 'cf'], [('rseg', q)])
                        for hh in range(2):
                            mm(ps[4 + hh][:, :], cf[:, GT, :], rseg[q][:, hh * 512:(hh + 1) * 512], True, True,
                               [('rseg', q), 'cf'], [('ps', 4 + hh)])
                            act(dec[q][:, hh * 512:(hh + 1) * 512], ps[4 + hh][:, :], AF.Exp, [('ps', 4 + hh)], [('dec', q)])
                    for c in range(4):
                        tr(ps[6][:, c * 128:(c + 1) * 128], xa[:, c, cs], cf[:, IDN, :], ['xa', 'cf'], [('ps', 6)])
                    x3 = v3(ps[6][:, :], 8)
                    tt('dve', v3(xp[q][:], 8), x3, dt_.unsqueeze(2).to_broadcast([128, 8, 64]), ALU.mult,
                       [('ps', 6), ('dts', q)], [('xp', q)])
                    tt('dve', v3(xpp[q][:], 8), x3, dtE_.unsqueeze(2).to_broadcast([128, 8, 64]), ALU.mult,
                       [('ps', 6), ('dts', q)], [('xpp', q)])
                    if outc:
                        tt('dve', v3(xd[q][:], 8), x3,
                           hvec[:, 2, g * 8:(g + 1) * 8].unsqueeze(2).to_broadcast([128, 8, 64]), ALU.mult,
                           [('ps', 6), 'hvec'], [('xd', q)])
                    pbb = ps[7][:, 0:64].bitcast(BF16)
                    tr(pbb, BT[:, cs], cb[:, IDN, :], ['BT', 'cb'], [('ps', 7)])
                    cp('act', Btok[q][:], pbb, [('ps', 7)], [('Btok', q)])
                    if outc:
                        col0 = ci * 128 - Q0L
                        mm(ps[7][:, 128:256], BT[:, cs], CT[:, cs], True, True, ['BT', 'CT'], [('ps', 7)])
                        tt('dve', CBm[q][:], ps[7][:, 128:256], cf[:, LE, :], ALU.mult, [('ps', 7), 'cf'], [('CBm', q)])
                        tt('pool', v3(Gb[q][:], 8), v3(dec[q][:], 8), CBm[q][:].unsqueeze(1).to_broadcast([128, 8, 128]),
                           ALU.mult, [('dec', q), ('CBm', q)], [('Gb', q)])
                        for hh in range(8):
                            mm(ps[0][:, hh * 64:(hh + 1) * 64], Gb[q][:, hh * 128:(hh + 1) * 128], xp[q][:, hh * 64:(hh + 1) * 64],
                               True, True, [('Gb', q), ('xp', q)], [('ps', 0)])
                        mm(ps[1][:, :], CT[:, cs], Sb[:], True, True, ['CT', 'Sb'], [('ps', 1)])
                        tt('dve', v3(y1[q][:], 8), v3(ps[1][:, :], 8), Ecs.unsqueeze(2).to_broadcast([128, 8, 64]), ALU.mult,
                           [('ps', 1), ('Eb', q)], [('y1', q)])
                        tt('dve', y1[q][:], y1[q][:], ps[0][:, :], ALU.add, [('y1', q), ('ps', 0)], [('y1', q)])
                        tt('pool', y1[q][:], y1[q][:], xd[q][:], ALU.add, [('y1', q), ('xd', q)], [('y1', q)])
                        for k in range(16):
                            mm(ps[2][:, :], hb[hs][:, k, cs], wB[:, k, 768:1280], k == 0, k == 15,
                               ['wB', ('hb', hs)], [('ps', 2)])
                        act(zs[q][:], ps[2][:, :], AF.Silu, [('ps', 2)], [('zs', q)])
                        tt('pool', y2[q][:], y1[q][:], zs[q][:], ALU.mult, [('y1', q), ('zs', q)], [('y2', q)])
                        act(junk[:], y2[q][:], AF.Square, [('y2', q)], ['junk', ('st1', q)], accum=st1[q][:, 0:1])
                        act(st1[q][:, 1:2], st1[q][:, 0:1], AF.Sqrt, [('st1', q)], [('st1', q)], scale=1.0 / 512, bias=EPS)
                        recip(st1[q][:, 1:2], st1[q][:, 1:2], [('st1', q)], [('st1', q)])
                        stt('dve', yn[q][:], y2[q][:], st1[q][:, 1:2], nrm[:], ALU.mult, ALU.mult,
                            [('y2', q), ('st1', q), 'nrm'], [('yn', q)])
                        pyt = ps[7][:, 256:512].bitcast(BF16)
                        for c in range(4):
                            tr(pyt[:, c * 128:(c + 1) * 128], yn[q][:, c * 128:(c + 1) * 128], cb[:, IDN, :],
                               [('yn', q), 'cb'], [('ps', 7)])
                        cp('act', yTs[q][:].rearrange("p a b -> p (a b)"), pyt, [('ps', 7)], [('yTs', q)])
                        dma('sp', y_v[:, g * 4:(g + 1) * 4, col0:col0 + 128], yTs[q][:], [('yTs', q)], ['y_d'], ('yst', q))
                    mm(ps[2][:, :], Btok[q][:], xpp[q][:], True, True, [('Btok', q), ('xpp', q)], [('ps', 2)])
                    tt('dve', v3(S[:], 8), v3(S[:], 8), Etot.unsqueeze(2).to_broadcast([128, 8, 64]), ALU.mult,
                       ['S', ('Eb', q)], ['S'])
                    tt('dve', S[:], S[:], ps[2][:, :], ALU.add, ['S', ('ps', 2)], ['S'])
                    cp('pool', Sb[:], S[:], ['S'], ['Sb'])

    P.barrier()
    with ExitStack() as sc:
        def sbc(name, shape, dt=F32):
            return sc.enter_context(nc.sbuf_tensor(name, list(shape), dt))
        R1 = sbc("R1", [128, 48, 512], BF16)
        R4 = sbc("R4", [128, 8192])
        R5 = sbc("R5", [128, 16, 512])
        wsl = [sbc("wsl%d" % i, [128, 48 * 128], BF16) for i in range(3)]
        sqb = [sbc("sqb%d" % i, [128, 512], BF16) for i in range(2)]
        rsb = sbc("rsb", [128, 512])
        tmpc = [sbc("tmpc%d" % i, [128, 512]) for i in range(2)]
        ug = [sbc("ug%d" % i, [128, 514]) for i in range(1)]
        uv = [sbc("uv%d" % i, [128, 514]) for i in range(1)]
        carry = sbc("carry", [128, 88, 2])
        merged = R4[:, 0:4096].bitcast(BF16).rearrange("p (k t) -> p k t", k=16)
        xt4 = R4[:].rearrange("p (k t) -> p k t", k=16)
        h2 = hb[0]
        wcnt = [0]

        def wslot():
            wcnt[0] += 1
            return wcnt[0] % 3

        def wload(slot, name, idx, ncols, first):
            if first:
                dma('pool', wsl[slot][:, 0:ncols], SRC[name][idx, :, :], (), [('wsl', slot)], ('wsl', slot), cast=True)
                dma('sp', SCR[name][idx, :, :], wsl[slot][:, 0:ncols], [('wsl', slot)], [('scr', name, idx)], 'wscst')
            else:
                dma('sp', wsl[slot][:, 0:ncols], SCR[name][idx, :, :], [('scr', name, idx)], [('wsl', slot)], ('wsl', slot))

        mset('dve', carry[:], 0.0, ['carry'])
        groups = [(0, 128)] + [(128 + 512 * i, 512) for i in range(4)]
        for gi, (c0g, n) in enumerate(groups):
            l0 = Q0L + c0g
            dma('sp', R1[:, 0:16, 0:n], osb_v[:, :, c0g:c0g + n], ['osb_d'], [('R1', k) for k in range(16)], 'ldo')
            dma('sp', R1[:, 16:48, 0:n], y_v[:, :, c0g:c0g + n], ['y_d'], [('R1', k) for k in range(16, 48)], 'ldy')
            dma('sp', hb[1][:, :, 0:n], hT_v[:, :, l0:l0 + n], [('hTd', l0 // 512)], [('hb', 1)], 'ldh')
            for c in range(16):
                s1, s2 = wslot(), wslot()
                wload(s1, "W2A", c, 6144, gi == 0)
                wload(s2, "W2B", c, 4096, gi == 0)
                for k in range(16):
                    mm(ps[0][:, 0:n], wsl[s1][:, k * 128:(k + 1) * 128], R1[:, k, 0:n], k == 0, k == 15,
                       [('wsl', s1), ('R1', k)], [('ps', 0)])
                for k in range(32):
                    mm(ps[1][:, 0:n], wsl[s2][:, k * 128:(k + 1) * 128], R1[:, 16 + k, 0:n], k == 0, k == 31,
                       [('wsl', s2), ('R1', 16 + k)], [('ps', 1)])
                for k in range(16):
                    mm(ps[2][:, 0:n], wsl[s1][:, (16 + k) * 128:(17 + k) * 128], hb[1][:, k, 0:n], k == 0, k == 15,
                       [('wsl', s1), ('hb', 1)], [('ps', 2)])
                for k in range(16):
                    mm(ps[3][:, 0:n], wsl[s1][:, (32 + k) * 128:(33 + k) * 128], hb[1][:, k, 0:n], k == 0, k == 15,
                       [('wsl', s1), ('hb', 1)], [('ps', 3)])
                a_, b_ = R5[:, 14, :], R5[:, 15, :]
                ka, kb_ = ('R5', 14), ('R5', 15)
                act(a_[:, 0:n], ps[2][:, 0:n], AF.Sigmoid, [('ps', 2)], [ka])
                act(b_[:, 0:n], ps[3][:, 0:n], AF.Sigmoid, [('ps', 3)], [kb_])
                tt('dve', a_[:, 0:n], a_[:, 0:n], ps[0][:, 0:n], ALU.mult, [ka, ('ps', 0)], [ka])
                tt('dve', b_[:, 0:n], b_[:, 0:n], ps[1][:, 0:n], ALU.mult, [kb_, ('ps', 1)], [kb_])
                tt('pool', merged[:, c, 0:n], a_[:, 0:n], b_[:, 0:n], ALU.add, [ka, kb_], [('R4', c // 2)])
            for c in range(16):
                s1 = wslot()
                wload(s1, "WO", c, 2048, gi == 0)
                b = pbank()
                for k in range(16):
                    mm(ps[b][:, 0:n], wsl[s1][:, k * 128:(k + 1) * 128], merged[:, k, 0:n], k == 0, k == 15,
                       [('wsl', s1), ('R4', k // 2)], [('ps', b)])
                cp('act', R5[:, c, 0:n], ps[b][:, 0:n], [('ps', b)], [('R5', c)])
                act(sqb[c % 2][:, 0:n], ps[b][:, 0:n], AF.Square, [('ps', b)], [('sqb', c % 2)])
                mm(ps[4][:, 0:n], cb[:, ONES, :], sqb[c % 2][:, 0:n], c == 0, c == 15, [('sqb', c % 2), 'cb'], [('ps', 4)])
            act(rsb[:, 0:n], ps[4][:, 0:n], AF.Sqrt, [('ps', 4)], ['rsb'], scale=1.0 / D, bias=EPS)
            recip(rsb[:, 0:n], rsb[:, 0:n], ['rsb'], ['rsb'])
            dma('sp', xt4[:, :, 0:n], xw_v[:, :, l0:l0 + n], (), [('R4', k) for k in range(16)], 'ldx')
            for c in range(16):
                stt('dve', tmpc[c % 2][:, 0:n], R5[:, c, 0:n], gains[:, 1, c:c + 1], rsb[:, 0:n], ALU.mult, ALU.mult,
                    [('R5', c), 'rsb', 'gains'], [('tmpc', c % 2)])
                tt('pool', xt4[:, c, 0:n], xt4[:, c, 0:n], tmpc[c % 2][:, 0:n], ALU.add, [('R4', c), ('tmpc', c % 2)],
                   [('R4', c)])
            for c in range(16):
                act(sqb[c % 2][:, 0:n], xt4[:, c, 0:n], AF.Square, [('R4', c)], [('sqb', c % 2)])
                mm(ps[4][:, 0:n], cb[:, ONES, :], sqb[c % 2][:, 0:n], c == 0, c == 15, [('sqb', c % 2), 'cb'], [('ps', 4)])
            act(rsb[:, 0:n], ps[4][:, 0:n], AF.Sqrt, [('ps', 4)], ['rsb'], scale=1.0 / D, bias=EPS)
            recip(rsb[:, 0:n], rsb[:, 0:n], ['rsb'], ['rsb'])
            for c in range(16):
                stt('dve', h2[:, c, 0:n], xt4[:, c, 0:n], gains[:, 2, c:c + 1], rsb[:, 0:n], ALU.mult, ALU.mult,
                    [('R4', c), 'rsb', 'gains'], [('hb', 0)])
            for fc in range(NFC):
                s1 = wslot()
                wload(s1, "WU", fc, 4096, gi == 0)
                q = 0
                for half, (pb_, ub, kq) in enumerate(((0, ug[q], ('ug', q)), (1, uv[q], ('uv', q)))):
                    for k in range(16):
                        mm(ps[pb_][:, 0:n], wsl[s1][:, k * 256 + half * 128:k * 256 + half * 128 + 128], h2[:, k, 0:n],
                           k == 0, k == 15, [('wsl', s1), ('hb', 0)], [('ps', pb_)])
                    cch = fc + half * NFC
                    cp('dve', ub[:, 0:2], carry[:, cch, :], ['carry'], [kq])
                    cp('act', ub[:, 2:2 + n], ps[pb_][:, 0:n], [('ps', pb_)], [kq])
                    if gi == 0:
                        ts1('dve', carry[:, cch, :], ub[:, n:n + 2], hval[:, 0:1], ALU.mult, [kq, 'hval'], ['carry'])
                    else:
                        cp('dve', carry[:, cch, :], ub[:, n:n + 2], [kq], ['carry'])
                    if gi > 0:
                        dst, kd = (R5[:, 0, :], ('R5', 0)) if half == 0 else (R5[:, 1, :], ('R5', 1))
                        eng = 'dve'
                        ts(eng, dst[:, 0:n], ub[:, 0:n], fconvw[:, cch, 0:1], fconvb[:, cch:cch + 1], ALU.mult, ALU.add,
                           [kq, 'fconv'], [kd])
                        stt(eng, dst[:, 0:n], ub[:, 1:n + 1], fconvw[:, cch, 1:2], dst[:, 0:n], ALU.mult, ALU.add,
                            [kq, 'fconv', kd], [kd])
                        stt(eng, dst[:, 0:n], ub[:, 2:n + 2], fconvw[:, cch, 2:3], dst[:, 0:n], ALU.mult, ALU.add,
                            [kq, 'fconv', kd], [kd])
                if gi > 0:
                    G_, V_, T_ = R5[:, 0, :], R5[:, 1, :], R5[:, 2 + fc % 2, :]
                    kT_ = ('R5', 2 + fc % 2)
                    tt('pool', T_[:, 0:n], G_[:, 0:n], G_[:, 0:n], ALU.mult, [('R5', 0)], [kT_])
                    ts('dve', T_[:, 0:n], T_[:, 0:n], 0.044715, 1.0, ALU.mult, ALU.add, [kT_], [kT_])
                    tt('pool', T_[:, 0:n], T_[:, 0:n], G_[:, 0:n], ALU.mult, [kT_, ('R5', 0)], [kT_])
                    act(T_[:, 0:n], T_[:, 0:n], AF.Sigmoid, [kT_], [kT_], scale=1.5957691216057308)
                    tt('dve', T_[:, 0:n], T_[:, 0:n], G_[:, 0:n], ALU.mult, [kT_, ('R5', 0)], [kT_])
                    tt('pool', R1[:, fc, 0:n], T_[:, 0:n], V_[:, 0:n], ALU.mult, [kT_, ('R5', 1)], [('R1', fc)])
            if gi == 0:
                continue
            for c in range(16):
                s1 = wslot()
                wload(s1, "WD", c, NFC * 128, gi == 1)
                b = pbank()
                for k in range(NFC):
                    mm(ps[b][:, 0:n], wsl[s1][:, k * 128:(k + 1) * 128], R1[:, k, 0:n], k == 0, k == NFC - 1,
                       [('wsl', s1), ('R1', k)], [('ps', b)])
                cp('act', R5[:, c, 0:n], ps[b][:, 0:n], [('ps', b)], [('R5', c)])
                act(sqb[c % 2][:, 0:n], ps[b][:, 0:n], AF.Square, [('ps', b)], [('sqb', c % 2)])
                mm(ps[4][:, 0:n], cb[:, ONES, :], sqb[c % 2][:, 0:n], c == 0, c == 15, [('sqb', c % 2), 'cb'], [('ps', 4)])
            act(rsb[:, 0:n], ps[4][:, 0:n], AF.Sqrt, [('ps', 4)], ['rsb'], scale=1.0 / D, bias=EPS)
            recip(rsb[:, 0:n], rsb[:, 0:n], ['rsb'], ['rsb'])
            for c in range(16):
                stt('dve', tmpc[c % 2][:, 0:n], R5[:, c, 0:n], gains[:, 3, c:c + 1], rsb[:, 0:n], ALU.mult, ALU.mult,
                    [('R5', c), 'rsb', 'gains'], [('tmpc', c % 2)])
                tt('pool', R5[:, c, 0:n], xt4[:, c, 0:n], tmpc[c % 2][:, 0:n], ALU.add, [('R4', c), ('tmpc', c % 2)],
                   [('R5', c)])
            t0 = c0g - 128
            dma('sp', out_v[:, :, t0:t0 + n], R5[:, :, 0:n], [('R5', c) for c in range(16)], ['out'], 'stout')

        P.emit(nc, es)
    es.close()
    return nc


_NC = None


def _prep_weights(w_in, w_sb_proj, w_ssd_proj, w_out, w_up, w_down):
    def blk(w, cols):
        K = w.shape[0]
        sub = w[:, cols].reshape(K // 128, 128, len(cols))
        return np.ascontiguousarray(sub.transpose(1, 0, 2)).reshape(128, -1)
    ar = np.arange
    WA = np.stack([blk(w_in, np.concatenate([ar(h * 128, h * 128 + 128), ar(2048 + h * 128, 2048 + h * 128 + 128),
                                             ar(4096 + h * 128, 4096 + h * 128 + 128)])) for h in range(16)])
    WB = np.stack([blk(w_in, np.concatenate([ar(10240 + g * 512, 10240 + g * 512 + 512),
                                             ar(14336 + g * 128, 14336 + g * 128 + 128),
                                             ar(15360 + g * 128, 15360 + g * 128 + 128),
                                             ar(6144 + g * 512, 6144 + g * 512 + 512),
                                             ar(16384 + g * 8, 16384 + g * 8 + 8)])) for g in range(8)])
    W2A = np.stack([np.concatenate([blk(w_sb_proj, ar(c * 128, c * 128 + 128)),
                                    blk(w_in, ar(16448 + c * 128, 16448 + c * 128 + 128)),
                                    blk(w_in, ar(18496 + c * 128, 18496 + c * 128 + 128))], axis=1) for c in range(16)])
    W2B = np.stack([blk(w_ssd_proj, ar(c * 128, c * 128 + 128)) for c in range(16)])
    WO = np.stack([blk(w_out, ar(c * 128, c * 128 + 128)) for c in range(16)])
    WU = np.stack([blk(w_up, np.concatenate([ar(fc * 128, fc * 128 + 128), ar(D_FF + fc * 128, D_FF + fc * 128 + 128)]))
                   for fc in range(NFC)])
    WD = np.stack([blk(w_down, ar(c * 128, c * 128 + 128)) for c in range(16)])
    return dict(WA=WA, WB=WB, W2A=W2A, W2B=W2B, WO=WO, WU=WU, WD=WD)


def kernel(x, norm_mix_pre, w_in, ssd_conv_w, ssd_conv_b, dt_bias, a_log, d_skip, ssd_norm,
           w_sb_proj, w_ssd_proj, w_out, norm_mix_post, norm_ffn_pre, w_up, ffn_conv_w,
           ffn_conv_b, w_down, norm_ffn_post):
    global _NC
    in_maps = _in_maps(x, norm_mix_pre, w_in, ssd_conv_w, ssd_conv_b, dt_bias, a_log, d_skip, ssd_norm,
                       w_sb_proj, w_ssd_proj, w_out, norm_mix_post, norm_ffn_pre, w_up, ffn_conv_w,
                       ffn_conv_b, w_down, norm_ffn_post)
    if _NC is None:
        _NC = build_nc()
    res = run_bass_kernel_spmd(_NC, in_maps, core_ids=list(range(8)))
    out = np.empty((2, 8192, D), np.float32)
    for core in range(8):
        b, c = core // 4, core % 4
        out[b, 2048 * c:2048 * (c + 1), :] = res.results[core]["outT"].T
    return out


def _in_maps(x, norm_mix_pre, w_in, ssd_conv_w, ssd_conv_b, dt_bias, a_log, d_skip, ssd_norm,
             w_sb_proj, w_ssd_proj, w_out, norm_mix_post, norm_ffn_pre, w_up, ffn_conv_w,
             ffn_conv_b, w_down, norm_ffn_post):
    f32 = np.float32
    x = np.asarray(x, f32)
    shared = _prep_weights(np.asarray(w_in, f32)[0], np.asarray(w_sb_proj, f32)[0], np.asarray(w_ssd_proj, f32)[0],
                           np.asarray(w_out, f32)[0], np.asarray(w_up, f32)[0], np.asarray(w_down, f32)[0])
    r = np.arange(128)
    cm = np.stack([(r[:, None] < r[None, :]), (r[:, None] <= r[None, :]), (r[:, None] >= r[None, :]),
                   (r[:, None] > r[None, :]), np.ones((128, 128), bool), np.eye(128, dtype=bool)], axis=1).astype(f32)
    shared["consts"] = np.ascontiguousarray(cm.reshape(128, 768))

    def pk(v):
        return np.asarray(v, f32).reshape(-1, 128).T
    shared["gains"] = np.ascontiguousarray(np.stack([pk(norm_mix_pre[0]), pk(norm_mix_post[0]), pk(norm_ffn_pre[0]),
                                                     pk(norm_ffn_post[0])], axis=1).reshape(128, 64))
    scw = np.asarray(ssd_conv_w, f32)[0]
    shared["sconvw"] = np.ascontiguousarray(scw.reshape(4, 48, 128).transpose(2, 1, 0).reshape(128, 192))
    shared["sconvb"] = np.ascontiguousarray(pk(ssd_conv_b[0]))
    fcw = np.asarray(ffn_conv_w, f32)[0]
    shared["fconvw"] = np.ascontiguousarray(fcw.reshape(3, 88, 128).transpose(2, 1, 0).reshape(128, 264))
    shared["fconvb"] = np.ascontiguousarray(pk(ffn_conv_b[0]))
    hv = np.stack([np.asarray(dt_bias, f32)[0], np.asarray(a_log, f32)[0], np.asarray(d_skip, f32)[0]])
    shared["hvec"] = np.ascontiguousarray(np.broadcast_to(hv.reshape(1, 192), (128, 192)))
    shared["ssdn"] = np.ascontiguousarray(np.broadcast_to(np.asarray(ssd_norm, f32)[0][None, :], (128, 4096)))

    in_maps = []
    for core in range(8):
        b, c = core // 4, core % 4
        end = 2048 * (c + 1)
        start = end - WIN
        xwin = np.zeros((D, WIN), f32)
        lo = max(start, 0)
        xwin[:, lo - start:] = x[b, lo:end, :].T
        tok = start + np.arange(WIN)
        val = (tok >= 0).astype(f32).reshape(64, 128).T
        m = dict(shared)
        m["xw"] = xwin
        m["valid"] = np.ascontiguousarray(val)
        m["hval"] = np.full((128, 1), 1.0 if c > 0 else 0.0, f32)
        in_maps.append(m)
    return in_maps
```

```python
import bisect
import os
from contextlib import ExitStack

import numpy as np
import concourse.bass as bass
import concourse.mybir as mybir
from concourse.bass_utils import run_bass_kernel_spmd

F32 = mybir.dt.float32
BF16 = mybir.dt.bfloat16
AF = mybir.ActivationFunctionType
ALU = mybir.AluOpType

D = 2048
WIN = 8192
NT = 16
Q0L = 6016
NQ = 2176
EPS = 1e-6
SCALE = 128 ** -0.5
D_FF = 5632
NFC = 44
QBS = [(0, 128)] + [(128 + 512 * i, 512) for i in range(4)]


class Prog:
    def __init__(self):
        self.ops = []
        self.bars = []

    def barrier(self):
        self.bars.append(len(self.ops))

    def add(self, eng, fn, R=(), W=(), dk=None):
        self.ops.append((eng, fn, tuple(R), tuple(W), dk))

    def emit(self, nc, es):
        ops = self.ops
        n = len(ops)
        lastw, readers = {}, {}
        deps = []
        for i, (eng, fn, R, W, dk) in enumerate(ops):
            d = set()
            for r in R:
                j = lastw.get(r)
                if j is not None:
                    d.add(j)
            for w in W:
                j = lastw.get(w)
                if j is not None:
                    d.add(j)
                rs = readers.get(w)
                if rs:
                    d.update(rs)
            for r in R:
                readers.setdefault(r, []).append(i)
            for w in W:
                lastw[w] = i
                readers[w] = []
            d.discard(i)
            deps.append(d)
        sig = [False] * n
        cdeps = []
        dma_idx = {}
        for i, op in enumerate(ops):
            if op[4] is not None:
                dma_idx.setdefault(op[4], []).append(i)
        for i, d in enumerate(deps):
            eng = ops[i][0]
            best = {}
            dks = set()
            for j in d:
                oj = ops[j]
                if oj[4] is not None:
                    dks.add(oj[4])
                    continue
                if oj[0] == 'pe' and eng == 'pe':
                    continue
                if best.get(oj[0], -1) < j:
                    best[oj[0]] = j
            for j in best.values():
                sig[j] = True
            cdeps.append((best, dks))
        eng_ops = {}
        for i, op in enumerate(ops):
            if op[4] is None:
                eng_ops.setdefault(op[0], []).append(i)
        for p in self.bars:
            for E in ('pe', 'act', 'dve', 'pool', 'sp'):
                first = next((i for i in range(p, n) if ops[i][0] == E), None)
                if first is None:
                    continue
                best, dks = cdeps[first]
                for E2, lst in eng_ops.items():
                    if E2 == 'pe' and E == 'pe':
                        continue
                    pos = bisect.bisect_left(lst, p)
                    if pos > 0:
                        j = lst[pos - 1]
                        if best.get(E2, -1) < j:
                            best[E2] = j
                            sig[j] = True
                for k, lst in dma_idx.items():
                    if lst and lst[0] < p:
                        dks.add(k)
        seq = [0] * n
        cnt = {}
        for i, op in enumerate(ops):
            if op[4] is None and sig[i]:
                cnt[op[0]] = cnt.get(op[0], 0) + 1
                seq[i] = cnt[op[0]]
        esem = {e: es.enter_context(nc.semaphore("se_" + e)) for e in ('pe', 'act', 'dve', 'pool')}
        dsem = {}
        for k in dma_idx:
            dsem[k] = es.enter_context(nc.semaphore("sd_%d" % len(dsem)))
        block = es.enter_context(nc.Block())

        def run(ename):
            def body(e):
                waited = {}
                for i, (eng, fn, R, W, dk) in enumerate(ops):
                    if eng != ename:
                        continue
                    best, dks = cdeps[i]
                    for se, j in best.items():
                        key = ('e', se)
                        if waited.get(key, 0) < seq[j]:
                            e.wait_ge(esem[se], seq[j])
                            waited[key] = seq[j]
                    for k in dks:
                        val = 16 * bisect.bisect_left(dma_idx[k], i)
                        key = ('d', k)
                        if waited.get(key, 0) < val:
                            e.wait_ge(dsem[k], val)
                            waited[key] = val
                    ins = fn(e)
                    if dk is not None:
                        ins.then_inc(dsem[dk], 16)
                    elif sig[i]:
                        ins.then_inc(esem[eng], 1)
                if ename == 'sp':
                    for k, lst in dma_idx.items():
                        e.wait_ge(dsem[k], 16 * len(lst))
            return body

        block.tensor(run('pe'))
        block.scalar(run('act'))
        block.vector(run('dve'))
        block.gpsimd(run('pool'))
        block.sync(run('sp'))


def build_nc():
    nc = bass.Bass("TRN2", target_bir_lowering=False)
    P = Prog()

    def din(name, shape, dt=F32):
        return nc.dram_tensor(name, list(shape), dt, kind="ExternalInput").ap()

    xw = din("xw", [D, WIN])
    WA = din("WA", [16, 128, 16 * 384])
    WB = din("WB", [8, 128, 16 * 1288])
    W2A = din("W2A", [16, 128, 48 * 128])
    W2B = din("W2B", [16, 128, 32 * 128])
    WO = din("WO", [16, 128, 16 * 128])
    WU = din("WU", [NFC, 128, 16 * 256])
    WD = din("WD", [16, 128, NFC * 128])
    consts_d = din("consts", [128, 6 * 128])
    gains_d = din("gains", [128, 64])
    sconvw_d = din("sconvw", [128, 48 * 4])
    sconvb_d = din("sconvb", [128, 48])
    fconvw_d = din("fconvw", [128, 88 * 3])
    fconvb_d = din("fconvb", [128, 88])
    hvec_d = din("hvec", [128, 3 * 64])
    ssdn_d = din("ssdn", [128, 4096])
    valid_d = din("valid", [128, 64])
    hval_d = din("hval", [128, 1])
    outT = nc.dram_tensor("outT", [D, 2048], F32, kind="ExternalOutput").ap()
    skind = "ExternalOutput" if os.environ.get("KDEBUG") else "Internal"
    hT_d = nc.dram_tensor("hT_d", [D, WIN], BF16, kind=skind).ap()
    osb_d = nc.dram_tensor("osb_d", [D, NQ], BF16, kind=skind).ap()
    y_d = nc.dram_tensor("y_d", [4096, NQ], BF16, kind=skind).ap()
    SCR = {"W2A": nc.dram_tensor("W2A_s", [16, 128, 48 * 128], BF16).ap(),
           "W2B": nc.dram_tensor("W2B_s", [16, 128, 32 * 128], BF16).ap(),
           "WO": nc.dram_tensor("WO_s", [16, 128, 16 * 128], BF16).ap(),
           "WU": nc.dram_tensor("WU_s", [NFC, 128, 16 * 256], BF16).ap(),
           "WD": nc.dram_tensor("WD_s", [16, 128, NFC * 128], BF16).ap()}
    SRC = {"W2A": W2A, "W2B": W2B, "WO": WO, "WU": WU, "WD": WD}

    xw_v = xw.rearrange("(k p) t -> p k t", p=128)
    hT_v = hT_d.rearrange("(k p) t -> p k t", p=128)
    osb_v = osb_d.rearrange("(k p) t -> p k t", p=128)
    y_v = y_d.rearrange("(k p) t -> p k t", p=128)
    out_v = outT.rearrange("(k p) t -> p k t", p=128)

    es = ExitStack()

    def sb(name, shape, dt=F32):
        return es.enter_context(nc.sbuf_tensor("s_" + name, list(shape), dt))

    ps = [es.enter_context(nc.psum_tensor("ps%d" % i, [128, 512], F32)) for i in range(8)]

    def mm(out, lhsT, rhs, start, stop, R, W):
        P.add('pe', lambda e: e.matmul(out, lhsT, rhs, start=start, stop=stop), R, W)

    def tr(out, in_, ident, R, W):
        P.add('pe', lambda e: e.transpose(out, in_, ident), R, W)

    def act(out, in_, func, R, W, scale=1.0, bias=0.0, accum=None):
        if accum is None:
            P.add('act', lambda e: e.activation(out=out, in_=in_, func=func, bias=bias, scale=scale), R, W)
        else:
            P.add('act', lambda e: e.activation(out=out, in_=in_, func=func, bias=bias, scale=scale,
                                                accum_out=accum), R, W)

    def tt(eng, out, a, b, op, R, W):
        P.add(eng, lambda e: e.tensor_tensor(out=out, in0=a, in1=b, op=op), R, W)

    def ts(eng, out, a, s1, s2, op0, op1, R, W):
        P.add(eng, lambda e: e.tensor_scalar(out=out, in0=a, scalar1=s1, scalar2=s2, op0=op0, op1=op1), R, W)

    def ts1(eng, out, a, s1, op0, R, W):
        P.add(eng, lambda e: e.tensor_single_scalar(out=out, in_=a, scalar=s1, op=op0), R, W)

    def stt(eng, out, a, s, b, op0, op1, R, W):
        eng = 'dve'
        P.add(eng, lambda e: e.scalar_tensor_tensor(out=out, in0=a, scalar=s, in1=b, op0=op0, op1=op1), R, W)

    def cp(eng, out, a, R, W):
        if eng == 'act':
            act(out, a, AF.Copy, R, W)
        else:
            P.add(eng, lambda e: e.tensor_copy(out=out, in_=a), R, W)

    def recip(out, a, R, W):
        P.add('dve', lambda e: e.reciprocal(out=out, in_=a), R, W)

    def mset(eng, out, val, W):
        P.add(eng, lambda e: e.memset(out, val), (), W)

    def dma(q, out, in_, R, W, dk, cast=False):
        if cast:
            P.add(q, lambda e: e.dma_start(out=out, in_=in_, max_dma_last_dim=4096), R, W, dk)
        else:
            P.add(q, lambda e: e.dma_start(out=out, in_=in_), R, W, dk)

    cf = sb("cf", [128, 6, 128])
    cb = sb("cb", [128, 6, 128], BF16)
    gains = sb("gains", [128, 4, 16])
    sconvw = sb("sconvw", [128, 48, 4])
    sconvb = sb("sconvb", [128, 48])
    fconvw = sb("fconvw", [128, 88, 3])
    fconvb = sb("fconvb", [128, 88])
    hvec = sb("hvec", [128, 3, 64])
    aneg = sb("aneg", [128, 64])
    valid = sb("valid", [128, 64])
    hval = sb("hval", [128, 1])
    dma('sp', cf[:].rearrange("p a b -> p (a b)"), consts_d[:, :], (), ['cf'], 'c0')
    dma('sp', gains[:].rearrange("p a b -> p (a b)"), gains_d[:, :], (), ['gains'], 'c0')
    dma('sp', sconvw[:].rearrange("p a b -> p (a b)"), sconvw_d[:, :], (), ['sconv'], 'c0')
    dma('sp', sconvb[:], sconvb_d[:, :], (), ['sconv'], 'c0')
    dma('sp', fconvw[:].rearrange("p a b -> p (a b)"), fconvw_d[:, :], (), ['fconv'], 'c0')
    dma('sp', fconvb[:], fconvb_d[:, :], (), ['fconv'], 'c0')
    dma('sp', hvec[:].rearrange("p a b -> p (a b)"), hvec_d[:, :], (), ['hvec'], 'c0')
    dma('sp', valid[:], valid_d[:, :], (), ['valid'], 'c0')
    dma('sp', hval[:], hval_d[:, :], (), ['hval'], 'c0')
    cp('dve', cb[:], cf[:], ['cf'], ['cb'])
    act(aneg[:], hvec[:, 1, :], AF.Exp, ['hvec'], ['aneg'])
    ts1('dve', aneg[:], aneg[:], -1.0, ALU.mult, ['aneg'], ['aneg'])
    LT, LE, GE, GT, ONES, IDN = range(6)

    hb = [sb("hb%d" % i, [128, 16, 512], BF16) for i in range(2)]
    hcnt = [0]

    def load_h(t0, n=512):
        s = hcnt[0] % 2
        hcnt[0] += 1
        dma('sp', hb[s][:, :, 0:n], hT_v[:, :, t0:t0 + n], [('hTd', t0 // 512)], [('hb', s)], ('hbld', s))
        return s

    pcnt = [0]

    def pbank():
        pcnt[0] += 1
        return pcnt[0] % 2

    with ExitStack() as s0:
        xt = [s0.enter_context(nc.sbuf_tensor("xt%d" % i, [128, 16, 512], F32)) for i in range(2)]
        sq = s0.enter_context(nc.sbuf_tensor("sq", [128, 16, 512], BF16))
        rs = s0.enter_context(nc.sbuf_tensor("rs0", [128, 512], F32))
        for t in range(NT):
            xs = t % 2
            hs = t % 2
            dma('sp', xt[xs][:], xw_v[:, :, t * 512:(t + 1) * 512], (), [('xt', xs)], ('xt', xs))
            act(sq[:], xt[xs][:], AF.Square, [('xt', xs)], ['sq'])
            b = pbank()
            for k in range(16):
                mm(ps[b][:, :], cb[:, ONES, :], sq[:, k, :], k == 0, k == 15, ['sq', 'cb'], [('ps', b)])
            act(rs[:], ps[b][:, :], AF.Sqrt, [('ps', b)], ['rs'], scale=1.0 / D, bias=EPS)
            recip(rs[:], rs[:], ['rs'], ['rs'])
            for k in range(16):
                stt('dve', hb[hs][:, k, :], xt[xs][:, k, :], gains[:, 0, k:k + 1], rs[:], ALU.mult, ALU.mult,
                    [('xt', xs), 'rs', 'gains'], [('hb', hs)])
            dma('sp', hT_v[:, :, t * 512:(t + 1) * 512], hb[hs][:], [('hb', hs)], [('hTd', t)], ('hbst', hs))
        hcnt[0] = 0

    P.barrier()
    with ExitStack() as sa:
        def sba(name, shape, dt=F32):
            return sa.enter_context(nc.sbuf_tensor(name, list(shape), dt))
        wq = [sba("wq%d" % i, [128, 16, 384], BF16) for i in range(2)]
        kT = [sba("kT%d" % i, [128, WIN], BF16) for i in range(2)]
        vv = [sba("vv%d" % i, [128, 64, 128], BF16) for i in range(2)]
        qT = [sba("qT%d" % i, [128, NQ], BF16) for i in range(2)]
        eb = [[sba("eb%d_%d" % (st, i), [128, 512], F32) for i in range(2)] for st in range(2)]
        spb = [[sba("spb%d_%d" % (st, i), [128, 512], BF16) for i in range(2)] for st in range(2)]
        gb = [[sba("gb%d_%d" % (st, i), [128, 512], F32) for i in range(2)] for st in range(2)]
        wb = [[sba("wb%d_%d" % (st, i), [128, 512], BF16) for i in range(2)] for st in range(2)]
        ob = [sba("ob%d" % i, [128, 512], BF16) for i in range(2)]
        vtb = [sba("vtb%d" % i, [128, 512], BF16) for i in range(2)]
        SBANKS = [(2, 4, 5), (3, 6, 7)]
        PAIRS = [[(1664, 256), (1920, 256)], [(1152, 512), (640, 512)], [(128, 512), (0, 128)]]
        ocnt = 0
        def capture(fn):
            saved = P.ops
            P.ops = []
            fn()
            out = P.ops
            P.ops = saved
            return out

        def proj_chunks(h):
            ws = h % 2
            chunks = []
            hs_box = [0]

            def part_k(t):
                if t == 0:
                    dma('pool', wq[ws][:].rearrange("p k c -> p (k c)"), WA[h, :, :], (), [('wq', ws)], ('wq', ws),
                        cast=True)
                hs_box[0] = load_h(t * 512)
                hs = hs_box[0]
                b = pbank()
                for k in range(16):
                    mm(ps[b][:, :], wq[ws][:, k, 128:256], hb[hs][:, k, :], k == 0, k == 15,
                       [('wq', ws), ('hb', hs)], [('ps', b)])
                cp('dve', kT[ws][:, t * 512:(t + 1) * 512], ps[b][:, :], [('ps', b)], [('kT', ws, t)])

            def part_v(t):
                hs = hs_box[0]
                b = pbank()
                for k in range(16):
                    mm(ps[b][:, :], wq[ws][:, k, 256:384], hb[hs][:, k, :], k == 0, k == 15,
                       [('wq', ws), ('hb', hs)], [('ps', b)])
                vs_ = t % 2
                cp('dve', vtb[vs_][:, :], ps[b][:, :], [('ps', b)], [('vtb', vs_)])
                b = pbank()
                pvt = ps[b][:, 0:256].bitcast(BF16)
                for s in range(4):
                    tr(pvt[:, s * 128:(s + 1) * 128], vtb[vs_][:, s * 128:(s + 1) * 128], cb[:, IDN, :],
                       [('vtb', vs_), 'cb'], [('ps', b)])
                cp('dve', vv[ws][:, t * 4:(t + 1) * 4, :].rearrange("p a b -> p (a b)"), pvt,
                   [('ps', b)], [('vv', ws, t)])

            def part_q(t):
                hs = hs_box[0]
                c0 = 384 if t == 11 else 0
                n = 512 - c0
                qc0 = t * 512 + c0 - Q0L
                b = pbank()
                for k in range(16):
                    mm(ps[b][:, 0:n], wq[ws][:, k, 0:128], hb[hs][:, k, c0:512], k == 0, k == 15,
                       [('wq', ws), ('hb', hs)], [('ps', b)])
                cp('dve', qT[ws][:, qc0:qc0 + n], ps[b][:, 0:n], [('ps', b)], [('qT', ws)])

            for t in range(NT):
                chunks.append(capture(lambda: part_k(t)))
                chunks.append(capture(lambda: part_v(t)))
                if t >= 11:
                    chunks.append(capture(lambda: part_q(t)))
            return chunks

        for ch in proj_chunks(0):
            P.ops.extend(ch)
        for h in range(16):
            ws = h % 2
            nxt = proj_chunks(h + 1) if h + 1 < 16 else []
            tot_rounds = sum(max((Q0L + q0 + nq) // 128 for (q0, nq) in pair) for pair in PAIRS)
            every = max(1, tot_rounds // (len(nxt) + 1)) if nxt else 0
            rounds_done = 0
            if os.environ.get("KDEBUG") and h == 0:
                dk_ = nc.dram_tensor("dbg_k", [128, WIN], BF16, kind="ExternalOutput").ap()
                dv_ = nc.dram_tensor("dbg_v", [128, WIN], BF16, kind="ExternalOutput").ap()
                dq_ = nc.dram_tensor("dbg_q", [128, NQ], BF16, kind="ExternalOutput").ap()
                dma('sp', dk_[:, :], kT[0][:, :], [('kT', 0, t) for t in range(16)], ['dbgk'], 'dbg')
                dma('sp', dv_[:, :], vv[0][:].rearrange("p a b -> p (a b)"), [('vv', 0, t) for t in range(16)], ['dbgv'], 'dbg')
                dma('sp', dq_[:, :], qT[0][:, :], [('qT', 0)], ['dbgq'], 'dbg')
                dw_ = nc.dram_tensor("dbg_w", [128, 6144], BF16, kind="ExternalOutput").ap()
                dma('sp', dw_[:, :], wq[0][:].rearrange("p k c -> p (k c)"), [('wq', 0)], ['dbgw'], 'dbg')
            for pair in PAIRS:
                sts = []
                for si, (q0, nq) in enumerate(pair):
                    gq0 = Q0L + q0
                    sts.append(dict(si=si, q0=q0, nq=nq, gq0=gq0, banks=SBANKS[si],
                                    blocks=list(range((gq0 + nq) // 128 - 1, -1, -1))))

                def geom(st, kb):
                    m = kb - st['gq0'] // 128
                    return m >= 0, 128 * max(m, 0)

                def zmm(st, i):
                    kb = st['blocks'][i]
                    _, c0 = geom(st, kb)
                    zb = st['banks'][0]
                    q0, nq = st['q0'], st['nq']
                    mm(ps[zb][:, c0:nq], kT[ws][:, kb * 128:(kb + 1) * 128], qT[ws][:, q0 + c0:q0 + nq], True, True,
                       [('kT', ws, kb // 4), ('qT', ws)], [('ps', zb)])
                for st in sts:
                    zmm(st, 0)
                for i in range(max(len(st['blocks']) for st in sts)):
                    live = [st for st in sts if i < len(st['blocks'])]
                    s_ = i % 2
                    for st in live:
                        si, nq = st['si'], st['nq']
                        zb = st['banks'][0]
                        diag, c0 = geom(st, st['blocks'][i])
                        act(eb[si][s_][:, c0:nq], ps[zb][:, c0:nq], AF.Exp, [('ps', zb)], [('eb', si, s_)], scale=SCALE)
                        if diag:
                            tt('dve', eb[si][s_][:, c0:c0 + 128], eb[si][s_][:, c0:c0 + 128], cf[:, LT, :], ALU.mult,
                               [('eb', si, s_), 'cf'], [('eb', si, s_)])
                        act(spb[si][s_][:, c0:nq], eb[si][s_][:, c0:nq], AF.Ln, [('eb', si, s_)], [('spb', si, s_)], bias=1.0)
                    for st in live:
                        if i + 1 < len(st['blocks']):
                            zmm(st, i + 1)
                    for st in live:
                        si, nq = st['si'], st['nq']
                        xb = st['banks'][1]
                        _, c0 = geom(st, st['blocks'][i])
                        mm(ps[xb][:, c0:nq], cb[:, GE, :], spb[si][s_][:, c0:nq], i == 0, False,
                           [('spb', si, s_), 'cb'], [('ps', xb)])
                    for st in live:
                        si, nq = st['si'], st['nq']
                        xb = st['banks'][1]
                        _, c0 = geom(st, st['blocks'][i])
                        act(gb[si][s_][:, c0:nq], ps[xb][:, c0:nq], AF.Exp, [('ps', xb)], [('gb', si, s_)], scale=-1.0)
                    for st in live:
                        si, nq = st['si'], st['nq']
                        _, c0 = geom(st, st['blocks'][i])
                        tt('dve', wb[si][s_][:, c0:nq], eb[si][s_][:, c0:nq], gb[si][s_][:, c0:nq], ALU.mult,
                           [('eb', si, s_), ('gb', si, s_)], [('wb', si, s_)])
                    for st in live:
                        si, nq = st['si'], st['nq']
                        xb, obk = st['banks'][1], st['banks'][2]
                        kb = st['blocks'][i]
                        _, c0 = geom(st, kb)
                        last = i == len(st['blocks']) - 1
                        mm(ps[obk][:, c0:nq], vv[ws][:, kb, :], wb[si][s_][:, c0:nq], i == 0, last,
                           [('wb', si, s_), ('vv', ws, kb // 4)], [('ps', obk)])
                        mm(ps[xb][:, c0:nq], cb[:, LT, :], spb[si][s_][:, c0:nq], False, last,
                           [('spb', si, s_), 'cb'], [('ps', xb)])
                    rounds_done += 1
                    if nxt and rounds_done % every == 0:
                        P.ops.extend(nxt.pop(0))
                for st in sts:
                    q0, nq, obk = st['q0'], st['nq'], st['banks'][2]
                    os_ = ocnt % 2
                    ocnt += 1
                    cp('dve', ob[os_][:, 0:nq], ps[obk][:, 0:nq], [('ps', obk)], [('ob', os_)])
                    dma('sp', osb_d[h * 128:(h + 1) * 128, q0:q0 + nq], ob[os_][:, 0:nq], [('ob', os_)], ['osb_d'],
                        ('obst', os_))
            while nxt:
                P.ops.extend(nxt.pop(0))

    P.barrier()
    with ExitStack() as sbx:
        def sbb(name, shape, dt=F32):
            return sbx.enter_context(nc.sbuf_tensor(name, list(shape), dt))
        wB = sbb("wB", [128, 16, 1288], BF16)
        pre = [sbb("pre%d" % i, [128, 6, 515]) for i in range(2)]
        dtT = sbb("dtT", [8, 512])
        acc = sbb("acc", [128, 6, 512])
        xa = sbb("xa", [128, 4, 512])
        BT = sbb("BT", [128, 512], BF16)
        CT = sbb("CT", [128, 512], BF16)
        nrm = sbb("nrm", [128, 512])
        S = sbb("S", [128, 512])
        Sb = sbb("Sb", [128, 512], BF16)
        dts = [sbb("dts%d" % i, [128, 128]) for i in range(2)]
        Eb = [sbb("Eb%d" % i, [128, 96]) for i in range(2)]
        rseg = [sbb("rseg%d" % i, [128, 1024]) for i in range(2)]
        dec = [sbb("dec%d" % i, [128, 1024]) for i in range(2)]
        Gb = [sbb("Gb%d" % i, [128, 1024], BF16) for i in range(2)]
        CBm = [sbb("CBm%d" % i, [128, 128]) for i in range(2)]
        xp = [sbb("xp%d" % i, [128, 512], BF16) for i in range(2)]
        xpp = [sbb("xpp%d" % i, [128, 512], BF16) for i in range(2)]
        xd = [sbb("xd%d" % i, [128, 512]) for i in range(2)]
        Btok = [sbb("Btok%d" % i, [128, 128], BF16) for i in range(2)]
        zs = [sbb("zs%d" % i, [128, 512]) for i in range(2)]
        y1 = [sbb("y1%d" % i, [128, 512]) for i in range(2)]
        y2 = [sbb("y2%d" % i, [128, 512]) for i in range(2)]
        junk = sbb("junk", [128, 512])
        st1 = [sbb("st1%d" % i, [128, 2]) for i in range(2)]
        yn = [sbb("yn%d" % i, [128, 512], BF16) for i in range(2)]
        yTs = [sbb("yTs%d" % i, [128, 4, 128], BF16) for i in range(2)]

        def v3(ap, h):
            return ap.rearrange("p (h l) -> p h l", h=h)

        def captureB(fn):
            saved = P.ops
            P.ops = []
            fn()
            out = P.ops
            P.ops = saved
            return out

        for g in range(8):
            dma('pool', wB[:].rearrange("p k c -> p (k c)"), WB[g, :, :], (), ['wB'], 'wB', cast=True)
            dma('sp', nrm[:], ssdn_d[:, g * 512:(g + 1) * 512], (), ['nrm'], 'nrm')
            mset('dve', S[:], 0.0, ['S'])
            mset('dve', Sb[:], 0.0, ['Sb'])
            hs_of = {}

            def proj_parts(t):
                parts = []
                pr = pre[t % 2]
                kp = ('pre', t % 2)

                def first():
                    hs_of[t] = load_h(t * 512)
                    if t == 0:
                        mset('dve', pr[:, :, 0:3], 0.0, [kp])
                    else:
                        cp('dve', pr[:, :, 0:3], pre[(t - 1) % 2][:, :, 512:515], [('pre', (t - 1) % 2)], [kp])

                def chunk(c):
                    hs = hs_of[t]
                    b = pbank()
                    w0 = c * 128 if c < 4 else 512 + (c - 4) * 128
                    for k in range(16):
                        mm(ps[b][:, :], wB[:, k, w0:w0 + 128], hb[hs][:, k, :], k == 0, k == 15,
                           ['wB', ('hb', hs)], [('ps', b)])
                    cp('act', pr[:, c, 3:515], ps[b][:, :], [('ps', b)], [kp])

                def dtpart():
                    hs = hs_of[t]
                    b = pbank()
                    for k in range(16):
                        mm(ps[b][0:8, :], wB[:, k, 1280:1288], hb[hs][:, k, :], k == 0, k == 15,
                           ['wB', ('hb', hs)], [('ps', b)])
                    cp('act', dtT[:, :], ps[b][0:8, :], [('ps', b)], ['dtT'])
                parts.append(captureB(lambda: (first(), chunk(0))))
                for c in range(1, 6):
                    parts.append(captureB(lambda: chunk(c)))
                parts.append(captureB(dtpart))
                return parts

            for part in proj_parts(0):
                P.ops.extend(part)
            for t in range(NT):
                hs = hs_of[t]
                pr = pre[t % 2]
                kp = ('pre', t % 2)
                nxt = proj_parts(t + 1) if t + 1 < NT else []
                tq = t % 2
                D_ = dts[tq]
                kD = ('dts', tq)
                for s in range(4):
                    mm(ps[3][:, s * 8:(s + 1) * 8], dtT[:, s * 128:(s + 1) * 128], cf[0:8, IDN, 0:8], True, True,
                       ['dtT', 'cf'], [('ps', 3)])
                tmp4 = D_[:, 64:96].rearrange("p (s h) -> p s h", s=4)
                dt4 = D_[:, 0:32].rearrange("p (s h) -> p s h", s=4)
                la4 = D_[:, 32:64].rearrange("p (s h) -> p s h", s=4)
                tt('dve', tmp4, ps[3][:, 0:32].rearrange("p (s h) -> p s h", s=4),
                   hvec[:, 0, g * 8:(g + 1) * 8].unsqueeze(1).to_broadcast([128, 4, 8]), ALU.add,
                   [('ps', 3), 'hvec'], [kD])
                act(D_[:, 64:96], D_[:, 64:96], AF.Exp, [kD], [kD])
                act(D_[:, 64:96], D_[:, 64:96], AF.Ln, [kD], [kD], bias=1.0)
                tt('dve', dt4, tmp4, valid[:, t * 4:(t + 1) * 4].unsqueeze(2).to_broadcast([128, 4, 8]), ALU.mult,
                   [kD, 'valid'], [kD])
                tt('dve', la4, dt4, aneg[:, g * 8:(g + 1) * 8].unsqueeze(1).to_broadcast([128, 4, 8]), ALU.mult,
                   [kD, 'aneg'], [kD])
                mm(ps[3][:, 32:64], cf[:, LE, :], D_[:, 32:64], True, True, [kD, 'cf'], [('ps', 3)])
                mm(ps[3][:, 64:96], cf[:, GT, :], D_[:, 32:64], True, True, [kD, 'cf'], [('ps', 3)])
                mm(ps[3][:, 96:128], cf[:, ONES, :], D_[:, 32:64], True, True, [kD, 'cf'], [('ps', 3)])
                act(Eb[tq][:], ps[3][:, 32:128], AF.Exp, [('ps', 3)], [('Eb', tq)])
                tt('dve', D_[:, 96:128], D_[:, 0:32], Eb[tq][:, 32:64], ALU.mult, [kD, ('Eb', tq)], [kD])
                for c in range(6):
                    ch = g * 4 + c if c < 4 else (32 + g if c == 4 else 40 + g)
                    eng = 'dve'
                    ts(eng, acc[:, c, :], pr[:, c, 0:512], sconvw[:, ch, 0:1], sconvb[:, ch:ch + 1], ALU.mult, ALU.add,
                       [kp, 'sconv'], [('acc', c)])
                    for kk in range(1, 4):
                        stt(eng, acc[:, c, :], pr[:, c, kk:kk + 512], sconvw[:, ch, kk:kk + 1], acc[:, c, :],
                            ALU.mult, ALU.add, [kp, 'sconv', ('acc', c)], [('acc', c)])
                    if c < 4:
                        act(xa[:, c, :], acc[:, c, :], AF.Silu, [('acc', c)], ['xa'])
                    elif c == 4:
                        act(BT[:], acc[:, c, :], AF.Silu, [('acc', c)], ['BT'])
                    else:
                        act(CT[:], acc[:, c, :], AF.Silu, [('acc', c)], ['CT'])
                for s in range(4):
                    ci = t * 4 + s
                    outc = ci >= 47
                    q = ci % 2
                    cs = slice(s * 128, (s + 1) * 128)
                    hsl = slice(s * 8, (s + 1) * 8)
                    dt_, la_, dtE_ = D_[:, 0:32][:, hsl], D_[:, 32:64][:, hsl], D_[:, 96:128][:, hsl]
                    Ecs, Etot = Eb[tq][:, 0:32][:, hsl], Eb[tq][:, 64:96][:, hsl]
                    if outc:
                        tt('dve', v3(rseg[q][:], 8), cf[:, LE, :].unsqueeze(1).to_broadcast([128, 8, 128]),
                           la_.unsqueeze(2).to_broadcast([128, 8, 128]), ALU.mult, [kD, 'cf'], [('rseg', q)])
                        for hh in range(2):
                            mm(ps[4 + hh][:, :], cf[:, GT, :], rseg[q][:, hh * 512:(hh + 1) * 512], True, True,
                               [('rseg', q), 'cf'], [('ps', 4 + hh)])
                            act(dec[q][:, hh * 512:(hh + 1) * 512], ps[4 + hh][:, :], AF.Exp, [('ps', 4 + hh)], [('dec', q)])
                    for c in range(4):
                        tr(ps[6][:, c * 128:(c + 1) * 128], xa[:, c, cs], cf[:, IDN, :], ['xa', 'cf'], [('ps', 6)])
                    x3 = v3(ps[6][:, :], 8)
                    tt('dve', v3(xp[q][:], 8), x3, dt_.unsqueeze(2).to_broadcast([128, 8, 64]), ALU.mult,
                       [('ps', 6), kD], [('xp', q)])
                    tt('dve', v3(xpp[q][:], 8), x3, dtE_.unsqueeze(2).to_broadcast([128, 8, 64]), ALU.mult,
                       [('ps', 6), kD], [('xpp', q)])
                    if outc:
                        tt('dve', v3(xd[q][:], 8), x3,
                           hvec[:, 2, g * 8:(g + 1) * 8].unsqueeze(2).to_broadcast([128, 8, 64]), ALU.mult,
                           [('ps', 6), 'hvec'], [('xd', q)])
                    pbb = ps[7][:, 0:64].bitcast(BF16)
                    tr(pbb, BT[:, cs], cb[:, IDN, :], ['BT', 'cb'], [('ps', 7)])
                    cp('act', Btok[q][:], pbb, [('ps', 7)], [('Btok', q)])
                    if outc:
                        col0 = ci * 128 - Q0L
                        mm(ps[7][:, 128:256], BT[:, cs], CT[:, cs], True, True, ['BT', 'CT'], [('ps', 7)])
                        tt('dve', CBm[q][:], ps[7][:, 128:256], cf[:, LE, :], ALU.mult, [('ps', 7), 'cf'], [('CBm', q)])
                        tt('pool', v3(Gb[q][:], 8), v3(dec[q][:], 8), CBm[q][:].unsqueeze(1).to_broadcast([128, 8, 128]),
                           ALU.mult, [('dec', q), ('CBm', q)], [('Gb', q)])
                        for hh in range(8):
                            mm(ps[0][:, hh * 64:(hh + 1) * 64], Gb[q][:, hh * 128:(hh + 1) * 128], xp[q][:, hh * 64:(hh + 1) * 64],
                               True, True, [('Gb', q), ('xp', q)], [('ps', 0)])
                        mm(ps[1][:, :], CT[:, cs], Sb[:], True, True, ['CT', 'Sb'], [('ps', 1)])
                        tt('dve', v3(y1[q][:], 8), v3(ps[1][:, :], 8), Ecs.unsqueeze(2).to_broadcast([128, 8, 64]), ALU.mult,
                           [('ps', 1), ('Eb', tq)], [('y1', q)])
                        tt('dve', y1[q][:], y1[q][:], ps[0][:, :], ALU.add, [('y1', q), ('ps', 0)], [('y1', q)])
                        tt('pool', y1[q][:], y1[q][:], xd[q][:], ALU.add, [('y1', q), ('xd', q)], [('y1', q)])
                        for k in range(16):
                            mm(ps[2][:, :], hb[hs][:, k, cs], wB[:, k, 768:1280], k == 0, k == 15,
                               ['wB', ('hb', hs)], [('ps', 2)])
                        act(zs[q][:], ps[2][:, :], AF.Silu, [('ps', 2)], [('zs', q)])
                        tt('pool', y2[q][:], y1[q][:], zs[q][:], ALU.mult, [('y1', q), ('zs', q)], [('y2', q)])
                        act(junk[:], y2[q][:], AF.Square, [('y2', q)], ['junk', ('st1', q)], accum=st1[q][:, 0:1])
                        act(st1[q][:, 1:2], st1[q][:, 0:1], AF.Sqrt, [('st1', q)], [('st1', q)], scale=1.0 / 512, bias=EPS)
                        recip(st1[q][:, 1:2], st1[q][:, 1:2], [('st1', q)], [('st1', q)])
                        stt('dve', yn[q][:], y2[q][:], st1[q][:, 1:2], nrm[:], ALU.mult, ALU.mult,
                            [('y2', q), ('st1', q), 'nrm'], [('yn', q)])
                        pyt = ps[7][:, 256:512].bitcast(BF16)
                        for c in range(4):
                            tr(pyt[:, c * 128:(c + 1) * 128], yn[q][:, c * 128:(c + 1) * 128], cb[:, IDN, :],
                               [('yn', q), 'cb'], [('ps', 7)])
                        cp('act', yTs[q][:].rearrange("p a b -> p (a b)"), pyt, [('ps', 7)], [('yTs', q)])
                        dma('sp', y_v[:, g * 4:(g + 1) * 4, col0:col0 + 128], yTs[q][:], [('yTs', q)], ['y_d'], ('yst', q))
                    mm(ps[2][:, :], Btok[q][:], xpp[q][:], True, True, [('Btok', q), ('xpp', q)], [('ps', 2)])
                    tt('dve', v3(S[:], 8), v3(S[:], 8), Etot.unsqueeze(2).to_broadcast([128, 8, 64]), ALU.mult,
                       ['S', ('Eb', tq)], ['S'])
                    tt('dve', S[:], S[:], ps[2][:, :], ALU.add, ['S', ('ps', 2)], ['S'])
                    cp('pool', Sb[:], S[:], ['S'], ['Sb'])
                    take = 2 if s < 3 else len(nxt)
                    for _ in range(min(take, len(nxt))):
                        P.ops.extend(nxt.pop(0))

    P.barrier()
    with ExitStack() as sc:
        def sbc(name, shape, dt=F32):
            return sc.enter_context(nc.sbuf_tensor(name, list(shape), dt))
        R1 = sbc("R1", [128, 48, 512], BF16)
        R4 = sbc("R4", [128, 8192])
        R5 = sbc("R5", [128, 16, 512])
        wsl = [sbc("wsl%d" % i, [128, 48 * 128], BF16) for i in range(3)]
        sqb = [sbc("sqb%d" % i, [128, 512], BF16) for i in range(2)]
        rsb = sbc("rsb", [128, 512])
        tmpc = [sbc("tmpc%d" % i, [128, 512]) for i in range(2)]
        ug = [sbc("ug%d" % i, [128, 514]) for i in range(1)]
        uv = [sbc("uv%d" % i, [128, 514]) for i in range(1)]
        carry = sbc("carry", [128, 88, 2])
        merged = R4[:, 0:4096].bitcast(BF16).rearrange("p (k t) -> p k t", k=16)
        xt4 = R4[:].rearrange("p (k t) -> p k t", k=16)
        h2 = hb[0]
        wcnt = [0]

        def wslot():
            wcnt[0] += 1
            return wcnt[0] % 3

        def wload(slot, name, idx, ncols, first):
            if first:
                dma('pool', wsl[slot][:, 0:ncols], SRC[name][idx, :, :], (), [('wsl', slot)], ('wsl', slot), cast=True)
                dma('sp', SCR[name][idx, :, :], wsl[slot][:, 0:ncols], [('wsl', slot)], [('scr', name, idx)], 'wscst')
            else:
                dma('sp', wsl[slot][:, 0:ncols], SCR[name][idx, :, :], [('scr', name, idx)], [('wsl', slot)], ('wsl', slot))

        mset('dve', carry[:], 0.0, ['carry'])
        groups = [(0, 128)] + [(128 + 512 * i, 512) for i in range(4)]
        for gi, (c0g, n) in enumerate(groups):
            l0 = Q0L + c0g
            dma('sp', R1[:, 0:16, 0:n], osb_v[:, :, c0g:c0g + n], ['osb_d'], [('R1', k) for k in range(16)], 'ldo')
            dma('sp', R1[:, 16:48, 0:n], y_v[:, :, c0g:c0g + n], ['y_d'], [('R1', k) for k in range(16, 48)], 'ldy')
            dma('sp', hb[1][:, :, 0:n], hT_v[:, :, l0:l0 + n], [('hTd', l0 // 512)], [('hb', 1)], 'ldh')
            for c in range(16):
                s1, s2 = wslot(), wslot()
                wload(s1, "W2A", c, 6144, gi == 0)
                wload(s2, "W2B", c, 4096, gi == 0)
                for k in range(16):
                    mm(ps[0][:, 0:n], wsl[s1][:, k * 128:(k + 1) * 128], R1[:, k, 0:n], k == 0, k == 15,
                       [('wsl', s1), ('R1', k)], [('ps', 0)])
                for k in range(32):
                    mm(ps[1][:, 0:n], wsl[s2][:, k * 128:(k + 1) * 128], R1[:, 16 + k, 0:n], k == 0, k == 31,
                       [('wsl', s2), ('R1', 16 + k)], [('ps', 1)])
                for k in range(16):
                    mm(ps[2][:, 0:n], wsl[s1][:, (16 + k) * 128:(17 + k) * 128], hb[1][:, k, 0:n], k == 0, k == 15,
                       [('wsl', s1), ('hb', 1)], [('ps', 2)])
                for k in range(16):
                    mm(ps[3][:, 0:n], wsl[s1][:, (32 + k) * 128:(33 + k) * 128], hb[1][:, k, 0:n], k == 0, k == 15,
                       [('wsl', s1), ('hb', 1)], [('ps', 3)])
                a_, b_ = R5[:, 14, :], R5[:, 15, :]
                ka, kb_ = ('R5', 14), ('R5', 15)
                act(a_[:, 0:n], ps[2][:, 0:n], AF.Sigmoid, [('ps', 2)], [ka])
                act(b_[:, 0:n], ps[3][:, 0:n], AF.Sigmoid, [('ps', 3)], [kb_])
                tt('dve', a_[:, 0:n], a_[:, 0:n], ps[0][:, 0:n], ALU.mult, [ka, ('ps', 0)], [ka])
                tt('dve', b_[:, 0:n], b_[:, 0:n], ps[1][:, 0:n], ALU.mult, [kb_, ('ps', 1)], [kb_])
                tt('pool', merged[:, c, 0:n], a_[:, 0:n], b_[:, 0:n], ALU.add, [ka, kb_], [('R4', c // 2)])
            for c in range(16):
                s1 = wslot()
                wload(s1, "WO", c, 2048, gi == 0)
                b = pbank()
                for k in range(16):
                    mm(ps[b][:, 0:n], wsl[s1][:, k * 128:(k + 1) * 128], merged[:, k, 0:n], k == 0, k == 15,
                       [('wsl', s1), ('R4', k // 2)], [('ps', b)])
                cp('act', R5[:, c, 0:n], ps[b][:, 0:n], [('ps', b)], [('R5', c)])
                act(sqb[c % 2][:, 0:n], ps[b][:, 0:n], AF.Square, [('ps', b)], [('sqb', c % 2)])
                mm(ps[4][:, 0:n], cb[:, ONES, :], sqb[c % 2][:, 0:n], c == 0, c == 15, [('sqb', c % 2), 'cb'], [('ps', 4)])
            act(rsb[:, 0:n], ps[4][:, 0:n], AF.Sqrt, [('ps', 4)], ['rsb'], scale=1.0 / D, bias=EPS)
            recip(rsb[:, 0:n], rsb[:, 0:n], ['rsb'], ['rsb'])
            dma('sp', xt4[:, :, 0:n], xw_v[:, :, l0:l0 + n], (), [('R4', k) for k in range(16)], 'ldx')
            for c in range(16):
                stt('dve', tmpc[c % 2][:, 0:n], R5[:, c, 0:n], gains[:, 1, c:c + 1], rsb[:, 0:n], ALU.mult, ALU.mult,
                    [('R5', c), 'rsb', 'gains'], [('tmpc', c % 2)])
                tt('pool', xt4[:, c, 0:n], xt4[:, c, 0:n], tmpc[c % 2][:, 0:n], ALU.add, [('R4', c), ('tmpc', c % 2)],
                   [('R4', c)])
            for c in range(16):
                act(sqb[c % 2][:, 0:n], xt4[:, c, 0:n], AF.Square, [('R4', c)], [('sqb', c % 2)])
                mm(ps[4][:, 0:n], cb[:, ONES, :], sqb[c % 2][:, 0:n], c == 0, c == 15, [('sqb', c % 2), 'cb'], [('ps', 4)])
            act(rsb[:, 0:n], ps[4][:, 0:n], AF.Sqrt, [('ps', 4)], ['rsb'], scale=1.0 / D, bias=EPS)
            recip(rsb[:, 0:n], rsb[:, 0:n], ['rsb'], ['rsb'])
            for c in range(16):
                stt('dve', h2[:, c, 0:n], xt4[:, c, 0:n], gains[:, 2, c:c + 1], rsb[:, 0:n], ALU.mult, ALU.mult,
                    [('R4', c), 'rsb', 'gains'], [('hb', 0)])
            for fc in range(NFC):
                s1 = wslot()
                wload(s1, "WU", fc, 4096, gi == 0)
                q = 0
                for half, (pb_, ub, kq) in enumerate(((0, ug[q], ('ug', q)), (1, uv[q], ('uv', q)))):
                    for k in range(16):
                        mm(ps[pb_][:, 0:n], wsl[s1][:, k * 256 + half * 128:k * 256 + half * 128 + 128], h2[:, k, 0:n],
                           k == 0, k == 15, [('wsl', s1), ('hb', 0)], [('ps', pb_)])
                    cch = fc + half * NFC
                    cp('dve', ub[:, 0:2], carry[:, cch, :], ['carry'], [kq])
                    cp('act', ub[:, 2:2 + n], ps[pb_][:, 0:n], [('ps', pb_)], [kq])
                    if gi == 0:
                        ts1('dve', carry[:, cch, :], ub[:, n:n + 2], hval[:, 0:1], ALU.mult, [kq, 'hval'], ['carry'])
                    else:
                        cp('dve', carry[:, cch, :], ub[:, n:n + 2], [kq], ['carry'])
                    if gi > 0:
                        dst, kd = (R5[:, 0, :], ('R5', 0)) if half == 0 else (R5[:, 1, :], ('R5', 1))
                        eng = 'dve'
                        ts(eng, dst[:, 0:n], ub[:, 0:n], fconvw[:, cch, 0:1], fconvb[:, cch:cch + 1], ALU.mult, ALU.add,
                           [kq, 'fconv'], [kd])
                        stt(eng, dst[:, 0:n], ub[:, 1:n + 1], fconvw[:, cch, 1:2], dst[:, 0:n], ALU.mult, ALU.add,
                            [kq, 'fconv', kd], [kd])
                        stt(eng, dst[:, 0:n], ub[:, 2:n + 2], fconvw[:, cch, 2:3], dst[:, 0:n], ALU.mult, ALU.add,
                            [kq, 'fconv', kd], [kd])
                if gi > 0:
                    G_, V_, T_ = R5[:, 0, :], R5[:, 1, :], R5[:, 2 + fc % 2, :]
                    kT_ = ('R5', 2 + fc % 2)
                    tt('pool', T_[:, 0:n], G_[:, 0:n], G_[:, 0:n], ALU.mult, [('R5', 0)], [kT_])
                    ts('dve', T_[:, 0:n], T_[:, 0:n], 0.044715, 1.0, ALU.mult, ALU.add, [kT_], [kT_])
                    tt('pool', T_[:, 0:n], T_[:, 0:n], G_[:, 0:n], ALU.mult, [kT_, ('R5', 0)], [kT_])
                    act(T_[:, 0:n], T_[:, 0:n], AF.Sigmoid, [kT_], [kT_], scale=1.5957691216057308)
                    tt('dve', T_[:, 0:n], T_[:, 0:n], G_[:, 0:n], ALU.mult, [kT_, ('R5', 0)], [kT_])
                    tt('pool', R1[:, fc, 0:n], T_[:, 0:n], V_[:, 0:n], ALU.mult, [kT_, ('R5', 1)], [('R1', fc)])
            if gi == 0:
                continue
            for c in range(16):
                s1 = wslot()
                wload(s1, "WD", c, NFC * 128, gi == 1)
                b = pbank()
                for k in range(NFC):
                    mm(ps[b][:, 0:n], wsl[s1][:, k * 128:(k + 1) * 128], R1[:, k, 0:n], k == 0, k == NFC - 1,
                       [('wsl', s1), ('R1', k)], [('ps', b)])
                cp('act', R5[:, c, 0:n], ps[b][:, 0:n], [('ps', b)], [('R5', c)])
                act(sqb[c % 2][:, 0:n], ps[b][:, 0:n], AF.Square, [('ps', b)], [('sqb', c % 2)])
                mm(ps[4][:, 0:n], cb[:, ONES, :], sqb[c % 2][:, 0:n], c == 0, c == 15, [('sqb', c % 2), 'cb'], [('ps', 4)])
            act(rsb[:, 0:n], ps[4][:, 0:n], AF.Sqrt, [('ps', 4)], ['rsb'], scale=1.0 / D, bias=EPS)
            recip(rsb[:, 0:n], rsb[:, 0:n], ['rsb'], ['rsb'])
            for c in range(16):
                stt('dve', tmpc[c % 2][:, 0:n], R5[:, c, 0:n], gains[:, 3, c:c + 1], rsb[:, 0:n], ALU.mult, ALU.mult,
                    [('R5', c), 'rsb', 'gains'], [('tmpc', c % 2)])
                tt('pool', R5[:, c, 0:n], xt4[:, c, 0:n], tmpc[c % 2][:, 0:n], ALU.add, [('R4', c), ('tmpc', c % 2)],
                   [('R5', c)])
            t0 = c0g - 128
            dma('sp', out_v[:, :, t0:t0 + n], R5[:, :, 0:n], [('R5', c) for c in range(16)], ['out'], 'stout')

        P.emit(nc, es)
    es.close()
    return nc


_NC = None


def _prep_weights(w_in, w_sb_proj, w_ssd_proj, w_out, w_up, w_down):
    def blk(w, cols):
        K = w.shape[0]
        sub = w[:, cols].reshape(K // 128, 128, len(cols))
        return np.ascontiguousarray(sub.transpose(1, 0, 2)).reshape(128, -1)
    ar = np.arange
    WA = np.stack([blk(w_in, np.concatenate([ar(h * 128, h * 128 + 128), ar(2048 + h * 128, 2048 + h * 128 + 128),
                                             ar(4096 + h * 128, 4096 + h * 128 + 128)])) for h in range(16)])
    WB = np.stack([blk(w_in, np.concatenate([ar(10240 + g * 512, 10240 + g * 512 + 512),
                                             ar(14336 + g * 128, 14336 + g * 128 + 128),
                                             ar(15360 + g * 128, 15360 + g * 128 + 128),
                                             ar(6144 + g * 512, 6144 + g * 512 + 512),
                                             ar(16384 + g * 8, 16384 + g * 8 + 8)])) for g in range(8)])
    W2A = np.stack([np.concatenate([blk(w_sb_proj, ar(c * 128, c * 128 + 128)),
                                    blk(w_in, ar(16448 + c * 128, 16448 + c * 128 + 128)),
                                    blk(w_in, ar(18496 + c * 128, 18496 + c * 128 + 128))], axis=1) for c in range(16)])
    W2B = np.stack([blk(w_ssd_proj, ar(c * 128, c * 128 + 128)) for c in range(16)])
    WO = np.stack([blk(w_out, ar(c * 128, c * 128 + 128)) for c in range(16)])
    WU = np.stack([blk(w_up, np.concatenate([ar(fc * 128, fc * 128 + 128), ar(D_FF + fc * 128, D_FF + fc * 128 + 128)]))
                   for fc in range(NFC)])
    WD = np.stack([blk(w_down, ar(c * 128, c * 128 + 128)) for c in range(16)])
    return dict(WA=WA, WB=WB, W2A=W2A, W2B=W2B, WO=WO, WU=WU, WD=WD)


def kernel(x, norm_mix_pre, w_in, ssd_conv_w, ssd_conv_b, dt_bias, a_log, d_skip, ssd_norm,
           w_sb_proj, w_ssd_proj, w_out, norm_mix_post, norm_ffn_pre, w_up, ffn_conv_w,
           ffn_conv_b, w_down, norm_ffn_post):
    global _NC
    in_maps = _in_maps(x, norm_mix_pre, w_in, ssd_conv_w, ssd_conv_b, dt_bias, a_log, d_skip, ssd_norm,
                       w_sb_proj, w_ssd_proj, w_out, norm_mix_post, norm_ffn_pre, w_up, ffn_conv_w,
                       ffn_conv_b, w_down, norm_ffn_post)
    if _NC is None:
        _NC = build_nc()
    res = run_bass_kernel_spmd(_NC, in_maps, core_ids=list(range(8)))
    out = np.empty((2, 8192, D), np.float32)
    for core in range(8):
        b, c = core // 4, core % 4
        out[b, 2048 * c:2048 * (c + 1), :] = res.results[core]["outT"].T
    return out


def _in_maps(x, norm_mix_pre, w_in, ssd_conv_w, ssd_conv_b, dt_bias, a_log, d_skip, ssd_norm,
             w_sb_proj, w_ssd_proj, w_out, norm_mix_post, norm_ffn_pre, w_up, ffn_conv_w,
             ffn_conv_b, w_down, norm_ffn_post):
    f32 = np.float32
    x = np.asarray(x, f32)
    shared = _prep_weights(np.asarray(w_in, f32)[0], np.asarray(w_sb_proj, f32)[0], np.asarray(w_ssd_proj, f32)[0],
                           np.asarray(w_out, f32)[0], np.asarray(w_up, f32)[0], np.asarray(w_down, f32)[0])
    r = np.arange(128)
    cm = np.stack([(r[:, None] < r[None, :]), (r[:, None] <= r[None, :]), (r[:, None] >= r[None, :]),
                   (r[:, None] > r[None, :]), np.ones((128, 128), bool), np.eye(128, dtype=bool)], axis=1).astype(f32)
    shared["consts"] = np.ascontiguousarray(cm.reshape(128, 768))

    def pk(v):
        return np.asarray(v, f32).reshape(-1, 128).T
    shared["gains"] = np.ascontiguousarray(np.stack([pk(norm_mix_pre[0]), pk(norm_mix_post[0]), pk(norm_ffn_pre[0]),
                                                     pk(norm_ffn_post[0])], axis=1).reshape(128, 64))
    scw = np.asarray(ssd_conv_w, f32)[0]
    shared["sconvw"] = np.ascontiguousarray(scw.reshape(4, 48, 128).transpose(2, 1, 0).reshape(128, 192))
    shared["sconvb"] = np.ascontiguousarray(pk(ssd_conv_b[0]))
    fcw = np.asarray(ffn_conv_w, f32)[0]
    shared["fconvw"] = np.ascontiguousarray(fcw.reshape(3, 88, 128).transpose(2, 1, 0).reshape(128, 264))
    shared["fconvb"] = np.ascontiguousarray(pk(ffn_conv_b[0]))
    hv = np.stack([np.asarray(dt_bias, f32)[0], np.asarray(a_log, f32)[0], np.asarray(d_skip, f32)[0]])
    shared["hvec"] = np.ascontiguousarray(np.broadcast_to(hv.reshape(1, 192), (128, 192)))
    shared["ssdn"] = np.ascontiguousarray(np.broadcast_to(np.asarray(ssd_norm, f32)[0][None, :], (128, 4096)))

    in_maps = []
    for core in range(8):
        b, c = core // 4, core % 4
        end = 2048 * (c + 1)
        start = end - WIN
        xwin = np.zeros((D, WIN), f32)
        lo = max(start, 0)
        xwin[:, lo - start:] = x[b, lo:end, :].T
        tok = start + np.arange(WIN)
        val = (tok >= 0).astype(f32).reshape(64, 128).T
        m = dict(shared)
        m["xw"] = xwin
        m["valid"] = np.ascontiguousarray(val)
        m["hval"] = np.full((128, 1), 1.0 if c > 0 else 0.0, f32)
        in_maps.append(m)
    return in_maps
```

```python
import bisect
import os
from contextlib import ExitStack

import numpy as np
import concourse.bass as bass
import concourse.mybir as mybir
from concourse.bass_utils import run_bass_kernel_spmd

F32 = mybir.dt.float32
BF16 = mybir.dt.bfloat16
AF = mybir.ActivationFunctionType
ALU = mybir.AluOpType

D = 2048
WIN = 8192
NT = 16
Q0L = 6016
NQ = 2176
EPS = 1e-6
SCALE = 128 ** -0.5
D_FF = 5632
NFC = 44
QBS = [(0, 128)] + [(128 + 512 * i, 512) for i in range(4)]


class Prog:
    def __init__(self):
        self.ops = []
        self.bars = []

    def barrier(self):
        self.bars.append(len(self.ops))

    def add(self, eng, fn, R=(), W=(), dk=None):
        self.ops.append((eng, fn, tuple(R), tuple(W), dk))

    def emit(self, nc, es):
        ops = self.ops
        n = len(ops)
        lastw, readers = {}, {}
        deps = []
        for i, (eng, fn, R, W, dk) in enumerate(ops):
            d = set()
            for r in R:
                j = lastw.get(r)
                if j is not None:
                    d.add(j)
            for w in W:
                j = lastw.get(w)
                if j is not None:
                    d.add(j)
                rs = readers.get(w)
                if rs:
                    d.update(rs)
            for r in R:
                readers.setdefault(r, []).append(i)
            for w in W:
                lastw[w] = i
                readers[w] = []
            d.discard(i)
            deps.append(d)
        sig = [False] * n
        cdeps = []
        dma_idx = {}
        for i, op in enumerate(ops):
            if op[4] is not None:
                dma_idx.setdefault(op[4], []).append(i)
        for i, d in enumerate(deps):
            eng = ops[i][0]
            best = {}
            dks = set()
            for j in d:
                oj = ops[j]
                if oj[4] is not None:
                    dks.add(oj[4])
                    continue
                if oj[0] == 'pe' and eng == 'pe':
                    continue
                if best.get(oj[0], -1) < j:
                    best[oj[0]] = j
            for j in best.values():
                sig[j] = True
            cdeps.append((best, dks))
        eng_ops = {}
        for i, op in enumerate(ops):
            if op[4] is None:
                eng_ops.setdefault(op[0], []).append(i)
        for p in self.bars:
            for E in ('pe', 'act', 'dve', 'pool', 'sp'):
                first = next((i for i in range(p, n) if ops[i][0] == E), None)
                if first is None:
                    continue
                best, dks = cdeps[first]
                for E2, lst in eng_ops.items():
                    if E2 == 'pe' and E == 'pe':
                        continue
                    pos = bisect.bisect_left(lst, p)
                    if pos > 0:
                        j = lst[pos - 1]
                        if best.get(E2, -1) < j:
                            best[E2] = j
                            sig[j] = True
                for k, lst in dma_idx.items():
                    if lst and lst[0] < p:
                        dks.add(k)
        seq = [0] * n
        cnt = {}
        for i, op in enumerate(ops):
            if op[4] is None and sig[i]:
                cnt[op[0]] = cnt.get(op[0], 0) + 1
                seq[i] = cnt[op[0]]
        esem = {e: es.enter_context(nc.semaphore("se_" + e)) for e in ('pe', 'act', 'dve', 'pool')}
        dsem = {}
        for k in dma_idx:
            dsem[k] = es.enter_context(nc.semaphore("sd_%d" % len(dsem)))
        block = es.enter_context(nc.Block())

        def run(ename):
            def body(e):
                waited = {}
                for i, (eng, fn, R, W, dk) in enumerate(ops):
                    if eng != ename:
                        continue
                    best, dks = cdeps[i]
                    for se, j in best.items():
                        key = ('e', se)
                        if waited.get(key, 0) < seq[j]:
                            e.wait_ge(esem[se], seq[j])
                            waited[key] = seq[j]
                    for k in dks:
                        val = 16 * bisect.bisect_left(dma_idx[k], i)
                        key = ('d', k)
                        if waited.get(key, 0) < val:
                            e.wait_ge(dsem[k], val)
                            waited[key] = val
                    ins = fn(e)
                    if dk is not None:
                        ins.then_inc(dsem[dk], 16)
                    elif sig[i]:
                        ins.then_inc(esem[eng], 1)
                if ename == 'sp':
                    for k, lst in dma_idx.items():
                        e.wait_ge(dsem[k], 16 * len(lst))
            return body

        block.tensor(run('pe'))
        block.scalar(run('act'))
        block.vector(run('dve'))
        block.gpsimd(run('pool'))
        block.sync(run('sp'))


def build_nc():
    nc = bass.Bass("TRN2", target_bir_lowering=False)
    P = Prog()

    def din(name, shape, dt=F32):
        return nc.dram_tensor(name, list(shape), dt, kind="ExternalInput").ap()

    xw = din("xw", [D, WIN])
    WA = din("WA", [16, 128, 16 * 384])
    WB = din("WB", [8, 128, 16 * 1288])
    W2A = din("W2A", [16, 128, 48 * 128])
    W2B = din("W2B", [16, 128, 32 * 128])
    WO = din("WO", [16, 128, 16 * 128])
    WU = din("WU", [NFC, 128, 16 * 256])
    WD = din("WD", [16, 128, NFC * 128])
    consts_d = din("consts", [128, 6 * 128])
    gains_d = din("gains", [128, 64])
    sconvw_d = din("sconvw", [128, 48 * 4])
    sconvb_d = din("sconvb", [128, 48])
    fconvw_d = din("fconvw", [128, 88 * 3])
    fconvb_d = din("fconvb", [128, 88])
    hvec_d = din("hvec", [128, 3 * 64])
    ssdn_d = din("ssdn", [128, 4096])
    valid_d = din("valid", [128, 64])
    hval_d = din("hval", [128, 1])
    outT = nc.dram_tensor("outT", [D, 2048], F32, kind="ExternalOutput").ap()
    skind = "ExternalOutput" if os.environ.get("KDEBUG") else "Internal"
    hT_d = nc.dram_tensor("hT_d", [D, WIN], BF16, kind=skind).ap()
    osb_d = nc.dram_tensor("osb_d", [D, NQ], BF16, kind=skind).ap()
    y_d = nc.dram_tensor("y_d", [4096, NQ], BF16, kind=skind).ap()
    SCR = {"W2A": nc.dram_tensor("W2A_s", [16, 128, 48 * 128], BF16).ap(),
           "W2B": nc.dram_tensor("W2B_s", [16, 128, 32 * 128], BF16).ap(),
           "WO": nc.dram_tensor("WO_s", [16, 128, 16 * 128], BF16).ap(),
           "WU": nc.dram_tensor("WU_s", [NFC, 128, 16 * 256], BF16).ap(),
           "WD": nc.dram_tensor("WD_s", [16, 128, NFC * 128], BF16).ap()}
    SRC = {"W2A": W2A, "W2B": W2B, "WO": WO, "WU": WU, "WD": WD}

    xw_v = xw.rearrange("(k p) t -> p k t", p=128)
    hT_v = hT_d.rearrange("(k p) t -> p k t", p=128)
    osb_v = osb_d.rearrange("(k p) t -> p k t", p=128)
    y_v = y_d.rearrange("(k p) t -> p k t", p=128)
    out_v = outT.rearrange("(k p) t -> p k t", p=128)

    es = ExitStack()

    def sb(name, shape, dt=F32):
        return es.enter_context(nc.sbuf_tensor("s_" + name, list(shape), dt))

    ps = [es.enter_context(nc.psum_tensor("ps%d" % i, [128, 512], F32)) for i in range(8)]

    def mm(out, lhsT, rhs, start, stop, R, W):
        P.add('pe', lambda e: e.matmul(out, lhsT, rhs, start=start, stop=stop), R, W)

    def tr(out, in_, ident, R, W):
        P.add('pe', lambda e: e.transpose(out, in_, ident), R, W)

    def act(out, in_, func, R, W, scale=1.0, bias=0.0, accum=None):
        if accum is None:
            P.add('act', lambda e: e.activation(out=out, in_=in_, func=func, bias=bias, scale=scale), R, W)
        else:
            P.add('act', lambda e: e.activation(out=out, in_=in_, func=func, bias=bias, scale=scale,
                                                accum_out=accum), R, W)

    def tt(eng, out, a, b, op, R, W):
        P.add(eng, lambda e: e.tensor_tensor(out=out, in0=a, in1=b, op=op), R, W)

    def ts(eng, out, a, s1, s2, op0, op1, R, W):
        P.add(eng, lambda e: e.tensor_scalar(out=out, in0=a, scalar1=s1, scalar2=s2, op0=op0, op1=op1), R, W)

    def ts1(eng, out, a, s1, op0, R, W):
        P.add(eng, lambda e: e.tensor_single_scalar(out=out, in_=a, scalar=s1, op=op0), R, W)

    def stt(eng, out, a, s, b, op0, op1, R, W):
        eng = 'dve'
        P.add(eng, lambda e: e.scalar_tensor_tensor(out=out, in0=a, scalar=s, in1=b, op0=op0, op1=op1), R, W)

    def cp(eng, out, a, R, W):
        if eng == 'act':
            act(out, a, AF.Copy, R, W)
        else:
            P.add(eng, lambda e: e.tensor_copy(out=out, in_=a), R, W)

    def recip(out, a, R, W):
        P.add('dve', lambda e: e.reciprocal(out=out, in_=a), R, W)

    def mset(eng, out, val, W):
        P.add(eng, lambda e: e.memset(out, val), (), W)

    def dma(q, out, in_, R, W, dk, cast=False):
        if cast:
            P.add(q, lambda e: e.dma_start(out=out, in_=in_, max_dma_last_dim=4096), R, W, dk)
        else:
            P.add(q, lambda e: e.dma_start(out=out, in_=in_), R, W, dk)

    cf = sb("cf", [128, 6, 128])
    cb = sb("cb", [128, 6, 128], BF16)
    gains = sb("gains", [128, 4, 16])
    sconvw = sb("sconvw", [128, 48, 4])
    sconvb = sb("sconvb", [128, 48])
    fconvw = sb("fconvw", [128, 88, 3])
    fconvb = sb("fconvb", [128, 88])
    hvec = sb("hvec", [128, 3, 64])
    aneg = sb("aneg", [128, 64])
    valid = sb("valid", [128, 64])
    hval = sb("hval", [128, 1])
    dma('sp', cf[:].rearrange("p a b -> p (a b)"), consts_d[:, :], (), ['cf'], 'c0')
    dma('sp', gains[:].rearrange("p a b -> p (a b)"), gains_d[:, :], (), ['gains'], 'c0')
    dma('sp', sconvw[:].rearrange("p a b -> p (a b)"), sconvw_d[:, :], (), ['sconv'], 'c0')
    dma('sp', sconvb[:], sconvb_d[:, :], (), ['sconv'], 'c0')
    dma('sp', fconvw[:].rearrange("p a b -> p (a b)"), fconvw_d[:, :], (), ['fconv'], 'c0')
    dma('sp', fconvb[:], fconvb_d[:, :], (), ['fconv'], 'c0')
    dma('sp', hvec[:].rearrange("p a b -> p (a b)"), hvec_d[:, :], (), ['hvec'], 'c0')
    dma('sp', valid[:], valid_d[:, :], (), ['valid'], 'c0')
    dma('sp', hval[:], hval_d[:, :], (), ['hval'], 'c0')
    cp('dve', cb[:], cf[:], ['cf'], ['cb'])
    act(aneg[:], hvec[:, 1, :], AF.Exp, ['hvec'], ['aneg'])
    ts1('dve', aneg[:], aneg[:], -1.0, ALU.mult, ['aneg'], ['aneg'])
    LT, LE, GE, GT, ONES, IDN = range(6)

    hb = [sb("hb%d" % i, [128, 16, 512], BF16) for i in range(2)]
    wsl_g = [sb("wslg%d" % i, [128, 48 * 128], BF16) for i in range(2)]
    hcnt = [0]

    def load_h(t0, n=512):
        s = hcnt[0] % 2
        hcnt[0] += 1
        dma('sp', hb[s][:, :, 0:n], hT_v[:, :, t0:t0 + n], [('hTd', t0 // 512)], [('hb', s)], ('hbld', s))
        return s

    pcnt = [0]

    def pbank():
        pcnt[0] += 1
        return pcnt[0] % 2

    with ExitStack() as s0:
        xt = [s0.enter_context(nc.sbuf_tensor("xt%d" % i, [128, 16, 512], F32)) for i in range(2)]
        sq = s0.enter_context(nc.sbuf_tensor("sq", [128, 16, 512], BF16))
        rs = s0.enter_context(nc.sbuf_tensor("rs0", [128, 512], F32))
        for t in range(NT):
            xs = t % 2
            hs = t % 2
            dma('sp', xt[xs][:], xw_v[:, :, t * 512:(t + 1) * 512], (), [('xt', xs)], ('xt', xs))
            act(sq[:], xt[xs][:], AF.Square, [('xt', xs)], ['sq'])
            b = pbank()
            for k in range(16):
                mm(ps[b][:, :], cb[:, ONES, :], sq[:, k, :], k == 0, k == 15, ['sq', 'cb'], [('ps', b)])
            act(rs[:], ps[b][:, :], AF.Sqrt, [('ps', b)], ['rs'], scale=1.0 / D, bias=EPS)
            recip(rs[:], rs[:], ['rs'], ['rs'])
            for k in range(16):
                stt('dve', hb[hs][:, k, :], xt[xs][:, k, :], gains[:, 0, k:k + 1], rs[:], ALU.mult, ALU.mult,
                    [('xt', xs), 'rs', 'gains'], [('hb', hs)])
            dma('sp', hT_v[:, :, t * 512:(t + 1) * 512], hb[hs][:], [('hb', hs)], [('hTd', t)], ('hbst', hs))
        hcnt[0] = 0

    P.barrier()
    with ExitStack() as sa:
        def sba(name, shape, dt=F32):
            return sa.enter_context(nc.sbuf_tensor(name, list(shape), dt))
        wq = [sba("wq%d" % i, [128, 16, 384], BF16) for i in range(2)]
        kT = [sba("kT%d" % i, [128, WIN], BF16) for i in range(2)]
        vv = [sba("vv%d" % i, [128, 64, 128], BF16) for i in range(2)]
        qT = [sba("qT%d" % i, [128, NQ], BF16) for i in range(2)]
        eb = [[sba("eb%d_%d" % (st, i), [128, 512], F32) for i in range(2)] for st in range(2)]
        spb = [[sba("spb%d_%d" % (st, i), [128, 512], BF16) for i in range(2)] for st in range(2)]
        gb = [[sba("gb%d_%d" % (st, i), [128, 512], F32) for i in range(2)] for st in range(2)]
        wb = [[sba("wb%d_%d" % (st, i), [128, 512], BF16) for i in range(2)] for st in range(2)]
        ob = [sba("ob%d" % i, [128, 512], BF16) for i in range(2)]
        vtb = [sba("vtb%d" % i, [128, 512], BF16) for i in range(2)]
        SBANKS = [(2, 4, 5), (3, 6, 7)]
        PAIRS = [[(1664, 256), (1920, 256)], [(1152, 512), (640, 512)], [(128, 512), (0, 128)]]
        ocnt = 0
        def capture(fn):
            saved = P.ops
            P.ops = []
            fn()
            out = P.ops
            P.ops = saved
            return out

        def proj_chunks(h):
            ws = h % 2
            chunks = []
            hs_box = [0]

            def part_k(t):
                if t == 0:
                    dma('pool', wq[ws][:].rearrange("p k c -> p (k c)"), WA[h, :, :], (), [('wq', ws)], ('wq', ws),
                        cast=True)
                hs_box[0] = load_h(t * 512)
                hs = hs_box[0]
                b = pbank()
                for k in range(16):
                    mm(ps[b][:, :], wq[ws][:, k, 128:256], hb[hs][:, k, :], k == 0, k == 15,
                       [('wq', ws), ('hb', hs)], [('ps', b)])
                cp('dve', kT[ws][:, t * 512:(t + 1) * 512], ps[b][:, :], [('ps', b)], [('kT', ws, t)])

            def part_v(t):
                hs = hs_box[0]
                b = pbank()
                for k in range(16):
                    mm(ps[b][:, :], wq[ws][:, k, 256:384], hb[hs][:, k, :], k == 0, k == 15,
                       [('wq', ws), ('hb', hs)], [('ps', b)])
                vs_ = t % 2
                cp('dve', vtb[vs_][:, :], ps[b][:, :], [('ps', b)], [('vtb', vs_)])
                b = pbank()
                pvt = ps[b][:, 0:256].bitcast(BF16)
                for s in range(4):
                    tr(pvt[:, s * 128:(s + 1) * 128], vtb[vs_][:, s * 128:(s + 1) * 128], cb[:, IDN, :],
                       [('vtb', vs_), 'cb'], [('ps', b)])
                cp('dve', vv[ws][:, t * 4:(t + 1) * 4, :].rearrange("p a b -> p (a b)"), pvt,
                   [('ps', b)], [('vv', ws, t)])

            def part_q(t):
                hs = hs_box[0]
                c0 = 384 if t == 11 else 0
                n = 512 - c0
                qc0 = t * 512 + c0 - Q0L
                b = pbank()
                for k in range(16):
                    mm(ps[b][:, 0:n], wq[ws][:, k, 0:128], hb[hs][:, k, c0:512], k == 0, k == 15,
                       [('wq', ws), ('hb', hs)], [('ps', b)])
                cp('dve', qT[ws][:, qc0:qc0 + n], ps[b][:, 0:n], [('ps', b)], [('qT', ws)])

            for t in range(NT):
                chunks.append(capture(lambda: part_k(t)))
                chunks.append(capture(lambda: part_v(t)))
                if t >= 11:
                    chunks.append(capture(lambda: part_q(t)))
            return chunks

        for ch in proj_chunks(0):
            P.ops.extend(ch)
        for h in range(16):
            ws = h % 2
            nxt = proj_chunks(h + 1) if h + 1 < 16 else []
            tot_rounds = sum(max((Q0L + q0 + nq) // 128 for (q0, nq) in pair) for pair in PAIRS)
            every = max(1, tot_rounds // (len(nxt) + 1)) if nxt else 0
            rounds_done = 0
            if os.environ.get("KDEBUG") and h == 0:
                dk_ = nc.dram_tensor("dbg_k", [128, WIN], BF16, kind="ExternalOutput").ap()
                dv_ = nc.dram_tensor("dbg_v", [128, WIN], BF16, kind="ExternalOutput").ap()
                dq_ = nc.dram_tensor("dbg_q", [128, NQ], BF16, kind="ExternalOutput").ap()
                dma('sp', dk_[:, :], kT[0][:, :], [('kT', 0, t) for t in range(16)], ['dbgk'], 'dbg')
                dma('sp', dv_[:, :], vv[0][:].rearrange("p a b -> p (a b)"), [('vv', 0, t) for t in range(16)], ['dbgv'], 'dbg')
                dma('sp', dq_[:, :], qT[0][:, :], [('qT', 0)], ['dbgq'], 'dbg')
                dw_ = nc.dram_tensor("dbg_w", [128, 6144], BF16, kind="ExternalOutput").ap()
                dma('sp', dw_[:, :], wq[0][:].rearrange("p k c -> p (k c)"), [('wq', 0)], ['dbgw'], 'dbg')
            for pair in PAIRS:
                sts = []
                for si, (q0, nq) in enumerate(pair):
                    gq0 = Q0L + q0
                    sts.append(dict(si=si, q0=q0, nq=nq, gq0=gq0, banks=SBANKS[si],
                                    blocks=list(range((gq0 + nq) // 128 - 1, -1, -1))))

                def geom(st, kb):
                    m = kb - st['gq0'] // 128
                    return m >= 0, 128 * max(m, 0)

                def zmm(st, i):
                    kb = st['blocks'][i]
                    _, c0 = geom(st, kb)
                    zb = st['banks'][0]
                    q0, nq = st['q0'], st['nq']
                    mm(ps[zb][:, c0:nq], kT[ws][:, kb * 128:(kb + 1) * 128], qT[ws][:, q0 + c0:q0 + nq], True, True,
                       [('kT', ws, kb // 4), ('qT', ws)], [('ps', zb)])
                for st in sts:
                    zmm(st, 0)
                for i in range(max(len(st['blocks']) for st in sts)):
                    live = [st for st in sts if i < len(st['blocks'])]
                    s_ = i % 2
                    for st in live:
                        si, nq = st['si'], st['nq']
                        zb = st['banks'][0]
                        diag, c0 = geom(st, st['blocks'][i])
                        act(eb[si][s_][:, c0:nq], ps[zb][:, c0:nq], AF.Exp, [('ps', zb)], [('eb', si, s_)], scale=SCALE)
                        if diag:
                            tt('dve', eb[si][s_][:, c0:c0 + 128], eb[si][s_][:, c0:c0 + 128], cf[:, LT, :], ALU.mult,
                               [('eb', si, s_), 'cf'], [('eb', si, s_)])
                        act(spb[si][s_][:, c0:nq], eb[si][s_][:, c0:nq], AF.Ln, [('eb', si, s_)], [('spb', si, s_)], bias=1.0)
                    for st in live:
                        if i + 1 < len(st['blocks']):
                            zmm(st, i + 1)
                    for st in live:
                        si, nq = st['si'], st['nq']
                        xb = st['banks'][1]
                        _, c0 = geom(st, st['blocks'][i])
                        mm(ps[xb][:, c0:nq], cb[:, GE, :], spb[si][s_][:, c0:nq], i == 0, False,
                           [('spb', si, s_), 'cb'], [('ps', xb)])
                    for st in live:
                        si, nq = st['si'], st['nq']
                        xb = st['banks'][1]
                        _, c0 = geom(st, st['blocks'][i])
                        act(gb[si][s_][:, c0:nq], ps[xb][:, c0:nq], AF.Exp, [('ps', xb)], [('gb', si, s_)], scale=-1.0)
                    for st in live:
                        si, nq = st['si'], st['nq']
                        _, c0 = geom(st, st['blocks'][i])
                        tt('dve', wb[si][s_][:, c0:nq], eb[si][s_][:, c0:nq], gb[si][s_][:, c0:nq], ALU.mult,
                           [('eb', si, s_), ('gb', si, s_)], [('wb', si, s_)])
                    for st in live:
                        si, nq = st['si'], st['nq']
                        xb, obk = st['banks'][1], st['banks'][2]
                        kb = st['blocks'][i]
                        _, c0 = geom(st, kb)
                        last = i == len(st['blocks']) - 1
                        mm(ps[obk][:, c0:nq], vv[ws][:, kb, :], wb[si][s_][:, c0:nq], i == 0, last,
                           [('wb', si, s_), ('vv', ws, kb // 4)], [('ps', obk)])
                        mm(ps[xb][:, c0:nq], cb[:, LT, :], spb[si][s_][:, c0:nq], False, last,
                           [('spb', si, s_), 'cb'], [('ps', xb)])
                    rounds_done += 1
                    if nxt and rounds_done % every == 0:
                        P.ops.extend(nxt.pop(0))
                for st in sts:
                    q0, nq, obk = st['q0'], st['nq'], st['banks'][2]
                    os_ = ocnt % 2
                    ocnt += 1
                    cp('dve', ob[os_][:, 0:nq], ps[obk][:, 0:nq], [('ps', obk)], [('ob', os_)])
                    dma('sp', osb_d[h * 128:(h + 1) * 128, q0:q0 + nq], ob[os_][:, 0:nq], [('ob', os_)], ['osb_d'],
                        ('obst', os_))
            while nxt:
                P.ops.extend(nxt.pop(0))

    P.barrier()
    with ExitStack() as sbx:
        def sbb(name, shape, dt=F32):
            return sbx.enter_context(nc.sbuf_tensor(name, list(shape), dt))
        wB = sbb("wB", [128, 16, 1288], BF16)
        pre = [sbb("pre%d" % i, [128, 6, 515]) for i in range(2)]
        dtT = sbb("dtT", [8, 512])
        acc = sbb("acc", [128, 6, 512])
        xa = sbb("xa", [128, 4, 512])
        BT = sbb("BT", [128, 512], BF16)
        CT = sbb("CT", [128, 512], BF16)
        nrm = sbb("nrm", [128, 512])
        S = sbb("S", [128, 512])
        Sb = sbb("Sb", [128, 512], BF16)
        dts = [sbb("dts%d" % i, [128, 128]) for i in range(2)]
        Eb = [sbb("Eb%d" % i, [128, 96]) for i in range(2)]
        rseg = [sbb("rseg%d" % i, [128, 1024]) for i in range(2)]
        dec = [sbb("dec%d" % i, [128, 1024]) for i in range(2)]
        Gb = [sbb("Gb%d" % i, [128, 1024], BF16) for i in range(2)]
        CBm = [sbb("CBm%d" % i, [128, 128]) for i in range(2)]
        xp = [sbb("xp%d" % i, [128, 512], BF16) for i in range(2)]
        xpp = [sbb("xpp%d" % i, [128, 512], BF16) for i in range(2)]
        xd = [sbb("xd%d" % i, [128, 512]) for i in range(2)]
        Btok = [sbb("Btok%d" % i, [128, 128], BF16) for i in range(2)]
        zs = [sbb("zs%d" % i, [128, 512]) for i in range(2)]
        y1 = [sbb("y1%d" % i, [128, 512]) for i in range(2)]
        y2 = [sbb("y2%d" % i, [128, 512]) for i in range(2)]
        junk = sbb("junk", [128, 512])
        st1 = [sbb("st1%d" % i, [128, 2]) for i in range(2)]
        yn = [sbb("yn%d" % i, [128, 512], BF16) for i in range(2)]
        yTs = [sbb("yTs%d" % i, [128, 4, 128], BF16) for i in range(2)]

        def v3(ap, h):
            return ap.rearrange("p (h l) -> p h l", h=h)

        def captureB(fn):
            saved = P.ops
            P.ops = []
            fn()
            out = P.ops
            P.ops = saved
            return out

        cjobs = []
        for c in range(16):
            cjobs += [("W2A", c, 6144), ("W2B", c, 4096)]
        cjobs += [("WO", c, 2048) for c in range(16)] + [("WU", fc, 4096) for fc in range(NFC)]
        cjobs += [("WD", c, NFC * 128) for c in range(16)]
        cj = [0]

        def conv_job():
            if cj[0] >= len(cjobs):
                return
            name, idx, ncols = cjobs[cj[0]]
            slot = cj[0] % 2
            cj[0] += 1
            dma('pool', wsl_g[slot][:, 0:ncols], SRC[name][idx, :, :], (), [('wsl', slot)], ('wsl', slot), cast=True)
            dma('sp', SCR[name][idx, :, :], wsl_g[slot][:, 0:ncols], [('wsl', slot)], [('scr', name, idx)], 'wscst')

        for g in range(8):
            dma('pool', wB[:].rearrange("p k c -> p (k c)"), WB[g, :, :], (), ['wB'], 'wB', cast=True)
            dma('sp', nrm[:], ssdn_d[:, g * 512:(g + 1) * 512], (), ['nrm'], 'nrm')
            mset('dve', S[:], 0.0, ['S'])
            mset('dve', Sb[:], 0.0, ['Sb'])
            hs_of = {}

            def proj_parts(t):
                parts = []
                pr = pre[t % 2]
                kp = ('pre', t % 2)

                def first():
                    hs_of[t] = load_h(t * 512)
                    if t == 0:
                        mset('dve', pr[:, :, 0:3], 0.0, [kp])
                    else:
                        cp('dve', pr[:, :, 0:3], pre[(t - 1) % 2][:, :, 512:515], [('pre', (t - 1) % 2)], [kp])

                def chunk(c):
                    hs = hs_of[t]
                    b = pbank()
                    w0 = c * 128 if c < 4 else 512 + (c - 4) * 128
                    for k in range(16):
                        mm(ps[b][:, :], wB[:, k, w0:w0 + 128], hb[hs][:, k, :], k == 0, k == 15,
                           ['wB', ('hb', hs)], [('ps', b)])
                    cp('act', pr[:, c, 3:515], ps[b][:, :], [('ps', b)], [kp])

                def dtpart():
                    hs = hs_of[t]
                    b = pbank()
                    for k in range(16):
                        mm(ps[b][0:8, :], wB[:, k, 1280:1288], hb[hs][:, k, :], k == 0, k == 15,
                           ['wB', ('hb', hs)], [('ps', b)])
                    cp('act', dtT[:, :], ps[b][0:8, :], [('ps', b)], ['dtT'])
                parts.append(captureB(lambda: (first(), chunk(0))))
                for c in range(1, 6):
                    parts.append(captureB(lambda: chunk(c)))
                parts.append(captureB(dtpart))
                return parts

            for part in proj_parts(0):
                P.ops.extend(part)
            for t in range(NT):
                hs = hs_of[t]
                pr = pre[t % 2]
                kp = ('pre', t % 2)
                nxt = proj_parts(t + 1) if t + 1 < NT else []
                tq = t % 2
                D_ = dts[tq]
                kD = ('dts', tq)
                for s in range(4):
                    mm(ps[3][:, s * 8:(s + 1) * 8], dtT[:, s * 128:(s + 1) * 128], cf[0:8, IDN, 0:8], True, True,
                       ['dtT', 'cf'], [('ps', 3)])
                tmp4 = D_[:, 64:96].rearrange("p (s h) -> p s h", s=4)
                dt4 = D_[:, 0:32].rearrange("p (s h) -> p s h", s=4)
                la4 = D_[:, 32:64].rearrange("p (s h) -> p s h", s=4)
                tt('dve', tmp4, ps[3][:, 0:32].rearrange("p (s h) -> p s h", s=4),
                   hvec[:, 0, g * 8:(g + 1) * 8].unsqueeze(1).to_broadcast([128, 4, 8]), ALU.add,
                   [('ps', 3), 'hvec'], [kD])
                act(D_[:, 64:96], D_[:, 64:96], AF.Exp, [kD], [kD])
                act(D_[:, 64:96], D_[:, 64:96], AF.Ln, [kD], [kD], bias=1.0)
                tt('dve', dt4, tmp4, valid[:, t * 4:(t + 1) * 4].unsqueeze(2).to_broadcast([128, 4, 8]), ALU.mult,
                   [kD, 'valid'], [kD])
                tt('dve', la4, dt4, aneg[:, g * 8:(g + 1) * 8].unsqueeze(1).to_broadcast([128, 4, 8]), ALU.mult,
                   [kD, 'aneg'], [kD])
                mm(ps[3][:, 32:64], cf[:, LE, :], D_[:, 32:64], True, True, [kD, 'cf'], [('ps', 3)])
                mm(ps[3][:, 64:96], cf[:, GT, :], D_[:, 32:64], True, True, [kD, 'cf'], [('ps', 3)])
                mm(ps[3][:, 96:128], cf[:, ONES, :], D_[:, 32:64], True, True, [kD, 'cf'], [('ps', 3)])
                act(Eb[tq][:], ps[3][:, 32:128], AF.Exp, [('ps', 3)], [('Eb', tq)])
                tt('dve', D_[:, 96:128], D_[:, 0:32], Eb[tq][:, 32:64], ALU.mult, [kD, ('Eb', tq)], [kD])
                for c in range(6):
                    ch = g * 4 + c if c < 4 else (32 + g if c == 4 else 40 + g)
                    eng = 'dve'
                    ts(eng, acc[:, c, :], pr[:, c, 0:512], sconvw[:, ch, 0:1], sconvb[:, ch:ch + 1], ALU.mult, ALU.add,
                       [kp, 'sconv'], [('acc', c)])
                    for kk in range(1, 4):
                        stt(eng, acc[:, c, :], pr[:, c, kk:kk + 512], sconvw[:, ch, kk:kk + 1], acc[:, c, :],
                            ALU.mult, ALU.add, [kp, 'sconv', ('acc', c)], [('acc', c)])
                    if c < 4:
                        act(xa[:, c, :], acc[:, c, :], AF.Silu, [('acc', c)], ['xa'])
                    elif c == 4:
                        act(BT[:], acc[:, c, :], AF.Silu, [('acc', c)], ['BT'])
                    else:
                        act(CT[:], acc[:, c, :], AF.Silu, [('acc', c)], ['CT'])
                for s in range(4):
                    ci = t * 4 + s
                    outc = ci >= 47
                    q = ci % 2
                    cs = slice(s * 128, (s + 1) * 128)
                    hsl = slice(s * 8, (s + 1) * 8)
                    dt_, la_, dtE_ = D_[:, 0:32][:, hsl], D_[:, 32:64][:, hsl], D_[:, 96:128][:, hsl]
                    Ecs, Etot = Eb[tq][:, 0:32][:, hsl], Eb[tq][:, 64:96][:, hsl]
                    if outc:
                        tt('dve', v3(rseg[q][:], 8), cf[:, LE, :].unsqueeze(1).to_broadcast([128, 8, 128]),
                           la_.unsqueeze(2).to_broadcast([128, 8, 128]), ALU.mult, [kD, 'cf'], [('rseg', q)])
                        for hh in range(2):
                            mm(ps[4 + hh][:, :], cf[:, GT, :], rseg[q][:, hh * 512:(hh + 1) * 512], True, True,
                               [('rseg', q), 'cf'], [('ps', 4 + hh)])
                            act(dec[q][:, hh * 512:(hh + 1) * 512], ps[4 + hh][:, :], AF.Exp, [('ps', 4 + hh)], [('dec', q)])
                    for c in range(4):
                        tr(ps[6][:, c * 128:(c + 1) * 128], xa[:, c, cs], cf[:, IDN, :], ['xa', 'cf'], [('ps', 6)])
                    x3 = v3(ps[6][:, :], 8)
                    tt('dve', v3(xp[q][:], 8), x3, dt_.unsqueeze(2).to_broadcast([128, 8, 64]), ALU.mult,
                       [('ps', 6), kD], [('xp', q)])
                    tt('dve', v3(xpp[q][:], 8), x3, dtE_.unsqueeze(2).to_broadcast([128, 8, 64]), ALU.mult,
                       [('ps', 6), kD], [('xpp', q)])
                    if outc:
                        tt('dve', v3(xd[q][:], 8), x3,
                           hvec[:, 2, g * 8:(g + 1) * 8].unsqueeze(2).to_broadcast([128, 8, 64]), ALU.mult,
                           [('ps', 6), 'hvec'], [('xd', q)])
                    pbb = ps[7][:, 0:64].bitcast(BF16)
                    tr(pbb, BT[:, cs], cb[:, IDN, :], ['BT', 'cb'], [('ps', 7)])
                    cp('act', Btok[q][:], pbb, [('ps', 7)], [('Btok', q)])
                    if outc:
                        col0 = ci * 128 - Q0L
                        mm(ps[7][:, 128:256], BT[:, cs], CT[:, cs], True, True, ['BT', 'CT'], [('ps', 7)])
                        tt('dve', CBm[q][:], ps[7][:, 128:256], cf[:, LE, :], ALU.mult, [('ps', 7), 'cf'], [('CBm', q)])
                        tt('pool', v3(Gb[q][:], 8), v3(dec[q][:], 8), CBm[q][:].unsqueeze(1).to_broadcast([128, 8, 128]),
                           ALU.mult, [('dec', q), ('CBm', q)], [('Gb', q)])
                        for hh in range(8):
                            mm(ps[0][:, hh * 64:(hh + 1) * 64], Gb[q][:, hh * 128:(hh + 1) * 128], xp[q][:, hh * 64:(hh + 1) * 64],
                               True, True, [('Gb', q), ('xp', q)], [('ps', 0)])
                        mm(ps[1][:, :], CT[:, cs], Sb[:], True, True, ['CT', 'Sb'], [('ps', 1)])
                        tt('dve', v3(y1[q][:], 8), v3(ps[1][:, :], 8), Ecs.unsqueeze(2).to_broadcast([128, 8, 64]), ALU.mult,
                           [('ps', 1), ('Eb', tq)], [('y1', q)])
                        tt('dve', y1[q][:], y1[q][:], ps[0][:, :], ALU.add, [('y1', q), ('ps', 0)], [('y1', q)])
                        tt('pool', y1[q][:], y1[q][:], xd[q][:], ALU.add, [('y1', q), ('xd', q)], [('y1', q)])
                        for k in range(16):
                            mm(ps[2][:, :], hb[hs][:, k, cs], wB[:, k, 768:1280], k == 0, k == 15,
                               ['wB', ('hb', hs)], [('ps', 2)])
                        act(zs[q][:], ps[2][:, :], AF.Silu, [('ps', 2)], [('zs', q)])
                        tt('pool', y2[q][:], y1[q][:], zs[q][:], ALU.mult, [('y1', q), ('zs', q)], [('y2', q)])
                        act(junk[:], y2[q][:], AF.Square, [('y2', q)], ['junk', ('st1', q)], accum=st1[q][:, 0:1])
                        act(st1[q][:, 1:2], st1[q][:, 0:1], AF.Sqrt, [('st1', q)], [('st1', q)], scale=1.0 / 512, bias=EPS)
                        recip(st1[q][:, 1:2], st1[q][:, 1:2], [('st1', q)], [('st1', q)])
                        stt('dve', yn[q][:], y2[q][:], st1[q][:, 1:2], nrm[:], ALU.mult, ALU.mult,
                            [('y2', q), ('st1', q), 'nrm'], [('yn', q)])
                        pyt = ps[7][:, 256:512].bitcast(BF16)
                        for c in range(4):
                            tr(pyt[:, c * 128:(c + 1) * 128], yn[q][:, c * 128:(c + 1) * 128], cb[:, IDN, :],
                               [('yn', q), 'cb'], [('ps', 7)])
                        cp('act', yTs[q][:].rearrange("p a b -> p (a b)"), pyt, [('ps', 7)], [('yTs', q)])
                        dma('sp', y_v[:, g * 4:(g + 1) * 4, col0:col0 + 128], yTs[q][:], [('yTs', q)], ['y_d'], ('yst', q))
                    mm(ps[2][:, :], Btok[q][:], xpp[q][:], True, True, [('Btok', q), ('xpp', q)], [('ps', 2)])
                    tt('dve', v3(S[:], 8), v3(S[:], 8), Etot.unsqueeze(2).to_broadcast([128, 8, 64]), ALU.mult,
                       ['S', ('Eb', tq)], ['S'])
                    tt('dve', S[:], S[:], ps[2][:, :], ALU.add, ['S', ('ps', 2)], ['S'])
                    cp('pool', Sb[:], S[:], ['S'], ['Sb'])
                    conv_job()
                    take = 2 if s < 3 else len(nxt)
                    for _ in range(min(take, len(nxt))):
                        P.ops.extend(nxt.pop(0))

    P.barrier()
    with ExitStack() as sc:
        def sbc(name, shape, dt=F32):
            return sc.enter_context(nc.sbuf_tensor(name, list(shape), dt))
        R1 = sbc("R1", [128, 48, 512], BF16)
        R4 = sbc("R4", [128, 8192])
        R5 = sbc("R5", [128, 16, 512])
        wsl = wsl_g + [sbc("wsl2", [128, 48 * 128], BF16)]
        sqb = [sbc("sqb%d" % i, [128, 512], BF16) for i in range(2)]
        rsb = sbc("rsb", [128, 512])
        tmpc = [sbc("tmpc%d" % i, [128, 512]) for i in range(2)]
        ug = [sbc("ug%d" % i, [128, 514]) for i in range(1)]
        uv = [sbc("uv%d" % i, [128, 514]) for i in range(1)]
        carry = sbc("carry", [128, 88, 2])
        merged = R4[:, 0:4096].bitcast(BF16).rearrange("p (k t) -> p k t", k=16)
        xt4 = R4[:].rearrange("p (k t) -> p k t", k=16)
        h2 = hb[0]
        wcnt = [0]

        def wslot():
            wcnt[0] += 1
            return wcnt[0] % 3

        def wload(slot, name, idx, ncols, first):
            if first:
                dma('pool', wsl[slot][:, 0:ncols], SRC[name][idx, :, :], (), [('wsl', slot)], ('wsl', slot), cast=True)
                dma('sp', SCR[name][idx, :, :], wsl[slot][:, 0:ncols], [('wsl', slot)], [('scr', name, idx)], 'wscst')
            else:
                dma('sp', wsl[slot][:, 0:ncols], SCR[name][idx, :, :], [('scr', name, idx)], [('wsl', slot)], ('wsl', slot))

        mset('dve', carry[:], 0.0, ['carry'])
        groups = [(0, 128)] + [(128 + 512 * i, 512) for i in range(4)]
        for gi, (c0g, n) in enumerate(groups):
            l0 = Q0L + c0g
            dma('sp', R1[:, 0:16, 0:n], osb_v[:, :, c0g:c0g + n], ['osb_d'], [('R1', k) for k in range(16)], 'ldo')
            dma('sp', R1[:, 16:48, 0:n], y_v[:, :, c0g:c0g + n], ['y_d'], [('R1', k) for k in range(16, 48)], 'ldy')
            dma('sp', hb[1][:, :, 0:n], hT_v[:, :, l0:l0 + n], [('hTd', l0 // 512)], [('hb', 1)], 'ldh')
            for c in range(16):
                s1, s2 = wslot(), wslot()
                wload(s1, "W2A", c, 6144, False)
                wload(s2, "W2B", c, 4096, False)
                for k in range(16):
                    mm(ps[0][:, 0:n], wsl[s1][:, k * 128:(k + 1) * 128], R1[:, k, 0:n], k == 0, k == 15,
                       [('wsl', s1), ('R1', k)], [('ps', 0)])
                for k in range(32):
                    mm(ps[1][:, 0:n], wsl[s2][:, k * 128:(k + 1) * 128], R1[:, 16 + k, 0:n], k == 0, k == 31,
                       [('wsl', s2), ('R1', 16 + k)], [('ps', 1)])
                for k in range(16):
                    mm(ps[2][:, 0:n], wsl[s1][:, (16 + k) * 128:(17 + k) * 128], hb[1][:, k, 0:n], k == 0, k == 15,
                       [('wsl', s1), ('hb', 1)], [('ps', 2)])
                for k in range(16):
                    mm(ps[3][:, 0:n], wsl[s1][:, (32 + k) * 128:(33 + k) * 128], hb[1][:, k, 0:n], k == 0, k == 15,
                       [('wsl', s1), ('hb', 1)], [('ps', 3)])
                a_, b_ = R5[:, 14, :], R5[:, 15, :]
                ka, kb_ = ('R5', 14), ('R5', 15)
                act(a_[:, 0:n], ps[2][:, 0:n], AF.Sigmoid, [('ps', 2)], [ka])
                act(b_[:, 0:n], ps[3][:, 0:n], AF.Sigmoid, [('ps', 3)], [kb_])
                tt('dve', a_[:, 0:n], a_[:, 0:n], ps[0][:, 0:n], ALU.mult, [ka, ('ps', 0)], [ka])
                tt('dve', b_[:, 0:n], b_[:, 0:n], ps[1][:, 0:n], ALU.mult, [kb_, ('ps', 1)], [kb_])
                tt('pool', merged[:, c, 0:n], a_[:, 0:n], b_[:, 0:n], ALU.add, [ka, kb_], [('R4', c // 2)])
            for c in range(16):
                s1 = wslot()
                wload(s1, "WO", c, 2048, False)
                b = pbank()
                for k in range(16):
                    mm(ps[b][:, 0:n], wsl[s1][:, k * 128:(k + 1) * 128], merged[:, k, 0:n], k == 0, k == 15,
                       [('wsl', s1), ('R4', k // 2)], [('ps', b)])
                cp('act', R5[:, c, 0:n], ps[b][:, 0:n], [('ps', b)], [('R5', c)])
                act(sqb[c % 2][:, 0:n], ps[b][:, 0:n], AF.Square, [('ps', b)], [('sqb', c % 2)])
                mm(ps[4][:, 0:n], cb[:, ONES, :], sqb[c % 2][:, 0:n], c == 0, c == 15, [('sqb', c % 2), 'cb'], [('ps', 4)])
            act(rsb[:, 0:n], ps[4][:, 0:n], AF.Sqrt, [('ps', 4)], ['rsb'], scale=1.0 / D, bias=EPS)
            recip(rsb[:, 0:n], rsb[:, 0:n], ['rsb'], ['rsb'])
            dma('sp', xt4[:, :, 0:n], xw_v[:, :, l0:l0 + n], (), [('R4', k) for k in range(16)], 'ldx')
            for c in range(16):
                stt('dve', tmpc[c % 2][:, 0:n], R5[:, c, 0:n], gains[:, 1, c:c + 1], rsb[:, 0:n], ALU.mult, ALU.mult,
                    [('R5', c), 'rsb', 'gains'], [('tmpc', c % 2)])
                tt('pool', xt4[:, c, 0:n], xt4[:, c, 0:n], tmpc[c % 2][:, 0:n], ALU.add, [('R4', c), ('tmpc', c % 2)],
                   [('R4', c)])
            for c in range(16):
                act(sqb[c % 2][:, 0:n], xt4[:, c, 0:n], AF.Square, [('R4', c)], [('sqb', c % 2)])
                mm(ps[4][:, 0:n], cb[:, ONES, :], sqb[c % 2][:, 0:n], c == 0, c == 15, [('sqb', c % 2), 'cb'], [('ps', 4)])
            act(rsb[:, 0:n], ps[4][:, 0:n], AF.Sqrt, [('ps', 4)], ['rsb'], scale=1.0 / D, bias=EPS)
            recip(rsb[:, 0:n], rsb[:, 0:n], ['rsb'], ['rsb'])
            for c in range(16):
                stt('dve', h2[:, c, 0:n], xt4[:, c, 0:n], gains[:, 2, c:c + 1], rsb[:, 0:n], ALU.mult, ALU.mult,
                    [('R4', c), 'rsb', 'gains'], [('hb', 0)])
            for fc in range(NFC):
                s1 = wslot()
                wload(s1, "WU", fc, 4096, False)
                q = 0
                for half, (pb_, ub, kq) in enumerate(((0, ug[q], ('ug', q)), (1, uv[q], ('uv', q)))):
                    for k in range(16):
                        mm(ps[pb_][:, 0:n], wsl[s1][:, k * 256 + half * 128:k * 256 + half * 128 + 128], h2[:, k, 0:n],
                           k == 0, k == 15, [('wsl', s1), ('hb', 0)], [('ps', pb_)])
                    cch = fc + half * NFC
                    cp('dve', ub[:, 0:2], carry[:, cch, :], ['carry'], [kq])
                    cp('act', ub[:, 2:2 + n], ps[pb_][:, 0:n], [('ps', pb_)], [kq])
                    if gi == 0:
                        ts1('dve', carry[:, cch, :], ub[:, n:n + 2], hval[:, 0:1], ALU.mult, [kq, 'hval'], ['carry'])
                    else:
                        cp('dve', carry[:, cch, :], ub[:, n:n + 2], [kq], ['carry'])
                    if gi > 0:
                        dst, kd = (R5[:, 0, :], ('R5', 0)) if half == 0 else (R5[:, 1, :], ('R5', 1))
                        eng = 'dve'
                        ts(eng, dst[:, 0:n], ub[:, 0:n], fconvw[:, cch, 0:1], fconvb[:, cch:cch + 1], ALU.mult, ALU.add,
                           [kq, 'fconv'], [kd])
                        stt(eng, dst[:, 0:n], ub[:, 1:n + 1], fconvw[:, cch, 1:2], dst[:, 0:n], ALU.mult, ALU.add,
                            [kq, 'fconv', kd], [kd])
                        stt(eng, dst[:, 0:n], ub[:, 2:n + 2], fconvw[:, cch, 2:3], dst[:, 0:n], ALU.mult, ALU.add,
                            [kq, 'fconv', kd], [kd])
                if gi > 0:
                    G_, V_, T_ = R5[:, 0, :], R5[:, 1, :], R5[:, 2 + fc % 2, :]
                    kT_ = ('R5', 2 + fc % 2)
                    tt('pool', T_[:, 0:n], G_[:, 0:n], G_[:, 0:n], ALU.mult, [('R5', 0)], [kT_])
                    ts('dve', T_[:, 0:n], T_[:, 0:n], 0.044715, 1.0, ALU.mult, ALU.add, [kT_], [kT_])
                    tt('pool', T_[:, 0:n], T_[:, 0:n], G_[:, 0:n], ALU.mult, [kT_, ('R5', 0)], [kT_])
                    act(T_[:, 0:n], T_[:, 0:n], AF.Sigmoid, [kT_], [kT_], scale=1.5957691216057308)
                    tt('dve', T_[:, 0:n], T_[:, 0:n], G_[:, 0:n], ALU.mult, [kT_, ('R5', 0)], [kT_])
                    tt('pool', R1[:, fc, 0:n], T_[:, 0:n], V_[:, 0:n], ALU.mult, [kT_, ('R5', 1)], [('R1', fc)])
            if gi == 0:
                continue
            for c in range(16):
                s1 = wslot()
                wload(s1, "WD", c, NFC * 128, False)
                b = pbank()
                for k in range(NFC):
                    mm(ps[b][:, 0:n], wsl[s1][:, k * 128:(k + 1) * 128], R1[:, k, 0:n], k == 0, k == NFC - 1,
                       [('wsl', s1), ('R1', k)], [('ps', b)])
                cp('act', R5[:, c, 0:n], ps[b][:, 0:n], [('ps', b)], [('R5', c)])
                act(sqb[c % 2][:, 0:n], ps[b][:, 0:n], AF.Square, [('ps', b)], [('sqb', c % 2)])
                mm(ps[4][:, 0:n], cb[:, ONES, :], sqb[c % 2][:, 0:n], c == 0, c == 15, [('sqb', c % 2), 'cb'], [('ps', 4)])
            act(rsb[:, 0:n], ps[4][:, 0:n], AF.Sqrt, [('ps', 4)], ['rsb'], scale=1.0 / D, bias=EPS)
            recip(rsb[:, 0:n], rsb[:, 0:n], ['rsb'], ['rsb'])
            for c in range(16):
                stt('dve', tmpc[c % 2][:, 0:n], R5[:, c, 0:n], gains[:, 3, c:c + 1], rsb[:, 0:n], ALU.mult, ALU.mult,
                    [('R5', c), 'rsb', 'gains'], [('tmpc', c % 2)])
                tt('pool', R5[:, c, 0:n], xt4[:, c, 0:n], tmpc[c % 2][:, 0:n], ALU.add, [('R4', c), ('tmpc', c % 2)],
                   [('R5', c)])
            t0 = c0g - 128
            dma('sp', out_v[:, :, t0:t0 + n], R5[:, :, 0:n], [('R5', c) for c in range(16)], ['out'], 'stout')

        P.emit(nc, es)
    es.close()
    return nc


_NC = None


def _prep_weights(w_in, w_sb_proj, w_ssd_proj, w_out, w_up, w_down):
    def blk(w, cols):
        K = w.shape[0]
        sub = w[:, cols].reshape(K // 128, 128, len(cols))
        return np.ascontiguousarray(sub.transpose(1, 0, 2)).reshape(128, -1)
    ar = np.arange
    WA = np.stack([blk(w_in, np.concatenate([ar(h * 128, h * 128 + 128), ar(2048 + h * 128, 2048 + h * 128 + 128),
                                             ar(4096 + h * 128, 4096 + h * 128 + 128)])) for h in range(16)])
    WB = np.stack([blk(w_in, np.concatenate([ar(10240 + g * 512, 10240 + g * 512 + 512),
                                             ar(14336 + g * 128, 14336 + g * 128 + 128),
                                             ar(15360 + g * 128, 15360 + g * 128 + 128),
                                             ar(6144 + g * 512, 6144 + g * 512 + 512),
                                             ar(16384 + g * 8, 16384 + g * 8 + 8)])) for g in range(8)])
    W2A = np.stack([np.concatenate([blk(w_sb_proj, ar(c * 128, c * 128 + 128)),
                                    blk(w_in, ar(16448 + c * 128, 16448 + c * 128 + 128)),
                                    blk(w_in, ar(18496 + c * 128, 18496 + c * 128 + 128))], axis=1) for c in range(16)])
    W2B = np.stack([blk(w_ssd_proj, ar(c * 128, c * 128 + 128)) for c in range(16)])
    WO = np.stack([blk(w_out, ar(c * 128, c * 128 + 128)) for c in range(16)])
    WU = np.stack([blk(w_up, np.concatenate([ar(fc * 128, fc * 128 + 128), ar(D_FF + fc * 128, D_FF + fc * 128 + 128)]))
                   for fc in range(NFC)])
    WD = np.stack([blk(w_down, ar(c * 128, c * 128 + 128)) for c in range(16)])
    return dict(WA=WA, WB=WB, W2A=W2A, W2B=W2B, WO=WO, WU=WU, WD=WD)


def kernel(x, norm_mix_pre, w_in, ssd_conv_w, ssd_conv_b, dt_bias, a_log, d_skip, ssd_norm,
           w_sb_proj, w_ssd_proj, w_out, norm_mix_post, norm_ffn_pre, w_up, ffn_conv_w,
           ffn_conv_b, w_down, norm_ffn_post):
    global _NC
    in_maps = _in_maps(x, norm_mix_pre, w_in, ssd_conv_w, ssd_conv_b, dt_bias, a_log, d_skip, ssd_norm,
                       w_sb_proj, w_ssd_proj, w_out, norm_mix_post, norm_ffn_pre, w_up, ffn_conv_w,
                       ffn_conv_b, w_down, norm_ffn_post)
    if _NC is None:
        _NC = build_nc()
    res = run_bass_kernel_spmd(_NC, in_maps, core_ids=list(range(8)))
    out = np.empty((2, 8192, D), np.float32)
    for core in range(8):
        b, c = core // 4, core % 4
        out[b, 2048 * c:2048 * (c + 1), :] = res.results[core]["outT"].T
    return out


def _in_maps(x, norm_mix_pre, w_in, ssd_conv_w, ssd_conv_b, dt_bias, a_log, d_skip, ssd_norm,
             w_sb_proj, w_ssd_proj, w_out, norm_mix_post, norm_ffn_pre, w_up, ffn_conv_w,
             ffn_conv_b, w_down, norm_ffn_post):
    f32 = np.float32
    x = np.asarray(x, f32)
    shared = _prep_weights(np.asarray(w_in, f32)[0], np.asarray(w_sb_proj, f32)[0], np.asarray(w_ssd_proj, f32)[0],
                           np.asarray(w_out, f32)[0], np.asarray(w_up, f32)[0], np.asarray(w_down, f32)[0])
    r = np.arange(128)
    cm = np.stack([(r[:, None] < r[None, :]), (r[:, None] <= r[None, :]), (r[:, None] >= r[None, :]),
                   (r[:, None] > r[None, :]), np.ones((128, 128), bool), np.eye(128, dtype=bool)], axis=1).astype(f32)
    shared["consts"] = np.ascontiguousarray(cm.reshape(128, 768))

    def pk(v):
        return np.asarray(v, f32).reshape(-1, 128).T
    shared["gains"] = np.ascontiguousarray(np.stack([pk(norm_mix_pre[0]), pk(norm_mix_post[0]), pk(norm_ffn_pre[0]),
                                                     pk(norm_ffn_post[0])], axis=1).reshape(128, 64))
    scw = np.asarray(ssd_conv_w, f32)[0]
    shared["sconvw"] = np.ascontiguousarray(scw.reshape(4, 48, 128).transpose(2, 1, 0).reshape(128, 192))
    shared["sconvb"] = np.ascontiguousarray(pk(ssd_conv_b[0]))
    fcw = np.asarray(ffn_conv_w, f32)[0]
    shared["fconvw"] = np.ascontiguousarray(fcw.reshape(3, 88, 128).transpose(2, 1, 0).reshape(128, 264))
    shared["fconvb"] = np.ascontiguousarray(pk(ffn_conv_b[0]))
    hv = np.stack([np.asarray(dt_bias, f32)[0], np.asarray(a_log, f32)[0], np.asarray(d_skip, f32)[0]])
    shared["hvec"] = np.ascontiguousarray(np.broadcast_to(hv.reshape(1, 192), (128, 192)))
    shared["ssdn"] = np.ascontiguousarray(np.broadcast_to(np.asarray(ssd_norm, f32)[0][None, :], (128, 4096)))

    in_maps = []
    for core in range(8):
        b, c = core // 4, core % 4
        end = 2048 * (c + 1)
        start = end - WIN
        xwin = np.zeros((D, WIN), f32)
        lo = max(start, 0)
        xwin[:, lo - start:] = x[b, lo:end, :].T
        tok = start + np.arange(WIN)
        val = (tok >= 0).astype(f32).reshape(64, 128).T
        m = dict(shared)
        m["xw"] = xwin
        m["valid"] = np.ascontiguousarray(val)
        m["hval"] = np.full((128, 1), 1.0 if c > 0 else 0.0, f32)
        in_maps.append(m)
    return in_maps
```

```python
import bisect
import os
from contextlib import ExitStack

import numpy as np
import concourse.bass as bass
import concourse.mybir as mybir
from concourse.bass_utils import run_bass_kernel_spmd

F32 = mybir.dt.float32
BF16 = mybir.dt.bfloat16
AF = mybir.ActivationFunctionType
ALU = mybir.AluOpType

D = 2048
WIN = 8192
NT = 16
Q0L = 6016
NQ = 2176
EPS = 1e-6
SCALE = 128 ** -0.5
D_FF = 5632
NFC = 44
QBS = [(0, 128)] + [(128 + 512 * i, 512) for i in range(4)]


class Prog:
    def __init__(self):
        self.ops = []
        self.bars = []

    def barrier(self):
        self.bars.append(len(self.ops))

    def add(self, eng, fn, R=(), W=(), dk=None):
        self.ops.append((eng, fn, tuple(R), tuple(W), dk))

    def emit(self, nc, es):
        ops = self.ops
        n = len(ops)
        lastw, readers = {}, {}
        deps = []
        for i, (eng, fn, R, W, dk) in enumerate(ops):
            d = set()
            for r in R:
                j = lastw.get(r)
                if j is not None:
                    d.add(j)
            for w in W:
                j = lastw.get(w)
                if j is not None:
                    d.add(j)
                rs = readers.get(w)
                if rs:
                    d.update(rs)
            for r in R:
                readers.setdefault(r, []).append(i)
            for w in W:
                lastw[w] = i
                readers[w] = []
            d.discard(i)
            deps.append(d)
        sig = [False] * n
        cdeps = []
        dma_idx = {}
        for i, op in enumerate(ops):
            if op[4] is not None:
                dma_idx.setdefault(op[4], []).append(i)
        for i, d in enumerate(deps):
            eng = ops[i][0]
            best = {}
            dks = set()
            for j in d:
                oj = ops[j]
                if oj[4] is not None:
                    dks.add(oj[4])
                    continue
                if oj[0] == 'pe' and eng == 'pe':
                    continue
                if best.get(oj[0], -1) < j:
                    best[oj[0]] = j
            for j in best.values():
                sig[j] = True
            cdeps.append((best, dks))
        eng_ops = {}
        for i, op in enumerate(ops):
            if op[4] is None:
                eng_ops.setdefault(op[0], []).append(i)
        for p in self.bars:
            for E in ('pe', 'act', 'dve', 'pool', 'sp'):
                first = next((i for i in range(p, n) if ops[i][0] == E), None)
                if first is None:
                    continue
                best, dks = cdeps[first]
                for E2, lst in eng_ops.items():
                    if E2 == 'pe' and E == 'pe':
                        continue
                    pos = bisect.bisect_left(lst, p)
                    if pos > 0:
                        j = lst[pos - 1]
                        if best.get(E2, -1) < j:
                            best[E2] = j
                            sig[j] = True
                for k, lst in dma_idx.items():
                    if lst and lst[0] < p:
                        dks.add(k)
        seq = [0] * n
        cnt = {}
        for i, op in enumerate(ops):
            if op[4] is None and sig[i]:
                cnt[op[0]] = cnt.get(op[0], 0) + 1
                seq[i] = cnt[op[0]]
        esem = {e: es.enter_context(nc.semaphore("se_" + e)) for e in ('pe', 'act', 'dve', 'pool')}
        dsem = {}
        for k in dma_idx:
            dsem[k] = es.enter_context(nc.semaphore("sd_%d" % len(dsem)))
        block = es.enter_context(nc.Block())

        def run(ename):
            def body(e):
                waited = {}
                for i, (eng, fn, R, W, dk) in enumerate(ops):
                    if eng != ename:
                        continue
                    best, dks = cdeps[i]
                    for se, j in best.items():
                        key = ('e', se)
                        if waited.get(key, 0) < seq[j]:
                            e.wait_ge(esem[se], seq[j])
                            waited[key] = seq[j]
                    for k in dks:
                        val = 16 * bisect.bisect_left(dma_idx[k], i)
                        key = ('d', k)
                        if waited.get(key, 0) < val:
                            e.wait_ge(dsem[k], val)
                            waited[key] = val
                    ins = fn(e)
                    if dk is not None:
                        ins.then_inc(dsem[dk], 16)
                    elif sig[i]:
                        ins.then_inc(esem[eng], 1)
                if ename == 'sp':
                    for k, lst in dma_idx.items():
                        e.wait_ge(dsem[k], 16 * len(lst))
            return body

        block.tensor(run('pe'))
        block.scalar(run('act'))
        block.vector(run('dve'))
        block.gpsimd(run('pool'))
        block.sync(run('sp'))


def build_nc():
    nc = bass.Bass("TRN2", target_bir_lowering=False)
    P = Prog()

    def din(name, shape, dt=F32):
        return nc.dram_tensor(name, list(shape), dt, kind="ExternalInput").ap()

    xw = din("xw", [D, WIN])
    WA = din("WA", [16, 128, 16 * 384])
    WB = din("WB", [8, 128, 16 * 1288])
    W2A = din("W2A", [16, 128, 48 * 128])
    W2B = din("W2B", [16, 128, 32 * 128])
    WO = din("WO", [16, 128, 16 * 128])
    WU = din("WU", [NFC, 128, 16 * 256])
    WD = din("WD", [16, 128, NFC * 128])
    consts_d = din("consts", [128, 6 * 128])
    gains_d = din("gains", [128, 64])
    sconvw_d = din("sconvw", [128, 48 * 4])
    sconvb_d = din("sconvb", [128, 48])
    fconvw_d = din("fconvw", [128, 88 * 3])
    fconvb_d = din("fconvb", [128, 88])
    hvec_d = din("hvec", [128, 3 * 64])
    ssdn_d = din("ssdn", [128, 4096])
    valid_d = din("valid", [128, 64])
    hval_d = din("hval", [128, 1])
    outT = nc.dram_tensor("outT", [D, 2048], F32, kind="ExternalOutput").ap()
    skind = "ExternalOutput" if os.environ.get("KDEBUG") else "Internal"
    hT_d = nc.dram_tensor("hT_d", [D, WIN], BF16, kind=skind).ap()
    osb_d = nc.dram_tensor("osb_d", [D, NQ], BF16, kind=skind).ap()
    y_d = nc.dram_tensor("y_d", [4096, NQ], BF16, kind=skind).ap()
    SCR = {"W2A": nc.dram_tensor("W2A_s", [16, 128, 48 * 128], BF16).ap(),
           "W2B": nc.dram_tensor("W2B_s", [16, 128, 32 * 128], BF16).ap(),
           "WO": nc.dram_tensor("WO_s", [16, 128, 16 * 128], BF16).ap(),
           "WU": nc.dram_tensor("WU_s", [NFC, 128, 16 * 256], BF16).ap(),
           "WD": nc.dram_tensor("WD_s", [16, 128, NFC * 128], BF16).ap()}
    SRC = {"W2A": W2A, "W2B": W2B, "WO": WO, "WU": WU, "WD": WD}

    xw_v = xw.rearrange("(k p) t -> p k t", p=128)
    hT_v = hT_d.rearrange("(k p) t -> p k t", p=128)
    osb_v = osb_d.rearrange("(k p) t -> p k t", p=128)
    y_v = y_d.rearrange("(k p) t -> p k t", p=128)
    out_v = outT.rearrange("(k p) t -> p k t", p=128)

    es = ExitStack()

    def sb(name, shape, dt=F32):
        return es.enter_context(nc.sbuf_tensor("s_" + name, list(shape), dt))

    ps = [es.enter_context(nc.psum_tensor("ps%d" % i, [128, 512], F32)) for i in range(8)]

    def mm(out, lhsT, rhs, start, stop, R, W):
        P.add('pe', lambda e: e.matmul(out, lhsT, rhs, start=start, stop=stop), R, W)

    def tr(out, in_, ident, R, W):
        P.add('pe', lambda e: e.transpose(out, in_, ident), R, W)

    def act(out, in_, func, R, W, scale=1.0, bias=0.0, accum=None):
        if accum is None:
            P.add('act', lambda e: e.activation(out=out, in_=in_, func=func, bias=bias, scale=scale), R, W)
        else:
            P.add('act', lambda e: e.activation(out=out, in_=in_, func=func, bias=bias, scale=scale,
                                                accum_out=accum), R, W)

    def tt(eng, out, a, b, op, R, W):
        P.add(eng, lambda e: e.tensor_tensor(out=out, in0=a, in1=b, op=op), R, W)

    def ts(eng, out, a, s1, s2, op0, op1, R, W):
        P.add(eng, lambda e: e.tensor_scalar(out=out, in0=a, scalar1=s1, scalar2=s2, op0=op0, op1=op1), R, W)

    def ts1(eng, out, a, s1, op0, R, W):
        P.add(eng, lambda e: e.tensor_single_scalar(out=out, in_=a, scalar=s1, op=op0), R, W)

    def stt(eng, out, a, s, b, op0, op1, R, W):
        eng = 'dve'
        P.add(eng, lambda e: e.scalar_tensor_tensor(out=out, in0=a, scalar=s, in1=b, op0=op0, op1=op1), R, W)

    def cp(eng, out, a, R, W):
        if eng == 'act':
            act(out, a, AF.Copy, R, W)
        else:
            P.add(eng, lambda e: e.tensor_copy(out=out, in_=a), R, W)

    def recip(out, a, R, W):
        P.add('dve', lambda e: e.reciprocal(out=out, in_=a), R, W)

    def mset(eng, out, val, W):
        P.add(eng, lambda e: e.memset(out, val), (), W)

    def dma(q, out, in_, R, W, dk, cast=False):
        if cast:
            P.add(q, lambda e: e.dma_start(out=out, in_=in_, max_dma_last_dim=4096), R, W, dk)
        else:
            P.add(q, lambda e: e.dma_start(out=out, in_=in_), R, W, dk)

    cf = sb("cf", [128, 6, 128])
    cb = sb("cb", [128, 6, 128], BF16)
    gains = sb("gains", [128, 4, 16])
    sconvw = sb("sconvw", [128, 48, 4])
    sconvb = sb("sconvb", [128, 48])
    fconvw = sb("fconvw", [128, 88, 3])
    fconvb = sb("fconvb", [128, 88])
    hvec = sb("hvec", [128, 3, 64])
    aneg = sb("aneg", [128, 64])
    valid = sb("valid", [128, 64])
    hval = sb("hval", [128, 1])
    dma('sp', cf[:].rearrange("p a b -> p (a b)"), consts_d[:, :], (), ['cf'], 'c0')
    dma('sp', gains[:].rearrange("p a b -> p (a b)"), gains_d[:, :], (), ['gains'], 'c0')
    dma('sp', sconvw[:].rearrange("p a b -> p (a b)"), sconvw_d[:, :], (), ['sconv'], 'c0')
    dma('sp', sconvb[:], sconvb_d[:, :], (), ['sconv'], 'c0')
    dma('sp', fconvw[:].rearrange("p a b -> p (a b)"), fconvw_d[:, :], (), ['fconv'], 'c0')
    dma('sp', fconvb[:], fconvb_d[:, :], (), ['fconv'], 'c0')
    dma('sp', hvec[:].rearrange("p a b -> p (a b)"), hvec_d[:, :], (), ['hvec'], 'c0')
    dma('sp', valid[:], valid_d[:, :], (), ['valid'], 'c0')
    dma('sp', hval[:], hval_d[:, :], (), ['hval'], 'c0')
    cp('dve', cb[:], cf[:], ['cf'], ['cb'])
    act(aneg[:], hvec[:, 1, :], AF.Exp, ['hvec'], ['aneg'])
    ts1('dve', aneg[:], aneg[:], -1.0, ALU.mult, ['aneg'], ['aneg'])
    LT, LE, GE, GT, ONES, IDN = range(6)

    hb = [sb("hb%d" % i, [128, 16, 512], BF16) for i in range(2)]
    wsl_g = [sb("wslg%d" % i, [128, 48 * 128], BF16) for i in range(2)]
    hcnt = [0]

    def load_h(t0, n=512):
        s = hcnt[0] % 2
        hcnt[0] += 1
        dma('sp', hb[s][:, :, 0:n], hT_v[:, :, t0:t0 + n], [('hTd', t0 // 512)], [('hb', s)], ('hbld', s))
        return s

    pcnt = [0]

    def pbank():
        pcnt[0] += 1
        return pcnt[0] % 2

    with ExitStack() as s0:
        xt = [s0.enter_context(nc.sbuf_tensor("xt%d" % i, [128, 16, 512], F32)) for i in range(2)]
        sq = s0.enter_context(nc.sbuf_tensor("sq", [128, 16, 512], BF16))
        rs = s0.enter_context(nc.sbuf_tensor("rs0", [128, 512], F32))
        for t in range(NT):
            xs = t % 2
            hs = t % 2
            dma('sp', xt[xs][:], xw_v[:, :, t * 512:(t + 1) * 512], (), [('xt', xs)], ('xt', xs))
            act(sq[:], xt[xs][:], AF.Square, [('xt', xs)], ['sq'])
            b = pbank()
            for k in range(16):
                mm(ps[b][:, :], cb[:, ONES, :], sq[:, k, :], k == 0, k == 15, ['sq', 'cb'], [('ps', b)])
            act(rs[:], ps[b][:, :], AF.Sqrt, [('ps', b)], ['rs'], scale=1.0 / D, bias=EPS)
            recip(rs[:], rs[:], ['rs'], ['rs'])
            for k in range(16):
                stt('dve', hb[hs][:, k, :], xt[xs][:, k, :], gains[:, 0, k:k + 1], rs[:], ALU.mult, ALU.mult,
                    [('xt', xs), 'rs', 'gains'], [('hb', hs)])
            dma('sp', hT_v[:, :, t * 512:(t + 1) * 512], hb[hs][:], [('hb', hs)], [('hTd', t)], ('hbst', hs))
        hcnt[0] = 0

    cjobs = []
    for c in range(16):
        cjobs += [("W2A", c, 6144), ("W2B", c, 4096)]
    cjobs += [("WO", c, 2048) for c in range(16)] + [("WU", fc, 4096) for fc in range(NFC)]
    cjobs += [("WD", c, NFC * 128) for c in range(16)]
    cj = [0]

    def conv_job():
        if cj[0] >= len(cjobs):
            return
        name, idx, ncols = cjobs[cj[0]]
        slot = cj[0] % 2
        cj[0] += 1
        dma('pool', wsl_g[slot][:, 0:ncols], SRC[name][idx, :, :], (), [('wsl', slot)], ('wsl', slot), cast=True)
        dma('sp', SCR[name][idx, :, :], wsl_g[slot][:, 0:ncols], [('wsl', slot)], [('scr', name, idx)], 'wscst')


    P.barrier()
    with ExitStack() as sa:
        def sba(name, shape, dt=F32):
            return sa.enter_context(nc.sbuf_tensor(name, list(shape), dt))
        wq = [sba("wq%d" % i, [128, 16, 384], BF16) for i in range(2)]
        kT = [sba("kT%d" % i, [128, WIN], BF16) for i in range(2)]
        vv = [sba("vv%d" % i, [128, 64, 128], BF16) for i in range(2)]
        qT = [sba("qT%d" % i, [128, NQ], BF16) for i in range(2)]
        eb = [[sba("eb%d_%d" % (st, i), [128, 512], F32) for i in range(2)] for st in range(2)]
        spb = [[sba("spb%d_%d" % (st, i), [128, 512], BF16) for i in range(2)] for st in range(2)]
        gb = [[sba("gb%d_%d" % (st, i), [128, 512], F32) for i in range(2)] for st in range(2)]
        wb = [[sba("wb%d_%d" % (st, i), [128, 512], BF16) for i in range(2)] for st in range(2)]
        ob = [sba("ob%d" % i, [128, 512], BF16) for i in range(2)]
        vtb = [sba("vtb%d" % i, [128, 512], BF16) for i in range(2)]
        SBANKS = [(2, 4, 5), (3, 6, 7)]
        PAIRS = [[(1664, 512), (1152, 512)], [(640, 512), (128, 512)], [(0, 128)]]
        ocnt = 0
        def capture(fn):
            saved = P.ops
            P.ops = []
            fn()
            out = P.ops
            P.ops = saved
            return out

        def proj_chunks(h):
            ws = h % 2
            chunks = []
            hs_box = [0]

            def part_k(t):
                if t == 0:
                    dma('pool', wq[ws][:].rearrange("p k c -> p (k c)"), WA[h, :, :], (), [('wq', ws)], ('wq', ws),
                        cast=True)
                hs_box[0] = load_h(t * 512)
                hs = hs_box[0]
                b = pbank()
                for k in range(16):
                    mm(ps[b][:, :], wq[ws][:, k, 128:256], hb[hs][:, k, :], k == 0, k == 15,
                       [('wq', ws), ('hb', hs)], [('ps', b)])
                cp('dve', kT[ws][:, t * 512:(t + 1) * 512], ps[b][:, :], [('ps', b)], [('kT', ws, t)])

            def part_v(t):
                hs = hs_box[0]
                b = pbank()
                for k in range(16):
                    mm(ps[b][:, :], wq[ws][:, k, 256:384], hb[hs][:, k, :], k == 0, k == 15,
                       [('wq', ws), ('hb', hs)], [('ps', b)])
                vs_ = t % 2
                cp('dve', vtb[vs_][:, :], ps[b][:, :], [('ps', b)], [('vtb', vs_)])
                b = pbank()
                pvt = ps[b][:, 0:256].bitcast(BF16)
                for s in range(4):
                    tr(pvt[:, s * 128:(s + 1) * 128], vtb[vs_][:, s * 128:(s + 1) * 128], cb[:, IDN, :],
                       [('vtb', vs_), 'cb'], [('ps', b)])
                cp('dve', vv[ws][:, t * 4:(t + 1) * 4, :].rearrange("p a b -> p (a b)"), pvt,
                   [('ps', b)], [('vv', ws, t)])

            def part_q(t):
                hs = hs_box[0]
                c0 = 384 if t == 11 else 0
                n = 512 - c0
                qc0 = t * 512 + c0 - Q0L
                b = pbank()
                for k in range(16):
                    mm(ps[b][:, 0:n], wq[ws][:, k, 0:128], hb[hs][:, k, c0:512], k == 0, k == 15,
                       [('wq', ws), ('hb', hs)], [('ps', b)])
                cp('dve', qT[ws][:, qc0:qc0 + n], ps[b][:, 0:n], [('ps', b)], [('qT', ws)])

            for t in range(NT):
                chunks.append(capture(lambda: part_k(t)))
                chunks.append(capture(lambda: part_v(t)))
                if t >= 11:
                    chunks.append(capture(lambda: part_q(t)))
            return chunks

        for ch in proj_chunks(0):
            P.ops.extend(ch)
        for h in range(16):
            ws = h % 2
            nxt = proj_chunks(h + 1) if h + 1 < 16 else []
            tot_rounds = sum(max((Q0L + q0 + nq) // 128 for (q0, nq) in pair) for pair in PAIRS)
            every = max(1, tot_rounds // (len(nxt) + 1)) if nxt else 0
            rounds_done = 0
            if os.environ.get("KDEBUG") and h == 0:
                dk_ = nc.dram_tensor("dbg_k", [128, WIN], BF16, kind="ExternalOutput").ap()
                dv_ = nc.dram_tensor("dbg_v", [128, WIN], BF16, kind="ExternalOutput").ap()
                dq_ = nc.dram_tensor("dbg_q", [128, NQ], BF16, kind="ExternalOutput").ap()
                dma('sp', dk_[:, :], kT[0][:, :], [('kT', 0, t) for t in range(16)], ['dbgk'], 'dbg')
                dma('sp', dv_[:, :], vv[0][:].rearrange("p a b -> p (a b)"), [('vv', 0, t) for t in range(16)], ['dbgv'], 'dbg')
                dma('sp', dq_[:, :], qT[0][:, :], [('qT', 0)], ['dbgq'], 'dbg')
                dw_ = nc.dram_tensor("dbg_w", [128, 6144], BF16, kind="ExternalOutput").ap()
                dma('sp', dw_[:, :], wq[0][:].rearrange("p k c -> p (k c)"), [('wq', 0)], ['dbgw'], 'dbg')
            for pair in PAIRS:
                sts = []
                for si, (q0, nq) in enumerate(pair):
                    gq0 = Q0L + q0
                    sts.append(dict(si=si, q0=q0, nq=nq, gq0=gq0, banks=SBANKS[si],
                                    blocks=list(range((gq0 + nq) // 128 - 1, -1, -1))))

                def geom(st, kb):
                    m = kb - st['gq0'] // 128
                    return m >= 0, 128 * max(m, 0)

                def zmm(st, i):
                    kb = st['blocks'][i]
                    _, c0 = geom(st, kb)
                    zb = st['banks'][0]
                    q0, nq = st['q0'], st['nq']
                    mm(ps[zb][:, c0:nq], kT[ws][:, kb * 128:(kb + 1) * 128], qT[ws][:, q0 + c0:q0 + nq], True, True,
                       [('kT', ws, kb // 4), ('qT', ws)], [('ps', zb)])
                for st in sts:
                    zmm(st, 0)
                for i in range(max(len(st['blocks']) for st in sts)):
                    live = [st for st in sts if i < len(st['blocks'])]
                    s_ = i % 2
                    for st in live:
                        si, nq = st['si'], st['nq']
                        zb = st['banks'][0]
                        diag, c0 = geom(st, st['blocks'][i])
                        act(eb[si][s_][:, c0:nq], ps[zb][:, c0:nq], AF.Exp, [('ps', zb)], [('eb', si, s_)], scale=SCALE)
                        if diag:
                            tt('dve', eb[si][s_][:, c0:c0 + 128], eb[si][s_][:, c0:c0 + 128], cf[:, LT, :], ALU.mult,
                               [('eb', si, s_), 'cf'], [('eb', si, s_)])
                        act(spb[si][s_][:, c0:nq], eb[si][s_][:, c0:nq], AF.Ln, [('eb', si, s_)], [('spb', si, s_)], bias=1.0)
                    for st in live:
                        if i + 1 < len(st['blocks']):
                            zmm(st, i + 1)
                    for st in live:
                        si, nq = st['si'], st['nq']
                        xb = st['banks'][1]
                        _, c0 = geom(st, st['blocks'][i])
                        mm(ps[xb][:, c0:nq], cb[:, GE, :], spb[si][s_][:, c0:nq], i == 0, False,
                           [('spb', si, s_), 'cb'], [('ps', xb)])
                    for st in live:
                        si, nq = st['si'], st['nq']
                        xb = st['banks'][1]
                        _, c0 = geom(st, st['blocks'][i])
                        act(gb[si][s_][:, c0:nq], ps[xb][:, c0:nq], AF.Exp, [('ps', xb)], [('gb', si, s_)], scale=-1.0)
                    for st in live:
                        si, nq = st['si'], st['nq']
                        _, c0 = geom(st, st['blocks'][i])
                        tt('dve', wb[si][s_][:, c0:nq], eb[si][s_][:, c0:nq], gb[si][s_][:, c0:nq], ALU.mult,
                           [('eb', si, s_), ('gb', si, s_)], [('wb', si, s_)])
                    for st in live:
                        si, nq = st['si'], st['nq']
                        xb, obk = st['banks'][1], st['banks'][2]
                        kb = st['blocks'][i]
                        _, c0 = geom(st, kb)
                        last = i == len(st['blocks']) - 1
                        mm(ps[obk][:, c0:nq], vv[ws][:, kb, :], wb[si][s_][:, c0:nq], i == 0, last,
                           [('wb', si, s_), ('vv', ws, kb // 4)], [('ps', obk)])
                        mm(ps[xb][:, c0:nq], cb[:, LT, :], spb[si][s_][:, c0:nq], False, last,
                           [('spb', si, s_), 'cb'], [('ps', xb)])
                    rounds_done += 1
                    if rounds_done % 4 == 2:
                        conv_job()
                    if nxt and len(pair) == 1:
                        P.ops.extend(nxt.pop(0))
                for st in sts:
                    q0, nq, obk = st['q0'], st['nq'], st['banks'][2]
                    os_ = ocnt % 2
                    ocnt += 1
                    cp('dve', ob[os_][:, 0:nq], ps[obk][:, 0:nq], [('ps', obk)], [('ob', os_)])
                    dma('sp', osb_d[h * 128:(h + 1) * 128, q0:q0 + nq], ob[os_][:, 0:nq], [('ob', os_)], ['osb_d'],
                        ('obst', os_))
            while nxt:
                P.ops.extend(nxt.pop(0))

    P.barrier()
    with ExitStack() as sbx:
        def sbb(name, shape, dt=F32):
            return sbx.enter_context(nc.sbuf_tensor(name, list(shape), dt))
        wB = sbb("wB", [128, 16, 1288], BF16)
        pre = [sbb("pre%d" % i, [128, 6, 515]) for i in range(2)]
        dtT = sbb("dtT", [8, 512])
        acc = sbb("acc", [128, 6, 512])
        xa = sbb("xa", [128, 4, 512])
        BT = sbb("BT", [128, 512], BF16)
        CT = sbb("CT", [128, 512], BF16)
        nrm = sbb("nrm", [128, 512])
        S = sbb("S", [128, 512])
        Sb = sbb("Sb", [128, 512], BF16)
        dts = [sbb("dts%d" % i, [128, 128]) for i in range(2)]
        Eb = [sbb("Eb%d" % i, [128, 96]) for i in range(2)]
        rseg = [sbb("rseg%d" % i, [128, 1024]) for i in range(2)]
        dec = [sbb("dec%d" % i, [128, 1024]) for i in range(2)]
        Gb = [sbb("Gb%d" % i, [128, 1024], BF16) for i in range(2)]
        CBm = [sbb("CBm%d" % i, [128, 128]) for i in range(2)]
        xp = [sbb("xp%d" % i, [128, 512], BF16) for i in range(2)]
        xpp = [sbb("xpp%d" % i, [128, 512], BF16) for i in range(2)]
        xd = [sbb("xd%d" % i, [128, 512]) for i in range(2)]
        Btok = [sbb("Btok%d" % i, [128, 128], BF16) for i in range(2)]
        zs = [sbb("zs%d" % i, [128, 512]) for i in range(2)]
        y1 = [sbb("y1%d" % i, [128, 512]) for i in range(2)]
        y2 = [sbb("y2%d" % i, [128, 512]) for i in range(2)]
        junk = sbb("junk", [128, 512])
        st1 = [sbb("st1%d" % i, [128, 2]) for i in range(2)]
        yn = [sbb("yn%d" % i, [128, 512], BF16) for i in range(2)]
        yTs = [sbb("yTs%d" % i, [128, 4, 128], BF16) for i in range(2)]

        def v3(ap, h):
            return ap.rearrange("p (h l) -> p h l", h=h)

        def captureB(fn):
            saved = P.ops
            P.ops = []
            fn()
            out = P.ops
            P.ops = saved
            return out

        for g in range(8):
            dma('pool', wB[:].rearrange("p k c -> p (k c)"), WB[g, :, :], (), ['wB'], 'wB', cast=True)
            dma('sp', nrm[:], ssdn_d[:, g * 512:(g + 1) * 512], (), ['nrm'], 'nrm')
            mset('dve', S[:], 0.0, ['S'])
            mset('dve', Sb[:], 0.0, ['Sb'])
            hs_of = {}

            def proj_parts(t):
                parts = []
                pr = pre[t % 2]
                kp = ('pre', t % 2)

                def first():
                    hs_of[t] = load_h(t * 512)
                    if t == 0:
                        mset('dve', pr[:, :, 0:3], 0.0, [kp])
                    else:
                        cp('dve', pr[:, :, 0:3], pre[(t - 1) % 2][:, :, 512:515], [('pre', (t - 1) % 2)], [kp])

                def chunk(c):
                    hs = hs_of[t]
                    b = pbank()
                    w0 = c * 128 if c < 4 else 512 + (c - 4) * 128
                    for k in range(16):
                        mm(ps[b][:, :], wB[:, k, w0:w0 + 128], hb[hs][:, k, :], k == 0, k == 15,
                           ['wB', ('hb', hs)], [('ps', b)])
                    cp('act', pr[:, c, 3:515], ps[b][:, :], [('ps', b)], [kp])

                def dtpart():
                    hs = hs_of[t]
                    b = pbank()
                    for k in range(16):
                        mm(ps[b][0:8, :], wB[:, k, 1280:1288], hb[hs][:, k, :], k == 0, k == 15,
                           ['wB', ('hb', hs)], [('ps', b)])
                    cp('act', dtT[:, :], ps[b][0:8, :], [('ps', b)], ['dtT'])
                parts.append(captureB(lambda: (first(), chunk(0))))
                for c in range(1, 6):
                    parts.append(captureB(lambda: chunk(c)))
                parts.append(captureB(dtpart))
                return parts

            for part in proj_parts(0):
                P.ops.extend(part)
            for t in range(NT):
                hs = hs_of[t]
                pr = pre[t % 2]
                kp = ('pre', t % 2)
                nxt = proj_parts(t + 1) if t + 1 < NT else []
                tq = t % 2
                D_ = dts[tq]
                kD = ('dts', tq)
                for s in range(4):
                    mm(ps[3][:, s * 8:(s + 1) * 8], dtT[:, s * 128:(s + 1) * 128], cf[0:8, IDN, 0:8], True, True,
                       ['dtT', 'cf'], [('ps', 3)])
                tmp4 = D_[:, 64:96].rearrange("p (s h) -> p s h", s=4)
                dt4 = D_[:, 0:32].rearrange("p (s h) -> p s h", s=4)
                la4 = D_[:, 32:64].rearrange("p (s h) -> p s h", s=4)
                tt('dve', tmp4, ps[3][:, 0:32].rearrange("p (s h) -> p s h", s=4),
                   hvec[:, 0, g * 8:(g + 1) * 8].unsqueeze(1).to_broadcast([128, 4, 8]), ALU.add,
                   [('ps', 3), 'hvec'], [kD])
                act(D_[:, 64:96], D_[:, 64:96], AF.Exp, [kD], [kD])
                act(D_[:, 64:96], D_[:, 64:96], AF.Ln, [kD], [kD], bias=1.0)
                tt('dve', dt4, tmp4, valid[:, t * 4:(t + 1) * 4].unsqueeze(2).to_broadcast([128, 4, 8]), ALU.mult,
                   [kD, 'valid'], [kD])
                tt('dve', la4, dt4, aneg[:, g * 8:(g + 1) * 8].unsqueeze(1).to_broadcast([128, 4, 8]), ALU.mult,
                   [kD, 'aneg'], [kD])
                mm(ps[3][:, 32:64], cf[:, LE, :], D_[:, 32:64], True, True, [kD, 'cf'], [('ps', 3)])
                mm(ps[3][:, 64:96], cf[:, GT, :], D_[:, 32:64], True, True, [kD, 'cf'], [('ps', 3)])
                mm(ps[3][:, 96:128], cf[:, ONES, :], D_[:, 32:64], True, True, [kD, 'cf'], [('ps', 3)])
                act(Eb[tq][:], ps[3][:, 32:128], AF.Exp, [('ps', 3)], [('Eb', tq)])
                tt('dve', D_[:, 96:128], D_[:, 0:32], Eb[tq][:, 32:64], ALU.mult, [kD, ('Eb', tq)], [kD])
                for c in range(6):
                    ch = g * 4 + c if c < 4 else (32 + g if c == 4 else 40 + g)
                    eng = 'dve'
                    ts(eng, acc[:, c, :], pr[:, c, 0:512], sconvw[:, ch, 0:1], sconvb[:, ch:ch + 1], ALU.mult, ALU.add,
                       [kp, 'sconv'], [('acc', c)])
                    for kk in range(1, 4):
                        stt(eng, acc[:, c, :], pr[:, c, kk:kk + 512], sconvw[:, ch, kk:kk + 1], acc[:, c, :],
                            ALU.mult, ALU.add, [kp, 'sconv', ('acc', c)], [('acc', c)])
                    if c < 4:
                        act(xa[:, c, :], acc[:, c, :], AF.Silu, [('acc', c)], ['xa'])
                    elif c == 4:
                        act(BT[:], acc[:, c, :], AF.Silu, [('acc', c)], ['BT'])
                    else:
                        act(CT[:], acc[:, c, :], AF.Silu, [('acc', c)], ['CT'])
                for s in range(4):
                    ci = t * 4 + s
                    outc = ci >= 47
                    q = ci % 2
                    cs = slice(s * 128, (s + 1) * 128)
                    hsl = slice(s * 8, (s + 1) * 8)
                    dt_, la_, dtE_ = D_[:, 0:32][:, hsl], D_[:, 32:64][:, hsl], D_[:, 96:128][:, hsl]
                    Ecs, Etot = Eb[tq][:, 0:32][:, hsl], Eb[tq][:, 64:96][:, hsl]
                    if outc:
                        tt('dve', v3(rseg[q][:], 8), cf[:, LE, :].unsqueeze(1).to_broadcast([128, 8, 128]),
                           la_.unsqueeze(2).to_broadcast([128, 8, 128]), ALU.mult, [kD, 'cf'], [('rseg', q)])
                        for hh in range(2):
                            mm(ps[4 + hh][:, :], cf[:, GT, :], rseg[q][:, hh * 512:(hh + 1) * 512], True, True,
                               [('rseg', q), 'cf'], [('ps', 4 + hh)])
                            act(dec[q][:, hh * 512:(hh + 1) * 512], ps[4 + hh][:, :], AF.Exp, [('ps', 4 + hh)], [('dec', q)])
                    for c in range(4):
                        tr(ps[6][:, c * 128:(c + 1) * 128], xa[:, c, cs], cf[:, IDN, :], ['xa', 'cf'], [('ps', 6)])
                    x3 = v3(ps[6][:, :], 8)
                    tt('dve', v3(xp[q][:], 8), x3, dt_.unsqueeze(2).to_broadcast([128, 8, 64]), ALU.mult,
                       [('ps', 6), kD], [('xp', q)])
                    tt('dve', v3(xpp[q][:], 8), x3, dtE_.unsqueeze(2).to_broadcast([128, 8, 64]), ALU.mult,
                       [('ps', 6), kD], [('xpp', q)])
                    if outc:
                        tt('dve', v3(xd[q][:], 8), x3,
                           hvec[:, 2, g * 8:(g + 1) * 8].unsqueeze(2).to_broadcast([128, 8, 64]), ALU.mult,
                           [('ps', 6), 'hvec'], [('xd', q)])
                    pbb = ps[7][:, 0:64].bitcast(BF16)
                    tr(pbb, BT[:, cs], cb[:, IDN, :], ['BT', 'cb'], [('ps', 7)])
                    cp('act', Btok[q][:], pbb, [('ps', 7)], [('Btok', q)])
                    if outc:
                        col0 = ci * 128 - Q0L
                        mm(ps[7][:, 128:256], BT[:, cs], CT[:, cs], True, True, ['BT', 'CT'], [('ps', 7)])
                        tt('dve', CBm[q][:], ps[7][:, 128:256], cf[:, LE, :], ALU.mult, [('ps', 7), 'cf'], [('CBm', q)])
                        tt('pool', v3(Gb[q][:], 8), v3(dec[q][:], 8), CBm[q][:].unsqueeze(1).to_broadcast([128, 8, 128]),
                           ALU.mult, [('dec', q), ('CBm', q)], [('Gb', q)])
                        for hh in range(8):
                            mm(ps[0][:, hh * 64:(hh + 1) * 64], Gb[q][:, hh * 128:(hh + 1) * 128], xp[q][:, hh * 64:(hh + 1) * 64],
                               True, True, [('Gb', q), ('xp', q)], [('ps', 0)])
                        mm(ps[1][:, :], CT[:, cs], Sb[:], True, True, ['CT', 'Sb'], [('ps', 1)])
                        tt('dve', v3(y1[q][:], 8), v3(ps[1][:, :], 8), Ecs.unsqueeze(2).to_broadcast([128, 8, 64]), ALU.mult,
                           [('ps', 1), ('Eb', tq)], [('y1', q)])
                        tt('dve', y1[q][:], y1[q][:], ps[0][:, :], ALU.add, [('y1', q), ('ps', 0)], [('y1', q)])
                        tt('pool', y1[q][:], y1[q][:], xd[q][:], ALU.add, [('y1', q), ('xd', q)], [('y1', q)])
                        for k in range(16):
                            mm(ps[2][:, :], hb[hs][:, k, cs], wB[:, k, 768:1280], k == 0, k == 15,
                               ['wB', ('hb', hs)], [('ps', 2)])
                        act(zs[q][:], ps[2][:, :], AF.Silu, [('ps', 2)], [('zs', q)])
                        tt('pool', y2[q][:], y1[q][:], zs[q][:], ALU.mult, [('y1', q), ('zs', q)], [('y2', q)])
                        act(junk[:], y2[q][:], AF.Square, [('y2', q)], ['junk', ('st1', q)], accum=st1[q][:, 0:1])
                        act(st1[q][:, 1:2], st1[q][:, 0:1], AF.Sqrt, [('st1', q)], [('st1', q)], scale=1.0 / 512, bias=EPS)
                        recip(st1[q][:, 1:2], st1[q][:, 1:2], [('st1', q)], [('st1', q)])
                        stt('dve', yn[q][:], y2[q][:], st1[q][:, 1:2], nrm[:], ALU.mult, ALU.mult,
                            [('y2', q), ('st1', q), 'nrm'], [('yn', q)])
                        pyt = ps[7][:, 256:512].bitcast(BF16)
                        for c in range(4):
                            tr(pyt[:, c * 128:(c + 1) * 128], yn[q][:, c * 128:(c + 1) * 128], cb[:, IDN, :],
                               [('yn', q), 'cb'], [('ps', 7)])
                        cp('act', yTs[q][:].rearrange("p a b -> p (a b)"), pyt, [('ps', 7)], [('yTs', q)])
                        dma('sp', y_v[:, g * 4:(g + 1) * 4, col0:col0 + 128], yTs[q][:], [('yTs', q)], ['y_d'], ('yst', q))
                    mm(ps[2][:, :], Btok[q][:], xpp[q][:], True, True, [('Btok', q), ('xpp', q)], [('ps', 2)])
                    tt('dve', v3(S[:], 8), v3(S[:], 8), Etot.unsqueeze(2).to_broadcast([128, 8, 64]), ALU.mult,
                       ['S', ('Eb', tq)], ['S'])
                    tt('dve', S[:], S[:], ps[2][:, :], ALU.add, ['S', ('ps', 2)], ['S'])
                    cp('pool', Sb[:], S[:], ['S'], ['Sb'])
                    take = 2 if s < 3 else len(nxt)
                    for _ in range(min(take, len(nxt))):
                        P.ops.extend(nxt.pop(0))

    P.barrier()
    with ExitStack() as sc:
        def sbc(name, shape, dt=F32):
            return sc.enter_context(nc.sbuf_tensor(name, list(shape), dt))
        R1 = sbc("R1", [128, 48, 512], BF16)
        R4 = sbc("R4", [128, 8192])
        R5 = sbc("R5", [128, 16, 512])
        wsl = wsl_g + [sbc("wsl2", [128, 48 * 128], BF16)]
        sqb = [sbc("sqb%d" % i, [128, 512], BF16) for i in range(2)]
        rsb = sbc("rsb", [128, 512])
        tmpc = [sbc("tmpc%d" % i, [128, 512]) for i in range(2)]
        ug = [sbc("ug%d" % i, [128, 514]) for i in range(1)]
        uv = [sbc("uv%d" % i, [128, 514]) for i in range(1)]
        carry = sbc("carry", [128, 88, 2])
        merged = R4[:, 0:4096].bitcast(BF16).rearrange("p (k t) -> p k t", k=16)
        xt4 = R4[:].rearrange("p (k t) -> p k t", k=16)
        h2 = hb[0]
        wcnt = [0]

        def wslot():
            wcnt[0] += 1
            return wcnt[0] % 3

        def wload(slot, name, idx, ncols, first):
            if first:
                dma('pool', wsl[slot][:, 0:ncols], SRC[name][idx, :, :], (), [('wsl', slot)], ('wsl', slot), cast=True)
                dma('sp', SCR[name][idx, :, :], wsl[slot][:, 0:ncols], [('wsl', slot)], [('scr', name, idx)], 'wscst')
            else:
                dma('sp', wsl[slot][:, 0:ncols], SCR[name][idx, :, :], [('scr', name, idx)], [('wsl', slot)], ('wsl', slot))

        mset('dve', carry[:], 0.0, ['carry'])
        groups = [(0, 128)] + [(128 + 512 * i, 512) for i in range(4)]
        for gi, (c0g, n) in enumerate(groups):
            l0 = Q0L + c0g
            dma('sp', R1[:, 0:16, 0:n], osb_v[:, :, c0g:c0g + n], ['osb_d'], [('R1', k) for k in range(16)], 'ldo')
            dma('sp', R1[:, 16:48, 0:n], y_v[:, :, c0g:c0g + n], ['y_d'], [('R1', k) for k in range(16, 48)], 'ldy')
            dma('sp', hb[1][:, :, 0:n], hT_v[:, :, l0:l0 + n], [('hTd', l0 // 512)], [('hb', 1)], 'ldh')
            for c in range(16):
                s1, s2 = wslot(), wslot()
                wload(s1, "W2A", c, 6144, False)
                wload(s2, "W2B", c, 4096, False)
                for k in range(16):
                    mm(ps[0][:, 0:n], wsl[s1][:, k * 128:(k + 1) * 128], R1[:, k, 0:n], k == 0, k == 15,
                       [('wsl', s1), ('R1', k)], [('ps', 0)])
                for k in range(32):
                    mm(ps[1][:, 0:n], wsl[s2][:, k * 128:(k + 1) * 128], R1[:, 16 + k, 0:n], k == 0, k == 31,
                       [('wsl', s2), ('R1', 16 + k)], [('ps', 1)])
                for k in range(16):
                    mm(ps[2][:, 0:n], wsl[s1][:, (16 + k) * 128:(17 + k) * 128], hb[1][:, k, 0:n], k == 0, k == 15,
                       [('wsl', s1), ('hb', 1)], [('ps', 2)])
                for k in range(16):
                    mm(ps[3][:, 0:n], wsl[s1][:, (32 + k) * 128:(33 + k) * 128], hb[1][:, k, 0:n], k == 0, k == 15,
                       [('wsl', s1), ('hb', 1)], [('ps', 3)])
                a_, b_ = R5[:, 14, :], R5[:, 15, :]
                ka, kb_ = ('R5', 14), ('R5', 15)
                act(a_[:, 0:n], ps[2][:, 0:n], AF.Sigmoid, [('ps', 2)], [ka])
                act(b_[:, 0:n], ps[3][:, 0:n], AF.Sigmoid, [('ps', 3)], [kb_])
                tt('dve', a_[:, 0:n], a_[:, 0:n], ps[0][:, 0:n], ALU.mult, [ka, ('ps', 0)], [ka])
                tt('dve', b_[:, 0:n], b_[:, 0:n], ps[1][:, 0:n], ALU.mult, [kb_, ('ps', 1)], [kb_])
                tt('pool', merged[:, c, 0:n], a_[:, 0:n], b_[:, 0:n], ALU.add, [ka, kb_], [('R4', c // 2)])
            for c in range(16):
                s1 = wslot()
                wload(s1, "WO", c, 2048, False)
                b = pbank()
                for k in range(16):
                    mm(ps[b][:, 0:n], wsl[s1][:, k * 128:(k + 1) * 128], merged[:, k, 0:n], k == 0, k == 15,
                       [('wsl', s1), ('R4', k // 2)], [('ps', b)])
                cp('act', R5[:, c, 0:n], ps[b][:, 0:n], [('ps', b)], [('R5', c)])
                act(sqb[c % 2][:, 0:n], ps[b][:, 0:n], AF.Square, [('ps', b)], [('sqb', c % 2)])
                mm(ps[4][:, 0:n], cb[:, ONES, :], sqb[c % 2][:, 0:n], c == 0, c == 15, [('sqb', c % 2), 'cb'], [('ps', 4)])
            act(rsb[:, 0:n], ps[4][:, 0:n], AF.Sqrt, [('ps', 4)], ['rsb'], scale=1.0 / D, bias=EPS)
            recip(rsb[:, 0:n], rsb[:, 0:n], ['rsb'], ['rsb'])
            dma('sp', xt4[:, :, 0:n], xw_v[:, :, l0:l0 + n], (), [('R4', k) for k in range(16)], 'ldx')
            for c in range(16):
                stt('dve', tmpc[c % 2][:, 0:n], R5[:, c, 0:n], gains[:, 1, c:c + 1], rsb[:, 0:n], ALU.mult, ALU.mult,
                    [('R5', c), 'rsb', 'gains'], [('tmpc', c % 2)])
                tt('pool', xt4[:, c, 0:n], xt4[:, c, 0:n], tmpc[c % 2][:, 0:n], ALU.add, [('R4', c), ('tmpc', c % 2)],
                   [('R4', c)])
            for c in range(16):
                act(sqb[c % 2][:, 0:n], xt4[:, c, 0:n], AF.Square, [('R4', c)], [('sqb', c % 2)])
                mm(ps[4][:, 0:n], cb[:, ONES, :], sqb[c % 2][:, 0:n], c == 0, c == 15, [('sqb', c % 2), 'cb'], [('ps', 4)])
            act(rsb[:, 0:n], ps[4][:, 0:n], AF.Sqrt, [('ps', 4)], ['rsb'], scale=1.0 / D, bias=EPS)
            recip(rsb[:, 0:n], rsb[:, 0:n], ['rsb'], ['rsb'])
            for c in range(16):
                stt('dve', h2[:, c, 0:n], xt4[:, c, 0:n], gains[:, 2, c:c + 1], rsb[:, 0:n], ALU.mult, ALU.mult,
                    [('R4', c), 'rsb', 'gains'], [('hb', 0)])
            for fc in range(NFC):
                s1 = wslot()
                wload(s1, "WU", fc, 4096, False)
                q = 0
                for half, (pb_, ub, kq) in enumerate(((0, ug[q], ('ug', q)), (1, uv[q], ('uv', q)))):
                    for k in range(16):
                        mm(ps[pb_][:, 0:n], wsl[s1][:, k * 256 + half * 128:k * 256 + half * 128 + 128], h2[:, k, 0:n],
                           k == 0, k == 15, [('wsl', s1), ('hb', 0)], [('ps', pb_)])
                    cch = fc + half * NFC
                    cp('dve', ub[:, 0:2], carry[:, cch, :], ['carry'], [kq])
                    cp('act', ub[:, 2:2 + n], ps[pb_][:, 0:n], [('ps', pb_)], [kq])
                    if gi == 0:
                        ts1('dve', carry[:, cch, :], ub[:, n:n + 2], hval[:, 0:1], ALU.mult, [kq, 'hval'], ['carry'])
                    else:
                        cp('dve', carry[:, cch, :], ub[:, n:n + 2], [kq], ['carry'])
                    if gi > 0:
                        dst, kd = (R5[:, 0, :], ('R5', 0)) if half == 0 else (R5[:, 1, :], ('R5', 1))
                        eng = 'dve'
                        ts(eng, dst[:, 0:n], ub[:, 0:n], fconvw[:, cch, 0:1], fconvb[:, cch:cch + 1], ALU.mult, ALU.add,
                           [kq, 'fconv'], [kd])
                        stt(eng, dst[:, 0:n], ub[:, 1:n + 1], fconvw[:, cch, 1:2], dst[:, 0:n], ALU.mult, ALU.add,
                            [kq, 'fconv', kd], [kd])
                        stt(eng, dst[:, 0:n], ub[:, 2:n + 2], fconvw[:, cch, 2:3], dst[:, 0:n], ALU.mult, ALU.add,
                            [kq, 'fconv', kd], [kd])
                if gi > 0:
                    G_, V_, T_ = R5[:, 0, :], R5[:, 1, :], R5[:, 2 + fc % 2, :]
                    kT_ = ('R5', 2 + fc % 2)
                    tt('pool', T_[:, 0:n], G_[:, 0:n], G_[:, 0:n], ALU.mult, [('R5', 0)], [kT_])
                    ts('dve', T_[:, 0:n], T_[:, 0:n], 0.044715, 1.0, ALU.mult, ALU.add, [kT_], [kT_])
                    tt('pool', T_[:, 0:n], T_[:, 0:n], G_[:, 0:n], ALU.mult, [kT_, ('R5', 0)], [kT_])
                    act(T_[:, 0:n], T_[:, 0:n], AF.Sigmoid, [kT_], [kT_], scale=1.5957691216057308)
                    tt('dve', T_[:, 0:n], T_[:, 0:n], G_[:, 0:n], ALU.mult, [kT_, ('R5', 0)], [kT_])
                    tt('pool', R1[:, fc, 0:n], T_[:, 0:n], V_[:, 0:n], ALU.mult, [kT_, ('R5', 1)], [('R1', fc)])
            if gi == 0:
                continue
            for c in range(16):
                s1 = wslot()
                wload(s1, "WD", c, NFC * 128, False)
                b = pbank()
                for k in range(NFC):
                    mm(ps[b][:, 0:n], wsl[s1][:, k * 128:(k + 1) * 128], R1[:, k, 0:n], k == 0, k == NFC - 1,
                       [('wsl', s1), ('R1', k)], [('ps', b)])
                cp('act', R5[:, c, 0:n], ps[b][:, 0:n], [('ps', b)], [('R5', c)])
                act(sqb[c % 2][:, 0:n], ps[b][:, 0:n], AF.Square, [('ps', b)], [('sqb', c % 2)])
                mm(ps[4][:, 0:n], cb[:, ONES, :], sqb[c % 2][:, 0:n], c == 0, c == 15, [('sqb', c % 2), 'cb'], [('ps', 4)])
            act(rsb[:, 0:n], ps[4][:, 0:n], AF.Sqrt, [('ps', 4)], ['rsb'], scale=1.0 / D, bias=EPS)
            recip(rsb[:, 0:n], rsb[:, 0:n], ['rsb'], ['rsb'])
            for c in range(16):
                stt('dve', tmpc[c % 2][:, 0:n], R5[:, c, 0:n], gains[:, 3, c:c + 1], rsb[:, 0:n], ALU.mult, ALU.mult,
                    [('R5', c), 'rsb', 'gains'], [('tmpc', c % 2)])
                tt('pool', R5[:, c, 0:n], xt4[:, c, 0:n], tmpc[c % 2][:, 0:n], ALU.add, [('R4', c), ('tmpc', c % 2)],
                   [('R5', c)])
            t0 = c0g - 128
            dma('sp', out_v[:, :, t0:t0 + n], R5[:, :, 0:n], [('R5', c) for c in range(16)], ['out'], 'stout')

        P.emit(nc, es)
    es.close()
    return nc


_NC = None


def _prep_weights(w_in, w_sb_proj, w_ssd_proj, w_out, w_up, w_down):
    def blk(w, cols):
        K = w.shape[0]
        sub = w[:, cols].reshape(K // 128, 128, len(cols))
        return np.ascontiguousarray(sub.transpose(1, 0, 2)).reshape(128, -1)
    ar = np.arange
    WA = np.stack([blk(w_in, np.concatenate([ar(h * 128, h * 128 + 128), ar(2048 + h * 128, 2048 + h * 128 + 128),
                                             ar(4096 + h * 128, 4096 + h * 128 + 128)])) for h in range(16)])
    WB = np.stack([blk(w_in, np.concatenate([ar(10240 + g * 512, 10240 + g * 512 + 512),
                                             ar(14336 + g * 128, 14336 + g * 128 + 128),
                                             ar(15360 + g * 128, 15360 + g * 128 + 128),
                                             ar(6144 + g * 512, 6144 + g * 512 + 512),
                                             ar(16384 + g * 8, 16384 + g * 8 + 8)])) for g in range(8)])
    W2A = np.stack([np.concatenate([blk(w_sb_proj, ar(c * 128, c * 128 + 128)),
                                    blk(w_in, ar(16448 + c * 128, 16448 + c * 128 + 128)),
                                    blk(w_in, ar(18496 + c * 128, 18496 + c * 128 + 128))], axis=1) for c in range(16)])
    W2B = np.stack([blk(w_ssd_proj, ar(c * 128, c * 128 + 128)) for c in range(16)])
    WO = np.stack([blk(w_out, ar(c * 128, c * 128 + 128)) for c in range(16)])
    WU = np.stack([blk(w_up, np.concatenate([ar(fc * 128, fc * 128 + 128), ar(D_FF + fc * 128, D_FF + fc * 128 + 128)]))
                   for fc in range(NFC)])
    WD = np.stack([blk(w_down, ar(c * 128, c * 128 + 128)) for c in range(16)])
    return dict(WA=WA, WB=WB, W2A=W2A, W2B=W2B, WO=WO, WU=WU, WD=WD)


def kernel(x, norm_mix_pre, w_in, ssd_conv_w, ssd_conv_b, dt_bias, a_log, d_skip, ssd_norm,
           w_sb_proj, w_ssd_proj, w_out, norm_mix_post, norm_ffn_pre, w_up, ffn_conv_w,
           ffn_conv_b, w_down, norm_ffn_post):
    global _NC
    in_maps = _in_maps(x, norm_mix_pre, w_in, ssd_conv_w, ssd_conv_b, dt_bias, a_log, d_skip, ssd_norm,
                       w_sb_proj, w_ssd_proj, w_out, norm_mix_post, norm_ffn_pre, w_up, ffn_conv_w,
                       ffn_conv_b, w_down, norm_ffn_post)
    if _NC is None:
        _NC = build_nc()
    res = run_bass_kernel_spmd(_NC, in_maps, core_ids=list(range(8)))
    out = np.empty((2, 8192, D), np.float32)
    for core in range(8):
        b, c = core // 4, core % 4
        out[b, 2048 * c:2048 * (c + 1), :] = res.results[core]["outT"].T
    return out


def _in_maps(x, norm_mix_pre, w_in, ssd_conv_w, ssd_conv_b, dt_bias, a_log, d_skip, ssd_norm,
             w_sb_proj, w_ssd_proj, w_out, norm_mix_post, norm_ffn_pre, w_up, ffn_conv_w,
             ffn_conv_b, w_down, norm_ffn_post):
    f32 = np.float32
    x = np.asarray(x, f32)
    shared = _prep_weights(np.asarray(w_in, f32)[0], np.asarray(w_sb_proj, f32)[0], np.asarray(w_ssd_proj, f32)[0],
                           np.asarray(w_out, f32)[0], np.asarray(w_up, f32)[0], np.asarray(w_down, f32)[0])
    r = np.arange(128)
    cm = np.stack([(r[:, None] < r[None, :]), (r[:, None] <= r[None, :]), (r[:, None] >= r[None, :]),
                   (r[:, None] > r[None, :]), np.ones((128, 128), bool), np.eye(128, dtype=bool)], axis=1).astype(f32)
    shared["consts"] = np.ascontiguousarray(cm.reshape(128, 768))

    def pk(v):
        return np.asarray(v, f32).reshape(-1, 128).T
    shared["gains"] = np.ascontiguousarray(np.stack([pk(norm_mix_pre[0]), pk(norm_mix_post[0]), pk(norm_ffn_pre[0]),
                                                     pk(norm_ffn_post[0])], axis=1).reshape(128, 64))
    scw = np.asarray(ssd_conv_w, f32)[0]
    shared["sconvw"] = np.ascontiguousarray(scw.reshape(4, 48, 128).transpose(2, 1, 0).reshape(128, 192))
    shared["sconvb"] = np.ascontiguousarray(pk(ssd_conv_b[0]))
    fcw = np.asarray(ffn_conv_w, f32)[0]
    shared["fconvw"] = np.ascontiguousarray(fcw.reshape(3, 88, 128).transpose(2, 1, 0).reshape(128, 264))
    shared["fconvb"] = np.ascontiguousarray(pk(ffn_conv_b[0]))
    hv = np.stack([np.asarray(dt_bias, f32)[0], np.asarray(a_log, f32)[0], np.asarray(d_skip, f32)[0]])
    shared["hvec"] = np.ascontiguousarray(np.broadcast_to(hv.reshape(1, 192), (128, 192)))
    shared["ssdn"] = np.ascontiguousarray(np.broadcast_to(np.asarray(ssd_norm, f32)[0][None, :], (128, 4096)))

    in_maps = []
    for core in range(8):
        b, c = core // 4, core % 4
        end = 2048 * (c + 1)
        start = end - WIN
        xwin = np.zeros((D, WIN), f32)
        lo = max(start, 0)
        xwin[:, lo - start:] = x[b, lo:end, :].T
        tok = start + np.arange(WIN)
        val = (tok >= 0).astype(f32).reshape(64, 128).T
        m = dict(shared)
        m["xw"] = xwin
        m["valid"] = np.ascontiguousarray(val)
        m["hval"] = np.full((128, 1), 1.0 if c > 0 else 0.0, f32)
        in_maps.append(m)
    return in_maps
```

```python
import bisect
import os
from contextlib import ExitStack

import numpy as np
import concourse.bass as bass
import concourse.mybir as mybir
from concourse.bass_utils import run_bass_kernel_spmd

F32 = mybir.dt.float32
BF16 = mybir.dt.bfloat16
AF = mybir.ActivationFunctionType
ALU = mybir.AluOpType

D = 2048
WIN = 8192
NT = 16
Q0L = 6016
NQ = 2176
EPS = 1e-6
SCALE = 128 ** -0.5
D_FF = 5632
NFC = 44
QBS = [(0, 128)] + [(128 + 512 * i, 512) for i in range(4)]


class Prog:
    def __init__(self):
        self.ops = []
        self.bars = []

    def barrier(self):
        self.bars.append(len(self.ops))

    def add(self, eng, fn, R=(), W=(), dk=None):
        self.ops.append((eng, fn, tuple(R), tuple(W), dk))

    def emit(self, nc, es):
        ops = self.ops
        n = len(ops)
        lastw, readers = {}, {}
        deps = []
        for i, (eng, fn, R, W, dk) in enumerate(ops):
            d = set()
            for r in R:
                j = lastw.get(r)
                if j is not None:
                    d.add(j)
            for w in W:
                j = lastw.get(w)
                if j is not None:
                    d.add(j)
                rs = readers.get(w)
                if rs:
                    d.update(rs)
            for r in R:
                readers.setdefault(r, []).append(i)
            for w in W:
                lastw[w] = i
                readers[w] = []
            d.discard(i)
            deps.append(d)
        sig = [False] * n
        cdeps = []
        dma_idx = {}
        for i, op in enumerate(ops):
            if op[4] is not None:
                dma_idx.setdefault(op[4], []).append(i)
        for i, d in enumerate(deps):
            eng = ops[i][0]
            best = {}
            dks = set()
            for j in d:
                oj = ops[j]
                if oj[4] is not None:
                    dks.add(oj[4])
                    continue
                if oj[0] == 'pe' and eng == 'pe':
                    continue
                if best.get(oj[0], -1) < j:
                    best[oj[0]] = j
            for j in best.values():
                sig[j] = True
            cdeps.append((best, dks))
        eng_ops = {}
        for i, op in enumerate(ops):
            if op[4] is None:
                eng_ops.setdefault(op[0], []).append(i)
        for p in self.bars:
            for E in ('pe', 'act', 'dve', 'pool', 'sp'):
                first = next((i for i in range(p, n) if ops[i][0] == E), None)
                if first is None:
                    continue
                best, dks = cdeps[first]
                for E2, lst in eng_ops.items():
                    if E2 == 'pe' and E == 'pe':
                        continue
                    pos = bisect.bisect_left(lst, p)
                    if pos > 0:
                        j = lst[pos - 1]
                        if best.get(E2, -1) < j:
                            best[E2] = j
                            sig[j] = True
                for k, lst in dma_idx.items():
                    if lst and lst[0] < p:
                        dks.add(k)
        seq = [0] * n
        cnt = {}
        for i, op in enumerate(ops):
            if op[4] is None and sig[i]:
                cnt[op[0]] = cnt.get(op[0], 0) + 1
                seq[i] = cnt[op[0]]
        esem = {e: es.enter_context(nc.semaphore("se_" + e)) for e in ('pe', 'act', 'dve', 'pool')}
        dsem = {}
        for k in dma_idx:
            dsem[k] = es.enter_context(nc.semaphore("sd_%d" % len(dsem)))
        block = es.enter_context(nc.Block())

        def run(ename):
            def body(e):
                waited = {}
                for i, (eng, fn, R, W, dk) in enumerate(ops):
                    if eng != ename:
                        continue
                    best, dks = cdeps[i]
                    for se, j in best.items():
                        key = ('e', se)
                        if waited.get(key, 0) < seq[j]:
                            e.wait_ge(esem[se], seq[j])
                            waited[key] = seq[j]
                    for k in dks:
                        val = 16 * bisect.bisect_left(dma_idx[k], i)
                        key = ('d', k)
                        if waited.get(key, 0) < val:
                            e.wait_ge(dsem[k], val)
                            waited[key] = val
                    ins = fn(e)
                    if dk is not None:
                        ins.then_inc(dsem[dk], 16)
                    elif sig[i]:
                        ins.then_inc(esem[eng], 1)
                if ename == 'sp':
                    for k, lst in dma_idx.items():
                        e.wait_ge(dsem[k], 16 * len(lst))
            return body

        block.tensor(run('pe'))
        block.scalar(run('act'))
        block.vector(run('dve'))
        block.gpsimd(run('pool'))
        block.sync(run('sp'))


def build_nc():
    nc = bass.Bass("TRN2", target_bir_lowering=False)
    P = Prog()

    def din(name, shape, dt=F32):
        return nc.dram_tensor(name, list(shape), dt, kind="ExternalInput").ap()

    xw = din("xw", [D, WIN])
    WA = din("WA", [16, 128, 16 * 384])
    WB = din("WB", [8, 128, 16 * 1288])
    W2A = din("W2A", [16, 128, 48 * 128])
    W2B = din("W2B", [16, 128, 32 * 128])
    WO = din("WO", [16, 128, 16 * 128])
    WU = din("WU", [NFC, 128, 16 * 256])
    WD = din("WD", [16, 128, NFC * 128])
    consts_d = din("consts", [128, 6 * 128])
    gains_d = din("gains", [128, 64])
    sconvw_d = din("sconvw", [128, 48 * 4])
    sconvb_d = din("sconvb", [128, 48])
    fconvw_d = din("fconvw", [128, 88 * 3])
    fconvb_d = din("fconvb", [128, 88])
    hvec_d = din("hvec", [128, 3 * 64])
    ssdn_d = din("ssdn", [128, 4096])
    valid_d = din("valid", [128, 64])
    hval_d = din("hval", [128, 1])
    outT = nc.dram_tensor("outT", [D, 2048], F32, kind="ExternalOutput").ap()
    skind = "ExternalOutput" if os.environ.get("KDEBUG") else "Internal"
    hT_d = nc.dram_tensor("hT_d", [D, WIN], BF16, kind=skind).ap()
    osb_d = nc.dram_tensor("osb_d", [D, NQ], BF16, kind=skind).ap()
    y_d = nc.dram_tensor("y_d", [4096, NQ], BF16, kind=skind).ap()
    SCR = {"W2A": nc.dram_tensor("W2A_s", [16, 128, 48 * 128], BF16).ap(),
           "W2B": nc.dram_tensor("W2B_s", [16, 128, 32 * 128], BF16).ap(),
           "WO": nc.dram_tensor("WO_s", [16, 128, 16 * 128], BF16).ap(),
           "WU": nc.dram_tensor("WU_s", [NFC, 128, 16 * 256], BF16).ap(),
           "WD": nc.dram_tensor("WD_s", [16, 128, NFC * 128], BF16).ap()}
    SRC = {"W2A": W2A, "W2B": W2B, "WO": WO, "WU": WU, "WD": WD}

    xw_v = xw.rearrange("(k p) t -> p k t", p=128)
    hT_v = hT_d.rearrange("(k p) t -> p k t", p=128)
    osb_v = osb_d.rearrange("(k p) t -> p k t", p=128)
    y_v = y_d.rearrange("(k p) t -> p k t", p=128)
    out_v = outT.rearrange("(k p) t -> p k t", p=128)

    es = ExitStack()

    def sb(name, shape, dt=F32):
        return es.enter_context(nc.sbuf_tensor("s_" + name, list(shape), dt))

    ps = [es.enter_context(nc.psum_tensor("ps%d" % i, [128, 512], F32)) for i in range(8)]

    def mm(out, lhsT, rhs, start, stop, R, W):
        P.add('pe', lambda e: e.matmul(out, lhsT, rhs, start=start, stop=stop), R, W)

    def tr(out, in_, ident, R, W):
        P.add('pe', lambda e: e.transpose(out, in_, ident), R, W)

    def act(out, in_, func, R, W, scale=1.0, bias=0.0, accum=None):
        if accum is None:
            P.add('act', lambda e: e.activation(out=out, in_=in_, func=func, bias=bias, scale=scale), R, W)
        else:
            P.add('act', lambda e: e.activation(out=out, in_=in_, func=func, bias=bias, scale=scale,
                                                accum_out=accum), R, W)

    def tt(eng, out, a, b, op, R, W):
        P.add(eng, lambda e: e.tensor_tensor(out=out, in0=a, in1=b, op=op), R, W)

    def ts(eng, out, a, s1, s2, op0, op1, R, W):
        P.add(eng, lambda e: e.tensor_scalar(out=out, in0=a, scalar1=s1, scalar2=s2, op0=op0, op1=op1), R, W)

    def ts1(eng, out, a, s1, op0, R, W):
        P.add(eng, lambda e: e.tensor_single_scalar(out=out, in_=a, scalar=s1, op=op0), R, W)

    def stt(eng, out, a, s, b, op0, op1, R, W):
        eng = 'dve'
        P.add(eng, lambda e: e.scalar_tensor_tensor(out=out, in0=a, scalar=s, in1=b, op0=op0, op1=op1), R, W)

    def cp(eng, out, a, R, W):
        if eng == 'act':
            act(out, a, AF.Copy, R, W)
        else:
            P.add(eng, lambda e: e.tensor_copy(out=out, in_=a), R, W)

    def recip(out, a, R, W):
        P.add('dve', lambda e: e.reciprocal(out=out, in_=a), R, W)

    def mset(eng, out, val, W):
        P.add(eng, lambda e: e.memset(out, val), (), W)

    def dma(q, out, in_, R, W, dk, cast=False):
        if cast:
            P.add(q, lambda e: e.dma_start(out=out, in_=in_, max_dma_last_dim=4096), R, W, dk)
        else:
            P.add(q, lambda e: e.dma_start(out=out, in_=in_), R, W, dk)

    cf = sb("cf", [128, 6, 128])
    cb = sb("cb", [128, 6, 128], BF16)
    gains = sb("gains", [128, 4, 16])
    sconvw = sb("sconvw", [128, 48, 4])
    sconvb = sb("sconvb", [128, 48])
    fconvw = sb("fconvw", [128, 88, 3])
    fconvb = sb("fconvb", [128, 88])
    hvec = sb("hvec", [128, 3, 64])
    aneg = sb("aneg", [128, 64])
    valid = sb("valid", [128, 64])
    hval = sb("hval", [128, 1])
    dma('sp', cf[:].rearrange("p a b -> p (a b)"), consts_d[:, :], (), ['cf'], 'c0')
    dma('sp', gains[:].rearrange("p a b -> p (a b)"), gains_d[:, :], (), ['gains'], 'c0')
    dma('sp', sconvw[:].rearrange("p a b -> p (a b)"), sconvw_d[:, :], (), ['sconv'], 'c0')
    dma('sp', sconvb[:], sconvb_d[:, :], (), ['sconv'], 'c0')
    dma('sp', fconvw[:].rearrange("p a b -> p (a b)"), fconvw_d[:, :], (), ['fconv'], 'c0')
    dma('sp', fconvb[:], fconvb_d[:, :], (), ['fconv'], 'c0')
    dma('sp', hvec[:].rearrange("p a b -> p (a b)"), hvec_d[:, :], (), ['hvec'], 'c0')
    dma('sp', valid[:], valid_d[:, :], (), ['valid'], 'c0')
    dma('sp', hval[:], hval_d[:, :], (), ['hval'], 'c0')
    cp('dve', cb[:], cf[:], ['cf'], ['cb'])
    act(aneg[:], hvec[:, 1, :], AF.Exp, ['hvec'], ['aneg'])
    ts1('dve', aneg[:], aneg[:], -1.0, ALU.mult, ['aneg'], ['aneg'])
    LT, LE, GE, GT, ONES, IDN = range(6)

    hb = [sb("hb%d" % i, [128, 16, 512], BF16) for i in range(2)]
    wsl_g = [sb("wslg%d" % i, [128, 48 * 128], BF16) for i in range(2)]
    hcnt = [0]

    def load_h(t0, n=512):
        s = hcnt[0] % 2
        hcnt[0] += 1
        dma('sp', hb[s][:, :, 0:n], hT_v[:, :, t0:t0 + n], [('hTd', t0 // 512)], [('hb', s)], ('hbld', s))
        return s

    pcnt = [0]

    def pbank():
        pcnt[0] += 1
        return pcnt[0] % 2

    with ExitStack() as s0:
        xt = [s0.enter_context(nc.sbuf_tensor("xt%d" % i, [128, 16, 512], F32)) for i in range(2)]
        sq = s0.enter_context(nc.sbuf_tensor("sq", [128, 16, 512], BF16))
        rs = s0.enter_context(nc.sbuf_tensor("rs0", [128, 512], F32))
        for t in range(NT):
            xs = t % 2
            hs = t % 2
            dma('sp', xt[xs][:], xw_v[:, :, t * 512:(t + 1) * 512], (), [('xt', xs)], ('xt', xs))
            act(sq[:], xt[xs][:], AF.Square, [('xt', xs)], ['sq'])
            b = pbank()
            for k in range(16):
                mm(ps[b][:, :], cb[:, ONES, :], sq[:, k, :], k == 0, k == 15, ['sq', 'cb'], [('ps', b)])
            act(rs[:], ps[b][:, :], AF.Sqrt, [('ps', b)], ['rs'], scale=1.0 / D, bias=EPS)
            recip(rs[:], rs[:], ['rs'], ['rs'])
            for k in range(16):
                stt('dve', hb[hs][:, k, :], xt[xs][:, k, :], gains[:, 0, k:k + 1], rs[:], ALU.mult, ALU.mult,
                    [('xt', xs), 'rs', 'gains'], [('hb', hs)])
            dma('sp', hT_v[:, :, t * 512:(t + 1) * 512], hb[hs][:], [('hb', hs)], [('hTd', t)], ('hbst', hs))
        hcnt[0] = 0

    cjobs = []
    for c in range(16):
        cjobs += [("W2A", c, 6144), ("W2B", c, 4096)]
    cjobs += [("WO", c, 2048) for c in range(16)] + [("WU", fc, 4096) for fc in range(NFC)]
    cjobs += [("WD", c, NFC * 128) for c in range(16)]
    cj = [0]

    def conv_job():
        if cj[0] >= len(cjobs):
            return
        name, idx, ncols = cjobs[cj[0]]
        slot = cj[0] % 2
        cj[0] += 1
        dma('pool', wsl_g[slot][:, 0:ncols], SRC[name][idx, :, :], (), [('wsl', slot)], ('wsl', slot), cast=True)
        dma('sp', SCR[name][idx, :, :], wsl_g[slot][:, 0:ncols], [('wsl', slot)], [('scr', name, idx)], 'wscst')


    P.barrier()
    with ExitStack() as sa:
        def sba(name, shape, dt=F32):
            return sa.enter_context(nc.sbuf_tensor(name, list(shape), dt))
        wq = [sba("wq%d" % i, [128, 16, 384], BF16) for i in range(2)]
        kT = [sba("kT%d" % i, [128, WIN], BF16) for i in range(2)]
        vv = [sba("vv%d" % i, [128, 64, 128], BF16) for i in range(2)]
        qT = [sba("qT%d" % i, [128, NQ], BF16) for i in range(2)]
        eb = [[sba("eb%d_%d" % (st, i), [128, 512], F32) for i in range(2)] for st in range(2)]
        spb = [[sba("spb%d_%d" % (st, i), [128, 512], BF16) for i in range(2)] for st in range(2)]
        gb = [[sba("gb%d_%d" % (st, i), [128, 512], F32) for i in range(2)] for st in range(2)]
        wb = [[sba("wb%d_%d" % (st, i), [128, 512], BF16) for i in range(2)] for st in range(2)]
        ob = [sba("ob%d" % i, [128, 512], BF16) for i in range(2)]
        vtb = [sba("vtb%d" % i, [128, 512], BF16) for i in range(2)]
        SBANKS = [(2, 4, 5), (3, 6, 7)]
        PAIRS = [[(1664, 512), (1152, 512)], [(640, 512), (128, 512)], [(0, 128)]]
        ocnt = 0
        def capture(fn):
            saved = P.ops
            P.ops = []
            fn()
            out = P.ops
            P.ops = saved
            return out

        def proj_chunks(h):
            ws = h % 2
            chunks = []
            hs_box = [0]

            def part_k(t):
                if t == 0:
                    dma('pool', wq[ws][:].rearrange("p k c -> p (k c)"), WA[h, :, :], (), [('wq', ws)], ('wq', ws),
                        cast=True)
                hs_box[0] = load_h(t * 512)
                hs = hs_box[0]
                b = pbank()
                for k in range(16):
                    mm(ps[b][:, :], wq[ws][:, k, 128:256], hb[hs][:, k, :], k == 0, k == 15,
                       [('wq', ws), ('hb', hs)], [('ps', b)])
                cp('dve', kT[ws][:, t * 512:(t + 1) * 512], ps[b][:, :], [('ps', b)], [('kT', ws, t)])

            def part_v(t):
                hs = hs_box[0]
                b = pbank()
                for k in range(16):
                    mm(ps[b][:, :], wq[ws][:, k, 256:384], hb[hs][:, k, :], k == 0, k == 15,
                       [('wq', ws), ('hb', hs)], [('ps', b)])
                vs_ = t % 2
                cp('dve', vtb[vs_][:, :], ps[b][:, :], [('ps', b)], [('vtb', vs_)])
                b = pbank()
                pvt = ps[b][:, 0:256].bitcast(BF16)
                for s in range(4):
                    tr(pvt[:, s * 128:(s + 1) * 128], vtb[vs_][:, s * 128:(s + 1) * 128], cb[:, IDN, :],
                       [('vtb', vs_), 'cb'], [('ps', b)])
                cp('dve', vv[ws][:, t * 4:(t + 1) * 4, :].rearrange("p a b -> p (a b)"), pvt,
                   [('ps', b)], [('vv', ws, t)])

            def part_q(t):
                hs = hs_box[0]
                c0 = 384 if t == 11 else 0
                n = 512 - c0
                qc0 = t * 512 + c0 - Q0L
                b = pbank()
                for k in range(16):
                    mm(ps[b][:, 0:n], wq[ws][:, k, 0:128], hb[hs][:, k, c0:512], k == 0, k == 15,
                       [('wq', ws), ('hb', hs)], [('ps', b)])
                cp('dve', qT[ws][:, qc0:qc0 + n], ps[b][:, 0:n], [('ps', b)], [('qT', ws)])

            for t in range(NT):
                chunks.append(capture(lambda: part_k(t)))
                chunks.append(capture(lambda: part_v(t)))
                if t >= 11:
                    chunks.append(capture(lambda: part_q(t)))
            return chunks

        for ch in proj_chunks(0):
            P.ops.extend(ch)
        for h in range(16):
            ws = h % 2
            nxt = proj_chunks(h + 1) if h + 1 < 16 else []
            tot_rounds = sum(max((Q0L + q0 + nq) // 128 for (q0, nq) in pair) for pair in PAIRS)
            every = max(1, tot_rounds // (len(nxt) + 1)) if nxt else 0
            rounds_done = 0
            if os.environ.get("KDEBUG") and h == 0:
                dk_ = nc.dram_tensor("dbg_k", [128, WIN], BF16, kind="ExternalOutput").ap()
                dv_ = nc.dram_tensor("dbg_v", [128, WIN], BF16, kind="ExternalOutput").ap()
                dq_ = nc.dram_tensor("dbg_q", [128, NQ], BF16, kind="ExternalOutput").ap()
                dma('sp', dk_[:, :], kT[0][:, :], [('kT', 0, t) for t in range(16)], ['dbgk'], 'dbg')
                dma('sp', dv_[:, :], vv[0][:].rearrange("p a b -> p (a b)"), [('vv', 0, t) for t in range(16)], ['dbgv'], 'dbg')
                dma('sp', dq_[:, :], qT[0][:, :], [('qT', 0)], ['dbgq'], 'dbg')
                dw_ = nc.dram_tensor("dbg_w", [128, 6144], BF16, kind="ExternalOutput").ap()
                dma('sp', dw_[:, :], wq[0][:].rearrange("p k c -> p (k c)"), [('wq', 0)], ['dbgw'], 'dbg')
            for pair in PAIRS:
                sts = []
                for si, (q0, nq) in enumerate(pair):
                    gq0 = Q0L + q0
                    sts.append(dict(si=si, q0=q0, nq=nq, gq0=gq0, banks=SBANKS[si],
                                    blocks=list(range((gq0 + nq) // 128 - 1, -1, -1))))

                def geom(st, kb):
                    m = kb - st['gq0'] // 128
                    return m >= 0, 128 * max(m, 0)

                def zmm(st, i):
                    kb = st['blocks'][i]
                    _, c0 = geom(st, kb)
                    zb = st['banks'][0]
                    q0, nq = st['q0'], st['nq']
                    mm(ps[zb][:, c0:nq], kT[ws][:, kb * 128:(kb + 1) * 128], qT[ws][:, q0 + c0:q0 + nq], True, True,
                       [('kT', ws, kb // 4), ('qT', ws)], [('ps', zb)])
                for st in sts:
                    zmm(st, 0)
                for i in range(max(len(st['blocks']) for st in sts)):
                    live = [st for st in sts if i < len(st['blocks'])]
                    s_ = i % 2
                    for st in live:
                        si, nq = st['si'], st['nq']
                        zb = st['banks'][0]
                        diag, c0 = geom(st, st['blocks'][i])
                        act(eb[si][s_][:, c0:nq], ps[zb][:, c0:nq], AF.Exp, [('ps', zb)], [('eb', si, s_)], scale=SCALE)
                        if diag:
                            tt('dve', eb[si][s_][:, c0:c0 + 128], eb[si][s_][:, c0:c0 + 128], cf[:, LT, :], ALU.mult,
                               [('eb', si, s_), 'cf'], [('eb', si, s_)])
                        act(spb[si][s_][:, c0:nq], eb[si][s_][:, c0:nq], AF.Ln, [('eb', si, s_)], [('spb', si, s_)], bias=1.0)
                    for st in live:
                        if i + 1 < len(st['blocks']):
                            zmm(st, i + 1)
                    for st in live:
                        si, nq = st['si'], st['nq']
                        xb = st['banks'][1]
                        _, c0 = geom(st, st['blocks'][i])
                        mm(ps[xb][:, c0:nq], cb[:, GE, :], spb[si][s_][:, c0:nq], i == 0, False,
                           [('spb', si, s_), 'cb'], [('ps', xb)])
                    for st in live:
                        si, nq = st['si'], st['nq']
                        xb = st['banks'][1]
                        _, c0 = geom(st, st['blocks'][i])
                        act(gb[si][s_][:, c0:nq], ps[xb][:, c0:nq], AF.Exp, [('ps', xb)], [('gb', si, s_)], scale=-1.0)
                    for st in live:
                        si, nq = st['si'], st['nq']
                        _, c0 = geom(st, st['blocks'][i])
                        tt('dve', wb[si][s_][:, c0:nq], eb[si][s_][:, c0:nq], gb[si][s_][:, c0:nq], ALU.mult,
                           [('eb', si, s_), ('gb', si, s_)], [('wb', si, s_)])
                    for st in live:
                        si, nq = st['si'], st['nq']
                        xb, obk = st['banks'][1], st['banks'][2]
                        kb = st['blocks'][i]
                        _, c0 = geom(st, kb)
                        last = i == len(st['blocks']) - 1
                        mm(ps[obk][:, c0:nq], vv[ws][:, kb, :], wb[si][s_][:, c0:nq], i == 0, last,
                           [('wb', si, s_), ('vv', ws, kb // 4)], [('ps', obk)])
                        mm(ps[xb][:, c0:nq], cb[:, LT, :], spb[si][s_][:, c0:nq], False, last,
                           [('spb', si, s_), 'cb'], [('ps', xb)])
                    rounds_done += 1
                    if rounds_done % 4 == 2:
                        conv_job()
                    if nxt and len(pair) == 1:
                        P.ops.extend(nxt.pop(0))
                for st in sts:
                    q0, nq, obk = st['q0'], st['nq'], st['banks'][2]
                    os_ = ocnt % 2
                    ocnt += 1
                    cp('dve', ob[os_][:, 0:nq], ps[obk][:, 0:nq], [('ps', obk)], [('ob', os_)])
                    dma('sp', osb_d[h * 128:(h + 1) * 128, q0:q0 + nq], ob[os_][:, 0:nq], [('ob', os_)], ['osb_d'],
                        ('obst', os_))
            while nxt:
                P.ops.extend(nxt.pop(0))

    P.barrier()
    with ExitStack() as sbx:
        def sbb(name, shape, dt=F32):
            return sbx.enter_context(nc.sbuf_tensor(name, list(shape), dt))
        wB = sbb("wB", [128, 16, 1288], BF16)
        pre = [sbb("pre%d" % i, [128, 6, 515]) for i in range(2)]
        dtT = sbb("dtT", [8, 512])
        acc = sbb("acc", [128, 6, 512])
        xa = sbb("xa", [128, 4, 512])
        BT = sbb("BT", [128, 512], BF16)
        CT = sbb("CT", [128, 512], BF16)
        nrm = sbb("nrm", [128, 512])
        S = sbb("S", [128, 512])
        Sb = sbb("Sb", [128, 512], BF16)
        dts = [sbb("dts%d" % i, [128, 128]) for i in range(2)]
        Eb = [sbb("Eb%d" % i, [128, 96]) for i in range(2)]
        rseg = [sbb("rseg%d" % i, [128, 1024]) for i in range(2)]
        dec = [sbb("dec%d" % i, [128, 1024]) for i in range(2)]
        Gb = [sbb("Gb%d" % i, [128, 1024], BF16) for i in range(2)]
        CBm = [sbb("CBm%d" % i, [128, 128]) for i in range(2)]
        xp = [sbb("xp%d" % i, [128, 512], BF16) for i in range(2)]
        xpp = [sbb("xpp%d" % i, [128, 512], BF16) for i in range(2)]
        xd = [sbb("xd%d" % i, [128, 512]) for i in range(2)]
        Btok = [sbb("Btok%d" % i, [128, 128], BF16) for i in range(2)]
        zs = [sbb("zs%d" % i, [128, 512]) for i in range(2)]
        y1 = [sbb("y1%d" % i, [128, 512]) for i in range(2)]
        y2 = [sbb("y2%d" % i, [128, 512]) for i in range(2)]
        junk = sbb("junk", [128, 512])
        st1 = [sbb("st1%d" % i, [128, 2]) for i in range(2)]
        yn = [sbb("yn%d" % i, [128, 512], BF16) for i in range(2)]
        yTs = [sbb("yTs%d" % i, [128, 4, 128], BF16) for i in range(2)]

        def v3(ap, h):
            return ap.rearrange("p (h l) -> p h l", h=h)

        def captureB(fn):
            saved = P.ops
            P.ops = []
            fn()
            out = P.ops
            P.ops = saved
            return out

        for g in range(8):
            dma('pool', wB[:].rearrange("p k c -> p (k c)"), WB[g, :, :], (), ['wB'], 'wB', cast=True)
            dma('sp', nrm[:], ssdn_d[:, g * 512:(g + 1) * 512], (), ['nrm'], 'nrm')
            mset('dve', S[:], 0.0, ['S'])
            mset('dve', Sb[:], 0.0, ['Sb'])
            hs_of = {}

            def proj_parts(t):
                parts = []
                pr = pre[t % 2]
                kp = ('pre', t % 2)

                def first():
                    hs_of[t] = load_h(t * 512)
                    if t == 0:
                        mset('dve', pr[:, :, 0:3], 0.0, [kp])
                    else:
                        cp('dve', pr[:, :, 0:3], pre[(t - 1) % 2][:, :, 512:515], [('pre', (t - 1) % 2)], [kp])

                def chunk(c):
                    hs = hs_of[t]
                    b = pbank()
                    w0 = c * 128 if c < 4 else 512 + (c - 4) * 128
                    for k in range(16):
                        mm(ps[b][:, :], wB[:, k, w0:w0 + 128], hb[hs][:, k, :], k == 0, k == 15,
                           ['wB', ('hb', hs)], [('ps', b)])
                    cp('act', pr[:, c, 3:515], ps[b][:, :], [('ps', b)], [kp])

                def dtpart():
                    hs = hs_of[t]
                    b = pbank()
                    for k in range(16):
                        mm(ps[b][0:8, :], wB[:, k, 1280:1288], hb[hs][:, k, :], k == 0, k == 15,
                           ['wB', ('hb', hs)], [('ps', b)])
                    cp('act', dtT[:, :], ps[b][0:8, :], [('ps', b)], ['dtT'])
                parts.append(captureB(lambda: (first(), chunk(0))))
                for c in range(1, 6 if t >= 10 else 5):
                    parts.append(captureB(lambda: chunk(c)))
                parts.append(captureB(dtpart))
                return parts

            for part in proj_parts(0):
                P.ops.extend(part)
            for t in range(NT):
                hs = hs_of[t]
                pr = pre[t % 2]
                kp = ('pre', t % 2)
                nxt = proj_parts(t + 1) if t + 1 < NT else []
                tq = t % 2
                D_ = dts[tq]
                kD = ('dts', tq)
                for s in range(4):
                    mm(ps[3][:, s * 8:(s + 1) * 8], dtT[:, s * 128:(s + 1) * 128], cf[0:8, IDN, 0:8], True, True,
                       ['dtT', 'cf'], [('ps', 3)])
                tmp4 = D_[:, 64:96].rearrange("p (s h) -> p s h", s=4)
                dt4 = D_[:, 0:32].rearrange("p (s h) -> p s h", s=4)
                la4 = D_[:, 32:64].rearrange("p (s h) -> p s h", s=4)
                tt('dve', tmp4, ps[3][:, 0:32].rearrange("p (s h) -> p s h", s=4),
                   hvec[:, 0, g * 8:(g + 1) * 8].unsqueeze(1).to_broadcast([128, 4, 8]), ALU.add,
                   [('ps', 3), 'hvec'], [kD])
                act(D_[:, 64:96], D_[:, 64:96], AF.Exp, [kD], [kD])
                act(D_[:, 64:96], D_[:, 64:96], AF.Ln, [kD], [kD], bias=1.0)
                tt('dve', dt4, tmp4, valid[:, t * 4:(t + 1) * 4].unsqueeze(2).to_broadcast([128, 4, 8]), ALU.mult,
                   [kD, 'valid'], [kD])
                tt('dve', la4, dt4, aneg[:, g * 8:(g + 1) * 8].unsqueeze(1).to_broadcast([128, 4, 8]), ALU.mult,
                   [kD, 'aneg'], [kD])
                mm(ps[3][:, 32:64], cf[:, LE, :], D_[:, 32:64], True, True, [kD, 'cf'], [('ps', 3)])
                mm(ps[3][:, 64:96], cf[:, GT, :], D_[:, 32:64], True, True, [kD, 'cf'], [('ps', 3)])
                mm(ps[3][:, 96:128], cf[:, ONES, :], D_[:, 32:64], True, True, [kD, 'cf'], [('ps', 3)])
                act(Eb[tq][:], ps[3][:, 32:128], AF.Exp, [('ps', 3)], [('Eb', tq)])
                tt('dve', D_[:, 96:128], D_[:, 0:32], Eb[tq][:, 32:64], ALU.mult, [kD, ('Eb', tq)], [kD])
                for c in range(6 if t >= 10 else 5):
                    ch = g * 4 + c if c < 4 else (32 + g if c == 4 else 40 + g)
                    eng = 'dve'
                    ts(eng, acc[:, c, :], pr[:, c, 0:512], sconvw[:, ch, 0:1], sconvb[:, ch:ch + 1], ALU.mult, ALU.add,
                       [kp, 'sconv'], [('acc', c)])
                    for kk in range(1, 4):
                        stt(eng, acc[:, c, :], pr[:, c, kk:kk + 512], sconvw[:, ch, kk:kk + 1], acc[:, c, :],
                            ALU.mult, ALU.add, [kp, 'sconv', ('acc', c)], [('acc', c)])
                    if c < 4:
                        act(xa[:, c, :], acc[:, c, :], AF.Silu, [('acc', c)], ['xa'])
                    elif c == 4:
                        act(BT[:], acc[:, c, :], AF.Silu, [('acc', c)], ['BT'])
                    else:
                        act(CT[:], acc[:, c, :], AF.Silu, [('acc', c)], ['CT'])
                for s in range(4):
                    ci = t * 4 + s
                    outc = ci >= 47
                    q = ci % 2
                    cs = slice(s * 128, (s + 1) * 128)
                    hsl = slice(s * 8, (s + 1) * 8)
                    dt_, la_, dtE_ = D_[:, 0:32][:, hsl], D_[:, 32:64][:, hsl], D_[:, 96:128][:, hsl]
                    Ecs, Etot = Eb[tq][:, 0:32][:, hsl], Eb[tq][:, 64:96][:, hsl]
                    if outc:
                        tt('dve', v3(rseg[q][:], 8), cf[:, LE, :].unsqueeze(1).to_broadcast([128, 8, 128]),
                           la_.unsqueeze(2).to_broadcast([128, 8, 128]), ALU.mult, [kD, 'cf'], [('rseg', q)])
                        for hh in range(2):
                            mm(ps[4 + hh][:, :], cf[:, GT, :], rseg[q][:, hh * 512:(hh + 1) * 512], True, True,
                               [('rseg', q), 'cf'], [('ps', 4 + hh)])
                            act(dec[q][:, hh * 512:(hh + 1) * 512], ps[4 + hh][:, :], AF.Exp, [('ps', 4 + hh)], [('dec', q)])
                    for c in range(4):
                        tr(ps[6][:, c * 128:(c + 1) * 128], xa[:, c, cs], cf[:, IDN, :], ['xa', 'cf'], [('ps', 6)])
                    x3 = v3(ps[6][:, :], 8)
                    tt('dve', v3(xp[q][:], 8), x3, dt_.unsqueeze(2).to_broadcast([128, 8, 64]), ALU.mult,
                       [('ps', 6), kD], [('xp', q)])
                    tt('dve', v3(xpp[q][:], 8), x3, dtE_.unsqueeze(2).to_broadcast([128, 8, 64]), ALU.mult,
                       [('ps', 6), kD], [('xpp', q)])
                    if outc:
                        tt('dve', v3(xd[q][:], 8), x3,
                           hvec[:, 2, g * 8:(g + 1) * 8].unsqueeze(2).to_broadcast([128, 8, 64]), ALU.mult,
                           [('ps', 6), 'hvec'], [('xd', q)])
                    pbb = ps[7][:, 0:64].bitcast(BF16)
                    tr(pbb, BT[:, cs], cb[:, IDN, :], ['BT', 'cb'], [('ps', 7)])
                    cp('act', Btok[q][:], pbb, [('ps', 7)], [('Btok', q)])
                    if outc:
                        col0 = ci * 128 - Q0L
                        mm(ps[7][:, 128:256], BT[:, cs], CT[:, cs], True, True, ['BT', 'CT'], [('ps', 7)])
                        tt('dve', CBm[q][:], ps[7][:, 128:256], cf[:, LE, :], ALU.mult, [('ps', 7), 'cf'], [('CBm', q)])
                        tt('pool', v3(Gb[q][:], 8), v3(dec[q][:], 8), CBm[q][:].unsqueeze(1).to_broadcast([128, 8, 128]),
                           ALU.mult, [('dec', q), ('CBm', q)], [('Gb', q)])
                        for hh in range(8):
                            mm(ps[0][:, hh * 64:(hh + 1) * 64], Gb[q][:, hh * 128:(hh + 1) * 128], xp[q][:, hh * 64:(hh + 1) * 64],
                               True, True, [('Gb', q), ('xp', q)], [('ps', 0)])
                        mm(ps[1][:, :], CT[:, cs], Sb[:], True, True, ['CT', 'Sb'], [('ps', 1)])
                        tt('dve', v3(y1[q][:], 8), v3(ps[1][:, :], 8), Ecs.unsqueeze(2).to_broadcast([128, 8, 64]), ALU.mult,
                           [('ps', 1), ('Eb', tq)], [('y1', q)])
                        tt('dve', y1[q][:], y1[q][:], ps[0][:, :], ALU.add, [('y1', q), ('ps', 0)], [('y1', q)])
                        tt('pool', y1[q][:], y1[q][:], xd[q][:], ALU.add, [('y1', q), ('xd', q)], [('y1', q)])
                        for k in range(16):
                            mm(ps[2][:, :], hb[hs][:, k, cs], wB[:, k, 768:1280], k == 0, k == 15,
                               ['wB', ('hb', hs)], [('ps', 2)])
                        act(zs[q][:], ps[2][:, :], AF.Silu, [('ps', 2)], [('zs', q)])
                        tt('pool', y2[q][:], y1[q][:], zs[q][:], ALU.mult, [('y1', q), ('zs', q)], [('y2', q)])
                        act(junk[:], y2[q][:], AF.Square, [('y2', q)], ['junk', ('st1', q)], accum=st1[q][:, 0:1])
                        act(st1[q][:, 1:2], st1[q][:, 0:1], AF.Sqrt, [('st1', q)], [('st1', q)], scale=1.0 / 512, bias=EPS)
                        recip(st1[q][:, 1:2], st1[q][:, 1:2], [('st1', q)], [('st1', q)])
                        stt('dve', yn[q][:], y2[q][:], st1[q][:, 1:2], nrm[:], ALU.mult, ALU.mult,
                            [('y2', q), ('st1', q), 'nrm'], [('yn', q)])
                        pyt = ps[7][:, 256:512].bitcast(BF16)
                        for c in range(4):
                            tr(pyt[:, c * 128:(c + 1) * 128], yn[q][:, c * 128:(c + 1) * 128], cb[:, IDN, :],
                               [('yn', q), 'cb'], [('ps', 7)])
                        cp('act', yTs[q][:].rearrange("p a b -> p (a b)"), pyt, [('ps', 7)], [('yTs', q)])
                        dma('sp', y_v[:, g * 4:(g + 1) * 4, col0:col0 + 128], yTs[q][:], [('yTs', q)], ['y_d'], ('yst', q))
                    mm(ps[2][:, :], Btok[q][:], xpp[q][:], True, True, [('Btok', q), ('xpp', q)], [('ps', 2)])
                    tt('dve', v3(S[:], 8), v3(S[:], 8), Etot.unsqueeze(2).to_broadcast([128, 8, 64]), ALU.mult,
                       ['S', ('Eb', tq)], ['S'])
                    tt('dve', S[:], S[:], ps[2][:, :], ALU.add, ['S', ('ps', 2)], ['S'])
                    cp('pool', Sb[:], S[:], ['S'], ['Sb'])
                    take = 2 if s < 3 else len(nxt)
                    for _ in range(min(take, len(nxt))):
                        P.ops.extend(nxt.pop(0))

    P.barrier()
    with ExitStack() as sc:
        def sbc(name, shape, dt=F32):
            return sc.enter_context(nc.sbuf_tensor(name, list(shape), dt))
        R1 = sbc("R1", [128, 48, 512], BF16)
        R4 = sbc("R4", [128, 8192])
        R5 = sbc("R5", [128, 16, 512])
        wsl = wsl_g + [sbc("wsl2", [128, 48 * 128], BF16)]
        sqb = [sbc("sqb%d" % i, [128, 512], BF16) for i in range(2)]
        rsb = sbc("rsb", [128, 512])
        tmpc = [sbc("tmpc%d" % i, [128, 512]) for i in range(2)]
        ug = [sbc("ug%d" % i, [128, 514]) for i in range(1)]
        uv = [sbc("uv%d" % i, [128, 514]) for i in range(1)]
        carry = sbc("carry", [128, 88, 2])
        merged = R4[:, 0:4096].bitcast(BF16).rearrange("p (k t) -> p k t", k=16)
        xt4 = R4[:].rearrange("p (k t) -> p k t", k=16)
        h2 = hb[0]
        wcnt = [0]

        def wslot():
            wcnt[0] += 1
            return wcnt[0] % 3

        def wload(slot, name, idx, ncols, first):
            if first:
                dma('pool', wsl[slot][:, 0:ncols], SRC[name][idx, :, :], (), [('wsl', slot)], ('wsl', slot), cast=True)
                dma('sp', SCR[name][idx, :, :], wsl[slot][:, 0:ncols], [('wsl', slot)], [('scr', name, idx)], 'wscst')
            else:
                dma('sp', wsl[slot][:, 0:ncols], SCR[name][idx, :, :], [('scr', name, idx)], [('wsl', slot)], ('wsl', slot))

        mset('dve', carry[:], 0.0, ['carry'])
        groups = [(0, 128)] + [(128 + 512 * i, 512) for i in range(4)]
        for gi, (c0g, n) in enumerate(groups):
            l0 = Q0L + c0g
            dma('sp', R1[:, 0:16, 0:n], osb_v[:, :, c0g:c0g + n], ['osb_d'], [('R1', k) for k in range(16)], 'ldo')
            dma('sp', R1[:, 16:48, 0:n], y_v[:, :, c0g:c0g + n], ['y_d'], [('R1', k) for k in range(16, 48)], 'ldy')
            dma('sp', hb[1][:, :, 0:n], hT_v[:, :, l0:l0 + n], [('hTd', l0 // 512)], [('hb', 1)], 'ldh')
            for c in range(16):
                s1, s2 = wslot(), wslot()
                wload(s1, "W2A", c, 6144, False)
                wload(s2, "W2B", c, 4096, False)
                for k in range(16):
                    mm(ps[0][:, 0:n], wsl[s1][:, k * 128:(k + 1) * 128], R1[:, k, 0:n], k == 0, k == 15,
                       [('wsl', s1), ('R1', k)], [('ps', 0)])
                for k in range(32):
                    mm(ps[1][:, 0:n], wsl[s2][:, k * 128:(k + 1) * 128], R1[:, 16 + k, 0:n], k == 0, k == 31,
                       [('wsl', s2), ('R1', 16 + k)], [('ps', 1)])
                for k in range(16):
                    mm(ps[2][:, 0:n], wsl[s1][:, (16 + k) * 128:(17 + k) * 128], hb[1][:, k, 0:n], k == 0, k == 15,
                       [('wsl', s1), ('hb', 1)], [('ps', 2)])
                for k in range(16):
                    mm(ps[3][:, 0:n], wsl[s1][:, (32 + k) * 128:(33 + k) * 128], hb[1][:, k, 0:n], k == 0, k == 15,
                       [('wsl', s1), ('hb', 1)], [('ps', 3)])
                a_, b_ = R5[:, 14, :], R5[:, 15, :]
                ka, kb_ = ('R5', 14), ('R5', 15)
                act(a_[:, 0:n], ps[2][:, 0:n], AF.Sigmoid, [('ps', 2)], [ka])
                act(b_[:, 0:n], ps[3][:, 0:n], AF.Sigmoid, [('ps', 3)], [kb_])
                tt('dve', a_[:, 0:n], a_[:, 0:n], ps[0][:, 0:n], ALU.mult, [ka, ('ps', 0)], [ka])
                tt('dve', b_[:, 0:n], b_[:, 0:n], ps[1][:, 0:n], ALU.mult, [kb_, ('ps', 1)], [kb_])
                tt('pool', merged[:, c, 0:n], a_[:, 0:n], b_[:, 0:n], ALU.add, [ka, kb_], [('R4', c // 2)])
            for c in range(16):
                s1 = wslot()
                wload(s1, "WO", c, 2048, False)
                b = pbank()
                for k in range(16):
                    mm(ps[b][:, 0:n], wsl[s1][:, k * 128:(k + 1) * 128], merged[:, k, 0:n], k == 0, k == 15,
                       [('wsl', s1), ('R4', k // 2)], [('ps', b)])
                cp('act', R5[:, c, 0:n], ps[b][:, 0:n], [('ps', b)], [('R5', c)])
                act(sqb[c % 2][:, 0:n], ps[b][:, 0:n], AF.Square, [('ps', b)], [('sqb', c % 2)])
                mm(ps[4][:, 0:n], cb[:, ONES, :], sqb[c % 2][:, 0:n], c == 0, c == 15, [('sqb', c % 2), 'cb'], [('ps', 4)])
            act(rsb[:, 0:n], ps[4][:, 0:n], AF.Sqrt, [('ps', 4)], ['rsb'], scale=1.0 / D, bias=EPS)
            recip(rsb[:, 0:n], rsb[:, 0:n], ['rsb'], ['rsb'])
            dma('sp', xt4[:, :, 0:n], xw_v[:, :, l0:l0 + n], (), [('R4', k) for k in range(16)], 'ldx')
            for c in range(16):
                stt('dve', tmpc[c % 2][:, 0:n], R5[:, c, 0:n], gains[:, 1, c:c + 1], rsb[:, 0:n], ALU.mult, ALU.mult,
                    [('R5', c), 'rsb', 'gains'], [('tmpc', c % 2)])
                tt('pool', xt4[:, c, 0:n], xt4[:, c, 0:n], tmpc[c % 2][:, 0:n], ALU.add, [('R4', c), ('tmpc', c % 2)],
                   [('R4', c)])
            for c in range(16):
                act(sqb[c % 2][:, 0:n], xt4[:, c, 0:n], AF.Square, [('R4', c)], [('sqb', c % 2)])
                mm(ps[4][:, 0:n], cb[:, ONES, :], sqb[c % 2][:, 0:n], c == 0, c == 15, [('sqb', c % 2), 'cb'], [('ps', 4)])
            act(rsb[:, 0:n], ps[4][:, 0:n], AF.Sqrt, [('ps', 4)], ['rsb'], scale=1.0 / D, bias=EPS)
            recip(rsb[:, 0:n], rsb[:, 0:n], ['rsb'], ['rsb'])
            for c in range(16):
                stt('dve', h2[:, c, 0:n], xt4[:, c, 0:n], gains[:, 2, c:c + 1], rsb[:, 0:n], ALU.mult, ALU.mult,
                    [('R4', c), 'rsb', 'gains'], [('hb', 0)])
            for fc in range(NFC):
                s1 = wslot()
                wload(s1, "WU", fc, 4096, False)
                q = 0
                for half, (pb_, ub, kq) in enumerate(((0, ug[q], ('ug', q)), (1, uv[q], ('uv', q)))):
                    for k in range(16):
                        mm(ps[pb_][:, 0:n], wsl[s1][:, k * 256 + half * 128:k * 256 + half * 128 + 128], h2[:, k, 0:n],
                           k == 0, k == 15, [('wsl', s1), ('hb', 0)], [('ps', pb_)])
                    cch = fc + half * NFC
                    cp('dve', ub[:, 0:2], carry[:, cch, :], ['carry'], [kq])
                    cp('act', ub[:, 2:2 + n], ps[pb_][:, 0:n], [('ps', pb_)], [kq])
                    if gi == 0:
                        ts1('dve', carry[:, cch, :], ub[:, n:n + 2], hval[:, 0:1], ALU.mult, [kq, 'hval'], ['carry'])
                    else:
                        cp('dve', carry[:, cch, :], ub[:, n:n + 2], [kq], ['carry'])
                    if gi > 0:
                        dst, kd = (R5[:, 0, :], ('R5', 0)) if half == 0 else (R5[:, 1, :], ('R5', 1))
                        eng = 'dve'
                        ts(eng, dst[:, 0:n], ub[:, 0:n], fconvw[:, cch, 0:1], fconvb[:, cch:cch + 1], ALU.mult, ALU.add,
                           [kq, 'fconv'], [kd])
                        stt(eng, dst[:, 0:n], ub[:, 1:n + 1], fconvw[:, cch, 1:2], dst[:, 0:n], ALU.mult, ALU.add,
                            [kq, 'fconv', kd], [kd])
                        stt(eng, dst[:, 0:n], ub[:, 2:n + 2], fconvw[:, cch, 2:3], dst[:, 0:n], ALU.mult, ALU.add,
                            [kq, 'fconv', kd], [kd])
                if gi > 0:
                    G_, V_, T_ = R5[:, 0, :], R5[:, 1, :], R5[:, 2 + fc % 2, :]
                    kT_ = ('R5', 2 + fc % 2)
                    tt('pool', T_[:, 0:n], G_[:, 0:n], G_[:, 0:n], ALU.mult, [('R5', 0)], [kT_])
                    ts('dve', T_[:, 0:n], T_[:, 0:n], 0.044715, 1.0, ALU.mult, ALU.add, [kT_], [kT_])
                    tt('pool', T_[:, 0:n], T_[:, 0:n], G_[:, 0:n], ALU.mult, [kT_, ('R5', 0)], [kT_])
                    act(T_[:, 0:n], T_[:, 0:n], AF.Sigmoid, [kT_], [kT_], scale=1.5957691216057308)
                    tt('dve', T_[:, 0:n], T_[:, 0:n], G_[:, 0:n], ALU.mult, [kT_, ('R5', 0)], [kT_])
                    tt('pool', R1[:, fc, 0:n], T_[:, 0:n], V_[:, 0:n], ALU.mult, [kT_, ('R5', 1)], [('R1', fc)])
            if gi == 0:
                continue
            for c in range(16):
                s1 = wslot()
                wload(s1, "WD", c, NFC * 128, False)
                b = pbank()
                for k in range(NFC):
                    mm(ps[b][:, 0:n], wsl[s1][:, k * 128:(k + 1) * 128], R1[:, k, 0:n], k == 0, k == NFC - 1,
                       [('wsl', s1), ('R1', k)], [('ps', b)])
                cp('act', R5[:, c, 0:n], ps[b][:, 0:n], [('ps', b)], [('R5', c)])
                act(sqb[c % 2][:, 0:n], ps[b][:, 0:n], AF.Square, [('ps', b)], [('sqb', c % 2)])
                mm(ps[4][:, 0:n], cb[:, ONES, :], sqb[c % 2][:, 0:n], c == 0, c == 15, [('sqb', c % 2), 'cb'], [('ps', 4)])
            act(rsb[:, 0:n], ps[4][:, 0:n], AF.Sqrt, [('ps', 4)], ['rsb'], scale=1.0 / D, bias=EPS)
            recip(rsb[:, 0:n], rsb[:, 0:n], ['rsb'], ['rsb'])
            for c in range(16):
                stt('dve', tmpc[c % 2][:, 0:n], R5[:, c, 0:n], gains[:, 3, c:c + 1], rsb[:, 0:n], ALU.mult, ALU.mult,
                    [('R5', c), 'rsb', 'gains'], [('tmpc', c % 2)])
                tt('pool', R5[:, c, 0:n], xt4[:, c, 0:n], tmpc[c % 2][:, 0:n], ALU.add, [('R4', c), ('tmpc', c % 2)],
                   [('R5', c)])
            t0 = c0g - 128
            dma('sp', out_v[:, :, t0:t0 + n], R5[:, :, 0:n], [('R5', c) for c in range(16)], ['out'], 'stout')

        P.emit(nc, es)
    es.close()
    return nc


_NC = None


def _prep_weights(w_in, w_sb_proj, w_ssd_proj, w_out, w_up, w_down):
    def blk(w, cols):
        K = w.shape[0]
        sub = w[:, cols].reshape(K // 128, 128, len(cols))
        return np.ascontiguousarray(sub.transpose(1, 0, 2)).reshape(128, -1)
    ar = np.arange
    WA = np.stack([blk(w_in, np.concatenate([ar(h * 128, h * 128 + 128), ar(2048 + h * 128, 2048 + h * 128 + 128),
                                             ar(4096 + h * 128, 4096 + h * 128 + 128)])) for h in range(16)])
    WB = np.stack([blk(w_in, np.concatenate([ar(10240 + g * 512, 10240 + g * 512 + 512),
                                             ar(14336 + g * 128, 14336 + g * 128 + 128),
                                             ar(15360 + g * 128, 15360 + g * 128 + 128),
                                             ar(6144 + g * 512, 6144 + g * 512 + 512),
                                             ar(16384 + g * 8, 16384 + g * 8 + 8)])) for g in range(8)])
    W2A = np.stack([np.concatenate([blk(w_sb_proj, ar(c * 128, c * 128 + 128)),
                                    blk(w_in, ar(16448 + c * 128, 16448 + c * 128 + 128)),
                                    blk(w_in, ar(18496 + c * 128, 18496 + c * 128 + 128))], axis=1) for c in range(16)])
    W2B = np.stack([blk(w_ssd_proj, ar(c * 128, c * 128 + 128)) for c in range(16)])
    WO = np.stack([blk(w_out, ar(c * 128, c * 128 + 128)) for c in range(16)])
    WU = np.stack([blk(w_up, np.concatenate([ar(fc * 128, fc * 128 + 128), ar(D_FF + fc * 128, D_FF + fc * 128 + 128)]))
                   for fc in range(NFC)])
    WD = np.stack([blk(w_down, ar(c * 128, c * 128 + 128)) for c in range(16)])
    return dict(WA=WA, WB=WB, W2A=W2A, W2B=W2B, WO=WO, WU=WU, WD=WD)


def kernel(x, norm_mix_pre, w_in, ssd_conv_w, ssd_conv_b, dt_bias, a_log, d_skip, ssd_norm,
           w_sb_proj, w_ssd_proj, w_out, norm_mix_post, norm_ffn_pre, w_up, ffn_conv_w,
           ffn_conv_b, w_down, norm_ffn_post):
    global _NC
    in_maps = _in_maps(x, norm_mix_pre, w_in, ssd_conv_w, ssd_conv_b, dt_bias, a_log, d_skip, ssd_norm,
                       w_sb_proj, w_ssd_proj, w_out, norm_mix_post, norm_ffn_pre, w_up, ffn_conv_w,
                       ffn_conv_b, w_down, norm_ffn_post)
    if _NC is None:
        _NC = build_nc()
    res = run_bass_kernel_spmd(_NC, in_maps, core_ids=list(range(8)))
    out = np.empty((2, 8192, D), np.float32)
    for core in range(8):
        b, c = core // 4, core % 4
        out[b, 2048 * c:2048 * (c + 1), :] = res.results[core]["outT"].T
    return out


def _in_maps(x, norm_mix_pre, w_in, ssd_conv_w, ssd_conv_b, dt_bias, a_log, d_skip, ssd_norm,
             w_sb_proj, w_ssd_proj, w_out, norm_mix_post, norm_ffn_pre, w_up, ffn_conv_w,
             ffn_conv_b, w_down, norm_ffn_post):
    f32 = np.float32
    x = np.asarray(x, f32)
    shared = _prep_weights(np.asarray(w_in, f32)[0], np.asarray(w_sb_proj, f32)[0], np.asarray(w_ssd_proj, f32)[0],
                           np.asarray(w_out, f32)[0], np.asarray(w_up, f32)[0], np.asarray(w_down, f32)[0])
    r = np.arange(128)
    cm = np.stack([(r[:, None] < r[None, :]), (r[:, None] <= r[None, :]), (r[:, None] >= r[None, :]),
                   (r[:, None] > r[None, :]), np.ones((128, 128), bool), np.eye(128, dtype=bool)], axis=1).astype(f32)
    shared["consts"] = np.ascontiguousarray(cm.reshape(128, 768))

    def pk(v):
        return np.asarray(v, f32).reshape(-1, 128).T
    shared["gains"] = np.ascontiguousarray(np.stack([pk(norm_mix_pre[0]), pk(norm_mix_post[0]), pk(norm_ffn_pre[0]),
                                                     pk(norm_ffn_post[0])], axis=1).reshape(128, 64))
    scw = np.asarray(ssd_conv_w, f32)[0]
    shared["sconvw"] = np.ascontiguousarray(scw.reshape(4, 48, 128).transpose(2, 1, 0).reshape(128, 192))
    shared["sconvb"] = np.ascontiguousarray(pk(ssd_conv_b[0]))
    fcw = np.asarray(ffn_conv_w, f32)[0]
    shared["fconvw"] = np.ascontiguousarray(fcw.reshape(3, 88, 128).transpose(2, 1, 0).reshape(128, 264))
    shared["fconvb"] = np.ascontiguousarray(pk(ffn_conv_b[0]))
    hv = np.stack([np.asarray(dt_bias, f32)[0], np.asarray(a_log, f32)[0], np.asarray(d_skip, f32)[0]])
    shared["hvec"] = np.ascontiguousarray(np.broadcast_to(hv.reshape(1, 192), (128, 192)))
    shared["ssdn"] = np.ascontiguousarray(np.broadcast_to(np.asarray(ssd_norm, f32)[0][None, :], (128, 4096)))

    in_maps = []
    for core in range(8):
        b, c = core // 4, core % 4
        end = 2048 * (c + 1)
        start = end - WIN
        xwin = np.zeros((D, WIN), f32)
        lo = max(start, 0)
        xwin[:, lo - start:] = x[b, lo:end, :].T
        tok = start + np.arange(WIN)
        val = (tok >= 0).astype(f32).reshape(64, 128).T
        m = dict(shared)
        m["xw"] = xwin
        m["valid"] = np.ascontiguousarray(val)
        m["hval"] = np.full((128, 1), 1.0 if c > 0 else 0.0, f32)
        in_maps.append(m)
    return in_maps
```
